# Optimizing a Trainium2 kernel written in Bass

```python
import math
import jax, jax.numpy as jnp
from jax import lax
import numpy as np

D_MODEL = 1024
BATCH = 16
SEQ = 2048
DEPTH = 4
DEC_BATCH = 8
DEC_SEQ = 32
PAST_LEN = 1024

CHUNK = 64
N_EVEN = (DEPTH + 1) // 2
N_ODD = DEPTH // 2
PLE_DIM = 256
RMS_EPS = 1e-6
A_WIDTH = 512
A_CONV = 3
B_WIDTH = 512
S5_GROUP = 16
S5_GROUPS = B_WIDTH // S5_GROUP
S5_STATE = 64
S5_BLOCK = 128
C_HEADS = 8
C_HEAD_DIM = 64
C_WIDTH = C_HEADS * C_HEAD_DIM
Q_BLOCK = 128
D_HEADS = 8
D_WIDTH = 512
D_HEAD_DIM = D_WIDTH // D_HEADS
GMLP_CHUNK = 128
D_FF = 2816
FFN_CONV = 3

EVEN_IN = 3 * A_WIDTH + B_WIDTH
ODD_IN = 3 * C_WIDTH + C_HEADS + 2 * D_WIDTH
MASK_VALUE = -1e30

kernel_name = "hybrid_streaming_encoder_step"


def rmsnorm(x, g):
    xf = x.astype(jnp.float32)
    y = xf * lax.rsqrt(jnp.mean(xf * xf, axis=-1, keepdims=True) + RMS_EPS)
    return (y * g.astype(jnp.float32)).astype(x.dtype)


def causal_dwconv(x, hist, w):
    L = x.shape[1]
    xx = jnp.concatenate([hist.astype(x.dtype), x], axis=1)
    y = w[0] * xx[:, :L]
    for t in range(1, w.shape[0]):
        y = y + w[t] * xx[:, t:t + L]
    return y, xx[:, L:]


def s5_discretize(a_re, a_im, log_dt, b_re, b_im):
    a_re = a_re.astype(jnp.float32); a_im = a_im.astype(jnp.float32)
    b_re = b_re.astype(jnp.float32); b_im = b_im.astype(jnp.float32)
    dt = jnp.exp(log_dt.astype(jnp.float32))[:, None]
    mag = jnp.exp(a_re * dt)
    ab_re = mag * jnp.cos(a_im * dt)
    ab_im = mag * jnp.sin(a_im * dt)
    z_re = ab_re - 1.0
    den = a_re * a_re + a_im * a_im
    f_re = (z_re * a_re + ab_im * a_im) / den
    f_im = (ab_im * a_re - z_re * a_im) / den
    bb_re = f_re[..., None] * b_re - f_im[..., None] * b_im
    bb_im = f_re[..., None] * b_im + f_im[..., None] * b_re
    return ab_re, ab_im, bb_re, bb_im


def _linrec_combine(e1, e2):
    a1r, a1i, b1r, b1i = e1
    a2r, a2i, b2r, b2i = e2
    return (a1r * a2r - a1i * a2i, a1r * a2i + a1i * a2r,
            a2r * b1r - a2i * b1i + b2r, a2r * b1i + a2i * b1r + b2i)


def s5_mixer(u, h_re, h_im, a_re, a_im, log_dt, b_re, b_im, c_re, c_im, d_skip, w_glu):
    bt, L, _ = u.shape
    uf = u.astype(jnp.float32).reshape(bt, L, S5_GROUPS, S5_GROUP)
    ab_re, ab_im, bb_re, bb_im = s5_discretize(a_re, a_im, log_dt, b_re, b_im)
    c_re = c_re.astype(jnp.float32); c_im = c_im.astype(jnp.float32)
    d_skip = d_skip.astype(jnp.float32)

    def run(ug, hr0, hi0):
        bu_re = jnp.einsum('blgc,gpc->blgp', ug, bb_re)
        bu_im = jnp.einsum('blgc,gpc->blgp', ug, bb_im)
        bu_re = bu_re.at[:, 0].add(ab_re * hr0 - ab_im * hi0)
        bu_im = bu_im.at[:, 0].add(ab_re * hi0 + ab_im * hr0)
        a_r = jnp.broadcast_to(ab_re, bu_re.shape)
        a_i = jnp.broadcast_to(ab_im, bu_im.shape)
        _, _, hs_re, hs_im = lax.associative_scan(_linrec_combine, (a_r, a_i, bu_re, bu_im), axis=1)
        y = (jnp.einsum('blgp,gcp->blgc', hs_re, c_re)
             - jnp.einsum('blgp,gcp->blgc', hs_im, c_im) + d_skip * ug)
        return y, hs_re[:, -1], hs_im[:, -1]

    h_re = h_re.astype(jnp.float32); h_im = h_im.astype(jnp.float32)
    if L > S5_BLOCK and L % S5_BLOCK == 0:
        nb = L // S5_BLOCK
        ub = uf.reshape(bt, nb, S5_BLOCK, S5_GROUPS, S5_GROUP).swapaxes(0, 1)

        def step(carry, ug):
            y, hr, hi = run(ug, carry[0], carry[1])
            return (hr, hi), y

        (hr, hi), ys = lax.scan(step, (h_re, h_im), ub)
        y = ys.swapaxes(0, 1).reshape(bt, L, B_WIDTH)
    else:
        y, hr, hi = run(uf, h_re, h_im)
        y = y.reshape(bt, L, B_WIDTH)
    g = jax.nn.gelu(y)
    out = g * jax.nn.sigmoid(g @ w_glu.astype(jnp.float32))
    return out.astype(u.dtype), hr, hi


def fox_block(q, k, v, fq, fk, qpos, kpos):
    s = jnp.einsum('bqhd,bkhd->bhqk', q, k).astype(jnp.float32) * (C_HEAD_DIM ** -0.5)
    s = s + (jnp.transpose(fq, (0, 2, 1))[:, :, :, None] - jnp.transpose(fk, (0, 2, 1))[:, :, None, :])
    s = jnp.where((kpos[None, :] <= qpos[:, None])[None, None], s, MASK_VALUE)
    p = jax.nn.softmax(s, axis=-1)
    return jnp.einsum('bhqk,bkhd->bqhd', p.astype(v.dtype), v)


def fox_prompt(q, k, v, logf):
    bt, L = q.shape[:2]
    F = jnp.cumsum(logf.astype(jnp.float32), axis=1)
    pos = jnp.arange(L)
    nb = L // Q_BLOCK
    qb = q.reshape(bt, nb, Q_BLOCK, C_HEADS, C_HEAD_DIM).swapaxes(0, 1)
    fb = F.reshape(bt, nb, Q_BLOCK, C_HEADS).swapaxes(0, 1)
    pb = pos.reshape(nb, Q_BLOCK)
    out = lax.map(lambda a: fox_block(a[0], k, v, a[1], F, a[2], pos), (qb, fb, pb))
    return out.swapaxes(0, 1).reshape(bt, L, C_HEADS, C_HEAD_DIM)


def fox_sample(q, k, v, logf, k_past, v_past, logf_past):
    past = k_past.shape[1]
    L = q.shape[1]
    k_all = jnp.concatenate([k_past.astype(k.dtype), k], axis=1)
    v_all = jnp.concatenate([v_past.astype(v.dtype), v], axis=1)
    F = jnp.cumsum(jnp.concatenate([logf_past.astype(jnp.float32), logf.astype(jnp.float32)], axis=1), axis=1)
    qpos = past + jnp.arange(L)
    kpos = jnp.arange(past + L)
    return fox_block(q, k_all, v_all, F[:, past:], F, qpos, kpos)


def gmlp_mix(u, vd, g_v, w_s, b_s):
    bt, L, _ = u.shape
    lc = min(L, GMLP_CHUNK)
    nc = L // lc
    vn = rmsnorm(vd, g_v)
    ws = jnp.where(jnp.tril(jnp.ones((lc, lc), bool)), w_s[:, :lc, :lc], 0).astype(vn.dtype)
    vh = vn.reshape(bt, nc, lc, D_HEADS, D_HEAD_DIM)
    mixed = jnp.einsum('hts,bcshd->bcthd', ws, vh) + b_s[:, :lc].T[:, :, None].astype(vn.dtype)
    return u * mixed.reshape(bt, L, D_WIDTH), vn


def run_trunk(x, pe, st, W):
    bt, L, _ = x.shape
    prompt = st is None
    h = x
    conv_a, ssm_re, ssm_im, ks, vs, lfs, gvs, ffs = [], [], [], [], [], [], [], []
    for i in range(DEPTH):
        j = i // 2
        hn = rmsnorm(h, W['g_mix_pre'][i])
        if i % 2 == 0:
            if prompt:
                hist = jnp.zeros((bt, A_CONV - 1, A_WIDTH), x.dtype)
                h0r = jnp.zeros((bt, S5_GROUPS, S5_STATE), jnp.float32)
                h0i = jnp.zeros((bt, S5_GROUPS, S5_STATE), jnp.float32)
            else:
                hist, h0r, h0i = st['conv_a'][j], st['ssm_re'][j], st['ssm_im'][j]
            z = hn @ W['w_even_in'][j]
            gb, gc, xa, u = jnp.split(z, [A_WIDTH, 2 * A_WIDTH, 3 * A_WIDTH], axis=-1)
            yc, new_hist = causal_dwconv(gc * xa, hist, W['w_conv_a'][j])
            ya = gb * yc
            yb, nr, ni = s5_mixer(u, h0r, h0i, W['s5_a_re'][j], W['s5_a_im'][j], W['s5_log_dt'][j],
                                  W['s5_b_re'][j], W['s5_b_im'][j], W['s5_c_re'][j], W['s5_c_im'][j],
                                  W['s5_d'][j], W['w_glu'][j])
            y = jnp.concatenate([ya, yb], axis=-1) @ W['w_even_out'][j]
            conv_a.append(new_hist); ssm_re.append(nr); ssm_im.append(ni)
        else:
            z = hn @ W['w_odd_in'][j]
            q, k, v, fl, u, vd = jnp.split(
                z, [C_WIDTH, 2 * C_WIDTH, 3 * C_WIDTH, 3 * C_WIDTH + C_HEADS,
                    3 * C_WIDTH + C_HEADS + D_WIDTH], axis=-1)
            q = q.reshape(bt, L, C_HEADS, C_HEAD_DIM)
            k = k.reshape(bt, L, C_HEADS, C_HEAD_DIM)
            v = v.reshape(bt, L, C_HEADS, C_HEAD_DIM)
            logf = jax.nn.log_sigmoid((fl + W['b_forget'][j]).astype(jnp.float32))
            if prompt:
                att = fox_prompt(q, k, v, logf)
            else:
                att = fox_sample(q, k, v, logf, st['k'][j], st['v'][j], st['logf'][j])
            yd, vn = gmlp_mix(u, vd, W['g_gmlp_v'][j], W['w_spatial'][j], W['b_spatial'][j])
            y = jnp.concatenate([att.reshape(bt, L, C_WIDTH), yd], axis=-1) @ W['w_odd_out'][j]
            ks.append(k); vs.append(v); lfs.append(logf); gvs.append(vn)
        h = h + rmsnorm(y, W['g_mix_post'][i])
        hn = rmsnorm(h, W['g_ffn_pre'][i])
        fh = jnp.zeros((bt, FFN_CONV - 1, 2 * D_FF), x.dtype) if prompt else st['ffn'][i]
        up = hn @ W['w_ffn_up'][i]
        uc, new_f = causal_dwconv(up, fh, W['w_ffn_conv'][i])
        gt, val = jnp.split(uc, 2, axis=-1)
        y = (jax.nn.gelu(gt) * val) @ W['w_ffn_down'][i]
        h = h + rmsnorm(y, W['g_ffn_post'][i])
        h = h + jax.nn.sigmoid(h @ W['w_ple_gate'][i]) * (pe[i] @ W['w_ple'][i])
        ffs.append(new_f)
    new_state = {'conv_a': jnp.stack(conv_a), 'ssm_re': jnp.stack(ssm_re), 'ssm_im': jnp.stack(ssm_im),
                 'k': jnp.stack(ks), 'v': jnp.stack(vs), 'logf': jnp.stack(lfs),
                 'gmlp_v': jnp.stack(gvs), 'ffn': jnp.stack(ffs)}
    return h, new_state


def setup_inputs(seed: int = 0) -> dict:
    key = jax.random.key(seed)
    ks = iter(jax.random.split(key, 48))

    def nrm(shape, scale):
        return jax.random.normal(next(ks), shape, jnp.float32) * scale

    def gain(shape):
        return 1.0 + nrm(shape, 0.02)

    n_idx = jnp.arange(S5_STATE, dtype=jnp.float32)
    return {
        'x_prompt': nrm((BATCH, SEQ, D_MODEL), 1.0),
        'x_sample': nrm((DEC_BATCH, DEC_SEQ, D_MODEL), 1.0),
        'p_prompt': nrm((DEPTH, BATCH, SEQ, PLE_DIM), 1.0),
        'p_sample': nrm((DEPTH, DEC_BATCH, DEC_SEQ, PLE_DIM), 1.0),
        'cache_conv_a': nrm((N_EVEN, DEC_BATCH, A_CONV - 1, A_WIDTH), 1.0),
        'state_ssm_re': nrm((N_EVEN, DEC_BATCH, S5_GROUPS, S5_STATE), 0.1),
        'state_ssm_im': nrm((N_EVEN, DEC_BATCH, S5_GROUPS, S5_STATE), 0.1),
        'cache_k': nrm((N_ODD, DEC_BATCH, PAST_LEN, C_HEADS, C_HEAD_DIM), 1.0),
        'cache_v': nrm((N_ODD, DEC_BATCH, PAST_LEN, C_HEADS, C_HEAD_DIM), 1.0),
        'cache_logf': jax.nn.log_sigmoid(2.0 + nrm((N_ODD, DEC_BATCH, PAST_LEN, C_HEADS), 0.5)),
        'cache_ffn_conv': nrm((DEPTH, DEC_BATCH, FFN_CONV - 1, 2 * D_FF), 1.0),
        'g_mix_pre': gain((DEPTH, D_MODEL)),
        'g_mix_post': gain((DEPTH, D_MODEL)),
        'g_ffn_pre': gain((DEPTH, D_MODEL)),
        'g_ffn_post': gain((DEPTH, D_MODEL)),
        'w_even_in': nrm((N_EVEN, D_MODEL, EVEN_IN), D_MODEL ** -0.5),
        'w_conv_a': nrm((N_EVEN, A_CONV, A_WIDTH), A_CONV ** -0.5),
        's5_a_re': -0.5 + nrm((N_EVEN, S5_GROUPS, S5_STATE), 0.01),
        's5_a_im': jnp.pi * n_idx + nrm((N_EVEN, S5_GROUPS, S5_STATE), 0.01),
        's5_log_dt': jax.random.uniform(next(ks), (N_EVEN, S5_GROUPS), jnp.float32,
                                        math.log(1e-3), math.log(1e-1)),
        's5_b_re': nrm((N_EVEN, S5_GROUPS, S5_STATE, S5_GROUP), (2 * S5_GROUP) ** -0.5),
        's5_b_im': nrm((N_EVEN, S5_GROUPS, S5_STATE, S5_GROUP), (2 * S5_GROUP) ** -0.5),
        's5_c_re': nrm((N_EVEN, S5_GROUPS, S5_GROUP, S5_STATE), (2 * S5_STATE) ** -0.5),
        's5_c_im': nrm((N_EVEN, S5_GROUPS, S5_GROUP, S5_STATE), (2 * S5_STATE) ** -0.5),
        's5_d': nrm((N_EVEN, S5_GROUPS, S5_GROUP), 1.0),
        'w_glu': nrm((N_EVEN, B_WIDTH, B_WIDTH), B_WIDTH ** -0.5),
        'w_even_out': nrm((N_EVEN, A_WIDTH + B_WIDTH, D_MODEL), (A_WIDTH + B_WIDTH) ** -0.5),
        'w_odd_in': nrm((N_ODD, D_MODEL, ODD_IN), D_MODEL ** -0.5),
        'b_forget': 2.0 + nrm((N_ODD, C_HEADS), 0.5),
        'w_spatial': nrm((N_ODD, D_HEADS, GMLP_CHUNK, GMLP_CHUNK), GMLP_CHUNK ** -0.5),
        'b_spatial': 1.0 + nrm((N_ODD, D_HEADS, GMLP_CHUNK), 0.01),
        'g_gmlp_v': gain((N_ODD, D_WIDTH)),
        'w_odd_out': nrm((N_ODD, C_WIDTH + D_WIDTH, D_MODEL), (C_WIDTH + D_WIDTH) ** -0.5),
        'w_ffn_up': nrm((DEPTH, D_MODEL, 2 * D_FF), D_MODEL ** -0.5),
        'w_ffn_conv': nrm((DEPTH, FFN_CONV, 2 * D_FF), FFN_CONV ** -0.5),
        'w_ffn_down': nrm((DEPTH, D_FF, D_MODEL), D_FF ** -0.5),
        'w_ple': nrm((DEPTH, PLE_DIM, D_MODEL), PLE_DIM ** -0.5),
        'w_ple_gate': nrm((DEPTH, D_MODEL, D_MODEL), D_MODEL ** -0.5),
    }


def reference(x_prompt, x_sample, p_prompt, p_sample, cache_conv_a, state_ssm_re, state_ssm_im,
              cache_k, cache_v, cache_logf, cache_ffn_conv,
              g_mix_pre, g_mix_post, g_ffn_pre, g_ffn_post,
              w_even_in, w_conv_a, s5_a_re, s5_a_im, s5_log_dt, s5_b_re, s5_b_im, s5_c_re, s5_c_im,
              s5_d, w_glu, w_even_out,
              w_odd_in, b_forget, w_spatial, b_spatial, g_gmlp_v, w_odd_out,
              w_ffn_up, w_ffn_conv, w_ffn_down, w_ple, w_ple_gate):
    W = {'g_mix_pre': g_mix_pre, 'g_mix_post': g_mix_post, 'g_ffn_pre': g_ffn_pre, 'g_ffn_post': g_ffn_post,
         'w_even_in': w_even_in, 'w_conv_a': w_conv_a, 's5_a_re': s5_a_re, 's5_a_im': s5_a_im,
         's5_log_dt': s5_log_dt, 's5_b_re': s5_b_re, 's5_b_im': s5_b_im, 's5_c_re': s5_c_re,
         's5_c_im': s5_c_im, 's5_d': s5_d, 'w_glu': w_glu, 'w_even_out': w_even_out,
         'w_odd_in': w_odd_in, 'b_forget': b_forget, 'w_spatial': w_spatial, 'b_spatial': b_spatial,
         'g_gmlp_v': g_gmlp_v, 'w_odd_out': w_odd_out, 'w_ffn_up': w_ffn_up, 'w_ffn_conv': w_ffn_conv,
         'w_ffn_down': w_ffn_down, 'w_ple': w_ple, 'w_ple_gate': w_ple_gate}
    y_prompt, sp = run_trunk(x_prompt, p_prompt, None, W)
    st = {'conv_a': cache_conv_a, 'ssm_re': state_ssm_re, 'ssm_im': state_ssm_im,
          'k': cache_k, 'v': cache_v, 'logf': cache_logf, 'ffn': cache_ffn_conv}
    y_sample, ss = run_trunk(x_sample, p_sample, st, W)
    return (y_prompt, y_sample,
            sp['conv_a'], sp['ssm_re'], sp['ssm_im'], sp['k'], sp['v'], sp['logf'], sp['ffn'],
            ss['conv_a'], ss['ssm_re'], ss['ssm_im'], ss['k'], ss['v'], ss['logf'], ss['gmlp_v'], ss['ffn'])
```

```python
import numpy as np
import concourse.bass as bass
import concourse.mybir as mybir

F32 = mybir.dt.float32
BF16 = mybir.dt.bfloat16
I32 = mybir.dt.int32
ALU = mybir.AluOpType
AF = mybir.ActivationFunctionType
AX = mybir.AxisListType

ENGS = ("pe", "act", "dve", "pool", "sp")
EPOCH = 30000


class T:
    _n = 0

    def __init__(self, handle, shape, name):
        self.h = handle
        self.shape = list(shape)
        self.name = name
        self.id = T._n
        T._n += 1
        st = [1] * len(shape)
        for i in range(len(shape) - 2, 0, -1):
            st[i] = st[i + 1] * shape[i + 1]
        self.st = st
        self.recs = []

    def __getitem__(self, idx):
        return self.h[idx]

    def iv(self, *idx):
        nd = len(self.shape) - 1
        idx = list(idx) + [None] * (nd - len(idx))
        rng = []
        for d, ix in enumerate(idx):
            n = self.shape[d + 1]
            if ix is None:
                rng.append((0, n))
            elif isinstance(ix, tuple):
                rng.append(ix)
            else:
                rng.append((ix, ix + 1))
        out = [(0, 0)]
        out = []

        def rec(d, base):
            if d == nd - 1:
                out.append((self, base + rng[d][0] * self.st[d + 1], base + (rng[d][1] - 1) * self.st[d + 1] + 1))
                return
            full = all(rng[k] == (0, self.shape[k + 1]) for k in range(d + 1, nd))
            if full:
                out.append((self, base + rng[d][0] * self.st[d + 1], base + rng[d][1] * self.st[d + 1]))
                return
            for i in range(rng[d][0], rng[d][1]):
                rec(d + 1, base + i * self.st[d + 1])

        rec(0, 0)
        return out


class V:
    def __init__(self, arena, off, shape, dt, name="v"):
        self.t = arena
        self.off = off
        self.shape = list(shape)
        self.dt = dt
        self.u = 1 if dt == BF16 else 2
        n = 1
        for x in shape[1:]:
            n *= x
        self.n = n
        a = arena.h[0:shape[0], off:off + n * self.u]
        if dt != BF16:
            a = a.bitcast(dt)
        if len(shape) == 3:
            a = a.rearrange("p (a b) -> p a b", a=shape[1])
        elif len(shape) == 4:
            a = a.rearrange("p (a b c) -> p a b c", a=shape[1], b=shape[2])
        self.a = a
        st = [1] * len(shape)
        for i in range(len(shape) - 2, 0, -1):
            st[i] = st[i + 1] * shape[i + 1]
        self.st = st

    def __getitem__(self, idx):
        return self.a[idx]

    def iv(self, *idx):
        nd = len(self.shape) - 1
        idx = list(idx) + [None] * (nd - len(idx))
        rng = []
        for d, ix in enumerate(idx):
            n = self.shape[d + 1]
            if ix is None:
                rng.append((0, n))
            elif isinstance(ix, tuple):
                rng.append(ix)
            else:
                rng.append((ix, ix + 1))
        out = []
        u = self.u
        off = self.off

        def rec(d, base):
            if d == nd - 1:
                out.append((self.t, off + u * (base + rng[d][0]), off + u * (base + rng[d][1])))
                return
            full = all(rng[k] == (0, self.shape[k + 1]) for k in range(d + 1, nd))
            if full:
                out.append((self.t, off + u * (base + rng[d][0] * self.st[d + 1]), off + u * (base + rng[d][1] * self.st[d + 1])))
                return
            for i in range(rng[d][0], rng[d][1]):
                rec(d + 1, base + i * self.st[d + 1])

        rec(0, 0)
        return out


class Op:
    __slots__ = ("eng", "emit", "deps", "signal", "tok", "waits", "dma", "slot")

    def __init__(self, eng, emit):
        self.eng = eng
        self.emit = emit
        self.deps = set()
        self.signal = False
        self.tok = None
        self.waits = None
        self.dma = False
        self.slot = None


class Prog:
    def __init__(self, nc, n_dma_slots=24):
        self.nc = nc
        self.ops = []
        self.n_dma_slots = n_dma_slots
        self.dma_rr = 0
        self.ctx = []
        self.bar = None

    def sb(self, name, shape, dt):
        g = self.nc.sbuf_tensor(name, list(shape), dt)
        h = g.__enter__()
        self.ctx.append(g)
        return T(h, shape, name)

    def ps(self, name, shape, dt=F32):
        g = self.nc.psum_tensor(name, list(shape), dt)
        h = g.__enter__()
        self.ctx.append(g)
        t = T(h, shape, name)
        t.psum = True
        return t

    def make_arena(self, nbytes):
        nbytes = nbytes // 256 * 256
        self.arena = self.sb("arena", [128, nbytes // 2], BF16)
        self.atop = 0
        self.bar = None

    def mark(self):
        return self.atop

    def release(self, m):
        self.atop = m
        self.barrier()

    def alloc(self, shape, dt, name="v"):
        es = 2 if dt == BF16 else 4
        n = 1
        for x in shape[1:]:
            n *= x
        nb = (n * es + 63) // 64 * 64
        off = self.atop
        self.atop += nb
        assert self.atop <= self.arena.shape[1] * 2, f"arena overflow {name} {self.atop}"
        return V(self.arena, off // 2, shape, dt, name)

    def barrier(self):
        last = {}
        for i in range(len(self.ops) - 1, -1, -1):
            o = self.ops[i]
            if o.dma:
                k = ("d", o.slot)
            else:
                k = o.eng
            if k not in last:
                last[k] = i
            if len(last) >= len(ENGS) + self.n_dma_slots:
                break
        self.bar = (set(last.values()), set())

    def _track(self, op_idx, reads, writes):
        op = self.ops[op_idx]
        isdma = op.dma
        pw = [(t, 0, t.shape[1]) for (t, lo, hi) in list(reads) + list(writes) if getattr(t, "psum", False)]
        if pw:
            reads = [x for x in reads if not getattr(x[0], "psum", False)]
            seen = set()
            writes = [x for x in writes if not getattr(x[0], "psum", False)]
            for x in pw:
                if id(x[0]) not in seen:
                    seen.add(id(x[0]))
                    writes.append(x)
        for (t, lo, hi) in reads:
            recs = t.recs
            keep = []
            for r in recs:
                (l2, h2, j, w) = r
                if w:
                    if l2 < hi and lo < h2:
                        op.deps.add(j)
                elif (not isdma) and lo <= l2 and h2 <= hi and self.ops[j].eng == op.eng and not self.ops[j].dma:
                    continue
                keep.append(r)
            keep.append((lo, hi, op_idx, False))
            t.recs = keep
        for (t, lo, hi) in writes:
            recs = t.recs
            keep = []
            for r in recs:
                (l2, h2, j, w) = r
                if l2 < hi and lo < h2:
                    if j != op_idx:
                        op.deps.add(j)
                    if lo <= l2 and h2 <= hi:
                        continue
                keep.append(r)
            keep.append((lo, hi, op_idx, True))
            t.recs = keep

    def op(self, eng, emit, reads=(), writes=(), dma=False):
        o = Op(eng, emit)
        o.dma = dma
        if self.bar is not None and eng not in self.bar[1]:
            o.deps |= self.bar[0]
            self.bar[1].add(eng)
        self.ops.append(o)
        self._track(len(self.ops) - 1, reads, writes)
        return o

    def dma(self, emit, reads=(), writes=(), queue="sp"):
        o = self.op(queue, emit, reads, writes, dma=True)
        o.slot = self.dma_rr % self.n_dma_slots
        self.dma_rr += 1
        return o

    def run(self):
        nc = self.nc
        ops = self.ops
        for o in ops:
            for j in o.deps:
                ops[j].signal = True
        cnt = {e: 0 for e in ENGS}
        slot_cnt = [0] * self.n_dma_slots
        slot_last = [None] * self.n_dma_slots
        for i, o in enumerate(ops):
            if o.dma:
                o.signal = True
                prev = slot_last[o.slot]
                if prev is not None:
                    o.deps.add(prev)
                slot_last[o.slot] = i
                slot_cnt[o.slot] += 1
                o.tok = ("d", o.slot, 16 * slot_cnt[o.slot])
            elif o.signal:
                cnt[o.eng] += 1
                n = cnt[o.eng]
                o.tok = ("e", o.eng, (n - 1) // EPOCH, (n - 1) % EPOCH + 1)
        n_ep = {e: (cnt[e] + EPOCH - 1) // EPOCH for e in ENGS}
        sems = {}
        guards = []
        for e in ENGS:
            for k in range(max(1, n_ep[e])):
                g = nc.semaphore(f"s_{e}_{k}")
                sems[("e", e, k)] = g.__enter__()
                guards.append(g)
        for s in range(self.n_dma_slots):
            g = nc.semaphore(f"s_dma_{s}")
            sems[("d", s)] = g.__enter__()
            guards.append(g)
        per_eng = {e: [] for e in ENGS}
        for i, o in enumerate(ops):
            per_eng[o.eng].append(i)
        self.n_waits = 0

        def tok_key(tok):
            if tok[0] == "d":
                return ("d", tok[1]), tok[2]
            return ("e", tok[1], tok[2]), tok[3]

        def emit_engine(engname, eng):
            waited = {}
            for i in per_eng[engname]:
                o = ops[i]
                need = {}
                for j in o.deps:
                    k, v = tok_key(ops[j].tok)
                    if need.get(k, 0) < v:
                        need[k] = v
                for k, v in need.items():
                    if waited.get(k, 0) >= v:
                        continue
                    waited[k] = v
                    eng.wait_ge(sems[k], v)
                    self.n_waits += 1
                inst = o.emit(eng)
                if o.signal:
                    k, v = tok_key(o.tok)
                    inst.then_inc(sems[k], 16 if o.dma else 1)
            if engname == "sp":
                for s in range(self.n_dma_slots):
                    if slot_cnt[s]:
                        eng.wait_ge(sems[("d", s)], 16 * slot_cnt[s])

        with nc.Block() as block:
            @block.tensor
            def _(e):
                emit_engine("pe", e)

            @block.scalar
            def _(e):
                emit_engine("act", e)

            @block.vector
            def _(e):
                emit_engine("dve", e)

            @block.gpsimd
            def _(e):
                emit_engine("pool", e)

            @block.sync
            def _(e):
                emit_engine("sp", e)
        for g in reversed(guards):
            g.__exit__(None, None, None)
        for g in reversed(self.ctx):
            g.__exit__(None, None, None)
        self.stats = dict(n_ops=len(ops), cnt=cnt, n_waits=self.n_waits)


import os
class _Stop(Exception):
    pass

def make_odd(L):
    STOP = float(os.environ.get('ODD_STOP', '99'))
    P = L["P"]; nps = L["nps"]; nps_held = L["nps_held"]; piv = L["piv"]; mm = L["mm"]; wblock = L["wblock"]; wcols = L["wcols"]
    hn = L["hn"]; h = L["h"]; ident = L["ident"]; onesf = L["onesf"]; onesb = L["onesb"]; maskw = L["maskw"]; negc = L["negc"]
    pre_norm = L["pre_norm"]; post_norm_add = L["post_norm_add"]; out_proj = L["out_proj"]
    w_odd_in = L["w_odd_in"]; b_forget = L["b_forget"]; w_spatial = L["w_spatial"]; b_spatial = L["b_spatial"]
    g_gmlp_v = L["g_gmlp_v"]; w_odd_out = L["w_odd_out"]
    ck = L["ck"]; cv = L["cv"]; clf = L["clf"]
    o_k_p = L["o_k_p"]; o_v_p = L["o_v_p"]; o_lf_p = L["o_lf_p"]; o_k_s = L["o_k_s"]; o_v_s = L["o_v_s"]; o_lf_s = L["o_lf_s"]; o_gv_s = L["o_gv_s"]
    EPS = 1e-6
    PAST = 1024

    def odd_mixer(j, layer, smp, seqi, T_):
        lm = P.mark()
        try:
            odd_body(j, layer, smp, seqi, T_)
        except _Stop:
            pass
        P.release(lm)

    def odd_body(j, layer, smp, seqi, T_):
        W = w_odd_in[j]
        NT = (T_ + 127) // 128
        Kaug = P.alloc([96, 8, T_], BF16, "Kaug")
        Vt = P.alloc([128, NT, 8, 65], BF16, "Vt")
        negF = P.alloc([128, NT, 8], F32, "negF")
        Fcar = P.alloc([128, 8], F32, "Fcar")
        WsT = P.alloc([128, 8, 128], BF16, "WsT")
        hselb = P.alloc([8, 4, 128], BF16, "hselb")
        bsph = P.alloc([8, 128], BF16, "bsph")
        bspl = P.alloc([8, 128], BF16, "bspl")
        gvb = P.alloc([128, 512], F32, "gvb")
        bfn = P.alloc([128, 8], F32, "bfn")
        wfl = P.alloc([128, 8, 8], BF16, "wfl")
        ones1 = P.alloc([128, 512], F32, "ones1")
        P.op("pool", lambda e: e.memset(Kaug[64:96, :, :], 0.0), writes=Kaug.iv())
        P.op("pool", lambda e: e.memset(Kaug[64:66, :, :], 1.0), writes=Kaug.iv())
        bsel = P.alloc([128, 64], BF16, "bsel")
        P.op("dve", lambda e: e.tensor_copy(out=bsel[:], in_=ident[:, 64:65].to_broadcast([128, 64])), reads=ident.iv(), writes=bsel.iv())
        P.op("pool", lambda e: e.memset(Vt[:], 1.0), writes=Vt.iv())
        P.op("pool", lambda e: e.memset(Fcar[:], 0.0), writes=Fcar.iv())
        P.op("pool", lambda e: e.memset(ones1[:], 1.0), writes=ones1.iv())
        P.dma(lambda e: e.dma_start(out=gvb[:], in_=g_gmlp_v[j:j + 1, :].broadcast_to([128, 512])), writes=gvb.iv())
        P.dma(lambda e: e.dma_start(out=bfn[64:66, :], in_=b_forget[j:j + 1, :].broadcast_to([2, 8])), writes=bfn.iv())
        P.op("dve", lambda e: e.tensor_scalar(out=bfn[64:66, :], in0=bfn[64:66, :], scalar1=-1.0, scalar2=None, op0=ALU.mult), reads=bfn.iv(), writes=bfn.iv())
        m0 = P.mark()
        hsel = P.alloc([8, 4, 128], F32, "hsel"); bsp = P.alloc([8, 128], F32, "bsp"); bspt = P.alloc([8, 128], F32, "bspt")
        P.dma(lambda e: e.dma_start(out=bsp[:], in_=b_spatial[j]), writes=bsp.iv())
        P.op("pool", lambda e: e.memset(hsel[:], 1.0), writes=hsel.iv())
        P.op("pool", lambda e: e.affine_select(out=hsel[:], in_=hsel[:], pattern=[[128, 4], [1, 128]], compare_op=ALU.is_ge, fill=0.0, base=0, channel_multiplier=-64), reads=hsel.iv(), writes=hsel.iv())
        P.op("pool", lambda e: e.affine_select(out=hsel[:], in_=hsel[:], pattern=[[-128, 4], [-1, 128]], compare_op=ALU.is_ge, fill=0.0, base=63, channel_multiplier=64), reads=hsel.iv(), writes=hsel.iv())
        P.op("dve", lambda e: e.tensor_copy(out=hselb[:], in_=hsel[:]), reads=hsel.iv(), writes=hselb.iv())
        P.op("dve", lambda e: e.tensor_copy(out=bsph[:], in_=bsp[:]), reads=bsp.iv(), writes=bsph.iv())
        P.op("dve", lambda e: e.tensor_copy(out=bspt[:], in_=bsph[:]), reads=bsph.iv(), writes=bspt.iv())
        P.op("dve", lambda e: e.tensor_tensor(out=bspt[:], in0=bsp[:], in1=bspt[:], op=ALU.subtract), reads=bsp.iv() + bspt.iv(), writes=bspt.iv())
        P.op("dve", lambda e: e.tensor_copy(out=bspl[:], in_=bspt[:]), reads=bspt.iv(), writes=bspl.iv())
        wsn = P.alloc([128, 8, 128], F32, "wsn")
        P.dma(lambda e: e.dma_start(out=wsn[:], in_=w_spatial[j].rearrange("h t s -> t h s")), writes=wsn.iv())
        P.op("pool", lambda e: e.affine_select(out=wsn[:], in_=wsn[:], pattern=[[0, 8], [-1, 128]], compare_op=ALU.is_ge, fill=0.0, base=0, channel_multiplier=1), reads=wsn.iv(), writes=wsn.iv())
        for hg in range(2):
            p = nps()
            for hl in range(4):
                hd = hg * 4 + hl
                P.op("pe", lambda e, p=p, hd=hd, hl=hl: e.transpose(p[:, hl * 128:(hl + 1) * 128], wsn[:, hd, :], ident[:]), reads=wsn.iv(hd) + ident.iv(), writes=piv(p, 128, hl * 128))
            P.op("act", lambda e, p=p, hg=hg: e.copy(out=WsT[:, hg * 4:(hg + 1) * 4, :], in_=p[:, :].rearrange("p (a b) -> p a b", a=4)), reads=piv(p, 512), writes=WsT.iv((hg * 4, hg * 4 + 4)))
        P.release(m0)
        if smp:
            Kc = P.alloc([96, 8, PAST], BF16, "Kc")
            Vc = P.alloc([128, 8, 8, 65], BF16, "Vc")
            Dc = P.alloc([128, 8, 8], F32, "Dc")
            m0 = P.mark()
            ctile = [P.alloc([128, 512], F32, f"ctile{a}") for a in range(2)]
            lfc = P.alloc([128, 8, 8], F32, "lfc")
            ustr = P.alloc([128, 128], F32, "ustr")
            P.op("pool", lambda e: e.memset(Kc[64:96, :, :], 0.0), writes=Kc.iv())
            P.op("pool", lambda e: e.memset(Kc[64:66, :, :], 1.0), writes=Kc.iv())
            P.op("pool", lambda e: e.memset(Vc[:], 1.0), writes=Vc.iv())
            P.op("pool", lambda e: e.affine_select(out=ustr[:], in_=onesf[:], pattern=[[-1, 128]], compare_op=ALU.is_gt, fill=0.0, base=0, channel_multiplier=1), reads=onesf.iv(), writes=ustr.iv())
            P.dma(lambda e: e.dma_start(out=lfc[:], in_=clf[j].rearrange("(t p) hd -> p t hd", p=128)), writes=lfc.iv())
            ustrb = P.alloc([128, 128], BF16, "ustrb"); lfh = P.alloc([128, 8, 8], BF16, "lfh"); lfl = P.alloc([128, 8, 8], BF16, "lfl"); lft = P.alloc([128, 8, 8], F32, "lft")
            P.op("dve", lambda e: e.tensor_copy(out=ustrb[:], in_=ustr[:]), reads=ustr.iv(), writes=ustrb.iv())
            P.op("dve", lambda e: e.tensor_copy(out=lfh[:], in_=lfc[:]), reads=lfc.iv(), writes=lfh.iv())
            P.op("dve", lambda e: e.tensor_copy(out=lft[:], in_=lfh[:]), reads=lfh.iv(), writes=lft.iv())
            P.op("dve", lambda e: e.tensor_tensor(out=lft[:], in0=lfc[:], in1=lft[:], op=ALU.subtract), reads=lfc.iv() + lft.iv(), writes=lft.iv())
            P.op("dve", lambda e: e.tensor_copy(out=lfl[:], in_=lft[:]), reads=lft.iv(), writes=lfl.iv())
            for t in range(8):
                cb = ctile[t % 2]
                P.dma(lambda e, cb=cb, t=t: e.dma_start(out=cb[:], in_=ck[j, t * 128:(t + 1) * 128, :]), writes=cb.iv())
                for hg in range(2):
                    p = nps()
                    for hl in range(4):
                        hd = hg * 4 + hl
                        P.op("pe", lambda e, p=p, cb=cb, hd=hd, hl=hl: e.transpose(p[0:64, hl * 128:(hl + 1) * 128], cb[:, hd * 64:(hd + 1) * 64], ident[:]), reads=cb.iv() + ident.iv(), writes=piv(p, 128, hl * 128))
                    P.op("act", lambda e, p=p, hg=hg, t=t: e.copy(out=Kc[0:64, hg * 4:(hg + 1) * 4, t * 128:(t + 1) * 128], in_=p[0:64, :].rearrange("p (a b) -> p a b", a=4)), reads=piv(p, 512), writes=Kc.iv())
                cb2 = ctile[(t + 1) % 2]
                P.dma(lambda e, cb2=cb2, t=t: e.dma_start(out=cb2[:], in_=cv[j, t * 128:(t + 1) * 128, :]), writes=cb2.iv())
                P.op("dve", lambda e, cb2=cb2, t=t: e.tensor_copy(out=Vc[:, t, :, 0:64], in_=cb2[:, :].rearrange("p (a b) -> p a b", a=8)), reads=cb2.iv(), writes=Vc.iv(t))
                p = nps()
                terms = [(ustrb[:], lfh[:, t, :], ustrb.iv() + lfh.iv(t)), (ustrb[:], lfl[:, t, :], ustrb.iv() + lfl.iv(t))]
                for t2 in range(t + 1, 8):
                    terms.append((onesb[:], lfh[:, t2, :], onesb.iv() + lfh.iv(t2)))
                    terms.append((onesb[:], lfl[:, t2, :], onesb.iv() + lfl.iv(t2)))
                mm(p[:, 0:8], piv(p, 8), terms)
                P.op("dve", lambda e, p=p, t=t: e.tensor_copy(out=Dc[:, t, :], in_=p[:, 0:8]), reads=piv(p, 8), writes=Dc.iv(t))
            P.release(m0)
        if STOP <= 1:
            raise _Stop()
        wblock(wcols(W, 1536, 8), 8, 8, dst=wfl[:], dst_iv=wfl.iv())

        def st_body(c0):
            n = min(512, T_ - c0)
            nt_ = (n + 127) // 128
            sm = P.mark()
            pre_norm(0, layer, c0, n)
            att = P.alloc([64, 8, n], BF16, "att"); ydb = P.alloc([128, 4, n], BF16, "ydb")
            sm2 = P.mark()
            Q = [P.alloc([96, n], BF16, f"Q{a}") for a in range(2)]
            wq = [P.alloc([128, 8, 66], BF16, f"wq{a}") for a in range(2)]
            wtok = P.alloc([128, 8, 512], BF16, "wtok")
            lfrow = P.alloc([128, 512], F32, "lfrow"); Frow = P.alloc([128, 512], F32, "Frow"); x8 = P.alloc([128, 512], F32, "x8")
            hib = P.alloc([128, 512], BF16, "hib")
            kvout = [lfrow, Frow]
            pT = [P.alloc([128, 512], BF16, f"pT{a}") for a in range(3)]
            s2 = P.alloc([128, 512], F32, "s2"); oT = P.alloc([128, 512], F32, "oT"); rden = P.alloc([128, 512], F32, "rden")
            rdh = P.alloc([128, 512], BF16, "rdh"); rdl = P.alloc([128, 512], BF16, "rdl")
            vnf = oT; vnpad = P.alloc([128, 8, 128], BF16, "vnpad"); junk = s2
            ssq = P.alloc([128, 2], F32, "ssq"); lfT = P.alloc([128, 4, 8], F32, "lfT")
            P.op("pool", lambda e: e.memset(vnpad[:], 0.0), writes=vnpad.iv())
            for qq in Q:
                P.op("pool", lambda e, qq=qq: e.memset(qq[64:96, :], 0.0), writes=qq.iv())
            for bb_ in (lfrow, Frow, rden):
                P.op("pool", lambda e, bb_=bb_: e.memset(bb_[64:96, :], 0.0), writes=bb_.iv())
            for bb_ in (rdh, rdl):
                P.op("pool", lambda e, bb_=bb_: e.memset(bb_[64:96, :], 0.0), writes=bb_.iv())

            def load_wtok(col0):
                for b in range(4):
                    wblock(wcols(W, col0 + b * 128, 128), 8, 128, dst=wtok[:, :, b * 128:(b + 1) * 128], dst_iv=wtok.iv(None, (b * 128, b * 128 + 128)))

            def tok_proj(ti):
                t0 = ti * 128
                nt = min(128, n - t0)
                p = nps()
                mm(p[0:nt, :], piv(p, 512), [(hn[:, kc, t0:t0 + nt], wtok[:, kc, :], hn.iv(kc, (t0, t0 + nt)) + wtok.iv(kc)) for kc in range(8)])
                return p, t0, nt
            if STOP <= 1.2:
                raise _Stop()
            load_wtok(1024)
            if STOP <= 1.4:
                raise _Stop()
            for ti in range(nt_):
                p, t0, nt = tok_proj(ti)
                if STOP <= 1.6:
                    raise _Stop()
                gt = (c0 + t0) // 128
                ko = kvout[ti % 2]
                P.op("act", lambda e, p=p, gt=gt, nt=nt: e.copy(out=Vt[0:nt, gt, :, 0:64], in_=p[0:nt, :].rearrange("p (a b) -> p a b", a=8)), reads=piv(p, 512), writes=Vt.iv(gt))
                if STOP <= 1.7:
                    raise _Stop()
                P.op("dve", lambda e, p=p, ko=ko, nt=nt: e.tensor_copy(out=ko[0:nt, :], in_=p[0:nt, :]), reads=piv(p, 512), writes=ko.iv())
                if STOP <= 1.8:
                    raise _Stop()
                dst = (o_v_s[j] if smp else o_v_p[j, seqi])[c0 + t0:c0 + t0 + nt, :]
                P.dma(lambda e, ko=ko, dst=dst, nt=nt: e.dma_start(out=dst, in_=ko[0:nt, :]), reads=ko.iv())
            load_wtok(512)
            for ti in range(nt_):
                p, t0, nt = tok_proj(ti)
                ko = kvout[ti % 2]
                P.op("dve", lambda e, p=p, ko=ko, nt=nt: e.tensor_copy(out=ko[0:nt, :], in_=p[0:nt, :]), reads=piv(p, 512), writes=ko.iv())
                dst = (o_k_s[j] if smp else o_k_p[j, seqi])[c0 + t0:c0 + t0 + nt, :]
                P.dma(lambda e, ko=ko, dst=dst, nt=nt: e.dma_start(out=dst, in_=ko[0:nt, :]), reads=ko.iv())
            if STOP <= 2:
                raise _Stop()
            for i in range(4):
                wv, wiv = wblock(wcols(W, 1544 + i * 128, 128), 8, 128)
                p = nps()
                mm(p[:, 0:n], piv(p, n), [(wv[:, kc, :], hn[:, kc, 0:n], wiv + hn.iv(kc, (0, n))) for kc in range(8)])
                P.op("act", lambda e, p=p, i=i: e.copy(out=ydb[:, i, :], in_=p[:, 0:n]), reads=piv(p, n), writes=ydb.iv(i))
            load_wtok(2056)
            for ti in range(nt_):
                p, t0, nt = tok_proj(ti)
                P.op("act", lambda e, p=p, nt=nt: e.activation(out=junk[0:nt, :], in_=p[0:nt, :], func=AF.Square, accum_out=ssq[0:nt, 0:1]), reads=piv(p, 512), writes=junk.iv() + ssq.iv())
                P.op("act", lambda e, nt=nt: e.activation(out=ssq[0:nt, 1:2], in_=ssq[0:nt, 0:1], func=AF.Sqrt, bias=EPS, scale=1.0 / 512), reads=ssq.iv(), writes=ssq.iv())
                P.op("dve", lambda e, nt=nt: e.reciprocal(out=ssq[0:nt, 1:2], in_=ssq[0:nt, 1:2]), reads=ssq.iv(), writes=ssq.iv())
                P.op("dve", lambda e, p=p, nt=nt: e.scalar_tensor_tensor(out=vnf[0:nt, :], in0=p[0:nt, :], scalar=ssq[0:nt, 1:2], in1=gvb[0:nt, :], op0=ALU.mult, op1=ALU.mult), reads=piv(p, 512) + ssq.iv() + gvb.iv(), writes=vnf.iv())
                if smp:
                    P.dma(lambda e, t0=t0, nt=nt: e.dma_start(out=o_gv_s[j, c0 + t0:c0 + t0 + nt, :], in_=vnf[0:nt, :]), reads=vnf.iv())
                v4 = vnf[0:nt, :].rearrange("p (a b c) -> p a b c", a=4, b=2)
                vp4 = vnpad[0:nt, :, :].rearrange("p (a b) c -> p a b c", b=2)
                P.op("act", lambda e, v4=v4, vp4=vp4: e.copy(out=vp4[:, :, 0, 0:64], in_=v4[:, :, 0, :]), reads=vnf.iv(), writes=vnpad.iv())
                P.op("act", lambda e, v4=v4, vp4=vp4: e.copy(out=vp4[:, :, 1, 64:128], in_=v4[:, :, 1, :]), reads=vnf.iv(), writes=vnpad.iv())
                pm = nps()
                for jj in range(4):
                    terms = []
                    for hd in (2 * jj, 2 * jj + 1):
                        terms.append((vnpad[0:nt, hd, :], WsT[0:nt, hd, 0:nt], vnpad.iv(hd) + WsT.iv(hd)))
                    terms.append((hselb[0:8, jj, :], bsph[0:8, 0:nt], hselb.iv(jj) + bsph.iv()))
                    terms.append((hselb[0:8, jj, :], bspl[0:8, 0:nt], hselb.iv(jj) + bspl.iv()))
                    mm(pm[:, jj * 128:jj * 128 + nt], piv(pm, nt, jj * 128), terms)
                    P.op("dve", lambda e, pm=pm, jj=jj, t0=t0, nt=nt: e.tensor_tensor(out=ydb[:, jj, t0:t0 + nt], in0=ydb[:, jj, t0:t0 + nt], in1=pm[:, jj * 128:jj * 128 + nt], op=ALU.mult), reads=ydb.iv(jj, (t0, t0 + nt)) + piv(pm, nt, jj * 128), writes=ydb.iv(jj, (t0, t0 + nt)))
            if STOP <= 3:
                raise _Stop()
            def chain(hd):
                Qh = Q[hd % 2]; wqh = wq[hd % 2]
                wblock(wcols(W, hd * 64, 64), 8, 64, dst=wqh[:, :, 0:64], dst_iv=wqh.iv(None, (0, 64)))
                P.op("pool", lambda e, wqh=wqh, hd=hd: e.tensor_copy(out=wqh[:, :, 64:65], in_=wfl[:, :, hd:hd + 1]), reads=wfl.iv(), writes=wqh.iv(None, (64, 65)))
                P.op("pool", lambda e, wqh=wqh, hd=hd: e.tensor_copy(out=wqh[:, :, 65:66], in_=wfl[:, :, hd:hd + 1]), reads=wfl.iv(), writes=wqh.iv(None, (65, 66)))
                pq = nps()
                mm(pq[0:66, 0:n], piv(pq, n), [(wqh[:, kc, :], hn[:, kc, 0:n], wqh.iv(kc) + hn.iv(kc, (0, n))) for kc in range(8)])
                P.op("act", lambda e, pq=pq, Qh=Qh: e.copy(out=Qh[0:64, :], in_=pq[0:64, 0:n]), reads=piv(pq, n), writes=Qh.iv())
                r = slice(64, 66)
                P.op("act", lambda e, pq=pq, hd=hd: e.activation(out=lfrow[r, 0:n], in_=pq[r, 0:n], func=AF.Exp, bias=bfn[r, hd:hd + 1], scale=-1.0), reads=piv(pq, n) + bfn.iv(), writes=lfrow.iv())
                P.op("act", lambda e: e.activation(out=lfrow[r, 0:n], in_=lfrow[r, 0:n], func=AF.Ln, bias=1.0, scale=1.0), reads=lfrow.iv(), writes=lfrow.iv())
                P.op("dve", lambda e: e.tensor_scalar(out=lfrow[r, 0:n], in0=lfrow[r, 0:n], scalar1=-1.0, scalar2=None, op0=ALU.mult), reads=lfrow.iv(), writes=lfrow.iv())
                P.op("dve", lambda e, hd=hd: e.tensor_tensor_scan(out=Frow[r, 0:n], data0=ones1[r, 0:n], data1=lfrow[r, 0:n], initial=Fcar[r, hd:hd + 1], op0=ALU.mult, op1=ALU.add), reads=ones1.iv() + lfrow.iv() + Fcar.iv(), writes=Frow.iv())
                P.op("dve", lambda e, hd=hd: e.tensor_copy(out=Fcar[r, hd:hd + 1], in_=Frow[r, n - 1:n]), reads=Frow.iv(), writes=Fcar.iv())
                P.op("dve", lambda e: e.tensor_scalar(out=x8[r, 0:n], in0=Frow[r, 0:n], scalar1=8.0, scalar2=None, op0=ALU.mult), reads=Frow.iv(), writes=x8.iv())
                P.op("dve", lambda e: e.tensor_copy(out=hib[r, 0:n], in_=x8[r, 0:n]), reads=x8.iv(), writes=hib.iv())
                P.op("dve", lambda e, Qh=Qh: e.scalar_tensor_tensor(out=Qh[r, :], in0=hib[r, 0:n], scalar=negc[r, 0:1], in1=x8[r, 0:n], op0=ALU.mult, op1=ALU.add), reads=hib.iv() + negc.iv() + x8.iv(), writes=Qh.iv())
                wv, wiv = wblock(wcols(W, 512 + hd * 64, 64), 8, 64)
                pk = nps()
                mm(pk[0:64, 0:n], piv(pk, n), [(wv[:, kc, :], hn[:, kc, 0:n], wiv + hn.iv(kc, (0, n))) for kc in range(8)])
                P.op("act", lambda e, pk=pk, hd=hd: e.copy(out=Kaug[0:64, hd, c0:c0 + n], in_=pk[0:64, 0:n]), reads=piv(pk, n), writes=Kaug.iv(hd, (c0, c0 + n)))

            def chainB(hd):
                pf = nps()
                for ti in range(nt_):
                    t0 = ti * 128; nt = min(128, n - t0)
                    P.op("pe", lambda e, pf=pf, ti=ti, t0=t0, nt=nt: e.transpose(pf[0:nt, 64 * ti:64 * ti + 32], Frow[64:96, t0:t0 + nt], ident[64:96, 64:96]), reads=Frow.iv() + ident.iv(), writes=piv(pf, 32, 64 * ti))
                    P.op("pe", lambda e, pf=pf, ti=ti, t0=t0, nt=nt: e.transpose(pf[0:nt, 64 * ti + 32:64 * ti + 64], lfrow[64:96, t0:t0 + nt], ident[64:96, 64:96]), reads=lfrow.iv() + ident.iv(), writes=piv(pf, 32, 64 * ti + 32))
                    gt = (c0 + t0) // 128
                    P.op("dve", lambda e, pf=pf, ti=ti, gt=gt, nt=nt, hd=hd: e.tensor_scalar(out=negF[0:nt, gt, hd:hd + 1], in0=pf[0:nt, 64 * ti:64 * ti + 1], scalar1=-1.0, scalar2=None, op0=ALU.mult), reads=piv(pf, 32, 64 * ti), writes=negF.iv(gt))
                    P.op("dve", lambda e, pf=pf, ti=ti, nt=nt, hd=hd: e.tensor_copy(out=lfT[0:nt, ti, hd:hd + 1], in_=pf[0:nt, 64 * ti + 32:64 * ti + 33]), reads=piv(pf, 32, 64 * ti + 32), writes=lfT.iv(ti))

            def attend(hd):
                Qh = Q[hd % 2]
                keys = []
                if smp:
                    for t in range(8):
                        keys.append((Kc[:, hd, t * 128:(t + 1) * 128], Kc.iv(hd), Vc[:, t, hd, :], Vc.iv(t), Dc[:, t, hd:hd + 1], Dc.iv(t), 128, None))
                nkt = (c0 + n + 127) // 128
                for kt in range(nkt):
                    nk = min(128, T_ - kt * 128)
                    rel = kt * 128 - c0
                    keys.append((Kaug[:, hd, kt * 128:kt * 128 + nk], Kaug.iv(hd, (kt * 128, kt * 128 + nk)), Vt[0:nk, kt, hd, :], Vt.iv(kt), negF[0:nk, kt, hd:hd + 1], negF.iv(kt), nk, rel if rel >= 0 else None))
                po = nps_held()
                pend = []

                def emit_pv(x):
                    (ki, va, viv, pt_, nk) = x
                    P.op("pe", lambda e, po=po, va=va, pt_=pt_, nk=nk, ki=ki, last=(ki == len(keys) - 1): e.matmul(po[0:65, 0:n], lhsT=va, rhs=pt_[0:nk, 0:n], start=(ki == 0), stop=last), reads=viv + pt_.iv() + (piv(po, n) if ki else []), writes=piv(po, n))
                for ki, (ka, kiv, va, viv, ba, biv, nk, rel) in enumerate(keys):
                    pscore = nps()
                    mm(pscore[0:nk, 0:n], piv(pscore, n), [(ka, Qh[:, :], kiv + Qh.iv())])
                    pt_ = pT[ki % 3]
                    if rel is None:
                        P.op("act", lambda e, pscore=pscore, pt_=pt_, ba=ba, nk=nk: e.activation(out=pt_[0:nk, 0:n], in_=pscore[0:nk, 0:n], func=AF.Exp, bias=ba, scale=0.125), reads=piv(pscore, n) + biv, writes=pt_.iv())
                    else:
                        P.op("dve", lambda e, pscore=pscore, nk=nk, rel=rel: e.tensor_tensor(out=s2[0:nk, 0:n], in0=pscore[0:nk, 0:n], in1=maskw[0:nk, 384 - rel:384 - rel + n], op=ALU.add), reads=piv(pscore, n) + maskw.iv(), writes=s2.iv())
                        P.op("act", lambda e, pt_=pt_, ba=ba, nk=nk: e.activation(out=pt_[0:nk, 0:n], in_=s2[0:nk, 0:n], func=AF.Exp, bias=ba, scale=0.125), reads=s2.iv() + biv, writes=pt_.iv())
                    pend.append((ki, va, viv, pt_, nk))
                    if len(pend) > 2:
                        emit_pv(pend.pop(0))
                for x in pend:
                    emit_pv(x)
                P.op("act", lambda e, po=po: e.copy(out=oT[0:64, 0:n], in_=po[0:64, 0:n]), reads=piv(po, n), writes=oT.iv())
                P.op("dve", lambda e, po=po: e.reciprocal(out=rden[64:65, 0:n], in_=po[64:65, 0:n]), reads=piv(po, n), writes=rden.iv())
                P.op("dve", lambda e: e.tensor_copy(out=rdh[64:65, 0:n], in_=rden[64:65, 0:n]), reads=rden.iv(), writes=rdh.iv())
                P.op("dve", lambda e: e.tensor_tensor(out=rden[64:65, 0:n], in0=rden[64:65, 0:n], in1=rdh[64:65, 0:n], op=ALU.subtract), reads=rden.iv() + rdh.iv(), writes=rden.iv())
                P.op("dve", lambda e: e.tensor_copy(out=rdl[64:65, 0:n], in_=rden[64:65, 0:n]), reads=rden.iv(), writes=rdl.iv())
                pb = nps()
                mm(pb[0:64, 0:n], piv(pb, n), [(bsel[64:96, :], rdh[64:96, 0:n], bsel.iv() + rdh.iv()), (bsel[64:96, :], rdl[64:96, 0:n], bsel.iv() + rdl.iv())])
                P.op("dve", lambda e, pb=pb, hd=hd: e.tensor_tensor(out=att[:, hd, :], in0=oT[0:64, 0:n], in1=pb[0:64, 0:n], op=ALU.mult), reads=oT.iv() + piv(pb, n), writes=att.iv(hd))
            chain(0)
            chainB(0)
            for hd in range(8):
                if hd + 1 < 8:
                    chain(hd + 1)
                if STOP > 4:
                    attend(hd)
                if hd + 1 < 8:
                    chainB(hd + 1)
            if STOP <= 5:
                raise _Stop()
            for ti in range(nt_):
                t0 = ti * 128; nt = min(128, n - t0)
                dst = (o_lf_s[j] if smp else o_lf_p[j, seqi])[c0 + t0:c0 + t0 + nt, :]
                P.dma(lambda e, dst=dst, ti=ti, nt=nt: e.dma_start(out=dst, in_=lfT[0:nt, ti, :]), reads=lfT.iv(ti))
            P.release(sm2)
            ysb = P.alloc([128, 8, n], F32, "ysbO")
            rl = [((lambda t0, nn, hd=hd: att[:, hd, t0:t0 + nn]), (lambda t0, nn, hd=hd: att.iv(hd, (t0, t0 + nn))), 64) for hd in range(8)]
            rl += [((lambda t0, nn, kc=kc: ydb[:, kc, t0:t0 + nn]), (lambda t0, nn, kc=kc: ydb.iv(kc, (t0, t0 + nn))), 128) for kc in range(4)]
            out_proj(w_odd_out[j], rl, ysb, n)
            post_norm_add(1, layer, ysb, c0, n)
            P.release(sm)

        for c0_ in range(0, T_, 512):
            st_body(c0_)

    return odd_mixer


import math
import numpy as np
import concourse.bass as bass
import concourse.mybir as mybir
from concourse.bass_utils import run_bass_kernel_spmd

D = 1024
DEPTH = 4
SEQ = 2048
SL = 32
PAST = 1024
DFF = 2816
EPS = 1e-6
NEG = -1.0e30
PI = math.pi


def build(nc, passes=(("p", 0), ("p", 1), ("s", 0)), nlayers=4):
    P = Prog(nc)

    def din(name, shape):
        return nc.dram_tensor(name, list(shape), F32, kind="ExternalInput").ap()

    def dout(name, shape):
        return nc.dram_tensor(name, list(shape), F32, kind="ExternalOutput").ap()

    xp = din("xp", [2, SEQ, D]); xs = din("xs", [SL, D])
    pp = din("pp", [4, 2, SEQ, 256]); psm = din("psm", [4, SL, 256])
    cconv = din("cconv", [2, 2, 512]); sre = din("sre", [2, 32, 64]); sim = din("sim", [2, 32, 64])
    ck = din("ck", [2, PAST, 512]); cv = din("cv", [2, PAST, 512]); clf = din("clf", [2, PAST, 8])
    cffn = din("cffn", [4, 2, 2 * DFF])
    g_mix_pre = din("g_mix_pre", [4, D]); g_mix_post = din("g_mix_post", [4, D])
    g_ffn_pre = din("g_ffn_pre", [4, D]); g_ffn_post = din("g_ffn_post", [4, D])
    w_even_in = din("w_even_in", [2, D, 2048]); w_conv_a = din("w_conv_a", [2, 3, 512])
    s5_a_re = din("s5_a_re", [2, 32, 64]); s5_a_im = din("s5_a_im", [2, 32, 64]); s5_log_dt = din("s5_log_dt", [2, 32])
    s5_b_re = din("s5_b_re", [2, 32, 64, 16]); s5_b_im = din("s5_b_im", [2, 32, 64, 16])
    s5_c_re = din("s5_c_re", [2, 32, 16, 64]); s5_c_im = din("s5_c_im", [2, 32, 16, 64])
    s5_d = din("s5_d", [2, 32, 16]); w_glu = din("w_glu", [2, 512, 512]); w_even_out = din("w_even_out", [2, D, D])
    w_odd_in = din("w_odd_in", [2, D, 2568]); b_forget = din("b_forget", [2, 8])
    w_spatial = din("w_spatial", [2, 8, 128, 128]); b_spatial = din("b_spatial", [2, 8, 128])
    g_gmlp_v = din("g_gmlp_v", [2, 512]); w_odd_out = din("w_odd_out", [2, D, D])
    w_ffn_up = din("w_ffn_up", [4, D, 2 * DFF]); w_ffn_conv = din("w_ffn_conv", [4, 3, 2 * DFF])
    w_ffn_down = din("w_ffn_down", [4, DFF, D]); w_ple = din("w_ple", [4, 256, D]); w_ple_gate = din("w_ple_gate", [4, D, D])

    yp = dout("yp", [2, SEQ, D]); ys = dout("ys", [SL, D])
    o_conv_p = dout("o_conv_p", [2, 2, 2, 512]); o_sre_p = dout("o_sre_p", [2, 2, 32, 64]); o_sim_p = dout("o_sim_p", [2, 2, 32, 64])
    o_k_p = dout("o_k_p", [2, 2, SEQ, 512]); o_v_p = dout("o_v_p", [2, 2, SEQ, 512]); o_lf_p = dout("o_lf_p", [2, 2, SEQ, 8])
    o_ffn_p = dout("o_ffn_p", [4, 2, 2, 2 * DFF])
    o_conv_s = dout("o_conv_s", [2, 2, 512]); o_sre_s = dout("o_sre_s", [2, 32, 64]); o_sim_s = dout("o_sim_s", [2, 32, 64])
    o_k_s = dout("o_k_s", [2, SL, 512]); o_v_s = dout("o_v_s", [2, SL, 512]); o_lf_s = dout("o_lf_s", [2, SL, 8])
    o_gv_s = dout("o_gv_s", [2, SL, 512]); o_ffn_s = dout("o_ffn_s", [4, 2, 2 * DFF])

    import os
    DBG = os.environ.get("KDBG", "0") == "1"
    if DBG:
        dbg_y = dout("dbg_y", [128, 8, 512]); dbg_h = dout("dbg_h", [128, 8, 512]); dbg_h0 = dout("dbg_h0", [128, 8, 512])
    h = P.sb("h", [128, 8, SEQ], F32)
    hn = P.sb("hn", [128, 8, 1024], BF16)
    NW = 4
    wbf = [P.sb(f"wbf{i}", [128, 1024], BF16) for i in range(NW)]
    sqP = P.sb("sqP", [128, 8, 512], BF16)
    rstdP = P.sb("rstdP", [128, 512], F32)
    ident = P.sb("ident", [128, 128], F32)
    onesf = P.sb("onesf", [128, 128], F32)
    onesb = P.sb("onesb", [128, 128], BF16)
    maskw = P.sb("maskw", [128, 896], F32)
    negc = P.sb("negc", [128, 1], F32)
    gains = P.sb("gains", [128, 4, 4, 8], F32)
    wcva = P.sb("wcva", [128, 2, 3, 4], F32)
    wcvf = P.sb("wcvf", [128, 4, 3, 44], F32)
    ps = [P.ps(f"ps{i}", [128, 512]) for i in range(8)]
    P.make_arena(212863 - (SEQ * 8 * 4 + 8 * 1024 * 2 + (4 * 2048 + 8192 + 2048) + 512 * 2 + 256 + 896 * 4 + 64 + 512 + 96 + 2112) - 600)
    st_ = {"ps": 0, "w": 0}
    tabs = nc.dram_tensor("rot_tabs", [2, 16, 2, 128, 512], F32).ap()

    class _DT:
        def __init__(self):
            self.recs = []
    tabsT = _DT()
    tabs_done = set()

    def tab_iv(j, q):
        k = (j * 16 + q) * 2
        return [(tabsT, k, k + 2)]
    base_wbufs = [(wbf[i].h, (lambda n, i=i: [(wbf[i], 0, n)])) for i in range(NW)]
    st_["wbufs"] = list(base_wbufs)

    def nps():
        st_["ps"] = (st_["ps"] + 1) % 6
        return ps[st_["ps"]]

    def nps_held():
        st_["psh"] = 1 - st_.get("psh", 0)
        return ps[6 + st_["psh"]]

    def piv(p, n, lo=0):
        return [(p, lo, lo + n)]

    P.op("pool", lambda e: e.memset(onesf[:], 1.0), writes=onesf.iv())
    P.op("pool", lambda e: e.memset(onesb[:], 1.0), writes=onesb.iv())
    P.op("pool", lambda e: e.affine_select(out=ident[:], in_=onesf[:], pattern=[[-1, 128]], compare_op=ALU.is_equal, fill=0.0, base=0, channel_multiplier=1), reads=onesf.iv(), writes=ident.iv())
    P.op("pool", lambda e: e.memset(maskw[:], 0.0), writes=maskw.iv())
    P.op("pool", lambda e: e.affine_select(out=maskw[:], in_=maskw[:], pattern=[[1, 896]], compare_op=ALU.is_ge, fill=NEG, base=-384, channel_multiplier=-1), reads=maskw.iv(), writes=maskw.iv())
    P.op("dve", lambda e: e.tensor_scalar(out=negc[:], in0=ident[:, 65:66], scalar1=-1.0, scalar2=None, op0=ALU.mult), reads=ident.iv(), writes=negc.iv())
    for kind, g in enumerate([g_mix_pre, g_mix_post, g_ffn_pre, g_ffn_post]):
        for l in range(4):
            P.dma(lambda e, g=g, kind=kind, l=l: e.dma_start(out=gains[:, kind, l, :], in_=g[l].rearrange("(c p) -> p c", p=128)), writes=gains.iv(kind, l))
    for j in range(2):
        for tp in range(3):
            P.dma(lambda e, j=j, tp=tp: e.dma_start(out=wcva[:, j, tp, :], in_=w_conv_a[j, tp].rearrange("(c p) -> p c", p=128)), writes=wcva.iv(j, tp))
    for l in range(4):
        for tp in range(3):
            P.dma(lambda e, l=l, tp=tp: e.dma_start(out=wcvf[:, l, tp, :], in_=w_ffn_conv[l, tp].rearrange("(c p) -> p c", p=128)), writes=wcvf.iv(l, tp))

    def wblock(src, KC, ncol, dst=None, dst_iv=None, kp=128):
        n = KC * ncol
        if dst is None:
            bufs = st_["wbufs"]
            k = st_["w"] % len(bufs)
            st_["w"] += 1
            ap2, ivf = bufs[k]
            dst = ap2[0:kp, 0:n].rearrange("p (a b) -> p a b", a=KC)
            dst_iv = ivf(n)
        P.dma(lambda e: e.dma_start(out=dst, in_=src), writes=dst_iv, queue="pool")
        return dst, dst_iv

    def wcols(w2d, m0, ncol, KC=8):
        return w2d.rearrange("(kc p) m -> p kc m", p=128)[:, 0:KC, m0:m0 + ncol]

    def mm(out_ap, out_iv, terms):
        rd = []
        for t in terms:
            rd += t[2]

        def emit(e):
            inst = None
            for i, t in enumerate(terms):
                inst = e.matmul(out_ap, lhsT=t[0], rhs=t[1], start=(i == 0), stop=(i == len(terms) - 1))
            return inst
        P.op("pe", emit, reads=rd, writes=out_iv)

    def rms_rstd(sq_terms, n, scale, rstd, M=128):
        p = nps()
        mm(p[0:M, 0:n], piv(p, n), [(onesb[:, 0:M], a, onesb.iv() + iv) for (a, iv) in sq_terms])
        P.op("act", lambda e: e.activation(out=rstd[0:M, 0:n], in_=p[0:M, 0:n], func=AF.Sqrt, bias=EPS, scale=scale), reads=piv(p, n), writes=rstd.iv((0, n)))
        P.op("dve", lambda e: e.reciprocal(out=rstd[0:M, 0:n], in_=rstd[0:M, 0:n]), reads=rstd.iv((0, n)), writes=rstd.iv((0, n)))

    def pre_norm(kind, layer, c0, n):
        sq = sqP; rstd = rstdP
        for t0 in range(0, n, 512):
            nn = min(512, n - t0)
            for c in range(8):
                P.op("act", lambda e, c=c, t0=t0, nn=nn: e.activation(out=sq[:, c, 0:nn], in_=h[:, c, c0 + t0:c0 + t0 + nn], func=AF.Square),
                     reads=h.iv(c, (c0 + t0, c0 + t0 + nn)), writes=sq.iv(c, (0, nn)))
            rms_rstd([(sq[:, c, 0:nn], sq.iv(c, (0, nn))) for c in range(8)], nn, 1.0 / D, rstd)
            for c in range(8):
                P.op("dve", lambda e, c=c, t0=t0, nn=nn: e.scalar_tensor_tensor(out=hn[:, c, t0:t0 + nn], in0=h[:, c, c0 + t0:c0 + t0 + nn], scalar=gains[:, kind, layer, c:c + 1], in1=rstd[:, 0:nn], op0=ALU.mult, op1=ALU.mult),
                     reads=h.iv(c, (c0 + t0, c0 + t0 + nn)) + gains.iv() + rstd.iv((0, nn)), writes=hn.iv(c, (t0, t0 + nn)))

    def post_norm_add(kind, layer, ysb, c0, n):
        sq = sqP; rstd = rstdP
        for t0 in range(0, n, 512):
            nn = min(512, n - t0)
            for c in range(8):
                P.op("act", lambda e, c=c, t0=t0, nn=nn: e.activation(out=sq[:, c, 0:nn], in_=ysb[:, c, t0:t0 + nn], func=AF.Square),
                     reads=ysb.iv(c, (t0, t0 + nn)), writes=sq.iv(c, (0, nn)))
            rms_rstd([(sq[:, c, 0:nn], sq.iv(c, (0, nn))) for c in range(8)], nn, 1.0 / D, rstd)
            for c in range(8):
                P.op("dve", lambda e, c=c, t0=t0, nn=nn: e.scalar_tensor_tensor(out=ysb[:, c, t0:t0 + nn], in0=ysb[:, c, t0:t0 + nn], scalar=gains[:, kind, layer, c:c + 1], in1=rstd[:, 0:nn], op0=ALU.mult, op1=ALU.mult),
                     reads=ysb.iv(c, (t0, t0 + nn)) + gains.iv() + rstd.iv((0, nn)), writes=ysb.iv(c, (t0, t0 + nn)))
                P.op("pool", lambda e, c=c, t0=t0, nn=nn: e.tensor_tensor(out=h[:, c, c0 + t0:c0 + t0 + nn], in0=h[:, c, c0 + t0:c0 + t0 + nn], in1=ysb[:, c, t0:t0 + nn], op=ALU.add),
                     reads=h.iv(c, (c0 + t0, c0 + t0 + nn)) + ysb.iv(c, (t0, t0 + nn)), writes=h.iv(c, (c0 + t0, c0 + t0 + nn)))

    def out_proj(w2d, rhs_list, ysb, n):
        for m_ in range(8):
            blocks = []
            kc0 = 0
            row = 0
            while kc0 < len(rhs_list):
                kp = rhs_list[kc0][2]
                kc1 = kc0
                while kc1 < len(rhs_list) and rhs_list[kc1][2] == kp and kc1 - kc0 < 8:
                    kc1 += 1
                KC = kc1 - kc0
                src = w2d[row:row + KC * kp, m_ * 128:(m_ + 1) * 128].rearrange("(kc p) m -> p kc m", p=kp)
                wv, wiv = wblock(src, KC, 128, kp=kp)
                blocks.append((kc0, kc1, wv, wiv, kp))
                row += KC * kp
                kc0 = kc1
            for t0 in range(0, n, 512):
                nn = min(512, n - t0)
                p = nps()
                terms = []
                for (a0, a1, wv, wiv, kp) in blocks:
                    for kc in range(a0, a1):
                        terms.append((wv[:, kc - a0, :], rhs_list[kc][0](t0, nn), wiv + rhs_list[kc][1](t0, nn)))
                mm(p[:, 0:nn], piv(p, nn), terms)
                P.op("act", lambda e, p=p, m_=m_, t0=t0, nn=nn: e.copy(out=ysb[:, m_, t0:t0 + nn], in_=p[:, 0:nn]), reads=piv(p, nn), writes=ysb.iv(m_, (t0, t0 + nn)))

    def even_prep(j):
        C = {}
        C["BTr"] = P.alloc([128, 16, 128], BF16, "BTr"); C["BTi"] = P.alloc([128, 16, 128], BF16, "BTi")
        C["CTr"] = P.alloc([128, 16, 128], BF16, "CTr"); C["CTi"] = P.alloc([128, 16, 128], BF16, "CTi")
        C["Dd"] = P.alloc([128, 4, 128], BF16, "Dd")
        C["pwr"] = P.alloc([128, 10, 16], F32, "pwr"); C["pwi"] = P.alloc([128, 10, 16], F32, "pwi"); C["npwi"] = P.alloc([128, 10, 16], F32, "npwi")
        C["hsr"] = P.alloc([128, 16], F32, "hsr"); C["hsi"] = P.alloc([128, 16], F32, "hsi")
        C["halo"] = P.alloc([128, 4, 2], F32, "haloA")
        C["upc"] = P.alloc([128, 9, 16], F32, "upc"); C["ups"] = P.alloc([128, 9, 16], F32, "ups"); C["nups"] = P.alloc([128, 9, 16], F32, "nups")
        C["rcol"] = P.alloc([128, 16], F32, "rcol")
        C["ones"] = P.alloc([128, 512], F32, "ones512")
        P.op("pool", lambda e: e.memset(C["ones"][:], 1.0), writes=C["ones"].iv())
        m = P.mark()
        are = P.alloc([128, 16], F32); aim = P.alloc([128, 16], F32); dt = P.alloc([128, 16], F32)
        t1 = P.alloc([128, 16], F32); t2 = P.alloc([128, 16], F32); mag = P.alloc([128, 16], F32)
        cs = P.alloc([128, 16], F32); sn = P.alloc([128, 16], F32); den = P.alloc([128, 16], F32)
        fr = P.alloc([128, 16], F32); fi = P.alloc([128, 16], F32); zr = P.alloc([128, 16], F32)
        bre = P.alloc([128, 16, 16], F32); bim = P.alloc([128, 16, 16], F32); bbr = P.alloc([128, 16, 16], F32); bbi = P.alloc([128, 16, 16], F32)
        tmp3 = P.alloc([128, 16, 16], F32)
        maskC = P.alloc([128, 4, 128], F32); natp = P.alloc([128, 4, 128], F32)
        cdup = P.alloc([128, 128], F32); cT = P.alloc([128, 128], F32); dcol = P.alloc([128, 4], F32)
        for g2 in range(2):
            sl = slice(g2 * 64, g2 * 64 + 64)
            P.dma(lambda e, sl=sl, g2=g2: e.dma_start(out=are[sl, :], in_=s5_a_re[j].rearrange("(q g) p -> g p q", g=2)[g2]), writes=are.iv())
            P.dma(lambda e, sl=sl, g2=g2: e.dma_start(out=aim[sl, :], in_=s5_a_im[j].rearrange("(q g) p -> g p q", g=2)[g2]), writes=aim.iv())
            P.dma(lambda e, sl=sl, g2=g2: e.dma_start(out=dt[sl, :], in_=s5_log_dt[j].rearrange("(q g) -> g q", g=2)[g2:g2 + 1, :].broadcast_to([64, 16])), writes=dt.iv())
            P.dma(lambda e, sl=sl, g2=g2: e.dma_start(out=bre[sl, :, :], in_=s5_b_re[j].rearrange("(q g) p c -> g p q c", g=2)[g2]), writes=bre.iv())
            P.dma(lambda e, sl=sl, g2=g2: e.dma_start(out=bim[sl, :, :], in_=s5_b_im[j].rearrange("(q g) p c -> g p q c", g=2)[g2]), writes=bim.iv())
        P.dma(lambda e: e.dma_start(out=dcol[:], in_=s5_d[j].rearrange("g c -> (g c)").rearrange("(f p) -> p f", p=128)), writes=dcol.iv())

        def A(eng, fn, rd, wr):
            P.op(eng, fn, reads=[x for v in rd for x in v.iv()], writes=[x for v in wr for x in v.iv()])
        A("act", lambda e: e.activation(out=dt[:], in_=dt[:], func=AF.Exp), [dt], [dt])
        A("dve", lambda e: e.tensor_tensor(out=t1[:], in0=are[:], in1=dt[:], op=ALU.mult), [are, dt], [t1])
        A("act", lambda e: e.activation(out=mag[:], in_=t1[:], func=AF.Exp), [t1], [mag])
        A("dve", lambda e: e.tensor_tensor(out=t1[:], in0=aim[:], in1=dt[:], op=ALU.mult), [aim, dt], [t1])
        ki = P.alloc([128, 16], I32)
        kf = P.alloc([128, 16], F32)

        def reduce_sin(dst, shift):
            A("dve", lambda e: e.tensor_scalar(out=t2[:], in0=t1[:], scalar1=shift, scalar2=1.0 / (2 * PI), op0=ALU.add, op1=ALU.mult), [t1], [t2])
            A("dve", lambda e: e.tensor_copy(out=ki[:], in_=t2[:]), [t2], [ki])
            A("dve", lambda e: e.tensor_copy(out=kf[:], in_=ki[:]), [ki], [kf])
            A("dve", lambda e: e.tensor_tensor(out=t2[:], in0=t2[:], in1=kf[:], op=ALU.subtract), [t2, kf], [t2])
            A("dve", lambda e: e.tensor_scalar(out=kf[:], in0=t2[:], scalar1=0.5, scalar2=None, op0=ALU.is_gt), [t2], [kf])
            A("dve", lambda e: e.tensor_tensor(out=t2[:], in0=t2[:], in1=kf[:], op=ALU.subtract), [t2, kf], [t2])
            A("dve", lambda e: e.tensor_scalar(out=kf[:], in0=t2[:], scalar1=-0.5, scalar2=None, op0=ALU.is_lt), [t2], [kf])
            A("dve", lambda e: e.tensor_tensor(out=t2[:], in0=t2[:], in1=kf[:], op=ALU.add), [t2, kf], [t2])
            A("act", lambda e: e.activation(out=dst[:], in_=t2[:], func=AF.Sin, scale=2 * PI), [t2], [dst])
        reduce_sin(sn, 0.0)
        reduce_sin(cs, 0.5 * PI)
        A("dve", lambda e: e.tensor_tensor(out=C["pwr"][:, 0, :], in0=cs[:], in1=mag[:], op=ALU.mult), [cs, mag], [C["pwr"]])
        A("dve", lambda e: e.tensor_tensor(out=C["pwi"][:, 0, :], in0=sn[:], in1=mag[:], op=ALU.mult), [sn, mag], [C["pwi"]])
        for l in range(1, 10):
            A("dve", lambda e, l=l: e.tensor_tensor(out=t1[:], in0=C["pwr"][:, l - 1, :], in1=C["pwr"][:, l - 1, :], op=ALU.mult), [C["pwr"]], [t1])
            A("dve", lambda e, l=l: e.tensor_tensor(out=t2[:], in0=C["pwi"][:, l - 1, :], in1=C["pwi"][:, l - 1, :], op=ALU.mult), [C["pwi"]], [t2])
            A("dve", lambda e, l=l: e.tensor_tensor(out=C["pwr"][:, l, :], in0=t1[:], in1=t2[:], op=ALU.subtract), [t1, t2], [C["pwr"]])
            A("dve", lambda e, l=l: e.scalar_tensor_tensor(out=C["pwi"][:, l, :], in0=C["pwr"][:, l - 1, :], scalar=2.0, in1=C["pwi"][:, l - 1, :], op0=ALU.mult, op1=ALU.mult), [C["pwr"], C["pwi"]], [C["pwi"]])
        A("dve", lambda e: e.tensor_scalar(out=C["npwi"][:], in0=C["pwi"][:], scalar1=-1.0, scalar2=None, op0=ALU.mult), [C["pwi"]], [C["npwi"]])
        A("dve", lambda e: e.tensor_copy(out=C["rcol"][:], in_=mag[:]), [mag], [C["rcol"]])
        A("dve", lambda e: e.tensor_copy(out=C["upc"][:, 0, :], in_=cs[:]), [cs], [C["upc"]])
        A("dve", lambda e: e.tensor_copy(out=C["ups"][:, 0, :], in_=sn[:]), [sn], [C["ups"]])
        for l in range(1, 9):
            A("dve", lambda e, l=l: e.tensor_tensor(out=t1[:], in0=C["upc"][:, l - 1, :], in1=C["upc"][:, l - 1, :], op=ALU.mult), [C["upc"]], [t1])
            A("dve", lambda e, l=l: e.tensor_tensor(out=t2[:], in0=C["ups"][:, l - 1, :], in1=C["ups"][:, l - 1, :], op=ALU.mult), [C["ups"]], [t2])
            A("dve", lambda e, l=l: e.tensor_tensor(out=C["upc"][:, l, :], in0=t1[:], in1=t2[:], op=ALU.subtract), [t1, t2], [C["upc"]])
            A("dve", lambda e, l=l: e.scalar_tensor_tensor(out=C["ups"][:, l, :], in0=C["upc"][:, l - 1, :], scalar=2.0, in1=C["ups"][:, l - 1, :], op0=ALU.mult, op1=ALU.mult), [C["upc"], C["ups"]], [C["ups"]])
        A("dve", lambda e: e.tensor_scalar(out=C["nups"][:], in0=C["ups"][:], scalar1=-1.0, scalar2=None, op0=ALU.mult), [C["ups"]], [C["nups"]])
        if j not in tabs_done:
            tabs_done.add(j)
            Tb = [[P.alloc([128, 512], F32, f"Tg{a}{b}") for b in range(2)] for a in range(2)]
            tmpg = P.alloc([128, 256], F32, "tmpg")
            for q in range(16):
                Tc, Ts = Tb[q % 2]
                P.op("dve", lambda e, Tc=Tc, q=q: e.tensor_copy(out=Tc[:, 0:1], in_=cs[:, q:q + 1]), reads=cs.iv(), writes=Tc.iv((0, 1)))
                P.op("dve", lambda e, Ts=Ts, q=q: e.tensor_copy(out=Ts[:, 0:1], in_=sn[:, q:q + 1]), reads=sn.iv(), writes=Ts.iv((0, 1)))
                for l in range(9):
                    w = 1 << l
                    cw = C["upc"][:, l, q:q + 1]; sw = C["ups"][:, l, q:q + 1]; nsw = C["nups"][:, l, q:q + 1]
                    rdp = C["upc"].iv(l) + C["ups"].iv(l) + C["nups"].iv(l)
                    P.op("dve", lambda e, Tc=Tc, w=w, cw=cw: e.tensor_scalar(out=tmpg[:, 0:w], in0=Tc[:, 0:w], scalar1=cw, scalar2=None, op0=ALU.mult), reads=Tc.iv((0, w)) + rdp, writes=tmpg.iv((0, w)))
                    P.op("dve", lambda e, Tc=Tc, Ts=Ts, w=w, nsw=nsw: e.scalar_tensor_tensor(out=Tc[:, w:2 * w], in0=Ts[:, 0:w], scalar=nsw, in1=tmpg[:, 0:w], op0=ALU.mult, op1=ALU.add), reads=Ts.iv((0, w)) + tmpg.iv((0, w)) + rdp, writes=Tc.iv((w, 2 * w)))
                    P.op("dve", lambda e, Ts=Ts, w=w, cw=cw: e.tensor_scalar(out=tmpg[:, 0:w], in0=Ts[:, 0:w], scalar1=cw, scalar2=None, op0=ALU.mult), reads=Ts.iv((0, w)) + rdp, writes=tmpg.iv((0, w)))
                    P.op("dve", lambda e, Tc=Tc, Ts=Ts, w=w, sw=sw: e.scalar_tensor_tensor(out=Ts[:, w:2 * w], in0=Tc[:, 0:w], scalar=sw, in1=tmpg[:, 0:w], op0=ALU.mult, op1=ALU.add), reads=Tc.iv((0, w)) + tmpg.iv((0, w)) + rdp, writes=Ts.iv((w, 2 * w)))
                P.dma(lambda e, Tc=Tc, q=q: e.dma_start(out=tabs[j, q, 0], in_=Tc[:]), reads=Tc.iv(), writes=tab_iv(j, q))
                P.dma(lambda e, Ts=Ts, q=q: e.dma_start(out=tabs[j, q, 1], in_=Ts[:]), reads=Ts.iv(), writes=tab_iv(j, q))
        A("dve", lambda e: e.tensor_scalar(out=zr[:], in0=C["pwr"][:, 0, :], scalar1=-1.0, scalar2=None, op0=ALU.add), [C["pwr"]], [zr])
        A("dve", lambda e: e.tensor_tensor(out=den[:], in0=are[:], in1=are[:], op=ALU.mult), [are], [den])
        A("dve", lambda e: e.tensor_tensor(out=t1[:], in0=aim[:], in1=aim[:], op=ALU.mult), [aim], [t1])
        A("dve", lambda e: e.tensor_tensor(out=den[:], in0=den[:], in1=t1[:], op=ALU.add), [den, t1], [den])
        A("dve", lambda e: e.reciprocal(out=den[:], in_=den[:]), [den], [den])
        A("dve", lambda e: e.tensor_tensor(out=t1[:], in0=zr[:], in1=are[:], op=ALU.mult), [zr, are], [t1])
        A("dve", lambda e: e.tensor_tensor(out=t2[:], in0=C["pwi"][:, 0, :], in1=aim[:], op=ALU.mult), [C["pwi"], aim], [t2])
        A("dve", lambda e: e.tensor_tensor(out=t1[:], in0=t1[:], in1=t2[:], op=ALU.add), [t1, t2], [t1])
        A("dve", lambda e: e.tensor_tensor(out=fr[:], in0=t1[:], in1=den[:], op=ALU.mult), [t1, den], [fr])
        A("dve", lambda e: e.tensor_tensor(out=t1[:], in0=C["pwi"][:, 0, :], in1=are[:], op=ALU.mult), [C["pwi"], are], [t1])
        A("dve", lambda e: e.tensor_tensor(out=t2[:], in0=zr[:], in1=aim[:], op=ALU.mult), [zr, aim], [t2])
        A("dve", lambda e: e.tensor_tensor(out=t1[:], in0=t1[:], in1=t2[:], op=ALU.subtract), [t1, t2], [t1])
        A("dve", lambda e: e.tensor_tensor(out=fi[:], in0=t1[:], in1=den[:], op=ALU.mult), [t1, den], [fi])
        frb = fr[:].unsqueeze(2).to_broadcast([128, 16, 16]); fib = fi[:].unsqueeze(2).to_broadcast([128, 16, 16])
        A("dve", lambda e: e.tensor_tensor(out=bbr[:], in0=bre[:], in1=frb, op=ALU.mult), [bre, fr], [bbr])
        A("dve", lambda e: e.tensor_tensor(out=tmp3[:], in0=bim[:], in1=fib, op=ALU.mult), [bim, fi], [tmp3])
        A("dve", lambda e: e.tensor_tensor(out=bbr[:], in0=bbr[:], in1=tmp3[:], op=ALU.subtract), [bbr, tmp3], [bbr])
        A("dve", lambda e: e.tensor_tensor(out=bbi[:], in0=bim[:], in1=frb, op=ALU.mult), [bim, fr], [bbi])
        A("dve", lambda e: e.tensor_tensor(out=tmp3[:], in0=bre[:], in1=fib, op=ALU.mult), [bre, fi], [tmp3])
        A("dve", lambda e: e.tensor_tensor(out=bbi[:], in0=bbi[:], in1=tmp3[:], op=ALU.add), [bbi, tmp3], [bbi])
        A("pool", lambda e: e.memset(maskC[:], 1.0), [], [maskC])
        for g2 in range(2):
            sl = slice(g2 * 64, g2 * 64 + 64)
            A("pool", lambda e, sl=sl, g2=g2: e.affine_select(out=maskC[sl, :, :], in_=maskC[sl, :, :], pattern=[[-32, 4], [1, 128]], compare_op=ALU.is_ge, fill=0.0, base=-16 * g2, channel_multiplier=0), [maskC], [maskC])
            A("pool", lambda e, sl=sl, g2=g2: e.affine_select(out=maskC[sl, :, :], in_=maskC[sl, :, :], pattern=[[32, 4], [-1, 128]], compare_op=ALU.is_ge, fill=0.0, base=16 * g2 + 15, channel_multiplier=0), [maskC], [maskC])
        for (bb, BT) in ((bbr, C["BTr"]), (bbi, C["BTi"])):
            for qg in range(4):
                p = nps()
                for ql in range(4):
                    q = qg * 4 + ql
                    A("dve", lambda e, bb=bb, q=q, ql=ql: e.tensor_tensor(out=natp[:, ql, :].rearrange("p (a b) -> p a b", a=8), in0=maskC[:, ql, :].rearrange("p (a b) -> p a b", a=8), in1=bb[:, q, :].unsqueeze(1).to_broadcast([128, 8, 16]), op=ALU.mult), [maskC, bb], [natp])
                    P.op("pe", lambda e, p=p, ql=ql: e.transpose(p[:, ql * 128:(ql + 1) * 128], natp[:, ql, :], ident[:]), reads=natp.iv(ql) + ident.iv(), writes=piv(p, 128, ql * 128))
                P.op("act", lambda e, p=p, BT=BT, qg=qg: e.copy(out=BT[:, qg * 4:(qg + 1) * 4, :], in_=p[:, :].rearrange("p (a b) -> p a b", a=4)), reads=piv(p, 512), writes=BT.iv((qg * 4, qg * 4 + 4)))
        for (csrc, CT, sgn) in ((s5_c_re, C["CTr"], 1.0), (s5_c_im, C["CTi"], -1.0)):
            for t in range(4):
                src = csrc[j].rearrange("g c p -> (g c) p")[t * 128:(t + 1) * 128, :]
                P.dma(lambda e, src=src: e.dma_start(out=cdup[:, 0:64], in_=src), writes=cdup.iv((0, 64)))
                P.dma(lambda e, src=src: e.dma_start(out=cdup[:, 64:128], in_=src), writes=cdup.iv((64, 128)))
                p = nps()
                P.op("pe", lambda e, p=p: e.transpose(p[:, 0:128], cdup[:], ident[:]), reads=cdup.iv() + ident.iv(), writes=piv(p, 128))
                P.op("act", lambda e, p=p, sgn=sgn: e.mul(out=cT[:], in_=p[:, 0:128], mul=sgn), reads=piv(p, 128), writes=cT.iv())
                for ql in range(4):
                    q = t * 4 + ql
                    A("dve", lambda e, CT=CT, q=q, ql=ql: e.tensor_tensor(out=CT[:, q, :], in0=cT[:], in1=maskC[:, ql, :], op=ALU.mult), [cT, maskC], [CT])
        for fc in range(4):
            A("dve", lambda e, fc=fc: e.tensor_scalar(out=C["Dd"][:, fc, :], in0=ident[:], scalar1=dcol[:, fc:fc + 1], scalar2=None, op0=ALU.mult), [ident, dcol], [C["Dd"]])
        P.release(m)
        return C

    def even_mixer(j, layer, smp, seqi, T_):
        lm = P.mark()
        C = even_prep(j)
        W = w_even_in[j]
        if smp:
            for g2 in range(2):
                sl = slice(g2 * 64, g2 * 64 + 64)
                P.dma(lambda e, sl=sl, g2=g2: e.dma_start(out=C["hsr"][sl, :], in_=sre[j].rearrange("(q g) p -> g p q", g=2)[g2]), writes=C["hsr"].iv())
                P.dma(lambda e, sl=sl, g2=g2: e.dma_start(out=C["hsi"][sl, :], in_=sim[j].rearrange("(q g) p -> g p q", g=2)[g2]), writes=C["hsi"].iv())
            for r_ in range(2):
                P.dma(lambda e, r_=r_: e.dma_start(out=C["halo"][:, :, r_], in_=cconv[j, r_].rearrange("(c p) -> p c", p=128)), writes=C["halo"].iv())
        else:
            P.op("pool", lambda e: e.memset(C["hsr"][:], 0.0), writes=C["hsr"].iv())
            P.op("pool", lambda e: e.memset(C["hsi"][:], 0.0), writes=C["hsi"].iv())
            P.op("pool", lambda e: e.memset(C["halo"][:], 0.0), writes=C["halo"].iv())
        nlev = int(math.log2(min(512, T_)))
        def st_body(c0):
            n = min(512, T_ - c0)
            sm = P.mark()
            pre_norm(0, layer, c0, n)
            yacat = P.alloc([128, 8, n], BF16, "yacat"); ubf = P.alloc([128, 4, n], BF16, "ubf")
            tgc = P.alloc([128, n], F32, "tgc"); prod = P.alloc([128, n + 2], F32, "prod"); cvb = P.alloc([128, n], F32, "cvb")
            wre = P.alloc([128, n], F32, "wre"); wim = P.alloc([128, n], F32, "wim"); gre = P.alloc([128, n], F32, "gre"); gim = P.alloc([128, n], F32, "gim")
            TB = [[P.alloc([128, n], F32, f"TB{a}{b}") for b in range(2)] for a in range(2)]
            rtab = P.alloc([128, n], F32, "rtab")
            hri = P.alloc([128, 4, 2, n], BF16, "hri"); gB = P.alloc([128, 4, n], BF16, "gB")
            sg = P.alloc([128, n], F32, "sg"); ysb = P.alloc([128, 8, n], F32, "ysb"); cc = P.alloc([128, 2, 16], F32, "cc")
            tt = P.alloc([128, 2, 16], F32, "tt")

            def proj_chunk(fch):
                wv, wiv = wblock(wcols(W, fch * 128, 128), 8, 128)
                p = nps()
                mm(p[:, 0:n], piv(p, n), [(wv[:, kc, :], hn[:, kc, 0:n], wiv + hn.iv(kc, (0, n))) for kc in range(8)])
                return p
            for i in range(4):
                p = proj_chunk(4 + i)
                P.op("act", lambda e, p=p: e.copy(out=tgc[:], in_=p[:, 0:n]), reads=piv(p, n), writes=tgc.iv())
                p = proj_chunk(8 + i)
                P.op("dve", lambda e, i=i: e.tensor_copy(out=prod[:, 0:2], in_=C["halo"][:, i, :]), reads=C["halo"].iv(i), writes=prod.iv((0, 2)))
                P.op("dve", lambda e, p=p: e.tensor_tensor(out=prod[:, 2:n + 2], in0=tgc[:], in1=p[:, 0:n], op=ALU.mult), reads=tgc.iv() + piv(p, n), writes=prod.iv((2, n + 2)))
                P.op("dve", lambda e, i=i: e.tensor_copy(out=C["halo"][:, i, :], in_=prod[:, n:n + 2]), reads=prod.iv((n, n + 2)), writes=C["halo"].iv(i))
                P.op("dve", lambda e, i=i: e.tensor_scalar(out=cvb[:], in0=prod[:, 0:n], scalar1=wcva[:, j, 0, i:i + 1], scalar2=None, op0=ALU.mult), reads=prod.iv((0, n)) + wcva.iv(), writes=cvb.iv())
                P.op("dve", lambda e, i=i: e.scalar_tensor_tensor(out=cvb[:], in0=prod[:, 1:n + 1], scalar=wcva[:, j, 1, i:i + 1], in1=cvb[:], op0=ALU.mult, op1=ALU.add), reads=prod.iv((1, n + 1)) + wcva.iv() + cvb.iv(), writes=cvb.iv())
                P.op("dve", lambda e, i=i: e.scalar_tensor_tensor(out=cvb[:], in0=prod[:, 2:n + 2], scalar=wcva[:, j, 2, i:i + 1], in1=cvb[:], op0=ALU.mult, op1=ALU.add), reads=prod.iv((2, n + 2)) + wcva.iv() + cvb.iv(), writes=cvb.iv())
                p = proj_chunk(i)
                P.op("dve", lambda e, p=p, i=i: e.tensor_tensor(out=yacat[:, i, :], in0=cvb[:], in1=p[:, 0:n], op=ALU.mult), reads=cvb.iv() + piv(p, n), writes=yacat.iv(i))
            for i in range(4):
                p = proj_chunk(12 + i)
                P.op("act", lambda e, p=p, i=i: e.copy(out=ubf[:, i, :], in_=p[:, 0:n]), reads=piv(p, n), writes=ubf.iv(i))
            t1 = tgc; t2 = cvb
            for fc in range(4):
                py = nps()
                yterms = []
                for ql in range(4):
                    q = fc * 4 + ql
                    Tc, Ts = TB[q % 2]
                    P.dma(lambda e, Tc=Tc, q=q: e.dma_start(out=Tc[:], in_=tabs[j, q, 0][:, 0:n]), reads=tab_iv(j, q), writes=Tc.iv())
                    P.dma(lambda e, Ts=Ts, q=q: e.dma_start(out=Ts[:], in_=tabs[j, q, 1][:, 0:n]), reads=tab_iv(j, q), writes=Ts.iv())
                    P.op("act", lambda e, q=q: e.activation(out=rtab[:], in_=C["ones"][:, 0:n], func=AF.Copy, scale=C["rcol"][:, q:q + 1]), reads=C["ones"].iv() + C["rcol"].iv(), writes=rtab.iv())
                    pb = nps()
                    mm(pb[:, 0:n], piv(pb, n), [(C["BTr"][:, q, :], ubf[:, fc, :], C["BTr"].iv(q) + ubf.iv(fc))])
                    pb2 = nps()
                    mm(pb2[:, 0:n], piv(pb2, n), [(C["BTi"][:, q, :], ubf[:, fc, :], C["BTi"].iv(q) + ubf.iv(fc))])

                    def TT(out, a, b, op, rd, wr):
                        P.op("dve", lambda e: e.tensor_tensor(out=out, in0=a, in1=b, op=op), reads=rd, writes=wr)
                    TT(t1[:], Tc[:], pb[:, 0:n], ALU.mult, Tc.iv() + piv(pb, n), t1.iv())
                    TT(t2[:], Ts[:], pb2[:, 0:n], ALU.mult, Ts.iv() + piv(pb2, n), t2.iv())
                    TT(wre[:], t1[:], t2[:], ALU.add, t1.iv() + t2.iv(), wre.iv())
                    TT(t1[:], Tc[:], pb2[:, 0:n], ALU.mult, Tc.iv() + piv(pb2, n), t1.iv())
                    TT(t2[:], Ts[:], pb[:, 0:n], ALU.mult, Ts.iv() + piv(pb, n), t2.iv())
                    TT(wim[:], t1[:], t2[:], ALU.subtract, t1.iv() + t2.iv(), wim.iv())
                    P.op("dve", lambda e, q=q: e.tensor_tensor_scan(out=gre[:], data0=rtab[:], data1=wre[:], initial=C["hsr"][:, q:q + 1], op0=ALU.mult, op1=ALU.add), reads=rtab.iv() + wre.iv() + C["hsr"].iv(), writes=gre.iv())
                    P.op("dve", lambda e, q=q: e.tensor_tensor_scan(out=gim[:], data0=rtab[:], data1=wim[:], initial=C["hsi"][:, q:q + 1], op0=ALU.mult, op1=ALU.add), reads=rtab.iv() + wim.iv() + C["hsi"].iv(), writes=gim.iv())
                    TT(t1[:], Tc[:], gre[:], ALU.mult, Tc.iv() + gre.iv(), t1.iv())
                    TT(t2[:], Ts[:], gim[:], ALU.mult, Ts.iv() + gim.iv(), t2.iv())
                    TT(wre[:], t1[:], t2[:], ALU.subtract, t1.iv() + t2.iv(), wre.iv())
                    TT(t1[:], Ts[:], gre[:], ALU.mult, Ts.iv() + gre.iv(), t1.iv())
                    TT(t2[:], Tc[:], gim[:], ALU.mult, Tc.iv() + gim.iv(), t2.iv())
                    TT(wim[:], t1[:], t2[:], ALU.add, t1.iv() + t2.iv(), wim.iv())
                    P.op("act", lambda e, ql=ql: e.copy(out=hri[:, ql, 0, :], in_=wre[:]), reads=wre.iv(), writes=hri.iv(ql, 0))
                    P.op("act", lambda e, ql=ql: e.copy(out=hri[:, ql, 1, :], in_=wim[:]), reads=wim.iv(), writes=hri.iv(ql, 1))
                    P.op("dve", lambda e, q=q: e.tensor_copy(out=C["hsr"][:, q:q + 1], in_=wre[:, n - 1:n]), reads=wre.iv((n - 1, n)), writes=C["hsr"].iv())
                    P.op("dve", lambda e, q=q: e.tensor_copy(out=C["hsi"][:, q:q + 1], in_=wim[:, n - 1:n]), reads=wim.iv((n - 1, n)), writes=C["hsi"].iv())
                    yterms.append((C["CTr"][:, q, :], hri[:, ql, 0, :], C["CTr"].iv(q) + hri.iv(ql, 0)))
                    yterms.append((C["CTi"][:, q, :], hri[:, ql, 1, :], C["CTi"].iv(q) + hri.iv(ql, 1)))
                yterms.append((C["Dd"][:, fc, :], ubf[:, fc, :], C["Dd"].iv(fc) + ubf.iv(fc)))
                mm(py[:, 0:n], piv(py, n), yterms)
                P.op("act", lambda e, py=py, fc=fc: e.activation(out=gB[:, fc, :], in_=py[:, 0:n], func=AF.Gelu_apprx_tanh), reads=piv(py, n), writes=gB.iv(fc))
            for m_ in range(4):
                wv, wiv = wblock(wcols(w_glu[j], m_ * 128, 128, KC=4), 4, 128)
                p = nps()
                mm(p[:, 0:n], piv(p, n), [(wv[:, kc, :], gB[:, kc, :], wiv + gB.iv(kc)) for kc in range(4)])
                P.op("act", lambda e, p=p: e.activation(out=sg[:], in_=p[:, 0:n], func=AF.Sigmoid), reads=piv(p, n), writes=sg.iv())
                P.op("dve", lambda e, m_=m_: e.tensor_tensor(out=yacat[:, 4 + m_, :], in0=gB[:, m_, :], in1=sg[:], op=ALU.mult), reads=gB.iv(m_) + sg.iv(), writes=yacat.iv(4 + m_))
            if DBG and c0 == 0 and layer == 0 and not smp:
                P.dma(lambda e: e.dma_start(out=dbg_h0, in_=h[:, :, 0:512]), reads=h.iv())
                for kc in range(8):
                    P.op("dve", lambda e, kc=kc: e.tensor_copy(out=ysb[:, kc, :], in_=yacat[:, kc, :]), reads=yacat.iv(kc), writes=ysb.iv(kc))
                P.dma(lambda e: e.dma_start(out=dbg_y, in_=ysb[:]), reads=ysb.iv())
            out_proj(w_even_out[j], [((lambda t0, nn, kc=kc: yacat[:, kc, t0:t0 + nn]), (lambda t0, nn, kc=kc: yacat.iv(kc, (t0, t0 + nn))), 128) for kc in range(8)], ysb, n)
            post_norm_add(1, layer, ysb, c0, n)
            if DBG and c0 == 0 and layer == 0 and not smp:
                P.dma(lambda e: e.dma_start(out=dbg_h, in_=h[:, :, 0:512]), reads=h.iv())
            P.release(sm)
        for c0_ in range(0, T_, 512):
            st_body(c0_)
        oc = o_conv_s[j] if smp else o_conv_p[j, seqi]
        for r_ in range(2):
            P.dma(lambda e, r_=r_: e.dma_start(out=oc[r_].rearrange("(c p) -> p c", p=128), in_=C["halo"][:, :, r_]), reads=C["halo"].iv())
        for (hs, od) in ((C["hsr"], o_sre_s[j] if smp else o_sre_p[j, seqi]), (C["hsi"], o_sim_s[j] if smp else o_sim_p[j, seqi])):
            for g2 in range(2):
                sl = slice(g2 * 64, g2 * 64 + 64)
                P.dma(lambda e, hs=hs, od=od, sl=sl, g2=g2: e.dma_start(out=od.rearrange("(q g) p -> g p q", g=2)[g2], in_=hs[sl, :]), reads=hs.iv())
        P.release(lm)

    def ffn(layer, smp, seqi, T_):
        lm = P.mark()
        halo = P.alloc([128, 44, 2], F32, "haloF")
        if smp:
            for r_ in range(2):
                P.dma(lambda e, r_=r_: e.dma_start(out=halo[:, :, r_], in_=cffn[layer, r_].rearrange("(c p) -> p c", p=128)), writes=halo.iv())
        else:
            P.op("pool", lambda e: e.memset(halo[:], 0.0), writes=halo.iv())
        Wup = w_ffn_up[layer]
        def st_body(c0):
            n = min(1024, T_ - c0)
            cts = [(t0, min(512, n - t0)) for t0 in range(0, n, 512)]
            sm = P.mark()
            pre_norm(2, layer, c0, n)
            act = P.alloc([128, 22, n], BF16, "act")
            ysb = P.alloc([128, 8, n], F32, "ysbF")
            m2 = P.mark()
            U = [[P.alloc([128, 514], F32, f"U{a}{b}") for b in range(2)] for a in range(2)]
            cvt2 = [[P.alloc([128, 512], F32, f"cvt{a}{b}") for a in range(2)] for b in range(2)]
            gg2 = [P.alloc([128, 512], F32, f"gg{b}") for b in range(2)]
            xw = [P.alloc([128, 1024], BF16, f"xw{b}") for b in range(2)]
            st_["wbufs"] = list(base_wbufs) + [(xw[b].a, (lambda n, b=b: xw[b].iv((0, n)))) for b in range(2)]
            it = [0]
            for c in range(22):
                wg, wgiv = wblock(wcols(Wup, c * 128, 128), 8, 128)
                wvv, wviv = wblock(wcols(Wup, DFF + c * 128, 128), 8, 128)
                for (t0, nn) in cts:
                    cvt = cvt2[it[0] % 2]; gg = gg2[it[0] % 2]
                    it[0] += 1
                    for a, (wv, wiv, ch) in enumerate(((wg, wgiv, c), (wvv, wviv, 22 + c))):
                        p = nps()
                        mm(p[:, 0:nn], piv(p, nn), [(wv[:, kc, :], hn[:, kc, t0:t0 + nn], wiv + hn.iv(kc, (t0, t0 + nn))) for kc in range(8)])
                        Ub = U[a][(t0 // 512) % 2]
                        P.op("act", lambda e, Ub=Ub, ch=ch: e.copy(out=Ub[:, 0:2], in_=halo[:, ch, :]), reads=halo.iv(ch), writes=Ub.iv((0, 2)))
                        P.op("act", lambda e, Ub=Ub, p=p, nn=nn: e.copy(out=Ub[:, 2:nn + 2], in_=p[:, 0:nn]), reads=piv(p, nn), writes=Ub.iv((2, nn + 2)))
                        P.op("act", lambda e, Ub=Ub, ch=ch, nn=nn: e.copy(out=halo[:, ch, :], in_=Ub[:, nn:nn + 2]), reads=Ub.iv((nn, nn + 2)), writes=halo.iv(ch))
                        cb = cvt[a]
                        eng = "dve"
                        P.op("act", lambda e, Ub=Ub, cb=cb, ch=ch, nn=nn: e.activation(out=cb[:, 0:nn], in_=Ub[:, 0:nn], func=AF.Copy, scale=wcvf[:, layer, 0, ch:ch + 1]), reads=Ub.iv((0, nn)) + wcvf.iv(), writes=cb.iv((0, nn)))
                        P.op(eng, lambda e, Ub=Ub, cb=cb, ch=ch, nn=nn: e.scalar_tensor_tensor(out=cb[:, 0:nn], in0=Ub[:, 1:nn + 1], scalar=wcvf[:, layer, 1, ch:ch + 1], in1=cb[:, 0:nn], op0=ALU.mult, op1=ALU.add), reads=Ub.iv((1, nn + 1)) + wcvf.iv() + cb.iv((0, nn)), writes=cb.iv((0, nn)))
                        P.op(eng, lambda e, Ub=Ub, cb=cb, ch=ch, nn=nn: e.scalar_tensor_tensor(out=cb[:, 0:nn], in0=Ub[:, 2:nn + 2], scalar=wcvf[:, layer, 2, ch:ch + 1], in1=cb[:, 0:nn], op0=ALU.mult, op1=ALU.add), reads=Ub.iv((2, nn + 2)) + wcvf.iv() + cb.iv((0, nn)), writes=cb.iv((0, nn)))
                    P.op("act", lambda e, nn=nn, gg=gg, cvt=cvt: e.activation(out=gg[:, 0:nn], in_=cvt[0][:, 0:nn], func=AF.Gelu_apprx_tanh), reads=cvt[0].iv((0, nn)), writes=gg.iv((0, nn)))
                    P.op("dve", lambda e, c=c, t0=t0, nn=nn, gg=gg, cvt=cvt: e.tensor_tensor(out=act[:, c, t0:t0 + nn], in0=gg[:, 0:nn], in1=cvt[1][:, 0:nn], op=ALU.mult), reads=gg.iv((0, nn)) + cvt[1].iv((0, nn)), writes=act.iv(c, (t0, t0 + nn)))
            st_["wbufs"] = list(base_wbufs)
            P.release(m2)
            xw2 = [P.alloc([128, 1024], BF16, f"xw2{b}") for b in range(4)]
            st_["wbufs"] = list(base_wbufs) + [(xw2[b].a, (lambda n, b=b: xw2[b].iv((0, n)))) for b in range(4)]
            out_proj(w_ffn_down[layer], [((lambda t0, nn, kc=kc: act[:, kc, t0:t0 + nn]), (lambda t0, nn, kc=kc: act.iv(kc, (t0, t0 + nn))), 128) for kc in range(22)], ysb, n)
            post_norm_add(3, layer, ysb, c0, n)
            st_["wbufs"] = list(base_wbufs)
            P.release(sm)
            sm = P.mark()
            peT = P.alloc([128, 2, n], BF16, "peT")
            xin = [P.alloc([128, 256], F32, f"xin{a}") for a in range(2)]
            sig = P.alloc([128, 512], F32, "sig")
            wpl = P.alloc([128, 2, 1024], BF16, "wpl")
            xw3 = [P.alloc([128, 1024], BF16, f"xw3{b}") for b in range(4)]
            st_["wbufs"] = list(base_wbufs) + [(xw3[b].a, (lambda n, b=b: xw3[b].iv((0, n)))) for b in range(4)]
            pesrc = psm[layer] if smp else pp[layer, seqi]
            for ti, t0 in enumerate(range(0, n, 128)):
                nt = min(128, n - t0)
                xb = xin[ti % 2]
                P.dma(lambda e, xb=xb, t0=t0, nt=nt: e.dma_start(out=xb[0:nt, :], in_=pesrc[c0 + t0:c0 + t0 + nt, :]), writes=xb.iv())
                p = nps()
                for dch in range(2):
                    P.op("pe", lambda e, p=p, xb=xb, dch=dch, nt=nt: e.transpose(p[:, dch * 128:dch * 128 + nt], xb[0:nt, dch * 128:(dch + 1) * 128], ident[0:nt, 0:nt]), reads=xb.iv() + ident.iv(), writes=piv(p, nt, dch * 128))
                    P.op("act", lambda e, p=p, dch=dch, t0=t0, nt=nt: e.copy(out=peT[:, dch, t0:t0 + nt], in_=p[:, dch * 128:dch * 128 + nt]), reads=piv(p, nt, dch * 128), writes=peT.iv(dch, (t0, t0 + nt)))
            for c in range(8):
                for (t0, nn) in cts:
                    P.op("act", lambda e, c=c, t0=t0, nn=nn: e.copy(out=hn[:, c, t0:t0 + nn], in_=h[:, c, c0 + t0:c0 + t0 + nn]), reads=h.iv(c, (c0 + t0, c0 + t0 + nn)), writes=hn.iv(c, (t0, t0 + nn)))
            for hh in range(2):
                wblock(w_ple[layer].rearrange("(kc p) m -> p kc m", p=128)[:, :, hh * 512:(hh + 1) * 512], 2, 512, dst=wpl[:, :, hh * 512:(hh + 1) * 512], dst_iv=wpl.iv(None, (hh * 512, hh * 512 + 512)))
            for m_ in range(8):
                wv, wiv = wblock(wcols(w_ple_gate[layer], m_ * 128, 128), 8, 128)
                for (t0, nn) in cts:
                    p = nps()
                    mm(p[:, 0:nn], piv(p, nn), [(wv[:, kc, :], hn[:, kc, t0:t0 + nn], wiv + hn.iv(kc, (t0, t0 + nn))) for kc in range(8)])
                    p2 = nps()
                    mm(p2[:, 0:nn], piv(p2, nn), [(wpl[:, dch, m_ * 128:(m_ + 1) * 128], peT[:, dch, t0:t0 + nn], wpl.iv(dch, (m_ * 128, m_ * 128 + 128)) + peT.iv(dch, (t0, t0 + nn))) for dch in range(2)])
                    P.op("act", lambda e, p=p, nn=nn: e.activation(out=sig[:, 0:nn], in_=p[:, 0:nn], func=AF.Sigmoid), reads=piv(p, nn), writes=sig.iv((0, nn)))
                    P.op("dve", lambda e, p2=p2, nn=nn: e.tensor_tensor(out=sig[:, 0:nn], in0=sig[:, 0:nn], in1=p2[:, 0:nn], op=ALU.mult), reads=sig.iv((0, nn)) + piv(p2, nn), writes=sig.iv((0, nn)))
                    P.op("dve", lambda e, m_=m_, t0=t0, nn=nn: e.tensor_tensor(out=h[:, m_, c0 + t0:c0 + t0 + nn], in0=h[:, m_, c0 + t0:c0 + t0 + nn], in1=sig[:, 0:nn], op=ALU.add), reads=h.iv(m_, (c0 + t0, c0 + t0 + nn)) + sig.iv((0, nn)), writes=h.iv(m_, (c0 + t0, c0 + t0 + nn)))
            st_["wbufs"] = list(base_wbufs)
            P.release(sm)
        for c0_ in range(0, T_, 1024):
            st_body(c0_)
        of = o_ffn_s[layer] if smp else o_ffn_p[layer, seqi]
        for r_ in range(2):
            P.dma(lambda e, r_=r_: e.dma_start(out=of[r_].rearrange("(c p) -> p c", p=128), in_=halo[:, :, r_]), reads=halo.iv())
        P.release(lm)

    def load_x(smp, seqi, T_):
        m = P.mark()
        xin = [P.alloc([128, D], F32, f"xl{a}") for a in range(2)]
        src = xs if smp else xp[seqi]
        for ti, t0 in enumerate(range(0, T_, 128)):
            nt = min(128, T_ - t0)
            xb = xin[ti % 2]
            P.dma(lambda e, xb=xb, t0=t0, nt=nt: e.dma_start(out=xb[0:nt, :], in_=src[t0:t0 + nt, :]), writes=xb.iv())
            for c in range(8):
                p = nps()
                P.op("pe", lambda e, p=p, xb=xb, c=c, nt=nt: e.transpose(p[:, 0:nt], xb[0:nt, c * 128:(c + 1) * 128], ident[0:nt, 0:nt]), reads=xb.iv((c * 128, c * 128 + 128)) + ident.iv(), writes=piv(p, nt))
                P.op("act" if c % 2 else "dve", (lambda e, p=p, c=c, t0=t0, nt=nt: e.copy(out=h[:, c, t0:t0 + nt], in_=p[:, 0:nt])) if c % 2 else (lambda e, p=p, c=c, t0=t0, nt=nt: e.tensor_copy(out=h[:, c, t0:t0 + nt], in_=p[:, 0:nt])), reads=piv(p, nt), writes=h.iv(c, (t0, t0 + nt)))
        P.release(m)

    def store_y(smp, seqi, T_):
        m = P.mark()
        yo = [P.alloc([128, D], F32, f"yo{a}") for a in range(2)]
        dst = ys if smp else yp[seqi]
        for ti, t0 in enumerate(range(0, T_, 128)):
            nt = min(128, T_ - t0)
            yb = yo[ti % 2]
            for c in range(8):
                p = nps()
                P.op("pe", lambda e, p=p, c=c, t0=t0, nt=nt: e.transpose(p[0:nt, 0:128], h[:, c, t0:t0 + nt], ident[:]), reads=h.iv(c, (t0, t0 + nt)) + ident.iv(), writes=piv(p, 128))
                P.op("act" if c % 2 else "dve", (lambda e, p=p, c=c, yb=yb, nt=nt: e.copy(out=yb[0:nt, c * 128:(c + 1) * 128], in_=p[0:nt, 0:128])) if c % 2 else (lambda e, p=p, c=c, yb=yb, nt=nt: e.tensor_copy(out=yb[0:nt, c * 128:(c + 1) * 128], in_=p[0:nt, 0:128])), reads=piv(p, 128), writes=yb.iv((c * 128, c * 128 + 128)))
            P.dma(lambda e, yb=yb, t0=t0, nt=nt: e.dma_start(out=dst[t0:t0 + nt, :], in_=yb[0:nt, :]), reads=yb.iv())
        P.release(m)

    odd_mixer = make_odd(locals())

    for (kind, seqi) in passes:
        smp = kind == "s"
        T_ = SL if smp else SEQ
        load_x(smp, seqi, T_)
        for layer in range(nlayers):
            if layer % 2 == 0:
                even_mixer(layer // 2, layer, smp, seqi, T_)
            else:
                odd_mixer(layer // 2, layer, smp, seqi, T_)
            ffn(layer, smp, seqi, T_)
        store_y(smp, seqi, T_)
    with nc.allow_non_contiguous_dma(reason="small param/state layouts"):
        P.run()
    return P


WNAMES = ["g_mix_pre", "g_mix_post", "g_ffn_pre", "g_ffn_post", "w_even_in", "w_conv_a", "s5_a_re", "s5_a_im", "s5_log_dt",
          "s5_b_re", "s5_b_im", "s5_c_re", "s5_c_im", "s5_d", "w_glu", "w_even_out", "w_odd_in", "b_forget", "w_spatial",
          "b_spatial", "g_gmlp_v", "w_odd_out", "w_ffn_up", "w_ffn_conv", "w_ffn_down", "w_ple", "w_ple_gate"]


def make_in_maps(inp):
    f = lambda a: np.ascontiguousarray(np.asarray(a, dtype=np.float32))
    W = {k: f(inp[k]) for k in WNAMES}
    maps = []
    for c in range(8):
        m = dict(W)
        m["xp"] = f(inp["x_prompt"][2 * c:2 * c + 2]); m["xs"] = f(inp["x_sample"][c])
        m["pp"] = f(inp["p_prompt"][:, 2 * c:2 * c + 2]); m["psm"] = f(inp["p_sample"][:, c])
        m["cconv"] = f(inp["cache_conv_a"][:, c]); m["sre"] = f(inp["state_ssm_re"][:, c]); m["sim"] = f(inp["state_ssm_im"][:, c])
        m["ck"] = f(np.asarray(inp["cache_k"])[:, c].reshape(2, 1024, 512)); m["cv"] = f(np.asarray(inp["cache_v"])[:, c].reshape(2, 1024, 512))
        m["clf"] = f(inp["cache_logf"][:, c]); m["cffn"] = f(inp["cache_ffn_conv"][:, c])
        maps.append(m)
    return maps


def gather(res):
    R = res
    cat = lambda name, ax: np.concatenate([r[name] for r in R], axis=ax)
    stk = lambda name, ax: np.stack([r[name] for r in R], axis=ax)
    y_prompt = cat("yp", 0)
    y_sample = stk("ys", 0)
    conv_p = cat("o_conv_p", 1); sre_p = cat("o_sre_p", 1); sim_p = cat("o_sim_p", 1)
    k_p = cat("o_k_p", 1).reshape(2, 16, 2048, 8, 64); v_p = cat("o_v_p", 1).reshape(2, 16, 2048, 8, 64)
    lf_p = cat("o_lf_p", 1); ffn_p = cat("o_ffn_p", 1)
    conv_s = stk("o_conv_s", 1); sre_s = stk("o_sre_s", 1); sim_s = stk("o_sim_s", 1)
    k_s = stk("o_k_s", 1).reshape(2, 8, 32, 8, 64); v_s = stk("o_v_s", 1).reshape(2, 8, 32, 8, 64)
    lf_s = stk("o_lf_s", 1); gv_s = stk("o_gv_s", 1); ffn_s = stk("o_ffn_s", 1)
    outs = (y_prompt, y_sample, conv_p, sre_p, sim_p, k_p, v_p, lf_p, ffn_p, conv_s, sre_s, sim_s, k_s, v_s, lf_s, gv_s, ffn_s)
    return tuple(np.ascontiguousarray(o.astype(np.float32)) for o in outs)


def kernel(**inputs):
    nc = bass.Bass("TRN2", target_bir_lowering=False)
    build(nc)
    maps = make_in_maps(inputs)
    res = run_bass_kernel_spmd(nc, maps, core_ids=list(range(8)))
    return gather(res.results)
```

```python
import numpy as np
import concourse.bass as bass
import concourse.mybir as mybir

F32 = mybir.dt.float32
BF16 = mybir.dt.bfloat16
I32 = mybir.dt.int32
ALU = mybir.AluOpType
AF = mybir.ActivationFunctionType
AX = mybir.AxisListType

ENGS = ("pe", "act", "dve", "pool", "sp")
EPOCH = 30000


class T:
    _n = 0

    def __init__(self, handle, shape, name):
        self.h = handle
        self.shape = list(shape)
        self.name = name
        self.id = T._n
        T._n += 1
        st = [1] * len(shape)
        for i in range(len(shape) - 2, 0, -1):
            st[i] = st[i + 1] * shape[i + 1]
        self.st = st
        self.recs = []

    def __getitem__(self, idx):
        return self.h[idx]

    def iv(self, *idx):
        nd = len(self.shape) - 1
        idx = list(idx) + [None] * (nd - len(idx))
        rng = []
        for d, ix in enumerate(idx):
            n = self.shape[d + 1]
            if ix is None:
                rng.append((0, n))
            elif isinstance(ix, tuple):
                rng.append(ix)
            else:
                rng.append((ix, ix + 1))
        out = [(0, 0)]
        out = []

        def rec(d, base):
            if d == nd - 1:
                out.append((self, base + rng[d][0] * self.st[d + 1], base + (rng[d][1] - 1) * self.st[d + 1] + 1))
                return
            full = all(rng[k] == (0, self.shape[k + 1]) for k in range(d + 1, nd))
            if full:
                out.append((self, base + rng[d][0] * self.st[d + 1], base + rng[d][1] * self.st[d + 1]))
                return
            for i in range(rng[d][0], rng[d][1]):
                rec(d + 1, base + i * self.st[d + 1])

        rec(0, 0)
        return out


class V:
    def __init__(self, arena, off, shape, dt, name="v"):
        self.t = arena
        self.off = off
        self.shape = list(shape)
        self.dt = dt
        self.u = 1 if dt == BF16 else 2
        n = 1
        for x in shape[1:]:
            n *= x
        self.n = n
        a = arena.h[0:shape[0], off:off + n * self.u]
        if dt != BF16:
            a = a.bitcast(dt)
        if len(shape) == 3:
            a = a.rearrange("p (a b) -> p a b", a=shape[1])
        elif len(shape) == 4:
            a = a.rearrange("p (a b c) -> p a b c", a=shape[1], b=shape[2])
        self.a = a
        st = [1] * len(shape)
        for i in range(len(shape) - 2, 0, -1):
            st[i] = st[i + 1] * shape[i + 1]
        self.st = st

    def __getitem__(self, idx):
        return self.a[idx]

    def iv(self, *idx):
        nd = len(self.shape) - 1
        idx = list(idx) + [None] * (nd - len(idx))
        rng = []
        for d, ix in enumerate(idx):
            n = self.shape[d + 1]
            if ix is None:
                rng.append((0, n))
            elif isinstance(ix, tuple):
                rng.append(ix)
            else:
                rng.append((ix, ix + 1))
        out = []
        u = self.u
        off = self.off

        def rec(d, base):
            if d == nd - 1:
                out.append((self.t, off + u * (base + rng[d][0]), off + u * (base + rng[d][1])))
                return
            full = all(rng[k] == (0, self.shape[k + 1]) for k in range(d + 1, nd))
            if full:
                out.append((self.t, off + u * (base + rng[d][0] * self.st[d + 1]), off + u * (base + rng[d][1] * self.st[d + 1])))
                return
            for i in range(rng[d][0], rng[d][1]):
                rec(d + 1, base + i * self.st[d + 1])

        rec(0, 0)
        return out


class Op:
    __slots__ = ("eng", "emit", "deps", "signal", "tok", "waits", "dma", "slot")

    def __init__(self, eng, emit):
        self.eng = eng
        self.emit = emit
        self.deps = set()
        self.signal = False
        self.tok = None
        self.waits = None
        self.dma = False
        self.slot = None


class Prog:
    def __init__(self, nc, n_dma_slots=48):
        self.nc = nc
        self.ops = []
        self.n_dma_slots = n_dma_slots
        self.dma_rr = 0
        self.dma_rr_sw = 0
        self.ctx = []
        self.bar = None

    def sb(self, name, shape, dt):
        g = self.nc.sbuf_tensor(name, list(shape), dt)
        h = g.__enter__()
        self.ctx.append(g)
        return T(h, shape, name)

    def ps(self, name, shape, dt=F32):
        g = self.nc.psum_tensor(name, list(shape), dt)
        h = g.__enter__()
        self.ctx.append(g)
        t = T(h, shape, name)
        t.psum = True
        return t

    def make_arena(self, nbytes):
        nbytes = nbytes // 256 * 256
        self.arena = self.sb("arena", [128, nbytes // 2], BF16)
        self.atop = 0
        self.bar = None

    def mark(self):
        return self.atop

    def release(self, m):
        self.atop = m
        self.barrier()

    def alloc(self, shape, dt, name="v"):
        es = 2 if dt == BF16 else 4
        n = 1
        for x in shape[1:]:
            n *= x
        nb = (n * es + 63) // 64 * 64
        off = self.atop
        self.atop += nb
        assert self.atop <= self.arena.shape[1] * 2, f"arena overflow {name} {self.atop}"
        return V(self.arena, off // 2, shape, dt, name)

    def barrier(self):
        last = {}
        for i in range(len(self.ops) - 1, -1, -1):
            o = self.ops[i]
            if o.dma:
                k = ("d", o.slot)
            else:
                k = o.eng
            if k not in last:
                last[k] = i
            if len(last) >= len(ENGS) + self.n_dma_slots:
                break
        self.bar = (set(last.values()), set())

    def _track(self, op_idx, reads, writes):
        op = self.ops[op_idx]
        isdma = op.dma
        pw = [(t, 0, t.shape[1]) for (t, lo, hi) in list(reads) + list(writes) if getattr(t, "psum", False)]
        if pw:
            reads = [x for x in reads if not getattr(x[0], "psum", False)]
            seen = set()
            writes = [x for x in writes if not getattr(x[0], "psum", False)]
            for x in pw:
                if id(x[0]) not in seen:
                    seen.add(id(x[0]))
                    writes.append(x)
        for (t, lo, hi) in reads:
            recs = t.recs
            keep = []
            for r in recs:
                (l2, h2, j, w) = r
                if w:
                    if l2 < hi and lo < h2:
                        op.deps.add(j)
                elif (not isdma) and lo <= l2 and h2 <= hi and self.ops[j].eng == op.eng and not self.ops[j].dma:
                    continue
                keep.append(r)
            keep.append((lo, hi, op_idx, False))
            t.recs = keep
        for (t, lo, hi) in writes:
            recs = t.recs
            keep = []
            for r in recs:
                (l2, h2, j, w) = r
                if l2 < hi and lo < h2:
                    if j != op_idx:
                        op.deps.add(j)
                    if lo <= l2 and h2 <= hi:
                        continue
                keep.append(r)
            keep.append((lo, hi, op_idx, True))
            t.recs = keep

    def op(self, eng, emit, reads=(), writes=(), dma=False):
        o = Op(eng, emit)
        o.dma = dma
        if self.bar is not None and eng not in self.bar[1]:
            o.deps |= self.bar[0]
            self.bar[1].add(eng)
        self.ops.append(o)
        self._track(len(self.ops) - 1, reads, writes)
        return o

    def dma(self, emit, reads=(), writes=(), queue="sp"):
        o = self.op(queue, emit, reads, writes, dma=True)
        half = self.n_dma_slots // 2
        if queue == "pool":
            o.slot = half + (self.dma_rr_sw % half)
            self.dma_rr_sw += 1
        else:
            o.slot = self.dma_rr % half
            self.dma_rr += 1
        return o

    def run(self):
        nc = self.nc
        ops = self.ops
        for o in ops:
            for j in o.deps:
                ops[j].signal = True
        cnt = {e: 0 for e in ENGS}
        slot_cnt = [0] * self.n_dma_slots
        slot_last = [None] * self.n_dma_slots
        for i, o in enumerate(ops):
            if o.dma:
                o.signal = True
                prev = slot_last[o.slot]
                if prev is not None:
                    o.deps.add(prev)
                slot_last[o.slot] = i
                slot_cnt[o.slot] += 1
                o.tok = ("d", o.slot, 16 * slot_cnt[o.slot])
            elif o.signal:
                cnt[o.eng] += 1
                n = cnt[o.eng]
                o.tok = ("e", o.eng, (n - 1) // EPOCH, (n - 1) % EPOCH + 1)
        n_ep = {e: (cnt[e] + EPOCH - 1) // EPOCH for e in ENGS}
        sems = {}
        guards = []
        for e in ENGS:
            for k in range(max(1, n_ep[e])):
                g = nc.semaphore(f"s_{e}_{k}")
                sems[("e", e, k)] = g.__enter__()
                guards.append(g)
        for s in range(self.n_dma_slots):
            g = nc.semaphore(f"s_dma_{s}")
            sems[("d", s)] = g.__enter__()
            guards.append(g)
        per_eng = {e: [] for e in ENGS}
        for i, o in enumerate(ops):
            per_eng[o.eng].append(i)
        self.n_waits = 0

        def tok_key(tok):
            if tok[0] == "d":
                return ("d", tok[1]), tok[2]
            return ("e", tok[1], tok[2]), tok[3]

        def emit_engine(engname, eng):
            waited = {}
            for i in per_eng[engname]:
                o = ops[i]
                need = {}
                for j in o.deps:
                    k, v = tok_key(ops[j].tok)
                    if need.get(k, 0) < v:
                        need[k] = v
                for k, v in need.items():
                    if waited.get(k, 0) >= v:
                        continue
                    waited[k] = v
                    eng.wait_ge(sems[k], v)
                    self.n_waits += 1
                inst = o.emit(eng)
                if o.signal:
                    k, v = tok_key(o.tok)
                    inst.then_inc(sems[k], 16 if o.dma else 1)
            if engname == "sp":
                for s in range(self.n_dma_slots):
                    if slot_cnt[s]:
                        eng.wait_ge(sems[("d", s)], 16 * slot_cnt[s])

        with nc.Block() as block:
            @block.tensor
            def _(e):
                emit_engine("pe", e)

            @block.scalar
            def _(e):
                emit_engine("act", e)

            @block.vector
            def _(e):
                emit_engine("dve", e)

            @block.gpsimd
            def _(e):
                emit_engine("pool", e)

            @block.sync
            def _(e):
                emit_engine("sp", e)
        for g in reversed(guards):
            g.__exit__(None, None, None)
        for g in reversed(self.ctx):
            g.__exit__(None, None, None)
        self.stats = dict(n_ops=len(ops), cnt=cnt, n_waits=self.n_waits)


import os
class _Stop(Exception):
    pass

def make_odd(L):
    STOP = float(os.environ.get('ODD_STOP', '99'))
    P = L["P"]; nps = L["nps"]; nps_held = L["nps_held"]; piv = L["piv"]; mm = L["mm"]; wblock = L["wblock"]; wcols = L["wcols"]
    hn = L["hn"]; h = L["h"]; ident = L["ident"]; onesf = L["onesf"]; onesb = L["onesb"]; maskw = L["maskw"]; negc = L["negc"]
    pre_norm = L["pre_norm"]; post_norm_add = L["post_norm_add"]; out_proj = L["out_proj"]
    w_odd_in = L["w_odd_in"]; b_forget = L["b_forget"]; w_spatial = L["w_spatial"]; b_spatial = L["b_spatial"]
    g_gmlp_v = L["g_gmlp_v"]; w_odd_out = L["w_odd_out"]
    ck = L["ck"]; cv = L["cv"]; clf = L["clf"]
    o_k_p = L["o_k_p"]; o_v_p = L["o_v_p"]; o_lf_p = L["o_lf_p"]; o_k_s = L["o_k_s"]; o_v_s = L["o_v_s"]; o_lf_s = L["o_lf_s"]; o_gv_s = L["o_gv_s"]
    EPS = 1e-6
    PAST = 1024

    def odd_mixer(j, layer, smp, seqi, T_):
        lm = P.mark()
        try:
            odd_body(j, layer, smp, seqi, T_)
        except _Stop:
            pass
        P.release(lm)

    def odd_body(j, layer, smp, seqi, T_):
        W = w_odd_in[j]
        NT = (T_ + 127) // 128
        Kaug = P.alloc([96, 8, T_], BF16, "Kaug")
        Vt = P.alloc([128, NT, 8, 65], BF16, "Vt")
        negF = P.alloc([128, NT, 8], F32, "negF")
        Fcar = P.alloc([128, 8], F32, "Fcar")
        WsT = P.alloc([128, 8, 128], BF16, "WsT")
        hselb = P.alloc([8, 4, 128], BF16, "hselb")
        bsph = P.alloc([8, 128], BF16, "bsph")
        bspl = P.alloc([8, 128], BF16, "bspl")
        gvb = P.alloc([128, 512], F32, "gvb")
        bfn = P.alloc([128, 8], F32, "bfn")
        wfl = P.alloc([128, 8, 8], BF16, "wfl")
        ones1 = P.alloc([128, 512], F32, "ones1")
        P.op("pool", lambda e: e.memset(Kaug[64:96, :, :], 0.0), writes=Kaug.iv())
        P.op("pool", lambda e: e.memset(Kaug[64:66, :, :], 1.0), writes=Kaug.iv())
        bsel = P.alloc([128, 64], BF16, "bsel")
        P.op("dve", lambda e: e.tensor_copy(out=bsel[:], in_=ident[:, 64:65].to_broadcast([128, 64])), reads=ident.iv(), writes=bsel.iv())
        P.op("pool", lambda e: e.memset(Vt[:], 1.0), writes=Vt.iv())
        P.op("pool", lambda e: e.memset(Fcar[:], 0.0), writes=Fcar.iv())
        P.op("pool", lambda e: e.memset(ones1[:], 1.0), writes=ones1.iv())
        P.dma(lambda e: e.dma_start(out=gvb[:], in_=g_gmlp_v[j:j + 1, :].broadcast_to([128, 512])), writes=gvb.iv())
        P.dma(lambda e: e.dma_start(out=bfn[64:66, :], in_=b_forget[j:j + 1, :].broadcast_to([2, 8])), writes=bfn.iv())
        P.op("dve", lambda e: e.tensor_scalar(out=bfn[64:66, :], in0=bfn[64:66, :], scalar1=-1.0, scalar2=None, op0=ALU.mult), reads=bfn.iv(), writes=bfn.iv())
        m0 = P.mark()
        hsel = P.alloc([8, 4, 128], F32, "hsel"); bsp = P.alloc([8, 128], F32, "bsp"); bspt = P.alloc([8, 128], F32, "bspt")
        P.dma(lambda e: e.dma_start(out=bsp[:], in_=b_spatial[j]), writes=bsp.iv())
        P.op("pool", lambda e: e.memset(hsel[:], 1.0), writes=hsel.iv())
        P.op("pool", lambda e: e.affine_select(out=hsel[:], in_=hsel[:], pattern=[[128, 4], [1, 128]], compare_op=ALU.is_ge, fill=0.0, base=0, channel_multiplier=-64), reads=hsel.iv(), writes=hsel.iv())
        P.op("pool", lambda e: e.affine_select(out=hsel[:], in_=hsel[:], pattern=[[-128, 4], [-1, 128]], compare_op=ALU.is_ge, fill=0.0, base=63, channel_multiplier=64), reads=hsel.iv(), writes=hsel.iv())
        P.op("dve", lambda e: e.tensor_copy(out=hselb[:], in_=hsel[:]), reads=hsel.iv(), writes=hselb.iv())
        P.op("dve", lambda e: e.tensor_copy(out=bsph[:], in_=bsp[:]), reads=bsp.iv(), writes=bsph.iv())
        P.op("dve", lambda e: e.tensor_copy(out=bspt[:], in_=bsph[:]), reads=bsph.iv(), writes=bspt.iv())
        P.op("dve", lambda e: e.tensor_tensor(out=bspt[:], in0=bsp[:], in1=bspt[:], op=ALU.subtract), reads=bsp.iv() + bspt.iv(), writes=bspt.iv())
        P.op("dve", lambda e: e.tensor_copy(out=bspl[:], in_=bspt[:]), reads=bspt.iv(), writes=bspl.iv())
        wsn = P.alloc([128, 8, 128], F32, "wsn")
        P.dma(lambda e: e.dma_start(out=wsn[:], in_=w_spatial[j].rearrange("h t s -> t h s")), writes=wsn.iv())
        P.op("pool", lambda e: e.affine_select(out=wsn[:], in_=wsn[:], pattern=[[0, 8], [-1, 128]], compare_op=ALU.is_ge, fill=0.0, base=0, channel_multiplier=1), reads=wsn.iv(), writes=wsn.iv())
        for hg in range(2):
            p = nps()
            for hl in range(4):
                hd = hg * 4 + hl
                P.op("pe", lambda e, p=p, hd=hd, hl=hl: e.transpose(p[:, hl * 128:(hl + 1) * 128], wsn[:, hd, :], ident[:]), reads=wsn.iv(hd) + ident.iv(), writes=piv(p, 128, hl * 128))
            P.op("act", lambda e, p=p, hg=hg: e.copy(out=WsT[:, hg * 4:(hg + 1) * 4, :], in_=p[:, :].rearrange("p (a b) -> p a b", a=4)), reads=piv(p, 512), writes=WsT.iv((hg * 4, hg * 4 + 4)))
        P.release(m0)
        if smp:
            Kc = P.alloc([96, 8, PAST], BF16, "Kc")
            Vc = P.alloc([128, 8, 8, 65], BF16, "Vc")
            Dc = P.alloc([128, 8, 8], F32, "Dc")
            m0 = P.mark()
            ctile = [P.alloc([128, 512], F32, f"ctile{a}") for a in range(2)]
            lfc = P.alloc([128, 8, 8], F32, "lfc")
            ustr = P.alloc([128, 128], F32, "ustr")
            P.op("pool", lambda e: e.memset(Kc[64:96, :, :], 0.0), writes=Kc.iv())
            P.op("pool", lambda e: e.memset(Kc[64:66, :, :], 1.0), writes=Kc.iv())
            P.op("pool", lambda e: e.memset(Vc[:], 1.0), writes=Vc.iv())
            P.op("pool", lambda e: e.affine_select(out=ustr[:], in_=onesf[:], pattern=[[-1, 128]], compare_op=ALU.is_gt, fill=0.0, base=0, channel_multiplier=1), reads=onesf.iv(), writes=ustr.iv())
            P.dma(lambda e: e.dma_start(out=lfc[:], in_=clf[j].rearrange("(t p) hd -> p t hd", p=128)), writes=lfc.iv())
            ustrb = P.alloc([128, 128], BF16, "ustrb"); lfh = P.alloc([128, 8, 8], BF16, "lfh"); lfl = P.alloc([128, 8, 8], BF16, "lfl"); lft = P.alloc([128, 8, 8], F32, "lft")
            P.op("dve", lambda e: e.tensor_copy(out=ustrb[:], in_=ustr[:]), reads=ustr.iv(), writes=ustrb.iv())
            P.op("dve", lambda e: e.tensor_copy(out=lfh[:], in_=lfc[:]), reads=lfc.iv(), writes=lfh.iv())
            P.op("dve", lambda e: e.tensor_copy(out=lft[:], in_=lfh[:]), reads=lfh.iv(), writes=lft.iv())
            P.op("dve", lambda e: e.tensor_tensor(out=lft[:], in0=lfc[:], in1=lft[:], op=ALU.subtract), reads=lfc.iv() + lft.iv(), writes=lft.iv())
            P.op("dve", lambda e: e.tensor_copy(out=lfl[:], in_=lft[:]), reads=lft.iv(), writes=lfl.iv())
            for t in range(8):
                cb = ctile[t % 2]
                P.dma(lambda e, cb=cb, t=t: e.dma_start(out=cb[:], in_=ck[j, t * 128:(t + 1) * 128, :]), writes=cb.iv())
                for hg in range(2):
                    p = nps()
                    for hl in range(4):
                        hd = hg * 4 + hl
                        P.op("pe", lambda e, p=p, cb=cb, hd=hd, hl=hl: e.transpose(p[0:64, hl * 128:(hl + 1) * 128], cb[:, hd * 64:(hd + 1) * 64], ident[:]), reads=cb.iv() + ident.iv(), writes=piv(p, 128, hl * 128))
                    P.op("act", lambda e, p=p, hg=hg, t=t: e.copy(out=Kc[0:64, hg * 4:(hg + 1) * 4, t * 128:(t + 1) * 128], in_=p[0:64, :].rearrange("p (a b) -> p a b", a=4)), reads=piv(p, 512), writes=Kc.iv())
                cb2 = ctile[(t + 1) % 2]
                P.dma(lambda e, cb2=cb2, t=t: e.dma_start(out=cb2[:], in_=cv[j, t * 128:(t + 1) * 128, :]), writes=cb2.iv())
                P.op("dve", lambda e, cb2=cb2, t=t: e.tensor_copy(out=Vc[:, t, :, 0:64], in_=cb2[:, :].rearrange("p (a b) -> p a b", a=8)), reads=cb2.iv(), writes=Vc.iv(t))
                p = nps()
                terms = [(ustrb[:], lfh[:, t, :], ustrb.iv() + lfh.iv(t)), (ustrb[:], lfl[:, t, :], ustrb.iv() + lfl.iv(t))]
                for t2 in range(t + 1, 8):
                    terms.append((onesb[:], lfh[:, t2, :], onesb.iv() + lfh.iv(t2)))
                    terms.append((onesb[:], lfl[:, t2, :], onesb.iv() + lfl.iv(t2)))
                mm(p[:, 0:8], piv(p, 8), terms)
                P.op("dve", lambda e, p=p, t=t: e.tensor_copy(out=Dc[:, t, :], in_=p[:, 0:8]), reads=piv(p, 8), writes=Dc.iv(t))
            P.release(m0)
        if STOP <= 1:
            raise _Stop()
        wblock(wcols(W, 1536, 8), 8, 8, dst=wfl[:], dst_iv=wfl.iv())

        def st_body(c0):
            n = min(512, T_ - c0)
            nt_ = (n + 127) // 128
            sm = P.mark()
            pre_norm(0, layer, c0, n)
            att = P.alloc([64, 8, n], BF16, "att"); ydb = P.alloc([128, 4, n], BF16, "ydb")
            sm2 = P.mark()
            Q = [P.alloc([96, n], BF16, f"Q{a}") for a in range(2)]
            wq = [P.alloc([128, 8, 66], BF16, f"wq{a}") for a in range(2)]
            wtok = P.alloc([128, 8, 512], BF16, "wtok")
            lfrow = P.alloc([128, 512], F32, "lfrow"); Frow = P.alloc([128, 512], F32, "Frow"); x8 = P.alloc([128, 512], F32, "x8")
            hib = P.alloc([128, 512], BF16, "hib")
            kvout = [lfrow, Frow]
            pT = [P.alloc([128, 512], BF16, f"pT{a}") for a in range(3)]
            s2 = P.alloc([128, 512], F32, "s2"); oT = P.alloc([128, 512], F32, "oT"); rden = P.alloc([128, 512], F32, "rden")
            rdh = P.alloc([128, 512], BF16, "rdh"); rdl = P.alloc([128, 512], BF16, "rdl")
            vnf = oT; vnpad = P.alloc([128, 8, 128], BF16, "vnpad"); junk = s2
            ssq = P.alloc([128, 2], F32, "ssq"); lfT = P.alloc([128, 4, 8], F32, "lfT")
            P.op("pool", lambda e: e.memset(vnpad[:], 0.0), writes=vnpad.iv())
            for qq in Q:
                P.op("pool", lambda e, qq=qq: e.memset(qq[64:96, :], 0.0), writes=qq.iv())
            for bb_ in (lfrow, Frow, rden):
                P.op("pool", lambda e, bb_=bb_: e.memset(bb_[64:96, :], 0.0), writes=bb_.iv())
            for bb_ in (rdh, rdl):
                P.op("pool", lambda e, bb_=bb_: e.memset(bb_[64:96, :], 0.0), writes=bb_.iv())

            def load_wtok(col0):
                for b in range(4):
                    wblock(wcols(W, col0 + b * 128, 128), 8, 128, dst=wtok[:, :, b * 128:(b + 1) * 128], dst_iv=wtok.iv(None, (b * 128, b * 128 + 128)))

            def tok_proj(ti):
                t0 = ti * 128
                nt = min(128, n - t0)
                p = nps()
                mm(p[0:nt, :], piv(p, 512), [(hn[:, kc, t0:t0 + nt], wtok[:, kc, :], hn.iv(kc, (t0, t0 + nt)) + wtok.iv(kc)) for kc in range(8)])
                return p, t0, nt
            if STOP <= 1.2:
                raise _Stop()
            load_wtok(1024)
            if STOP <= 1.4:
                raise _Stop()
            for ti in range(nt_):
                p, t0, nt = tok_proj(ti)
                if STOP <= 1.6:
                    raise _Stop()
                gt = (c0 + t0) // 128
                ko = kvout[ti % 2]
                P.op("act", lambda e, p=p, gt=gt, nt=nt: e.copy(out=Vt[0:nt, gt, :, 0:64], in_=p[0:nt, :].rearrange("p (a b) -> p a b", a=8)), reads=piv(p, 512), writes=Vt.iv(gt))
                if STOP <= 1.7:
                    raise _Stop()
                P.op("dve", lambda e, p=p, ko=ko, nt=nt: e.tensor_copy(out=ko[0:nt, :], in_=p[0:nt, :]), reads=piv(p, 512), writes=ko.iv())
                if STOP <= 1.8:
                    raise _Stop()
                dst = (o_v_s[j] if smp else o_v_p[j, seqi])[c0 + t0:c0 + t0 + nt, :]
                P.dma(lambda e, ko=ko, dst=dst, nt=nt: e.dma_start(out=dst, in_=ko[0:nt, :]), reads=ko.iv())
            load_wtok(512)
            for ti in range(nt_):
                p, t0, nt = tok_proj(ti)
                ko = kvout[ti % 2]
                P.op("dve", lambda e, p=p, ko=ko, nt=nt: e.tensor_copy(out=ko[0:nt, :], in_=p[0:nt, :]), reads=piv(p, 512), writes=ko.iv())
                dst = (o_k_s[j] if smp else o_k_p[j, seqi])[c0 + t0:c0 + t0 + nt, :]
                P.dma(lambda e, ko=ko, dst=dst, nt=nt: e.dma_start(out=dst, in_=ko[0:nt, :]), reads=ko.iv())
            if STOP <= 2:
                raise _Stop()
            for i in range(4):
                wv, wiv = wblock(wcols(W, 1544 + i * 128, 128), 8, 128)
                p = nps()
                mm(p[:, 0:n], piv(p, n), [(wv[:, kc, :], hn[:, kc, 0:n], wiv + hn.iv(kc, (0, n))) for kc in range(8)])
                P.op("act", lambda e, p=p, i=i: e.copy(out=ydb[:, i, :], in_=p[:, 0:n]), reads=piv(p, n), writes=ydb.iv(i))
            load_wtok(2056)
            for ti in range(nt_):
                p, t0, nt = tok_proj(ti)
                P.op("act", lambda e, p=p, nt=nt: e.activation(out=junk[0:nt, :], in_=p[0:nt, :], func=AF.Square, accum_out=ssq[0:nt, 0:1]), reads=piv(p, 512), writes=junk.iv() + ssq.iv())
                P.op("act", lambda e, nt=nt: e.activation(out=ssq[0:nt, 1:2], in_=ssq[0:nt, 0:1], func=AF.Sqrt, bias=EPS, scale=1.0 / 512), reads=ssq.iv(), writes=ssq.iv())
                P.op("dve", lambda e, nt=nt: e.reciprocal(out=ssq[0:nt, 1:2], in_=ssq[0:nt, 1:2]), reads=ssq.iv(), writes=ssq.iv())
                P.op("dve", lambda e, p=p, nt=nt: e.scalar_tensor_tensor(out=vnf[0:nt, :], in0=p[0:nt, :], scalar=ssq[0:nt, 1:2], in1=gvb[0:nt, :], op0=ALU.mult, op1=ALU.mult), reads=piv(p, 512) + ssq.iv() + gvb.iv(), writes=vnf.iv())
                if smp:
                    P.dma(lambda e, t0=t0, nt=nt: e.dma_start(out=o_gv_s[j, c0 + t0:c0 + t0 + nt, :], in_=vnf[0:nt, :]), reads=vnf.iv())
                v4 = vnf[0:nt, :].rearrange("p (a b c) -> p a b c", a=4, b=2)
                vp4 = vnpad[0:nt, :, :].rearrange("p (a b) c -> p a b c", b=2)
                P.op("act", lambda e, v4=v4, vp4=vp4: e.copy(out=vp4[:, :, 0, 0:64], in_=v4[:, :, 0, :]), reads=vnf.iv(), writes=vnpad.iv())
                P.op("act", lambda e, v4=v4, vp4=vp4: e.copy(out=vp4[:, :, 1, 64:128], in_=v4[:, :, 1, :]), reads=vnf.iv(), writes=vnpad.iv())
                pm = nps()
                for jj in range(4):
                    terms = []
                    for hd in (2 * jj, 2 * jj + 1):
                        terms.append((vnpad[0:nt, hd, :], WsT[0:nt, hd, 0:nt], vnpad.iv(hd) + WsT.iv(hd)))
                    terms.append((hselb[0:8, jj, :], bsph[0:8, 0:nt], hselb.iv(jj) + bsph.iv()))
                    terms.append((hselb[0:8, jj, :], bspl[0:8, 0:nt], hselb.iv(jj) + bspl.iv()))
                    mm(pm[:, jj * 128:jj * 128 + nt], piv(pm, nt, jj * 128), terms)
                    P.op("dve", lambda e, pm=pm, jj=jj, t0=t0, nt=nt: e.tensor_tensor(out=ydb[:, jj, t0:t0 + nt], in0=ydb[:, jj, t0:t0 + nt], in1=pm[:, jj * 128:jj * 128 + nt], op=ALU.mult), reads=ydb.iv(jj, (t0, t0 + nt)) + piv(pm, nt, jj * 128), writes=ydb.iv(jj, (t0, t0 + nt)))
            if STOP <= 3:
                raise _Stop()
            def chain(hd):
                Qh = Q[hd % 2]; wqh = wq[hd % 2]
                wblock(wcols(W, hd * 64, 64), 8, 64, dst=wqh[:, :, 0:64], dst_iv=wqh.iv(None, (0, 64)))
                P.op("pool", lambda e, wqh=wqh, hd=hd: e.tensor_copy(out=wqh[:, :, 64:65], in_=wfl[:, :, hd:hd + 1]), reads=wfl.iv(), writes=wqh.iv(None, (64, 65)))
                P.op("pool", lambda e, wqh=wqh, hd=hd: e.tensor_copy(out=wqh[:, :, 65:66], in_=wfl[:, :, hd:hd + 1]), reads=wfl.iv(), writes=wqh.iv(None, (65, 66)))
                pq = nps()
                mm(pq[0:66, 0:n], piv(pq, n), [(wqh[:, kc, :], hn[:, kc, 0:n], wqh.iv(kc) + hn.iv(kc, (0, n))) for kc in range(8)])
                P.op("act", lambda e, pq=pq, Qh=Qh: e.copy(out=Qh[0:64, :], in_=pq[0:64, 0:n]), reads=piv(pq, n), writes=Qh.iv())
                r = slice(64, 66)
                P.op("act", lambda e, pq=pq, hd=hd: e.activation(out=lfrow[r, 0:n], in_=pq[r, 0:n], func=AF.Exp, bias=bfn[r, hd:hd + 1], scale=-1.0), reads=piv(pq, n) + bfn.iv(), writes=lfrow.iv())
                P.op("act", lambda e: e.activation(out=lfrow[r, 0:n], in_=lfrow[r, 0:n], func=AF.Ln, bias=1.0, scale=1.0), reads=lfrow.iv(), writes=lfrow.iv())
                P.op("dve", lambda e: e.tensor_scalar(out=lfrow[r, 0:n], in0=lfrow[r, 0:n], scalar1=-1.0, scalar2=None, op0=ALU.mult), reads=lfrow.iv(), writes=lfrow.iv())
                P.op("dve", lambda e, hd=hd: e.tensor_tensor_scan(out=Frow[r, 0:n], data0=ones1[r, 0:n], data1=lfrow[r, 0:n], initial=Fcar[r, hd:hd + 1], op0=ALU.mult, op1=ALU.add), reads=ones1.iv() + lfrow.iv() + Fcar.iv(), writes=Frow.iv())
                P.op("dve", lambda e, hd=hd: e.tensor_copy(out=Fcar[r, hd:hd + 1], in_=Frow[r, n - 1:n]), reads=Frow.iv(), writes=Fcar.iv())
                P.op("dve", lambda e: e.tensor_scalar(out=x8[r, 0:n], in0=Frow[r, 0:n], scalar1=8.0, scalar2=None, op0=ALU.mult), reads=Frow.iv(), writes=x8.iv())
                P.op("dve", lambda e: e.tensor_copy(out=hib[r, 0:n], in_=x8[r, 0:n]), reads=x8.iv(), writes=hib.iv())
                P.op("dve", lambda e, Qh=Qh: e.scalar_tensor_tensor(out=Qh[r, :], in0=hib[r, 0:n], scalar=negc[r, 0:1], in1=x8[r, 0:n], op0=ALU.mult, op1=ALU.add), reads=hib.iv() + negc.iv() + x8.iv(), writes=Qh.iv())
                wv, wiv = wblock(wcols(W, 512 + hd * 64, 64), 8, 64)
                pk = nps()
                mm(pk[0:64, 0:n], piv(pk, n), [(wv[:, kc, :], hn[:, kc, 0:n], wiv + hn.iv(kc, (0, n))) for kc in range(8)])
                P.op("act", lambda e, pk=pk, hd=hd: e.copy(out=Kaug[0:64, hd, c0:c0 + n], in_=pk[0:64, 0:n]), reads=piv(pk, n), writes=Kaug.iv(hd, (c0, c0 + n)))

            def chainB(hd):
                pf = nps()
                for ti in range(nt_):
                    t0 = ti * 128; nt = min(128, n - t0)
                    P.op("pe", lambda e, pf=pf, ti=ti, t0=t0, nt=nt: e.transpose(pf[0:nt, 64 * ti:64 * ti + 32], Frow[64:96, t0:t0 + nt], ident[64:96, 64:96]), reads=Frow.iv() + ident.iv(), writes=piv(pf, 32, 64 * ti))
                    P.op("pe", lambda e, pf=pf, ti=ti, t0=t0, nt=nt: e.transpose(pf[0:nt, 64 * ti + 32:64 * ti + 64], lfrow[64:96, t0:t0 + nt], ident[64:96, 64:96]), reads=lfrow.iv() + ident.iv(), writes=piv(pf, 32, 64 * ti + 32))
                    gt = (c0 + t0) // 128
                    P.op("dve", lambda e, pf=pf, ti=ti, gt=gt, nt=nt, hd=hd: e.tensor_scalar(out=negF[0:nt, gt, hd:hd + 1], in0=pf[0:nt, 64 * ti:64 * ti + 1], scalar1=-1.0, scalar2=None, op0=ALU.mult), reads=piv(pf, 32, 64 * ti), writes=negF.iv(gt))
                    P.op("dve", lambda e, pf=pf, ti=ti, nt=nt, hd=hd: e.tensor_copy(out=lfT[0:nt, ti, hd:hd + 1], in_=pf[0:nt, 64 * ti + 32:64 * ti + 33]), reads=piv(pf, 32, 64 * ti + 32), writes=lfT.iv(ti))

            def attend(hd):
                Qh = Q[hd % 2]
                keys = []
                if smp:
                    for t in range(8):
                        keys.append((Kc[:, hd, t * 128:(t + 1) * 128], Kc.iv(hd), Vc[:, t, hd, :], Vc.iv(t), Dc[:, t, hd:hd + 1], Dc.iv(t), 128, None))
                nkt = (c0 + n + 127) // 128
                for kt in range(nkt):
                    nk = min(128, T_ - kt * 128)
                    rel = kt * 128 - c0
                    keys.append((Kaug[:, hd, kt * 128:kt * 128 + nk], Kaug.iv(hd, (kt * 128, kt * 128 + nk)), Vt[0:nk, kt, hd, :], Vt.iv(kt), negF[0:nk, kt, hd:hd + 1], negF.iv(kt), nk, rel if rel >= 0 else None))
                po = nps_held()
                pend = []

                def emit_pv(x):
                    (ki, va, viv, pt_, nk) = x
                    P.op("pe", lambda e, po=po, va=va, pt_=pt_, nk=nk, ki=ki, last=(ki == len(keys) - 1): e.matmul(po[0:65, 0:n], lhsT=va, rhs=pt_[0:nk, 0:n], start=(ki == 0), stop=last), reads=viv + pt_.iv() + (piv(po, n) if ki else []), writes=piv(po, n))
                for ki, (ka, kiv, va, viv, ba, biv, nk, rel) in enumerate(keys):
                    pscore = nps()
                    mm(pscore[0:nk, 0:n], piv(pscore, n), [(ka, Qh[:, :], kiv + Qh.iv())])
                    pt_ = pT[ki % 3]
                    if rel is None:
                        P.op("act", lambda e, pscore=pscore, pt_=pt_, ba=ba, nk=nk: e.activation(out=pt_[0:nk, 0:n], in_=pscore[0:nk, 0:n], func=AF.Exp, bias=ba, scale=0.125), reads=piv(pscore, n) + biv, writes=pt_.iv())
                    else:
                        P.op("dve", lambda e, pscore=pscore, nk=nk, rel=rel: e.tensor_tensor(out=s2[0:nk, 0:n], in0=pscore[0:nk, 0:n], in1=maskw[0:nk, 384 - rel:384 - rel + n], op=ALU.add), reads=piv(pscore, n) + maskw.iv(), writes=s2.iv())
                        P.op("act", lambda e, pt_=pt_, ba=ba, nk=nk: e.activation(out=pt_[0:nk, 0:n], in_=s2[0:nk, 0:n], func=AF.Exp, bias=ba, scale=0.125), reads=s2.iv() + biv, writes=pt_.iv())
                    pend.append((ki, va, viv, pt_, nk))
                    if len(pend) > 2:
                        emit_pv(pend.pop(0))
                for x in pend:
                    emit_pv(x)
                P.op("act", lambda e, po=po: e.copy(out=oT[0:64, 0:n], in_=po[0:64, 0:n]), reads=piv(po, n), writes=oT.iv())
                P.op("dve", lambda e, po=po: e.reciprocal(out=rden[64:65, 0:n], in_=po[64:65, 0:n]), reads=piv(po, n), writes=rden.iv())
                P.op("dve", lambda e: e.tensor_copy(out=rdh[64:65, 0:n], in_=rden[64:65, 0:n]), reads=rden.iv(), writes=rdh.iv())
                P.op("dve", lambda e: e.tensor_tensor(out=rden[64:65, 0:n], in0=rden[64:65, 0:n], in1=rdh[64:65, 0:n], op=ALU.subtract), reads=rden.iv() + rdh.iv(), writes=rden.iv())
                P.op("dve", lambda e: e.tensor_copy(out=rdl[64:65, 0:n], in_=rden[64:65, 0:n]), reads=rden.iv(), writes=rdl.iv())
                pb = nps()
                mm(pb[0:64, 0:n], piv(pb, n), [(bsel[64:96, :], rdh[64:96, 0:n], bsel.iv() + rdh.iv()), (bsel[64:96, :], rdl[64:96, 0:n], bsel.iv() + rdl.iv())])
                P.op("dve", lambda e, pb=pb, hd=hd: e.tensor_tensor(out=att[:, hd, :], in0=oT[0:64, 0:n], in1=pb[0:64, 0:n], op=ALU.mult), reads=oT.iv() + piv(pb, n), writes=att.iv(hd))
            chain(0)
            chainB(0)
            for hd in range(8):
                if hd + 1 < 8:
                    chain(hd + 1)
                if STOP > 4:
                    attend(hd)
                if hd + 1 < 8:
                    chainB(hd + 1)
            if STOP <= 5:
                raise _Stop()
            for ti in range(nt_):
                t0 = ti * 128; nt = min(128, n - t0)
                dst = (o_lf_s[j] if smp else o_lf_p[j, seqi])[c0 + t0:c0 + t0 + nt, :]
                P.dma(lambda e, dst=dst, ti=ti, nt=nt: e.dma_start(out=dst, in_=lfT[0:nt, ti, :]), reads=lfT.iv(ti))
            P.release(sm2)
            ysb = P.alloc([128, 8, n], F32, "ysbO")
            rl = [((lambda t0, nn, hd=hd: att[:, hd, t0:t0 + nn]), (lambda t0, nn, hd=hd: att.iv(hd, (t0, t0 + nn))), 64) for hd in range(8)]
            rl += [((lambda t0, nn, kc=kc: ydb[:, kc, t0:t0 + nn]), (lambda t0, nn, kc=kc: ydb.iv(kc, (t0, t0 + nn))), 128) for kc in range(4)]
            out_proj(w_odd_out[j], rl, ysb, n)
            post_norm_add(1, layer, ysb, c0, n)
            P.release(sm)

        for c0_ in range(0, T_, 512):
            st_body(c0_)

    return odd_mixer


import math
import numpy as np
import concourse.bass as bass
import concourse.mybir as mybir
from concourse.bass_utils import run_bass_kernel_spmd

D = 1024
DEPTH = 4
SEQ = 2048
SL = 32
PAST = 1024
DFF = 2816
EPS = 1e-6
NEG = -1.0e30
PI = math.pi


def build(nc, passes=(("p", 0), ("p", 1), ("s", 0)), nlayers=4):
    P = Prog(nc)

    def din(name, shape):
        return nc.dram_tensor(name, list(shape), F32, kind="ExternalInput").ap()

    def dout(name, shape):
        return nc.dram_tensor(name, list(shape), F32, kind="ExternalOutput").ap()

    xp = din("xp", [2, SEQ, D]); xs = din("xs", [SL, D])
    pp = din("pp", [4, 2, SEQ, 256]); psm = din("psm", [4, SL, 256])
    cconv = din("cconv", [2, 2, 512]); sre = din("sre", [2, 32, 64]); sim = din("sim", [2, 32, 64])
    ck = din("ck", [2, PAST, 512]); cv = din("cv", [2, PAST, 512]); clf = din("clf", [2, PAST, 8])
    cffn = din("cffn", [4, 2, 2 * DFF])
    g_mix_pre = din("g_mix_pre", [4, D]); g_mix_post = din("g_mix_post", [4, D])
    g_ffn_pre = din("g_ffn_pre", [4, D]); g_ffn_post = din("g_ffn_post", [4, D])
    w_even_in = din("w_even_in", [2, D, 2048]); w_conv_a = din("w_conv_a", [2, 3, 512])
    s5_a_re = din("s5_a_re", [2, 32, 64]); s5_a_im = din("s5_a_im", [2, 32, 64]); s5_log_dt = din("s5_log_dt", [2, 32])
    s5_b_re = din("s5_b_re", [2, 32, 64, 16]); s5_b_im = din("s5_b_im", [2, 32, 64, 16])
    s5_c_re = din("s5_c_re", [2, 32, 16, 64]); s5_c_im = din("s5_c_im", [2, 32, 16, 64])
    s5_d = din("s5_d", [2, 32, 16]); w_glu = din("w_glu", [2, 512, 512]); w_even_out = din("w_even_out", [2, D, D])
    w_odd_in = din("w_odd_in", [2, D, 2568]); b_forget = din("b_forget", [2, 8])
    w_spatial = din("w_spatial", [2, 8, 128, 128]); b_spatial = din("b_spatial", [2, 8, 128])
    g_gmlp_v = din("g_gmlp_v", [2, 512]); w_odd_out = din("w_odd_out", [2, D, D])
    w_ffn_up = din("w_ffn_up", [4, D, 2 * DFF]); w_ffn_conv = din("w_ffn_conv", [4, 3, 2 * DFF])
    w_ffn_down = din("w_ffn_down", [4, DFF, D]); w_ple = din("w_ple", [4, 256, D]); w_ple_gate = din("w_ple_gate", [4, D, D])

    yp = dout("yp", [2, SEQ, D]); ys = dout("ys", [SL, D])
    o_conv_p = dout("o_conv_p", [2, 2, 2, 512]); o_sre_p = dout("o_sre_p", [2, 2, 32, 64]); o_sim_p = dout("o_sim_p", [2, 2, 32, 64])
    o_k_p = dout("o_k_p", [2, 2, SEQ, 512]); o_v_p = dout("o_v_p", [2, 2, SEQ, 512]); o_lf_p = dout("o_lf_p", [2, 2, SEQ, 8])
    o_ffn_p = dout("o_ffn_p", [4, 2, 2, 2 * DFF])
    o_conv_s = dout("o_conv_s", [2, 2, 512]); o_sre_s = dout("o_sre_s", [2, 32, 64]); o_sim_s = dout("o_sim_s", [2, 32, 64])
    o_k_s = dout("o_k_s", [2, SL, 512]); o_v_s = dout("o_v_s", [2, SL, 512]); o_lf_s = dout("o_lf_s", [2, SL, 8])
    o_gv_s = dout("o_gv_s", [2, SL, 512]); o_ffn_s = dout("o_ffn_s", [4, 2, 2 * DFF])

    import os
    DBG = os.environ.get("KDBG", "0") == "1"
    if DBG:
        dbg_y = dout("dbg_y", [128, 8, 512]); dbg_h = dout("dbg_h", [128, 8, 512]); dbg_h0 = dout("dbg_h0", [128, 8, 512])
    h = P.sb("h", [128, 8, SEQ], F32)
    hn = P.sb("hn", [128, 8, 1024], BF16)
    NW = 4
    wbf = [P.sb(f"wbf{i}", [128, 1024], BF16) for i in range(NW)]
    sqP = P.sb("sqP", [128, 8, 512], BF16)
    rstdP = P.sb("rstdP", [128, 512], F32)
    ident = P.sb("ident", [128, 128], F32)
    onesf = P.sb("onesf", [128, 128], F32)
    onesb = P.sb("onesb", [128, 128], BF16)
    maskw = P.sb("maskw", [128, 896], F32)
    negc = P.sb("negc", [128, 1], F32)
    gains = P.sb("gains", [128, 4, 4, 8], F32)
    wcva = P.sb("wcva", [128, 2, 3, 4], F32)
    wcvf = P.sb("wcvf", [128, 4, 3, 44], F32)
    ps = [P.ps(f"ps{i}", [128, 512]) for i in range(8)]
    P.make_arena(212863 - (SEQ * 8 * 4 + 8 * 1024 * 2 + (4 * 2048 + 8192 + 2048) + 512 * 2 + 256 + 896 * 4 + 64 + 512 + 96 + 2112) - 600)
    st_ = {"ps": 0, "w": 0}
    tabs = nc.dram_tensor("rot_tabs", [2, 16, 2, 128, 512], F32).ap()

    class _DT:
        def __init__(self):
            self.recs = []
    tabsT = _DT()
    tabs_done = set()

    def tab_iv(j, q):
        k = (j * 16 + q) * 2
        return [(tabsT, k, k + 2)]
    base_wbufs = [(wbf[i].h, (lambda n, i=i: [(wbf[i], 0, n)])) for i in range(NW)]
    st_["wbufs"] = list(base_wbufs)

    def nps():
        st_["ps"] = (st_["ps"] + 1) % 6
        return ps[st_["ps"]]

    def nps_held():
        st_["psh"] = 1 - st_.get("psh", 0)
        return ps[6 + st_["psh"]]

    def piv(p, n, lo=0):
        return [(p, lo, lo + n)]

    P.op("pool", lambda e: e.memset(onesf[:], 1.0), writes=onesf.iv())
    P.op("pool", lambda e: e.memset(onesb[:], 1.0), writes=onesb.iv())
    P.op("pool", lambda e: e.affine_select(out=ident[:], in_=onesf[:], pattern=[[-1, 128]], compare_op=ALU.is_equal, fill=0.0, base=0, channel_multiplier=1), reads=onesf.iv(), writes=ident.iv())
    P.op("pool", lambda e: e.memset(maskw[:], 0.0), writes=maskw.iv())
    P.op("pool", lambda e: e.affine_select(out=maskw[:], in_=maskw[:], pattern=[[1, 896]], compare_op=ALU.is_ge, fill=NEG, base=-384, channel_multiplier=-1), reads=maskw.iv(), writes=maskw.iv())
    P.op("dve", lambda e: e.tensor_scalar(out=negc[:], in0=ident[:, 65:66], scalar1=-1.0, scalar2=None, op0=ALU.mult), reads=ident.iv(), writes=negc.iv())
    for kind, g in enumerate([g_mix_pre, g_mix_post, g_ffn_pre, g_ffn_post]):
        for l in range(4):
            P.dma(lambda e, g=g, kind=kind, l=l: e.dma_start(out=gains[:, kind, l, :], in_=g[l].rearrange("(c p) -> p c", p=128)), writes=gains.iv(kind, l))
    for j in range(2):
        for tp in range(3):
            P.dma(lambda e, j=j, tp=tp: e.dma_start(out=wcva[:, j, tp, :], in_=w_conv_a[j, tp].rearrange("(c p) -> p c", p=128)), writes=wcva.iv(j, tp))
    for l in range(4):
        for tp in range(3):
            P.dma(lambda e, l=l, tp=tp: e.dma_start(out=wcvf[:, l, tp, :], in_=w_ffn_conv[l, tp].rearrange("(c p) -> p c", p=128)), writes=wcvf.iv(l, tp))

    def wblock(src, KC, ncol, dst=None, dst_iv=None, kp=128):
        n = KC * ncol
        if dst is None:
            bufs = st_["wbufs"]
            k = st_["w"] % len(bufs)
            st_["w"] += 1
            ap2, ivf = bufs[k]
            dst = ap2[0:kp, 0:n].rearrange("p (a b) -> p a b", a=KC)
            dst_iv = ivf(n)
        P.dma(lambda e: e.dma_start(out=dst, in_=src), writes=dst_iv, queue="pool")
        return dst, dst_iv

    def wcols(w2d, m0, ncol, KC=8):
        return w2d.rearrange("(kc p) m -> p kc m", p=128)[:, 0:KC, m0:m0 + ncol]

    def mm(out_ap, out_iv, terms):
        rd = []
        for t in terms:
            rd += t[2]

        def emit(e):
            inst = None
            for i, t in enumerate(terms):
                inst = e.matmul(out_ap, lhsT=t[0], rhs=t[1], start=(i == 0), stop=(i == len(terms) - 1))
            return inst
        P.op("pe", emit, reads=rd, writes=out_iv)

    def rms_rstd(sq_terms, n, scale, rstd, M=128):
        p = nps()
        mm(p[0:M, 0:n], piv(p, n), [(onesb[:, 0:M], a, onesb.iv() + iv) for (a, iv) in sq_terms])
        P.op("act", lambda e: e.activation(out=rstd[0:M, 0:n], in_=p[0:M, 0:n], func=AF.Sqrt, bias=EPS, scale=scale), reads=piv(p, n), writes=rstd.iv((0, n)))
        P.op("dve", lambda e: e.reciprocal(out=rstd[0:M, 0:n], in_=rstd[0:M, 0:n]), reads=rstd.iv((0, n)), writes=rstd.iv((0, n)))

    def pre_norm(kind, layer, c0, n):
        sq = sqP; rstd = rstdP
        for t0 in range(0, n, 512):
            nn = min(512, n - t0)
            for c in range(8):
                P.op("act", lambda e, c=c, t0=t0, nn=nn: e.activation(out=sq[:, c, 0:nn], in_=h[:, c, c0 + t0:c0 + t0 + nn], func=AF.Square),
                     reads=h.iv(c, (c0 + t0, c0 + t0 + nn)), writes=sq.iv(c, (0, nn)))
            rms_rstd([(sq[:, c, 0:nn], sq.iv(c, (0, nn))) for c in range(8)], nn, 1.0 / D, rstd)
            for c in range(8):
                P.op("dve", lambda e, c=c, t0=t0, nn=nn: e.scalar_tensor_tensor(out=hn[:, c, t0:t0 + nn], in0=h[:, c, c0 + t0:c0 + t0 + nn], scalar=gains[:, kind, layer, c:c + 1], in1=rstd[:, 0:nn], op0=ALU.mult, op1=ALU.mult),
                     reads=h.iv(c, (c0 + t0, c0 + t0 + nn)) + gains.iv() + rstd.iv((0, nn)), writes=hn.iv(c, (t0, t0 + nn)))

    def post_norm_add(kind, layer, ysb, c0, n):
        sq = sqP; rstd = rstdP
        for t0 in range(0, n, 512):
            nn = min(512, n - t0)
            for c in range(8):
                P.op("act", lambda e, c=c, t0=t0, nn=nn: e.activation(out=sq[:, c, 0:nn], in_=ysb[:, c, t0:t0 + nn], func=AF.Square),
                     reads=ysb.iv(c, (t0, t0 + nn)), writes=sq.iv(c, (0, nn)))
            rms_rstd([(sq[:, c, 0:nn], sq.iv(c, (0, nn))) for c in range(8)], nn, 1.0 / D, rstd)
            for c in range(8):
                P.op("dve", lambda e, c=c, t0=t0, nn=nn: e.scalar_tensor_tensor(out=ysb[:, c, t0:t0 + nn], in0=ysb[:, c, t0:t0 + nn], scalar=gains[:, kind, layer, c:c + 1], in1=rstd[:, 0:nn], op0=ALU.mult, op1=ALU.mult),
                     reads=ysb.iv(c, (t0, t0 + nn)) + gains.iv() + rstd.iv((0, nn)), writes=ysb.iv(c, (t0, t0 + nn)))
                P.op("pool", lambda e, c=c, t0=t0, nn=nn: e.tensor_tensor(out=h[:, c, c0 + t0:c0 + t0 + nn], in0=h[:, c, c0 + t0:c0 + t0 + nn], in1=ysb[:, c, t0:t0 + nn], op=ALU.add),
                     reads=h.iv(c, (c0 + t0, c0 + t0 + nn)) + ysb.iv(c, (t0, t0 + nn)), writes=h.iv(c, (c0 + t0, c0 + t0 + nn)))

    def out_proj(w2d, rhs_list, ysb, n):
        for m_ in range(8):
            blocks = []
            kc0 = 0
            row = 0
            while kc0 < len(rhs_list):
                kp = rhs_list[kc0][2]
                kc1 = kc0
                while kc1 < len(rhs_list) and rhs_list[kc1][2] == kp and kc1 - kc0 < 8:
                    kc1 += 1
                KC = kc1 - kc0
                src = w2d[row:row + KC * kp, m_ * 128:(m_ + 1) * 128].rearrange("(kc p) m -> p kc m", p=kp)
                wv, wiv = wblock(src, KC, 128, kp=kp)
                blocks.append((kc0, kc1, wv, wiv, kp))
                row += KC * kp
                kc0 = kc1
            for t0 in range(0, n, 512):
                nn = min(512, n - t0)
                p = nps()
                terms = []
                for (a0, a1, wv, wiv, kp) in blocks:
                    for kc in range(a0, a1):
                        terms.append((wv[:, kc - a0, :], rhs_list[kc][0](t0, nn), wiv + rhs_list[kc][1](t0, nn)))
                mm(p[:, 0:nn], piv(p, nn), terms)
                P.op("act", lambda e, p=p, m_=m_, t0=t0, nn=nn: e.copy(out=ysb[:, m_, t0:t0 + nn], in_=p[:, 0:nn]), reads=piv(p, nn), writes=ysb.iv(m_, (t0, t0 + nn)))

    def even_prep(j):
        C = {}
        C["BTr"] = P.alloc([128, 16, 128], BF16, "BTr"); C["BTi"] = P.alloc([128, 16, 128], BF16, "BTi")
        C["CTr"] = P.alloc([128, 16, 128], BF16, "CTr"); C["CTi"] = P.alloc([128, 16, 128], BF16, "CTi")
        C["Dd"] = P.alloc([128, 4, 128], BF16, "Dd")
        C["pwr"] = P.alloc([128, 10, 16], F32, "pwr"); C["pwi"] = P.alloc([128, 10, 16], F32, "pwi"); C["npwi"] = P.alloc([128, 10, 16], F32, "npwi")
        C["hsr"] = P.alloc([128, 16], F32, "hsr"); C["hsi"] = P.alloc([128, 16], F32, "hsi")
        C["halo"] = P.alloc([128, 4, 2], F32, "haloA")
        C["upc"] = P.alloc([128, 9, 16], F32, "upc"); C["ups"] = P.alloc([128, 9, 16], F32, "ups"); C["nups"] = P.alloc([128, 9, 16], F32, "nups")
        C["rcol"] = P.alloc([128, 16], F32, "rcol")
        C["ones"] = P.alloc([128, 512], F32, "ones512")
        P.op("pool", lambda e: e.memset(C["ones"][:], 1.0), writes=C["ones"].iv())
        m = P.mark()
        are = P.alloc([128, 16], F32); aim = P.alloc([128, 16], F32); dt = P.alloc([128, 16], F32)
        t1 = P.alloc([128, 16], F32); t2 = P.alloc([128, 16], F32); mag = P.alloc([128, 16], F32)
        cs = P.alloc([128, 16], F32); sn = P.alloc([128, 16], F32); den = P.alloc([128, 16], F32)
        fr = P.alloc([128, 16], F32); fi = P.alloc([128, 16], F32); zr = P.alloc([128, 16], F32)
        bre = P.alloc([128, 16, 16], F32); bim = P.alloc([128, 16, 16], F32); bbr = P.alloc([128, 16, 16], F32); bbi = P.alloc([128, 16, 16], F32)
        tmp3 = P.alloc([128, 16, 16], F32)
        maskC = P.alloc([128, 4, 128], F32); natp = P.alloc([128, 4, 128], F32)
        cdup = P.alloc([128, 128], F32); cT = P.alloc([128, 128], F32); dcol = P.alloc([128, 4], F32)
        for g2 in range(2):
            sl = slice(g2 * 64, g2 * 64 + 64)
            P.dma(lambda e, sl=sl, g2=g2: e.dma_start(out=are[sl, :], in_=s5_a_re[j].rearrange("(q g) p -> g p q", g=2)[g2]), writes=are.iv())
            P.dma(lambda e, sl=sl, g2=g2: e.dma_start(out=aim[sl, :], in_=s5_a_im[j].rearrange("(q g) p -> g p q", g=2)[g2]), writes=aim.iv())
            P.dma(lambda e, sl=sl, g2=g2: e.dma_start(out=dt[sl, :], in_=s5_log_dt[j].rearrange("(q g) -> g q", g=2)[g2:g2 + 1, :].broadcast_to([64, 16])), writes=dt.iv())
            P.dma(lambda e, sl=sl, g2=g2: e.dma_start(out=bre[sl, :, :], in_=s5_b_re[j].rearrange("(q g) p c -> g p q c", g=2)[g2]), writes=bre.iv())
            P.dma(lambda e, sl=sl, g2=g2: e.dma_start(out=bim[sl, :, :], in_=s5_b_im[j].rearrange("(q g) p c -> g p q c", g=2)[g2]), writes=bim.iv())
        P.dma(lambda e: e.dma_start(out=dcol[:], in_=s5_d[j].rearrange("g c -> (g c)").rearrange("(f p) -> p f", p=128)), writes=dcol.iv())

        def A(eng, fn, rd, wr):
            P.op(eng, fn, reads=[x for v in rd for x in v.iv()], writes=[x for v in wr for x in v.iv()])
        A("act", lambda e: e.activation(out=dt[:], in_=dt[:], func=AF.Exp), [dt], [dt])
        A("dve", lambda e: e.tensor_tensor(out=t1[:], in0=are[:], in1=dt[:], op=ALU.mult), [are, dt], [t1])
        A("act", lambda e: e.activation(out=mag[:], in_=t1[:], func=AF.Exp), [t1], [mag])
        A("dve", lambda e: e.tensor_tensor(out=t1[:], in0=aim[:], in1=dt[:], op=ALU.mult), [aim, dt], [t1])
        ki = P.alloc([128, 16], I32)
        kf = P.alloc([128, 16], F32)

        def reduce_sin(dst, shift):
            A("dve", lambda e: e.tensor_scalar(out=t2[:], in0=t1[:], scalar1=shift, scalar2=1.0 / (2 * PI), op0=ALU.add, op1=ALU.mult), [t1], [t2])
            A("dve", lambda e: e.tensor_copy(out=ki[:], in_=t2[:]), [t2], [ki])
            A("dve", lambda e: e.tensor_copy(out=kf[:], in_=ki[:]), [ki], [kf])
            A("dve", lambda e: e.tensor_tensor(out=t2[:], in0=t2[:], in1=kf[:], op=ALU.subtract), [t2, kf], [t2])
            A("dve", lambda e: e.tensor_scalar(out=kf[:], in0=t2[:], scalar1=0.5, scalar2=None, op0=ALU.is_gt), [t2], [kf])
            A("dve", lambda e: e.tensor_tensor(out=t2[:], in0=t2[:], in1=kf[:], op=ALU.subtract), [t2, kf], [t2])
            A("dve", lambda e: e.tensor_scalar(out=kf[:], in0=t2[:], scalar1=-0.5, scalar2=None, op0=ALU.is_lt), [t2], [kf])
            A("dve", lambda e: e.tensor_tensor(out=t2[:], in0=t2[:], in1=kf[:], op=ALU.add), [t2, kf], [t2])
            A("act", lambda e: e.activation(out=dst[:], in_=t2[:], func=AF.Sin, scale=2 * PI), [t2], [dst])
        reduce_sin(sn, 0.0)
        reduce_sin(cs, 0.5 * PI)
        A("dve", lambda e: e.tensor_tensor(out=C["pwr"][:, 0, :], in0=cs[:], in1=mag[:], op=ALU.mult), [cs, mag], [C["pwr"]])
        A("dve", lambda e: e.tensor_tensor(out=C["pwi"][:, 0, :], in0=sn[:], in1=mag[:], op=ALU.mult), [sn, mag], [C["pwi"]])
        for l in range(1, 10):
            A("dve", lambda e, l=l: e.tensor_tensor(out=t1[:], in0=C["pwr"][:, l - 1, :], in1=C["pwr"][:, l - 1, :], op=ALU.mult), [C["pwr"]], [t1])
            A("dve", lambda e, l=l: e.tensor_tensor(out=t2[:], in0=C["pwi"][:, l - 1, :], in1=C["pwi"][:, l - 1, :], op=ALU.mult), [C["pwi"]], [t2])
            A("dve", lambda e, l=l: e.tensor_tensor(out=C["pwr"][:, l, :], in0=t1[:], in1=t2[:], op=ALU.subtract), [t1, t2], [C["pwr"]])
            A("dve", lambda e, l=l: e.scalar_tensor_tensor(out=C["pwi"][:, l, :], in0=C["pwr"][:, l - 1, :], scalar=2.0, in1=C["pwi"][:, l - 1, :], op0=ALU.mult, op1=ALU.mult), [C["pwr"], C["pwi"]], [C["pwi"]])
        A("dve", lambda e: e.tensor_scalar(out=C["npwi"][:], in0=C["pwi"][:], scalar1=-1.0, scalar2=None, op0=ALU.mult), [C["pwi"]], [C["npwi"]])
        A("dve", lambda e: e.tensor_copy(out=C["rcol"][:], in_=mag[:]), [mag], [C["rcol"]])
        A("dve", lambda e: e.tensor_copy(out=C["upc"][:, 0, :], in_=cs[:]), [cs], [C["upc"]])
        A("dve", lambda e: e.tensor_copy(out=C["ups"][:, 0, :], in_=sn[:]), [sn], [C["ups"]])
        for l in range(1, 9):
            A("dve", lambda e, l=l: e.tensor_tensor(out=t1[:], in0=C["upc"][:, l - 1, :], in1=C["upc"][:, l - 1, :], op=ALU.mult), [C["upc"]], [t1])
            A("dve", lambda e, l=l: e.tensor_tensor(out=t2[:], in0=C["ups"][:, l - 1, :], in1=C["ups"][:, l - 1, :], op=ALU.mult), [C["ups"]], [t2])
            A("dve", lambda e, l=l: e.tensor_tensor(out=C["upc"][:, l, :], in0=t1[:], in1=t2[:], op=ALU.subtract), [t1, t2], [C["upc"]])
            A("dve", lambda e, l=l: e.scalar_tensor_tensor(out=C["ups"][:, l, :], in0=C["upc"][:, l - 1, :], scalar=2.0, in1=C["ups"][:, l - 1, :], op0=ALU.mult, op1=ALU.mult), [C["upc"], C["ups"]], [C["ups"]])
        A("dve", lambda e: e.tensor_scalar(out=C["nups"][:], in0=C["ups"][:], scalar1=-1.0, scalar2=None, op0=ALU.mult), [C["ups"]], [C["nups"]])
        if j not in tabs_done:
            tabs_done.add(j)
            Tb = [[P.alloc([128, 512], F32, f"Tg{a}{b}") for b in range(2)] for a in range(2)]
            tmpg = P.alloc([128, 256], F32, "tmpg")
            for q in range(16):
                Tc, Ts = Tb[q % 2]
                P.op("dve", lambda e, Tc=Tc, q=q: e.tensor_copy(out=Tc[:, 0:1], in_=cs[:, q:q + 1]), reads=cs.iv(), writes=Tc.iv((0, 1)))
                P.op("dve", lambda e, Ts=Ts, q=q: e.tensor_copy(out=Ts[:, 0:1], in_=sn[:, q:q + 1]), reads=sn.iv(), writes=Ts.iv((0, 1)))
                for l in range(9):
                    w = 1 << l
                    cw = C["upc"][:, l, q:q + 1]; sw = C["ups"][:, l, q:q + 1]; nsw = C["nups"][:, l, q:q + 1]
                    rdp = C["upc"].iv(l) + C["ups"].iv(l) + C["nups"].iv(l)
                    P.op("dve", lambda e, Tc=Tc, w=w, cw=cw: e.tensor_scalar(out=tmpg[:, 0:w], in0=Tc[:, 0:w], scalar1=cw, scalar2=None, op0=ALU.mult), reads=Tc.iv((0, w)) + rdp, writes=tmpg.iv((0, w)))
                    P.op("dve", lambda e, Tc=Tc, Ts=Ts, w=w, nsw=nsw: e.scalar_tensor_tensor(out=Tc[:, w:2 * w], in0=Ts[:, 0:w], scalar=nsw, in1=tmpg[:, 0:w], op0=ALU.mult, op1=ALU.add), reads=Ts.iv((0, w)) + tmpg.iv((0, w)) + rdp, writes=Tc.iv((w, 2 * w)))
                    P.op("dve", lambda e, Ts=Ts, w=w, cw=cw: e.tensor_scalar(out=tmpg[:, 0:w], in0=Ts[:, 0:w], scalar1=cw, scalar2=None, op0=ALU.mult), reads=Ts.iv((0, w)) + rdp, writes=tmpg.iv((0, w)))
                    P.op("dve", lambda e, Tc=Tc, Ts=Ts, w=w, sw=sw: e.scalar_tensor_tensor(out=Ts[:, w:2 * w], in0=Tc[:, 0:w], scalar=sw, in1=tmpg[:, 0:w], op0=ALU.mult, op1=ALU.add), reads=Tc.iv((0, w)) + tmpg.iv((0, w)) + rdp, writes=Ts.iv((w, 2 * w)))
                P.dma(lambda e, Tc=Tc, q=q: e.dma_start(out=tabs[j, q, 0], in_=Tc[:]), reads=Tc.iv(), writes=tab_iv(j, q))
                P.dma(lambda e, Ts=Ts, q=q: e.dma_start(out=tabs[j, q, 1], in_=Ts[:]), reads=Ts.iv(), writes=tab_iv(j, q))
        A("dve", lambda e: e.tensor_scalar(out=zr[:], in0=C["pwr"][:, 0, :], scalar1=-1.0, scalar2=None, op0=ALU.add), [C["pwr"]], [zr])
        A("dve", lambda e: e.tensor_tensor(out=den[:], in0=are[:], in1=are[:], op=ALU.mult), [are], [den])
        A("dve", lambda e: e.tensor_tensor(out=t1[:], in0=aim[:], in1=aim[:], op=ALU.mult), [aim], [t1])
        A("dve", lambda e: e.tensor_tensor(out=den[:], in0=den[:], in1=t1[:], op=ALU.add), [den, t1], [den])
        A("dve", lambda e: e.reciprocal(out=den[:], in_=den[:]), [den], [den])
        A("dve", lambda e: e.tensor_tensor(out=t1[:], in0=zr[:], in1=are[:], op=ALU.mult), [zr, are], [t1])
        A("dve", lambda e: e.tensor_tensor(out=t2[:], in0=C["pwi"][:, 0, :], in1=aim[:], op=ALU.mult), [C["pwi"], aim], [t2])
        A("dve", lambda e: e.tensor_tensor(out=t1[:], in0=t1[:], in1=t2[:], op=ALU.add), [t1, t2], [t1])
        A("dve", lambda e: e.tensor_tensor(out=fr[:], in0=t1[:], in1=den[:], op=ALU.mult), [t1, den], [fr])
        A("dve", lambda e: e.tensor_tensor(out=t1[:], in0=C["pwi"][:, 0, :], in1=are[:], op=ALU.mult), [C["pwi"], are], [t1])
        A("dve", lambda e: e.tensor_tensor(out=t2[:], in0=zr[:], in1=aim[:], op=ALU.mult), [zr, aim], [t2])
        A("dve", lambda e: e.tensor_tensor(out=t1[:], in0=t1[:], in1=t2[:], op=ALU.subtract), [t1, t2], [t1])
        A("dve", lambda e: e.tensor_tensor(out=fi[:], in0=t1[:], in1=den[:], op=ALU.mult), [t1, den], [fi])
        frb = fr[:].unsqueeze(2).to_broadcast([128, 16, 16]); fib = fi[:].unsqueeze(2).to_broadcast([128, 16, 16])
        A("dve", lambda e: e.tensor_tensor(out=bbr[:], in0=bre[:], in1=frb, op=ALU.mult), [bre, fr], [bbr])
        A("dve", lambda e: e.tensor_tensor(out=tmp3[:], in0=bim[:], in1=fib, op=ALU.mult), [bim, fi], [tmp3])
        A("dve", lambda e: e.tensor_tensor(out=bbr[:], in0=bbr[:], in1=tmp3[:], op=ALU.subtract), [bbr, tmp3], [bbr])
        A("dve", lambda e: e.tensor_tensor(out=bbi[:], in0=bim[:], in1=frb, op=ALU.mult), [bim, fr], [bbi])
        A("dve", lambda e: e.tensor_tensor(out=tmp3[:], in0=bre[:], in1=fib, op=ALU.mult), [bre, fi], [tmp3])
        A("dve", lambda e: e.tensor_tensor(out=bbi[:], in0=bbi[:], in1=tmp3[:], op=ALU.add), [bbi, tmp3], [bbi])
        A("pool", lambda e: e.memset(maskC[:], 1.0), [], [maskC])
        for g2 in range(2):
            sl = slice(g2 * 64, g2 * 64 + 64)
            A("pool", lambda e, sl=sl, g2=g2: e.affine_select(out=maskC[sl, :, :], in_=maskC[sl, :, :], pattern=[[-32, 4], [1, 128]], compare_op=ALU.is_ge, fill=0.0, base=-16 * g2, channel_multiplier=0), [maskC], [maskC])
            A("pool", lambda e, sl=sl, g2=g2: e.affine_select(out=maskC[sl, :, :], in_=maskC[sl, :, :], pattern=[[32, 4], [-1, 128]], compare_op=ALU.is_ge, fill=0.0, base=16 * g2 + 15, channel_multiplier=0), [maskC], [maskC])
        for (bb, BT) in ((bbr, C["BTr"]), (bbi, C["BTi"])):
            for qg in range(4):
                p = nps()
                for ql in range(4):
                    q = qg * 4 + ql
                    A("dve", lambda e, bb=bb, q=q, ql=ql: e.tensor_tensor(out=natp[:, ql, :].rearrange("p (a b) -> p a b", a=8), in0=maskC[:, ql, :].rearrange("p (a b) -> p a b", a=8), in1=bb[:, q, :].unsqueeze(1).to_broadcast([128, 8, 16]), op=ALU.mult), [maskC, bb], [natp])
                    P.op("pe", lambda e, p=p, ql=ql: e.transpose(p[:, ql * 128:(ql + 1) * 128], natp[:, ql, :], ident[:]), reads=natp.iv(ql) + ident.iv(), writes=piv(p, 128, ql * 128))
                P.op("act", lambda e, p=p, BT=BT, qg=qg: e.copy(out=BT[:, qg * 4:(qg + 1) * 4, :], in_=p[:, :].rearrange("p (a b) -> p a b", a=4)), reads=piv(p, 512), writes=BT.iv((qg * 4, qg * 4 + 4)))
        for (csrc, CT, sgn) in ((s5_c_re, C["CTr"], 1.0), (s5_c_im, C["CTi"], -1.0)):
            for t in range(4):
                src = csrc[j].rearrange("g c p -> (g c) p")[t * 128:(t + 1) * 128, :]
                P.dma(lambda e, src=src: e.dma_start(out=cdup[:, 0:64], in_=src), writes=cdup.iv((0, 64)))
                P.dma(lambda e, src=src: e.dma_start(out=cdup[:, 64:128], in_=src), writes=cdup.iv((64, 128)))
                p = nps()
                P.op("pe", lambda e, p=p: e.transpose(p[:, 0:128], cdup[:], ident[:]), reads=cdup.iv() + ident.iv(), writes=piv(p, 128))
                P.op("act", lambda e, p=p, sgn=sgn: e.mul(out=cT[:], in_=p[:, 0:128], mul=sgn), reads=piv(p, 128), writes=cT.iv())
                for ql in range(4):
                    q = t * 4 + ql
                    A("dve", lambda e, CT=CT, q=q, ql=ql: e.tensor_tensor(out=CT[:, q, :], in0=cT[:], in1=maskC[:, ql, :], op=ALU.mult), [cT, maskC], [CT])
        for fc in range(4):
            A("dve", lambda e, fc=fc: e.tensor_scalar(out=C["Dd"][:, fc, :], in0=ident[:], scalar1=dcol[:, fc:fc + 1], scalar2=None, op0=ALU.mult), [ident, dcol], [C["Dd"]])
        P.release(m)
        return C

    def even_mixer(j, layer, smp, seqi, T_):
        lm = P.mark()
        C = even_prep(j)
        W = w_even_in[j]
        if smp:
            for g2 in range(2):
                sl = slice(g2 * 64, g2 * 64 + 64)
                P.dma(lambda e, sl=sl, g2=g2: e.dma_start(out=C["hsr"][sl, :], in_=sre[j].rearrange("(q g) p -> g p q", g=2)[g2]), writes=C["hsr"].iv())
                P.dma(lambda e, sl=sl, g2=g2: e.dma_start(out=C["hsi"][sl, :], in_=sim[j].rearrange("(q g) p -> g p q", g=2)[g2]), writes=C["hsi"].iv())
            for r_ in range(2):
                P.dma(lambda e, r_=r_: e.dma_start(out=C["halo"][:, :, r_], in_=cconv[j, r_].rearrange("(c p) -> p c", p=128)), writes=C["halo"].iv())
        else:
            P.op("pool", lambda e: e.memset(C["hsr"][:], 0.0), writes=C["hsr"].iv())
            P.op("pool", lambda e: e.memset(C["hsi"][:], 0.0), writes=C["hsi"].iv())
            P.op("pool", lambda e: e.memset(C["halo"][:], 0.0), writes=C["halo"].iv())
        nlev = int(math.log2(min(512, T_)))
        def st_body(c0):
            n = min(512, T_ - c0)
            sm = P.mark()
            pre_norm(0, layer, c0, n)
            yacat = P.alloc([128, 8, n], BF16, "yacat"); ubf = P.alloc([128, 4, n], BF16, "ubf")
            tgc = P.alloc([128, n], F32, "tgc"); prod = P.alloc([128, n + 2], F32, "prod"); cvb = P.alloc([128, n], F32, "cvb")
            wre = P.alloc([128, n], F32, "wre"); wim = P.alloc([128, n], F32, "wim"); gre = P.alloc([128, n], F32, "gre"); gim = P.alloc([128, n], F32, "gim")
            TB = [[P.alloc([128, n], F32, f"TB{a}{b}") for b in range(2)] for a in range(2)]
            rtab = P.alloc([128, n], F32, "rtab")
            hri = P.alloc([128, 4, 2, n], BF16, "hri"); gB = P.alloc([128, 4, n], BF16, "gB")
            sg = P.alloc([128, n], F32, "sg"); ysb = P.alloc([128, 8, n], F32, "ysb"); cc = P.alloc([128, 2, 16], F32, "cc")
            tt = P.alloc([128, 2, 16], F32, "tt")

            def proj_chunk(fch):
                wv, wiv = wblock(wcols(W, fch * 128, 128), 8, 128)
                p = nps()
                mm(p[:, 0:n], piv(p, n), [(wv[:, kc, :], hn[:, kc, 0:n], wiv + hn.iv(kc, (0, n))) for kc in range(8)])
                return p
            for i in range(4):
                p = proj_chunk(4 + i)
                P.op("act", lambda e, p=p: e.copy(out=tgc[:], in_=p[:, 0:n]), reads=piv(p, n), writes=tgc.iv())
                p = proj_chunk(8 + i)
                P.op("dve", lambda e, i=i: e.tensor_copy(out=prod[:, 0:2], in_=C["halo"][:, i, :]), reads=C["halo"].iv(i), writes=prod.iv((0, 2)))
                P.op("dve", lambda e, p=p: e.tensor_tensor(out=prod[:, 2:n + 2], in0=tgc[:], in1=p[:, 0:n], op=ALU.mult), reads=tgc.iv() + piv(p, n), writes=prod.iv((2, n + 2)))
                P.op("dve", lambda e, i=i: e.tensor_copy(out=C["halo"][:, i, :], in_=prod[:, n:n + 2]), reads=prod.iv((n, n + 2)), writes=C["halo"].iv(i))
                P.op("dve", lambda e, i=i: e.tensor_scalar(out=cvb[:], in0=prod[:, 0:n], scalar1=wcva[:, j, 0, i:i + 1], scalar2=None, op0=ALU.mult), reads=prod.iv((0, n)) + wcva.iv(), writes=cvb.iv())
                P.op("dve", lambda e, i=i: e.scalar_tensor_tensor(out=cvb[:], in0=prod[:, 1:n + 1], scalar=wcva[:, j, 1, i:i + 1], in1=cvb[:], op0=ALU.mult, op1=ALU.add), reads=prod.iv((1, n + 1)) + wcva.iv() + cvb.iv(), writes=cvb.iv())
                P.op("dve", lambda e, i=i: e.scalar_tensor_tensor(out=cvb[:], in0=prod[:, 2:n + 2], scalar=wcva[:, j, 2, i:i + 1], in1=cvb[:], op0=ALU.mult, op1=ALU.add), reads=prod.iv((2, n + 2)) + wcva.iv() + cvb.iv(), writes=cvb.iv())
                p = proj_chunk(i)
                P.op("dve", lambda e, p=p, i=i: e.tensor_tensor(out=yacat[:, i, :], in0=cvb[:], in1=p[:, 0:n], op=ALU.mult), reads=cvb.iv() + piv(p, n), writes=yacat.iv(i))
            for i in range(4):
                p = proj_chunk(12 + i)
                P.op("act", lambda e, p=p, i=i: e.copy(out=ubf[:, i, :], in_=p[:, 0:n]), reads=piv(p, n), writes=ubf.iv(i))
            t1 = tgc; t2 = cvb
            for fc in range(4):
                py = nps()
                yterms = []
                for ql in range(4):
                    q = fc * 4 + ql
                    Tc, Ts = TB[q % 2]
                    P.dma(lambda e, Tc=Tc, q=q: e.dma_start(out=Tc[:], in_=tabs[j, q, 0][:, 0:n]), reads=tab_iv(j, q), writes=Tc.iv())
                    P.dma(lambda e, Ts=Ts, q=q: e.dma_start(out=Ts[:], in_=tabs[j, q, 1][:, 0:n]), reads=tab_iv(j, q), writes=Ts.iv())
                    P.op("act", lambda e, q=q: e.activation(out=rtab[:], in_=C["ones"][:, 0:n], func=AF.Copy, scale=C["rcol"][:, q:q + 1]), reads=C["ones"].iv() + C["rcol"].iv(), writes=rtab.iv())
                    pb = nps()
                    mm(pb[:, 0:n], piv(pb, n), [(C["BTr"][:, q, :], ubf[:, fc, :], C["BTr"].iv(q) + ubf.iv(fc))])
                    pb2 = nps()
                    mm(pb2[:, 0:n], piv(pb2, n), [(C["BTi"][:, q, :], ubf[:, fc, :], C["BTi"].iv(q) + ubf.iv(fc))])

                    def TT(out, a, b, op, rd, wr):
                        P.op("dve", lambda e: e.tensor_tensor(out=out, in0=a, in1=b, op=op), reads=rd, writes=wr)
                    t3 = sg; t4v = prod[:, 0:n]; t4iv = prod.iv((0, n))
                    TT(t1[:], Tc[:], pb[:, 0:n], ALU.mult, Tc.iv() + piv(pb, n), t1.iv())
                    TT(t2[:], Ts[:], pb2[:, 0:n], ALU.mult, Ts.iv() + piv(pb2, n), t2.iv())
                    TT(t3[:], Tc[:], pb2[:, 0:n], ALU.mult, Tc.iv() + piv(pb2, n), t3.iv())
                    TT(t4v, Ts[:], pb[:, 0:n], ALU.mult, Ts.iv() + piv(pb, n), t4iv)
                    TT(wre[:], t1[:], t2[:], ALU.add, t1.iv() + t2.iv(), wre.iv())
                    TT(wim[:], t3[:], t4v, ALU.subtract, t3.iv() + t4iv, wim.iv())
                    P.op("dve", lambda e, q=q: e.tensor_tensor_scan(out=gre[:], data0=rtab[:], data1=wre[:], initial=C["hsr"][:, q:q + 1], op0=ALU.mult, op1=ALU.add), reads=rtab.iv() + wre.iv() + C["hsr"].iv(), writes=gre.iv())
                    P.op("dve", lambda e, q=q: e.tensor_tensor_scan(out=gim[:], data0=rtab[:], data1=wim[:], initial=C["hsi"][:, q:q + 1], op0=ALU.mult, op1=ALU.add), reads=rtab.iv() + wim.iv() + C["hsi"].iv(), writes=gim.iv())
                    TT(t1[:], Tc[:], gre[:], ALU.mult, Tc.iv() + gre.iv(), t1.iv())
                    TT(t3[:], Ts[:], gre[:], ALU.mult, Ts.iv() + gre.iv(), t3.iv())
                    TT(t2[:], Ts[:], gim[:], ALU.mult, Ts.iv() + gim.iv(), t2.iv())
                    TT(t4v, Tc[:], gim[:], ALU.mult, Tc.iv() + gim.iv(), t4iv)
                    TT(wre[:], t1[:], t2[:], ALU.subtract, t1.iv() + t2.iv(), wre.iv())
                    TT(wim[:], t3[:], t4v, ALU.add, t3.iv() + t4iv, wim.iv())
                    P.op("act", lambda e, ql=ql: e.copy(out=hri[:, ql, 0, :], in_=wre[:]), reads=wre.iv(), writes=hri.iv(ql, 0))
                    P.op("act", lambda e, ql=ql: e.copy(out=hri[:, ql, 1, :], in_=wim[:]), reads=wim.iv(), writes=hri.iv(ql, 1))
                    P.op("dve", lambda e, q=q: e.tensor_copy(out=C["hsr"][:, q:q + 1], in_=wre[:, n - 1:n]), reads=wre.iv((n - 1, n)), writes=C["hsr"].iv())
                    P.op("dve", lambda e, q=q: e.tensor_copy(out=C["hsi"][:, q:q + 1], in_=wim[:, n - 1:n]), reads=wim.iv((n - 1, n)), writes=C["hsi"].iv())
                    yterms.append((C["CTr"][:, q, :], hri[:, ql, 0, :], C["CTr"].iv(q) + hri.iv(ql, 0)))
                    yterms.append((C["CTi"][:, q, :], hri[:, ql, 1, :], C["CTi"].iv(q) + hri.iv(ql, 1)))
                yterms.append((C["Dd"][:, fc, :], ubf[:, fc, :], C["Dd"].iv(fc) + ubf.iv(fc)))
                mm(py[:, 0:n], piv(py, n), yterms)
                P.op("act", lambda e, py=py, fc=fc: e.activation(out=gB[:, fc, :], in_=py[:, 0:n], func=AF.Gelu_apprx_tanh), reads=piv(py, n), writes=gB.iv(fc))
            for m_ in range(4):
                wv, wiv = wblock(wcols(w_glu[j], m_ * 128, 128, KC=4), 4, 128)
                p = nps()
                mm(p[:, 0:n], piv(p, n), [(wv[:, kc, :], gB[:, kc, :], wiv + gB.iv(kc)) for kc in range(4)])
                P.op("act", lambda e, p=p: e.activation(out=sg[:], in_=p[:, 0:n], func=AF.Sigmoid), reads=piv(p, n), writes=sg.iv())
                P.op("dve", lambda e, m_=m_: e.tensor_tensor(out=yacat[:, 4 + m_, :], in0=gB[:, m_, :], in1=sg[:], op=ALU.mult), reads=gB.iv(m_) + sg.iv(), writes=yacat.iv(4 + m_))
            if DBG and c0 == 0 and layer == 0 and not smp:
                P.dma(lambda e: e.dma_start(out=dbg_h0, in_=h[:, :, 0:512]), reads=h.iv())
                for kc in range(8):
                    P.op("dve", lambda e, kc=kc: e.tensor_copy(out=ysb[:, kc, :], in_=yacat[:, kc, :]), reads=yacat.iv(kc), writes=ysb.iv(kc))
                P.dma(lambda e: e.dma_start(out=dbg_y, in_=ysb[:]), reads=ysb.iv())
            out_proj(w_even_out[j], [((lambda t0, nn, kc=kc: yacat[:, kc, t0:t0 + nn]), (lambda t0, nn, kc=kc: yacat.iv(kc, (t0, t0 + nn))), 128) for kc in range(8)], ysb, n)
            post_norm_add(1, layer, ysb, c0, n)
            if DBG and c0 == 0 and layer == 0 and not smp:
                P.dma(lambda e: e.dma_start(out=dbg_h, in_=h[:, :, 0:512]), reads=h.iv())
            P.release(sm)
        for c0_ in range(0, T_, 512):
            st_body(c0_)
        oc = o_conv_s[j] if smp else o_conv_p[j, seqi]
        for r_ in range(2):
            P.dma(lambda e, r_=r_: e.dma_start(out=oc[r_].rearrange("(c p) -> p c", p=128), in_=C["halo"][:, :, r_]), reads=C["halo"].iv())
        for (hs, od) in ((C["hsr"], o_sre_s[j] if smp else o_sre_p[j, seqi]), (C["hsi"], o_sim_s[j] if smp else o_sim_p[j, seqi])):
            for g2 in range(2):
                sl = slice(g2 * 64, g2 * 64 + 64)
                P.dma(lambda e, hs=hs, od=od, sl=sl, g2=g2: e.dma_start(out=od.rearrange("(q g) p -> g p q", g=2)[g2], in_=hs[sl, :]), reads=hs.iv())
        P.release(lm)

    def ffn(layer, smp, seqi, T_):
        lm = P.mark()
        halo = P.alloc([128, 44, 2], F32, "haloF")
        if smp:
            for r_ in range(2):
                P.dma(lambda e, r_=r_: e.dma_start(out=halo[:, :, r_], in_=cffn[layer, r_].rearrange("(c p) -> p c", p=128)), writes=halo.iv())
        else:
            P.op("pool", lambda e: e.memset(halo[:], 0.0), writes=halo.iv())
        Wup = w_ffn_up[layer]
        def st_body(c0):
            n = min(1024, T_ - c0)
            cts = [(t0, min(512, n - t0)) for t0 in range(0, n, 512)]
            sm = P.mark()
            pre_norm(2, layer, c0, n)
            act = P.alloc([128, 22, n], BF16, "act")
            ysb = P.alloc([128, 8, n], F32, "ysbF")
            m2 = P.mark()
            U = [[P.alloc([128, 514], F32, f"U{a}{b}") for b in range(2)] for a in range(2)]
            cvt2 = [[P.alloc([128, 512], F32, f"cvt{a}{b}") for a in range(2)] for b in range(2)]
            gg2 = [P.alloc([128, 512], F32, f"gg{b}") for b in range(2)]
            xw = [P.alloc([128, 1024], BF16, f"xw{b}") for b in range(2)]
            st_["wbufs"] = list(base_wbufs) + [(xw[b].a, (lambda n, b=b: xw[b].iv((0, n)))) for b in range(2)]
            it = [0]
            for c in range(22):
                wg, wgiv = wblock(wcols(Wup, c * 128, 128), 8, 128)
                wvv, wviv = wblock(wcols(Wup, DFF + c * 128, 128), 8, 128)
                for (t0, nn) in cts:
                    cvt = cvt2[it[0] % 2]; gg = gg2[it[0] % 2]
                    it[0] += 1
                    for a, (wv, wiv, ch) in enumerate(((wg, wgiv, c), (wvv, wviv, 22 + c))):
                        p = nps()
                        mm(p[:, 0:nn], piv(p, nn), [(wv[:, kc, :], hn[:, kc, t0:t0 + nn], wiv + hn.iv(kc, (t0, t0 + nn))) for kc in range(8)])
                        Ub = U[a][(t0 // 512) % 2]
                        P.op("act", lambda e, Ub=Ub, ch=ch: e.copy(out=Ub[:, 0:2], in_=halo[:, ch, :]), reads=halo.iv(ch), writes=Ub.iv((0, 2)))
                        P.op("act", lambda e, Ub=Ub, p=p, nn=nn: e.copy(out=Ub[:, 2:nn + 2], in_=p[:, 0:nn]), reads=piv(p, nn), writes=Ub.iv((2, nn + 2)))
                        P.op("act", lambda e, Ub=Ub, ch=ch, nn=nn: e.copy(out=halo[:, ch, :], in_=Ub[:, nn:nn + 2]), reads=Ub.iv((nn, nn + 2)), writes=halo.iv(ch))
                        cb = cvt[a]
                        eng = "dve"
                        P.op("act", lambda e, Ub=Ub, cb=cb, ch=ch, nn=nn: e.activation(out=cb[:, 0:nn], in_=Ub[:, 0:nn], func=AF.Copy, scale=wcvf[:, layer, 0, ch:ch + 1]), reads=Ub.iv((0, nn)) + wcvf.iv(), writes=cb.iv((0, nn)))
                        P.op(eng, lambda e, Ub=Ub, cb=cb, ch=ch, nn=nn: e.scalar_tensor_tensor(out=cb[:, 0:nn], in0=Ub[:, 1:nn + 1], scalar=wcvf[:, layer, 1, ch:ch + 1], in1=cb[:, 0:nn], op0=ALU.mult, op1=ALU.add), reads=Ub.iv((1, nn + 1)) + wcvf.iv() + cb.iv((0, nn)), writes=cb.iv((0, nn)))
                        P.op(eng, lambda e, Ub=Ub, cb=cb, ch=ch, nn=nn: e.scalar_tensor_tensor(out=cb[:, 0:nn], in0=Ub[:, 2:nn + 2], scalar=wcvf[:, layer, 2, ch:ch + 1], in1=cb[:, 0:nn], op0=ALU.mult, op1=ALU.add), reads=Ub.iv((2, nn + 2)) + wcvf.iv() + cb.iv((0, nn)), writes=cb.iv((0, nn)))
                    P.op("act", lambda e, nn=nn, gg=gg, cvt=cvt: e.activation(out=gg[:, 0:nn], in_=cvt[0][:, 0:nn], func=AF.Gelu_apprx_tanh), reads=cvt[0].iv((0, nn)), writes=gg.iv((0, nn)))
                    P.op("dve", lambda e, c=c, t0=t0, nn=nn, gg=gg, cvt=cvt: e.tensor_tensor(out=act[:, c, t0:t0 + nn], in0=gg[:, 0:nn], in1=cvt[1][:, 0:nn], op=ALU.mult), reads=gg.iv((0, nn)) + cvt[1].iv((0, nn)), writes=act.iv(c, (t0, t0 + nn)))
            st_["wbufs"] = list(base_wbufs)
            P.release(m2)
            xw2 = [P.alloc([128, 1024], BF16, f"xw2{b}") for b in range(4)]
            st_["wbufs"] = list(base_wbufs) + [(xw2[b].a, (lambda n, b=b: xw2[b].iv((0, n)))) for b in range(4)]
            out_proj(w_ffn_down[layer], [((lambda t0, nn, kc=kc: act[:, kc, t0:t0 + nn]), (lambda t0, nn, kc=kc: act.iv(kc, (t0, t0 + nn))), 128) for kc in range(22)], ysb, n)
            post_norm_add(3, layer, ysb, c0, n)
            st_["wbufs"] = list(base_wbufs)
            P.release(sm)
            sm = P.mark()
            peT = P.alloc([128, 2, n], BF16, "peT")
            xin = [P.alloc([128, 256], F32, f"xin{a}") for a in range(2)]
            sig = P.alloc([128, 512], F32, "sig")
            wpl = P.alloc([128, 2, 1024], BF16, "wpl")
            xw3 = [P.alloc([128, 1024], BF16, f"xw3{b}") for b in range(4)]
            st_["wbufs"] = list(base_wbufs) + [(xw3[b].a, (lambda n, b=b: xw3[b].iv((0, n)))) for b in range(4)]
            pesrc = psm[layer] if smp else pp[layer, seqi]
            for ti, t0 in enumerate(range(0, n, 128)):
                nt = min(128, n - t0)
                xb = xin[ti % 2]
                P.dma(lambda e, xb=xb, t0=t0, nt=nt: e.dma_start(out=xb[0:nt, :], in_=pesrc[c0 + t0:c0 + t0 + nt, :]), writes=xb.iv())
                p = nps()
                for dch in range(2):
                    P.op("pe", lambda e, p=p, xb=xb, dch=dch, nt=nt: e.transpose(p[:, dch * 128:dch * 128 + nt], xb[0:nt, dch * 128:(dch + 1) * 128], ident[0:nt, 0:nt]), reads=xb.iv() + ident.iv(), writes=piv(p, nt, dch * 128))
                    P.op("act", lambda e, p=p, dch=dch, t0=t0, nt=nt: e.copy(out=peT[:, dch, t0:t0 + nt], in_=p[:, dch * 128:dch * 128 + nt]), reads=piv(p, nt, dch * 128), writes=peT.iv(dch, (t0, t0 + nt)))
            for c in range(8):
                for (t0, nn) in cts:
                    P.op("act", lambda e, c=c, t0=t0, nn=nn: e.copy(out=hn[:, c, t0:t0 + nn], in_=h[:, c, c0 + t0:c0 + t0 + nn]), reads=h.iv(c, (c0 + t0, c0 + t0 + nn)), writes=hn.iv(c, (t0, t0 + nn)))
            for hh in range(2):
                wblock(w_ple[layer].rearrange("(kc p) m -> p kc m", p=128)[:, :, hh * 512:(hh + 1) * 512], 2, 512, dst=wpl[:, :, hh * 512:(hh + 1) * 512], dst_iv=wpl.iv(None, (hh * 512, hh * 512 + 512)))
            for m_ in range(8):
                wv, wiv = wblock(wcols(w_ple_gate[layer], m_ * 128, 128), 8, 128)
                for (t0, nn) in cts:
                    p = nps()
                    mm(p[:, 0:nn], piv(p, nn), [(wv[:, kc, :], hn[:, kc, t0:t0 + nn], wiv + hn.iv(kc, (t0, t0 + nn))) for kc in range(8)])
                    p2 = nps()
                    mm(p2[:, 0:nn], piv(p2, nn), [(wpl[:, dch, m_ * 128:(m_ + 1) * 128], peT[:, dch, t0:t0 + nn], wpl.iv(dch, (m_ * 128, m_ * 128 + 128)) + peT.iv(dch, (t0, t0 + nn))) for dch in range(2)])
                    P.op("act", lambda e, p=p, nn=nn: e.activation(out=sig[:, 0:nn], in_=p[:, 0:nn], func=AF.Sigmoid), reads=piv(p, nn), writes=sig.iv((0, nn)))
                    P.op("dve", lambda e, p2=p2, nn=nn: e.tensor_tensor(out=sig[:, 0:nn], in0=sig[:, 0:nn], in1=p2[:, 0:nn], op=ALU.mult), reads=sig.iv((0, nn)) + piv(p2, nn), writes=sig.iv((0, nn)))
                    P.op("dve", lambda e, m_=m_, t0=t0, nn=nn: e.tensor_tensor(out=h[:, m_, c0 + t0:c0 + t0 + nn], in0=h[:, m_, c0 + t0:c0 + t0 + nn], in1=sig[:, 0:nn], op=ALU.add), reads=h.iv(m_, (c0 + t0, c0 + t0 + nn)) + sig.iv((0, nn)), writes=h.iv(m_, (c0 + t0, c0 + t0 + nn)))
            st_["wbufs"] = list(base_wbufs)
            P.release(sm)
        for c0_ in range(0, T_, 1024):
            st_body(c0_)
        of = o_ffn_s[layer] if smp else o_ffn_p[layer, seqi]
        for r_ in range(2):
            P.dma(lambda e, r_=r_: e.dma_start(out=of[r_].rearrange("(c p) -> p c", p=128), in_=halo[:, :, r_]), reads=halo.iv())
        P.release(lm)

    def load_x(smp, seqi, T_):
        m = P.mark()
        xin = [P.alloc([128, D], F32, f"xl{a}") for a in range(2)]
        src = xs if smp else xp[seqi]
        for ti, t0 in enumerate(range(0, T_, 128)):
            nt = min(128, T_ - t0)
            xb = xin[ti % 2]
            P.dma(lambda e, xb=xb, t0=t0, nt=nt: e.dma_start(out=xb[0:nt, :], in_=src[t0:t0 + nt, :]), writes=xb.iv())
            for c in range(8):
                p = nps()
                P.op("pe", lambda e, p=p, xb=xb, c=c, nt=nt: e.transpose(p[:, 0:nt], xb[0:nt, c * 128:(c + 1) * 128], ident[0:nt, 0:nt]), reads=xb.iv((c * 128, c * 128 + 128)) + ident.iv(), writes=piv(p, nt))
                P.op("act" if c % 2 else "dve", (lambda e, p=p, c=c, t0=t0, nt=nt: e.copy(out=h[:, c, t0:t0 + nt], in_=p[:, 0:nt])) if c % 2 else (lambda e, p=p, c=c, t0=t0, nt=nt: e.tensor_copy(out=h[:, c, t0:t0 + nt], in_=p[:, 0:nt])), reads=piv(p, nt), writes=h.iv(c, (t0, t0 + nt)))
        P.release(m)

    def store_y(smp, seqi, T_):
        m = P.mark()
        yo = [P.alloc([128, D], F32, f"yo{a}") for a in range(2)]
        dst = ys if smp else yp[seqi]
        for ti, t0 in enumerate(range(0, T_, 128)):
            nt = min(128, T_ - t0)
            yb = yo[ti % 2]
            for c in range(8):
                p = nps()
                P.op("pe", lambda e, p=p, c=c, t0=t0, nt=nt: e.transpose(p[0:nt, 0:128], h[:, c, t0:t0 + nt], ident[:]), reads=h.iv(c, (t0, t0 + nt)) + ident.iv(), writes=piv(p, 128))
                P.op("act" if c % 2 else "dve", (lambda e, p=p, c=c, yb=yb, nt=nt: e.copy(out=yb[0:nt, c * 128:(c + 1) * 128], in_=p[0:nt, 0:128])) if c % 2 else (lambda e, p=p, c=c, yb=yb, nt=nt: e.tensor_copy(out=yb[0:nt, c * 128:(c + 1) * 128], in_=p[0:nt, 0:128])), reads=piv(p, 128), writes=yb.iv((c * 128, c * 128 + 128)))
            P.dma(lambda e, yb=yb, t0=t0, nt=nt: e.dma_start(out=dst[t0:t0 + nt, :], in_=yb[0:nt, :]), reads=yb.iv())
        P.release(m)

    odd_mixer = make_odd(locals())

    for (kind, seqi) in passes:
        smp = kind == "s"
        T_ = SL if smp else SEQ
        if smp:
            pm_ = P.mark()
            xws = [P.alloc([128, 1024], BF16, f"xws{b}") for b in range(16)]
            base_wbufs.extend([(xws[b].a, (lambda n, b=b: xws[b].iv((0, n)))) for b in range(16)])
            st_["wbufs"] = list(base_wbufs)
        load_x(smp, seqi, T_)
        for layer in range(nlayers):
            if layer % 2 == 0:
                even_mixer(layer // 2, layer, smp, seqi, T_)
            else:
                odd_mixer(layer // 2, layer, smp, seqi, T_)
            ffn(layer, smp, seqi, T_)
        store_y(smp, seqi, T_)
    with nc.allow_non_contiguous_dma(reason="small param/state layouts"):
        P.run()
    return P


WNAMES = ["g_mix_pre", "g_mix_post", "g_ffn_pre", "g_ffn_post", "w_even_in", "w_conv_a", "s5_a_re", "s5_a_im", "s5_log_dt",
          "s5_b_re", "s5_b_im", "s5_c_re", "s5_c_im", "s5_d", "w_glu", "w_even_out", "w_odd_in", "b_forget", "w_spatial",
          "b_spatial", "g_gmlp_v", "w_odd_out", "w_ffn_up", "w_ffn_conv", "w_ffn_down", "w_ple", "w_ple_gate"]


def make_in_maps(inp):
    f = lambda a: np.ascontiguousarray(np.asarray(a, dtype=np.float32))
    W = {k: f(inp[k]) for k in WNAMES}
    maps = []
    for c in range(8):
        m = dict(W)
        m["xp"] = f(inp["x_prompt"][2 * c:2 * c + 2]); m["xs"] = f(inp["x_sample"][c])
        m["pp"] = f(inp["p_prompt"][:, 2 * c:2 * c + 2]); m["psm"] = f(inp["p_sample"][:, c])
        m["cconv"] = f(inp["cache_conv_a"][:, c]); m["sre"] = f(inp["state_ssm_re"][:, c]); m["sim"] = f(inp["state_ssm_im"][:, c])
        m["ck"] = f(np.asarray(inp["cache_k"])[:, c].reshape(2, 1024, 512)); m["cv"] = f(np.asarray(inp["cache_v"])[:, c].reshape(2, 1024, 512))
        m["clf"] = f(inp["cache_logf"][:, c]); m["cffn"] = f(inp["cache_ffn_conv"][:, c])
        maps.append(m)
    return maps


def gather(res):
    R = res
    cat = lambda name, ax: np.concatenate([r[name] for r in R], axis=ax)
    stk = lambda name, ax: np.stack([r[name] for r in R], axis=ax)
    y_prompt = cat("yp", 0)
    y_sample = stk("ys", 0)
    conv_p = cat("o_conv_p", 1); sre_p = cat("o_sre_p", 1); sim_p = cat("o_sim_p", 1)
    k_p = cat("o_k_p", 1).reshape(2, 16, 2048, 8, 64); v_p = cat("o_v_p", 1).reshape(2, 16, 2048, 8, 64)
    lf_p = cat("o_lf_p", 1); ffn_p = cat("o_ffn_p", 1)
    conv_s = stk("o_conv_s", 1); sre_s = stk("o_sre_s", 1); sim_s = stk("o_sim_s", 1)
    k_s = stk("o_k_s", 1).reshape(2, 8, 32, 8, 64); v_s = stk("o_v_s", 1).reshape(2, 8, 32, 8, 64)
    lf_s = stk("o_lf_s", 1); gv_s = stk("o_gv_s", 1); ffn_s = stk("o_ffn_s", 1)
    outs = (y_prompt, y_sample, conv_p, sre_p, sim_p, k_p, v_p, lf_p, ffn_p, conv_s, sre_s, sim_s, k_s, v_s, lf_s, gv_s, ffn_s)
    return tuple(np.ascontiguousarray(o.astype(np.float32)) for o in outs)


def kernel(**inputs):
    nc = bass.Bass("TRN2", target_bir_lowering=False)
    build(nc)
    maps = make_in_maps(inputs)
    res = run_bass_kernel_spmd(nc, maps, core_ids=list(range(8)))
    return gather(res.results)
```

```python
import numpy as np
import concourse.bass as bass
import concourse.mybir as mybir

F32 = mybir.dt.float32
BF16 = mybir.dt.bfloat16
I32 = mybir.dt.int32
ALU = mybir.AluOpType
AF = mybir.ActivationFunctionType
AX = mybir.AxisListType

ENGS = ("pe", "act", "dve", "pool", "sp")
EPOCH = 30000


class T:
    _n = 0

    def __init__(self, handle, shape, name):
        self.h = handle
        self.shape = list(shape)
        self.name = name
        self.id = T._n
        T._n += 1
        st = [1] * len(shape)
        for i in range(len(shape) - 2, 0, -1):
            st[i] = st[i + 1] * shape[i + 1]
        self.st = st
        self.recs = []

    def __getitem__(self, idx):
        return self.h[idx]

    def iv(self, *idx):
        nd = len(self.shape) - 1
        idx = list(idx) + [None] * (nd - len(idx))
        rng = []
        for d, ix in enumerate(idx):
            n = self.shape[d + 1]
            if ix is None:
                rng.append((0, n))
            elif isinstance(ix, tuple):
                rng.append(ix)
            else:
                rng.append((ix, ix + 1))
        out = [(0, 0)]
        out = []

        def rec(d, base):
            if d == nd - 1:
                out.append((self, base + rng[d][0] * self.st[d + 1], base + (rng[d][1] - 1) * self.st[d + 1] + 1))
                return
            full = all(rng[k] == (0, self.shape[k + 1]) for k in range(d + 1, nd))
            if full:
                out.append((self, base + rng[d][0] * self.st[d + 1], base + rng[d][1] * self.st[d + 1]))
                return
            for i in range(rng[d][0], rng[d][1]):
                rec(d + 1, base + i * self.st[d + 1])

        rec(0, 0)
        return out


class V:
    def __init__(self, arena, off, shape, dt, name="v"):
        self.t = arena
        self.off = off
        self.shape = list(shape)
        self.dt = dt
        self.u = 1 if dt == BF16 else 2
        n = 1
        for x in shape[1:]:
            n *= x
        self.n = n
        a = arena.h[0:shape[0], off:off + n * self.u]
        if dt != BF16:
            a = a.bitcast(dt)
        if len(shape) == 3:
            a = a.rearrange("p (a b) -> p a b", a=shape[1])
        elif len(shape) == 4:
            a = a.rearrange("p (a b c) -> p a b c", a=shape[1], b=shape[2])
        self.a = a
        st = [1] * len(shape)
        for i in range(len(shape) - 2, 0, -1):
            st[i] = st[i + 1] * shape[i + 1]
        self.st = st

    def __getitem__(self, idx):
        return self.a[idx]

    def iv(self, *idx):
        nd = len(self.shape) - 1
        idx = list(idx) + [None] * (nd - len(idx))
        rng = []
        for d, ix in enumerate(idx):
            n = self.shape[d + 1]
            if ix is None:
                rng.append((0, n))
            elif isinstance(ix, tuple):
                rng.append(ix)
            else:
                rng.append((ix, ix + 1))
        out = []
        u = self.u
        off = self.off

        def rec(d, base):
            if d == nd - 1:
                out.append((self.t, off + u * (base + rng[d][0]), off + u * (base + rng[d][1])))
                return
            full = all(rng[k] == (0, self.shape[k + 1]) for k in range(d + 1, nd))
            if full:
                out.append((self.t, off + u * (base + rng[d][0] * self.st[d + 1]), off + u * (base + rng[d][1] * self.st[d + 1])))
                return
            for i in range(rng[d][0], rng[d][1]):
                rec(d + 1, base + i * self.st[d + 1])

        rec(0, 0)
        return out


class Op:
    __slots__ = ("eng", "emit", "deps", "signal", "tok", "waits", "dma", "slot")

    def __init__(self, eng, emit):
        self.eng = eng
        self.emit = emit
        self.deps = set()
        self.signal = False
        self.tok = None
        self.waits = None
        self.dma = False
        self.slot = None


class Prog:
    def __init__(self, nc, n_dma_slots=48):
        self.nc = nc
        self.ops = []
        self.n_dma_slots = n_dma_slots
        self.dma_rr = 0
        self.dma_rr_sw = 0
        self.ctx = []
        self.bar = None

    def sb(self, name, shape, dt):
        g = self.nc.sbuf_tensor(name, list(shape), dt)
        h = g.__enter__()
        self.ctx.append(g)
        return T(h, shape, name)

    def ps(self, name, shape, dt=F32):
        g = self.nc.psum_tensor(name, list(shape), dt)
        h = g.__enter__()
        self.ctx.append(g)
        t = T(h, shape, name)
        t.psum = True
        return t

    def make_arena(self, nbytes):
        nbytes = nbytes // 256 * 256
        self.arena = self.sb("arena", [128, nbytes // 2], BF16)
        self.atop = 0
        self.bar = None

    def mark(self):
        return self.atop

    def release(self, m):
        self.atop = m

    def alloc(self, shape, dt, name="v"):
        es = 2 if dt == BF16 else 4
        n = 1
        for x in shape[1:]:
            n *= x
        nb = (n * es + 63) // 64 * 64
        off = self.atop
        self.atop += nb
        assert self.atop <= self.arena.shape[1] * 2, f"arena overflow {name} {self.atop}"
        return V(self.arena, off // 2, shape, dt, name)

    def barrier(self):
        last = {}
        for i in range(len(self.ops) - 1, -1, -1):
            o = self.ops[i]
            if o.dma:
                k = ("d", o.slot)
            else:
                k = o.eng
            if k not in last:
                last[k] = i
            if len(last) >= len(ENGS) + self.n_dma_slots:
                break
        self.bar = (set(last.values()), set())

    def _track(self, op_idx, reads, writes):
        op = self.ops[op_idx]
        isdma = op.dma
        pw = [(t, 0, t.shape[1]) for (t, lo, hi) in list(reads) + list(writes) if getattr(t, "psum", False)]
        if pw:
            reads = [x for x in reads if not getattr(x[0], "psum", False)]
            seen = set()
            writes = [x for x in writes if not getattr(x[0], "psum", False)]
            for x in pw:
                if id(x[0]) not in seen:
                    seen.add(id(x[0]))
                    writes.append(x)
        for (t, lo, hi) in reads:
            recs = t.recs
            keep = []
            for r in recs:
                (l2, h2, j, w) = r
                if w:
                    if l2 < hi and lo < h2:
                        op.deps.add(j)
                elif (not isdma) and lo <= l2 and h2 <= hi and self.ops[j].eng == op.eng and not self.ops[j].dma:
                    continue
                keep.append(r)
            keep.append((lo, hi, op_idx, False))
            t.recs = keep
        for (t, lo, hi) in writes:
            recs = t.recs
            keep = []
            for r in recs:
                (l2, h2, j, w) = r
                if l2 < hi and lo < h2:
                    if j != op_idx:
                        op.deps.add(j)
                    if lo <= l2 and h2 <= hi:
                        continue
                keep.append(r)
            keep.append((lo, hi, op_idx, True))
            t.recs = keep

    def op(self, eng, emit, reads=(), writes=(), dma=False):
        o = Op(eng, emit)
        o.dma = dma
        if self.bar is not None and eng not in self.bar[1]:
            o.deps |= self.bar[0]
            self.bar[1].add(eng)
        self.ops.append(o)
        self._track(len(self.ops) - 1, reads, writes)
        return o

    def dma(self, emit, reads=(), writes=(), queue="sp"):
        o = self.op(queue, emit, reads, writes, dma=True)
        half = self.n_dma_slots // 2
        if queue == "pool":
            o.slot = half + (self.dma_rr_sw % half)
            self.dma_rr_sw += 1
        else:
            o.slot = self.dma_rr % half
            self.dma_rr += 1
        return o

    def run(self):
        nc = self.nc
        ops = self.ops
        for o in ops:
            for j in o.deps:
                ops[j].signal = True
        cnt = {e: 0 for e in ENGS}
        slot_cnt = [0] * self.n_dma_slots
        slot_last = [None] * self.n_dma_slots
        for i, o in enumerate(ops):
            if o.dma:
                o.signal = True
                prev = slot_last[o.slot]
                if prev is not None:
                    o.deps.add(prev)
                slot_last[o.slot] = i
                slot_cnt[o.slot] += 1
                o.tok = ("d", o.slot, 16 * slot_cnt[o.slot])
            elif o.signal:
                cnt[o.eng] += 1
                n = cnt[o.eng]
                o.tok = ("e", o.eng, (n - 1) // EPOCH, (n - 1) % EPOCH + 1)
        n_ep = {e: (cnt[e] + EPOCH - 1) // EPOCH for e in ENGS}
        sems = {}
        guards = []
        for e in ENGS:
            for k in range(max(1, n_ep[e])):
                g = nc.semaphore(f"s_{e}_{k}")
                sems[("e", e, k)] = g.__enter__()
                guards.append(g)
        for s in range(self.n_dma_slots):
            g = nc.semaphore(f"s_dma_{s}")
            sems[("d", s)] = g.__enter__()
            guards.append(g)
        per_eng = {e: [] for e in ENGS}
        for i, o in enumerate(ops):
            per_eng[o.eng].append(i)
        self.n_waits = 0

        def tok_key(tok):
            if tok[0] == "d":
                return ("d", tok[1]), tok[2]
            return ("e", tok[1], tok[2]), tok[3]

        def emit_engine(engname, eng):
            waited = {}
            for i in per_eng[engname]:
                o = ops[i]
                need = {}
                for j in o.deps:
                    k, v = tok_key(ops[j].tok)
                    if need.get(k, 0) < v:
                        need[k] = v
                for k, v in need.items():
                    if waited.get(k, 0) >= v:
                        continue
                    waited[k] = v
                    eng.wait_ge(sems[k], v)
                    self.n_waits += 1
                inst = o.emit(eng)
                if o.signal:
                    k, v = tok_key(o.tok)
                    inst.then_inc(sems[k], 16 if o.dma else 1)
            if engname == "sp":
                for s in range(self.n_dma_slots):
                    if slot_cnt[s]:
                        eng.wait_ge(sems[("d", s)], 16 * slot_cnt[s])

        with nc.Block() as block:
            @block.tensor
            def _(e):
                emit_engine("pe", e)

            @block.scalar
            def _(e):
                emit_engine("act", e)

            @block.vector
            def _(e):
                emit_engine("dve", e)

            @block.gpsimd
            def _(e):
                emit_engine("pool", e)

            @block.sync
            def _(e):
                emit_engine("sp", e)
        for g in reversed(guards):
            g.__exit__(None, None, None)
        for g in reversed(self.ctx):
            g.__exit__(None, None, None)
        self.stats = dict(n_ops=len(ops), cnt=cnt, n_waits=self.n_waits)


import os
class _Stop(Exception):
    pass

def make_odd(L):
    STOP = float(os.environ.get('ODD_STOP', '99'))
    P = L["P"]; nps = L["nps"]; nps_held = L["nps_held"]; piv = L["piv"]; mm = L["mm"]; wblock = L["wblock"]; wcols = L["wcols"]
    hn = L["hn"]; h = L["h"]; ident = L["ident"]; onesf = L["onesf"]; onesb = L["onesb"]; maskw = L["maskw"]; negc = L["negc"]
    pre_norm = L["pre_norm"]; post_norm_add = L["post_norm_add"]; out_proj = L["out_proj"]
    w_odd_in = L["w_odd_in"]; b_forget = L["b_forget"]; w_spatial = L["w_spatial"]; b_spatial = L["b_spatial"]
    g_gmlp_v = L["g_gmlp_v"]; w_odd_out = L["w_odd_out"]
    ck = L["ck"]; cv = L["cv"]; clf = L["clf"]
    o_k_p = L["o_k_p"]; o_v_p = L["o_v_p"]; o_lf_p = L["o_lf_p"]; o_k_s = L["o_k_s"]; o_v_s = L["o_v_s"]; o_lf_s = L["o_lf_s"]; o_gv_s = L["o_gv_s"]
    EPS = 1e-6
    PAST = 1024

    def odd_mixer(j, layer, smp, seqi, T_):
        lm = P.mark()
        try:
            odd_body(j, layer, smp, seqi, T_)
        except _Stop:
            pass
        P.release(lm)

    def odd_body(j, layer, smp, seqi, T_):
        W = w_odd_in[j]
        NT = (T_ + 127) // 128
        Kaug = P.alloc([96, 8, T_], BF16, "Kaug")
        Vt = P.alloc([128, NT, 8, 65], BF16, "Vt")
        negF = P.alloc([128, NT, 8], F32, "negF")
        Fcar = P.alloc([128, 8], F32, "Fcar")
        WsT = P.alloc([128, 8, 128], BF16, "WsT")
        hselb = P.alloc([8, 4, 128], BF16, "hselb")
        bsph = P.alloc([8, 128], BF16, "bsph")
        bspl = P.alloc([8, 128], BF16, "bspl")
        gvb = P.alloc([128, 512], F32, "gvb")
        bfn = P.alloc([128, 8], F32, "bfn")
        wfl = P.alloc([128, 8, 8], BF16, "wfl")
        ones1 = P.alloc([128, 512], F32, "ones1")
        P.op("pool", lambda e: e.memset(Kaug[64:96, :, :], 0.0), writes=Kaug.iv())
        P.op("pool", lambda e: e.memset(Kaug[64:66, :, :], 1.0), writes=Kaug.iv())
        bsel = P.alloc([128, 64], BF16, "bsel")
        P.op("dve", lambda e: e.tensor_copy(out=bsel[:], in_=ident[:, 64:65].to_broadcast([128, 64])), reads=ident.iv(), writes=bsel.iv())
        P.op("pool", lambda e: e.memset(Vt[:], 1.0), writes=Vt.iv())
        P.op("pool", lambda e: e.memset(Fcar[:], 0.0), writes=Fcar.iv())
        P.op("pool", lambda e: e.memset(ones1[:], 1.0), writes=ones1.iv())
        P.dma(lambda e: e.dma_start(out=gvb[:], in_=g_gmlp_v[j:j + 1, :].broadcast_to([128, 512])), writes=gvb.iv())
        P.dma(lambda e: e.dma_start(out=bfn[64:66, :], in_=b_forget[j:j + 1, :].broadcast_to([2, 8])), writes=bfn.iv())
        P.op("dve", lambda e: e.tensor_scalar(out=bfn[64:66, :], in0=bfn[64:66, :], scalar1=-1.0, scalar2=None, op0=ALU.mult), reads=bfn.iv(), writes=bfn.iv())
        m0 = P.mark()
        hsel = P.alloc([8, 4, 128], F32, "hsel"); bsp = P.alloc([8, 128], F32, "bsp"); bspt = P.alloc([8, 128], F32, "bspt")
        P.dma(lambda e: e.dma_start(out=bsp[:], in_=b_spatial[j]), writes=bsp.iv())
        P.op("pool", lambda e: e.memset(hsel[:], 1.0), writes=hsel.iv())
        P.op("pool", lambda e: e.affine_select(out=hsel[:], in_=hsel[:], pattern=[[128, 4], [1, 128]], compare_op=ALU.is_ge, fill=0.0, base=0, channel_multiplier=-64), reads=hsel.iv(), writes=hsel.iv())
        P.op("pool", lambda e: e.affine_select(out=hsel[:], in_=hsel[:], pattern=[[-128, 4], [-1, 128]], compare_op=ALU.is_ge, fill=0.0, base=63, channel_multiplier=64), reads=hsel.iv(), writes=hsel.iv())
        P.op("dve", lambda e: e.tensor_copy(out=hselb[:], in_=hsel[:]), reads=hsel.iv(), writes=hselb.iv())
        P.op("dve", lambda e: e.tensor_copy(out=bsph[:], in_=bsp[:]), reads=bsp.iv(), writes=bsph.iv())
        P.op("dve", lambda e: e.tensor_copy(out=bspt[:], in_=bsph[:]), reads=bsph.iv(), writes=bspt.iv())
        P.op("dve", lambda e: e.tensor_tensor(out=bspt[:], in0=bsp[:], in1=bspt[:], op=ALU.subtract), reads=bsp.iv() + bspt.iv(), writes=bspt.iv())
        P.op("dve", lambda e: e.tensor_copy(out=bspl[:], in_=bspt[:]), reads=bspt.iv(), writes=bspl.iv())
        wsn = P.alloc([128, 8, 128], F32, "wsn")
        P.dma(lambda e: e.dma_start(out=wsn[:], in_=w_spatial[j].rearrange("h t s -> t h s")), writes=wsn.iv())
        P.op("pool", lambda e: e.affine_select(out=wsn[:], in_=wsn[:], pattern=[[0, 8], [-1, 128]], compare_op=ALU.is_ge, fill=0.0, base=0, channel_multiplier=1), reads=wsn.iv(), writes=wsn.iv())
        for hg in range(2):
            p = nps()
            for hl in range(4):
                hd = hg * 4 + hl
                P.op("pe", lambda e, p=p, hd=hd, hl=hl: e.transpose(p[:, hl * 128:(hl + 1) * 128], wsn[:, hd, :], ident[:]), reads=wsn.iv(hd) + ident.iv(), writes=piv(p, 128, hl * 128))
            P.op("act", lambda e, p=p, hg=hg: e.copy(out=WsT[:, hg * 4:(hg + 1) * 4, :], in_=p[:, :].rearrange("p (a b) -> p a b", a=4)), reads=piv(p, 512), writes=WsT.iv((hg * 4, hg * 4 + 4)))
        P.release(m0)
        if smp:
            Kc = P.alloc([96, 8, PAST], BF16, "Kc")
            Vc = P.alloc([128, 8, 8, 65], BF16, "Vc")
            Dc = P.alloc([128, 8, 8], F32, "Dc")
            m0 = P.mark()
            ctile = [P.alloc([128, 512], F32, f"ctile{a}") for a in range(2)]
            lfc = P.alloc([128, 8, 8], F32, "lfc")
            ustr = P.alloc([128, 128], F32, "ustr")
            P.op("pool", lambda e: e.memset(Kc[64:96, :, :], 0.0), writes=Kc.iv())
            P.op("pool", lambda e: e.memset(Kc[64:66, :, :], 1.0), writes=Kc.iv())
            P.op("pool", lambda e: e.memset(Vc[:], 1.0), writes=Vc.iv())
            P.op("pool", lambda e: e.affine_select(out=ustr[:], in_=onesf[:], pattern=[[-1, 128]], compare_op=ALU.is_gt, fill=0.0, base=0, channel_multiplier=1), reads=onesf.iv(), writes=ustr.iv())
            P.dma(lambda e: e.dma_start(out=lfc[:], in_=clf[j].rearrange("(t p) hd -> p t hd", p=128)), writes=lfc.iv())
            ustrb = P.alloc([128, 128], BF16, "ustrb"); lfh = P.alloc([128, 8, 8], BF16, "lfh"); lfl = P.alloc([128, 8, 8], BF16, "lfl"); lft = P.alloc([128, 8, 8], F32, "lft")
            P.op("dve", lambda e: e.tensor_copy(out=ustrb[:], in_=ustr[:]), reads=ustr.iv(), writes=ustrb.iv())
            P.op("dve", lambda e: e.tensor_copy(out=lfh[:], in_=lfc[:]), reads=lfc.iv(), writes=lfh.iv())
            P.op("dve", lambda e: e.tensor_copy(out=lft[:], in_=lfh[:]), reads=lfh.iv(), writes=lft.iv())
            P.op("dve", lambda e: e.tensor_tensor(out=lft[:], in0=lfc[:], in1=lft[:], op=ALU.subtract), reads=lfc.iv() + lft.iv(), writes=lft.iv())
            P.op("dve", lambda e: e.tensor_copy(out=lfl[:], in_=lft[:]), reads=lft.iv(), writes=lfl.iv())
            for t in range(8):
                cb = ctile[t % 2]
                P.dma(lambda e, cb=cb, t=t: e.dma_start(out=cb[:], in_=ck[j, t * 128:(t + 1) * 128, :]), writes=cb.iv())
                for hg in range(2):
                    p = nps()
                    for hl in range(4):
                        hd = hg * 4 + hl
                        P.op("pe", lambda e, p=p, cb=cb, hd=hd, hl=hl: e.transpose(p[0:64, hl * 128:(hl + 1) * 128], cb[:, hd * 64:(hd + 1) * 64], ident[:]), reads=cb.iv() + ident.iv(), writes=piv(p, 128, hl * 128))
                    P.op("act", lambda e, p=p, hg=hg, t=t: e.copy(out=Kc[0:64, hg * 4:(hg + 1) * 4, t * 128:(t + 1) * 128], in_=p[0:64, :].rearrange("p (a b) -> p a b", a=4)), reads=piv(p, 512), writes=Kc.iv())
                cb2 = ctile[(t + 1) % 2]
                P.dma(lambda e, cb2=cb2, t=t: e.dma_start(out=cb2[:], in_=cv[j, t * 128:(t + 1) * 128, :]), writes=cb2.iv())
                P.op("dve", lambda e, cb2=cb2, t=t: e.tensor_copy(out=Vc[:, t, :, 0:64], in_=cb2[:, :].rearrange("p (a b) -> p a b", a=8)), reads=cb2.iv(), writes=Vc.iv(t))
                p = nps()
                terms = [(ustrb[:], lfh[:, t, :], ustrb.iv() + lfh.iv(t)), (ustrb[:], lfl[:, t, :], ustrb.iv() + lfl.iv(t))]
                for t2 in range(t + 1, 8):
                    terms.append((onesb[:], lfh[:, t2, :], onesb.iv() + lfh.iv(t2)))
                    terms.append((onesb[:], lfl[:, t2, :], onesb.iv() + lfl.iv(t2)))
                mm(p[:, 0:8], piv(p, 8), terms)
                P.op("dve", lambda e, p=p, t=t: e.tensor_copy(out=Dc[:, t, :], in_=p[:, 0:8]), reads=piv(p, 8), writes=Dc.iv(t))
            P.release(m0)
        if STOP <= 1:
            raise _Stop()
        wblock(wcols(W, 1536, 8), 8, 8, dst=wfl[:], dst_iv=wfl.iv())

        def st_body(c0):
            n = min(512, T_ - c0)
            nt_ = (n + 127) // 128
            sm = P.mark()
            ho = ((c0 // 512) % 2) * 512
            pre_norm(0, layer, c0, n, ho)
            att = P.alloc([64, 8, n], BF16, "att"); ydb = P.alloc([128, 4, n], BF16, "ydb")
            sm2 = P.mark()
            Q = [P.alloc([96, n], BF16, f"Q{a}") for a in range(2)]
            wq = [P.alloc([128, 8, 66], BF16, f"wq{a}") for a in range(2)]
            wtok = P.alloc([128, 8, 512], BF16, "wtok")
            lfrow = P.alloc([128, 512], F32, "lfrow"); Frow = P.alloc([128, 512], F32, "Frow"); x8 = P.alloc([128, 512], F32, "x8")
            hib = P.alloc([128, 512], BF16, "hib")
            kvout = [lfrow, Frow]
            pT = [P.alloc([128, 512], BF16, f"pT{a}") for a in range(3)]
            s2 = P.alloc([128, 512], F32, "s2"); oT = P.alloc([128, 512], F32, "oT"); rden = P.alloc([128, 512], F32, "rden")
            rdh = P.alloc([128, 512], BF16, "rdh"); rdl = P.alloc([128, 512], BF16, "rdl")
            vnf = oT; vnpad = P.alloc([128, 8, 128], BF16, "vnpad"); junk = s2
            ssq = P.alloc([128, 2], F32, "ssq"); lfT = P.alloc([128, 4, 8], F32, "lfT")
            P.op("pool", lambda e: e.memset(vnpad[:], 0.0), writes=vnpad.iv())
            for qq in Q:
                P.op("pool", lambda e, qq=qq: e.memset(qq[64:96, :], 0.0), writes=qq.iv())
            for bb_ in (lfrow, Frow, rden):
                P.op("pool", lambda e, bb_=bb_: e.memset(bb_[64:96, :], 0.0), writes=bb_.iv())
            for bb_ in (rdh, rdl):
                P.op("pool", lambda e, bb_=bb_: e.memset(bb_[64:96, :], 0.0), writes=bb_.iv())

            def load_wtok(col0):
                for b in range(4):
                    wblock(wcols(W, col0 + b * 128, 128), 8, 128, dst=wtok[:, :, b * 128:(b + 1) * 128], dst_iv=wtok.iv(None, (b * 128, b * 128 + 128)))

            def tok_proj(ti):
                t0 = ti * 128
                nt = min(128, n - t0)
                p = nps()
                mm(p[0:nt, :], piv(p, 512), [(hn[:, kc, ho + t0:ho + t0 + nt], wtok[:, kc, :], hn.iv(kc, (ho + t0, ho + t0 + nt)) + wtok.iv(kc)) for kc in range(8)])
                return p, t0, nt
            if STOP <= 1.2:
                raise _Stop()
            load_wtok(1024)
            if STOP <= 1.4:
                raise _Stop()
            for ti in range(nt_):
                p, t0, nt = tok_proj(ti)
                if STOP <= 1.6:
                    raise _Stop()
                gt = (c0 + t0) // 128
                ko = kvout[ti % 2]
                P.op("act", lambda e, p=p, gt=gt, nt=nt: e.copy(out=Vt[0:nt, gt, :, 0:64], in_=p[0:nt, :].rearrange("p (a b) -> p a b", a=8)), reads=piv(p, 512), writes=Vt.iv(gt))
                if STOP <= 1.7:
                    raise _Stop()
                P.op("dve", lambda e, p=p, ko=ko, nt=nt: e.tensor_copy(out=ko[0:nt, :], in_=p[0:nt, :]), reads=piv(p, 512), writes=ko.iv())
                if STOP <= 1.8:
                    raise _Stop()
                dst = (o_v_s[j] if smp else o_v_p[j, seqi])[c0 + t0:c0 + t0 + nt, :]
                P.dma(lambda e, ko=ko, dst=dst, nt=nt: e.dma_start(out=dst, in_=ko[0:nt, :]), reads=ko.iv())
            load_wtok(512)
            for ti in range(nt_):
                p, t0, nt = tok_proj(ti)
                ko = kvout[ti % 2]
                P.op("dve", lambda e, p=p, ko=ko, nt=nt: e.tensor_copy(out=ko[0:nt, :], in_=p[0:nt, :]), reads=piv(p, 512), writes=ko.iv())
                dst = (o_k_s[j] if smp else o_k_p[j, seqi])[c0 + t0:c0 + t0 + nt, :]
                P.dma(lambda e, ko=ko, dst=dst, nt=nt: e.dma_start(out=dst, in_=ko[0:nt, :]), reads=ko.iv())
            if STOP <= 2:
                raise _Stop()
            for i in range(4):
                wv, wiv = wblock(wcols(W, 1544 + i * 128, 128), 8, 128)
                p = nps()
                mm(p[:, 0:n], piv(p, n), [(wv[:, kc, :], hn[:, kc, ho:ho + n], wiv + hn.iv(kc, (ho, ho + n))) for kc in range(8)])
                P.op("act", lambda e, p=p, i=i: e.copy(out=ydb[:, i, :], in_=p[:, 0:n]), reads=piv(p, n), writes=ydb.iv(i))
            load_wtok(2056)
            for ti in range(nt_):
                p, t0, nt = tok_proj(ti)
                P.op("act", lambda e, p=p, nt=nt: e.activation(out=junk[0:nt, :], in_=p[0:nt, :], func=AF.Square, accum_out=ssq[0:nt, 0:1]), reads=piv(p, 512), writes=junk.iv() + ssq.iv())
                P.op("act", lambda e, nt=nt: e.activation(out=ssq[0:nt, 1:2], in_=ssq[0:nt, 0:1], func=AF.Sqrt, bias=EPS, scale=1.0 / 512), reads=ssq.iv(), writes=ssq.iv())
                P.op("dve", lambda e, nt=nt: e.reciprocal(out=ssq[0:nt, 1:2], in_=ssq[0:nt, 1:2]), reads=ssq.iv(), writes=ssq.iv())
                P.op("dve", lambda e, p=p, nt=nt: e.scalar_tensor_tensor(out=vnf[0:nt, :], in0=p[0:nt, :], scalar=ssq[0:nt, 1:2], in1=gvb[0:nt, :], op0=ALU.mult, op1=ALU.mult), reads=piv(p, 512) + ssq.iv() + gvb.iv(), writes=vnf.iv())
                if smp:
                    P.dma(lambda e, t0=t0, nt=nt: e.dma_start(out=o_gv_s[j, c0 + t0:c0 + t0 + nt, :], in_=vnf[0:nt, :]), reads=vnf.iv())
                v4 = vnf[0:nt, :].rearrange("p (a b c) -> p a b c", a=4, b=2)
                vp4 = vnpad[0:nt, :, :].rearrange("p (a b) c -> p a b c", b=2)
                P.op("act", lambda e, v4=v4, vp4=vp4: e.copy(out=vp4[:, :, 0, 0:64], in_=v4[:, :, 0, :]), reads=vnf.iv(), writes=vnpad.iv())
                P.op("act", lambda e, v4=v4, vp4=vp4: e.copy(out=vp4[:, :, 1, 64:128], in_=v4[:, :, 1, :]), reads=vnf.iv(), writes=vnpad.iv())
                pm = nps()
                for jj in range(4):
                    terms = []
                    for hd in (2 * jj, 2 * jj + 1):
                        terms.append((vnpad[0:nt, hd, :], WsT[0:nt, hd, 0:nt], vnpad.iv(hd) + WsT.iv(hd)))
                    terms.append((hselb[0:8, jj, :], bsph[0:8, 0:nt], hselb.iv(jj) + bsph.iv()))
                    terms.append((hselb[0:8, jj, :], bspl[0:8, 0:nt], hselb.iv(jj) + bspl.iv()))
                    mm(pm[:, jj * 128:jj * 128 + nt], piv(pm, nt, jj * 128), terms)
                    P.op("dve", lambda e, pm=pm, jj=jj, t0=t0, nt=nt: e.tensor_tensor(out=ydb[:, jj, t0:t0 + nt], in0=ydb[:, jj, t0:t0 + nt], in1=pm[:, jj * 128:jj * 128 + nt], op=ALU.mult), reads=ydb.iv(jj, (t0, t0 + nt)) + piv(pm, nt, jj * 128), writes=ydb.iv(jj, (t0, t0 + nt)))
            if STOP <= 3:
                raise _Stop()
            def chain(hd):
                Qh = Q[hd % 2]; wqh = wq[hd % 2]
                wblock(wcols(W, hd * 64, 64), 8, 64, dst=wqh[:, :, 0:64], dst_iv=wqh.iv(None, (0, 64)))
                P.op("pool", lambda e, wqh=wqh, hd=hd: e.tensor_copy(out=wqh[:, :, 64:65], in_=wfl[:, :, hd:hd + 1]), reads=wfl.iv(), writes=wqh.iv(None, (64, 65)))
                P.op("pool", lambda e, wqh=wqh, hd=hd: e.tensor_copy(out=wqh[:, :, 65:66], in_=wfl[:, :, hd:hd + 1]), reads=wfl.iv(), writes=wqh.iv(None, (65, 66)))
                pq = nps()
                mm(pq[0:66, 0:n], piv(pq, n), [(wqh[:, kc, :], hn[:, kc, ho:ho + n], wqh.iv(kc) + hn.iv(kc, (ho, ho + n))) for kc in range(8)])
                P.op("act", lambda e, pq=pq, Qh=Qh: e.copy(out=Qh[0:64, :], in_=pq[0:64, 0:n]), reads=piv(pq, n), writes=Qh.iv())
                r = slice(64, 66)
                P.op("act", lambda e, pq=pq, hd=hd: e.activation(out=lfrow[r, 0:n], in_=pq[r, 0:n], func=AF.Exp, bias=bfn[r, hd:hd + 1], scale=-1.0), reads=piv(pq, n) + bfn.iv(), writes=lfrow.iv())
                P.op("act", lambda e: e.activation(out=lfrow[r, 0:n], in_=lfrow[r, 0:n], func=AF.Ln, bias=1.0, scale=1.0), reads=lfrow.iv(), writes=lfrow.iv())
                P.op("dve", lambda e: e.tensor_scalar(out=lfrow[r, 0:n], in0=lfrow[r, 0:n], scalar1=-1.0, scalar2=None, op0=ALU.mult), reads=lfrow.iv(), writes=lfrow.iv())
                P.op("dve", lambda e, hd=hd: e.tensor_tensor_scan(out=Frow[r, 0:n], data0=ones1[r, 0:n], data1=lfrow[r, 0:n], initial=Fcar[r, hd:hd + 1], op0=ALU.mult, op1=ALU.add), reads=ones1.iv() + lfrow.iv() + Fcar.iv(), writes=Frow.iv())
                P.op("dve", lambda e, hd=hd: e.tensor_copy(out=Fcar[r, hd:hd + 1], in_=Frow[r, n - 1:n]), reads=Frow.iv(), writes=Fcar.iv())
                P.op("dve", lambda e: e.tensor_scalar(out=x8[r, 0:n], in0=Frow[r, 0:n], scalar1=8.0, scalar2=None, op0=ALU.mult), reads=Frow.iv(), writes=x8.iv())
                P.op("dve", lambda e: e.tensor_copy(out=hib[r, 0:n], in_=x8[r, 0:n]), reads=x8.iv(), writes=hib.iv())
                P.op("dve", lambda e, Qh=Qh: e.scalar_tensor_tensor(out=Qh[r, :], in0=hib[r, 0:n], scalar=negc[r, 0:1], in1=x8[r, 0:n], op0=ALU.mult, op1=ALU.add), reads=hib.iv() + negc.iv() + x8.iv(), writes=Qh.iv())
                wv, wiv = wblock(wcols(W, 512 + hd * 64, 64), 8, 64)
                pk = nps()
                mm(pk[0:64, 0:n], piv(pk, n), [(wv[:, kc, :], hn[:, kc, ho:ho + n], wiv + hn.iv(kc, (ho, ho + n))) for kc in range(8)])
                P.op("act", lambda e, pk=pk, hd=hd: e.copy(out=Kaug[0:64, hd, c0:c0 + n], in_=pk[0:64, 0:n]), reads=piv(pk, n), writes=Kaug.iv(hd, (c0, c0 + n)))

            def chainB(hd):
                pf = nps()
                for ti in range(nt_):
                    t0 = ti * 128; nt = min(128, n - t0)
                    P.op("pe", lambda e, pf=pf, ti=ti, t0=t0, nt=nt: e.transpose(pf[0:nt, 64 * ti:64 * ti + 32], Frow[64:96, t0:t0 + nt], ident[64:96, 64:96]), reads=Frow.iv() + ident.iv(), writes=piv(pf, 32, 64 * ti))
                    P.op("pe", lambda e, pf=pf, ti=ti, t0=t0, nt=nt: e.transpose(pf[0:nt, 64 * ti + 32:64 * ti + 64], lfrow[64:96, t0:t0 + nt], ident[64:96, 64:96]), reads=lfrow.iv() + ident.iv(), writes=piv(pf, 32, 64 * ti + 32))
                    gt = (c0 + t0) // 128
                    P.op("dve", lambda e, pf=pf, ti=ti, gt=gt, nt=nt, hd=hd: e.tensor_scalar(out=negF[0:nt, gt, hd:hd + 1], in0=pf[0:nt, 64 * ti:64 * ti + 1], scalar1=-1.0, scalar2=None, op0=ALU.mult), reads=piv(pf, 32, 64 * ti), writes=negF.iv(gt))
                    P.op("dve", lambda e, pf=pf, ti=ti, nt=nt, hd=hd: e.tensor_copy(out=lfT[0:nt, ti, hd:hd + 1], in_=pf[0:nt, 64 * ti + 32:64 * ti + 33]), reads=piv(pf, 32, 64 * ti + 32), writes=lfT.iv(ti))

            def attend(hd):
                Qh = Q[hd % 2]
                keys = []
                if smp:
                    for t in range(8):
                        keys.append((Kc[:, hd, t * 128:(t + 1) * 128], Kc.iv(hd), Vc[:, t, hd, :], Vc.iv(t), Dc[:, t, hd:hd + 1], Dc.iv(t), 128, None))
                nkt = (c0 + n + 127) // 128
                for kt in range(nkt):
                    nk = min(128, T_ - kt * 128)
                    rel = kt * 128 - c0
                    keys.append((Kaug[:, hd, kt * 128:kt * 128 + nk], Kaug.iv(hd, (kt * 128, kt * 128 + nk)), Vt[0:nk, kt, hd, :], Vt.iv(kt), negF[0:nk, kt, hd:hd + 1], negF.iv(kt), nk, rel if rel >= 0 else None))
                po = nps_held()
                pend = []

                def emit_pv(x):
                    (ki, va, viv, pt_, nk, q0) = x
                    P.op("pe", lambda e, po=po, va=va, pt_=pt_, nk=nk, ki=ki, q0=q0, last=(ki == len(keys) - 1): e.matmul(po[0:65, q0:n], lhsT=va, rhs=pt_[0:nk, q0:n], start=(ki == 0), stop=last), reads=viv + pt_.iv() + (piv(po, n) if ki else []), writes=piv(po, n))
                for ki, (ka, kiv, va, viv, ba, biv, nk, rel) in enumerate(keys):
                    pscore = nps()
                    q0 = rel if (rel is not None and ki > 0) else 0
                    mm(pscore[0:nk, q0:n], piv(pscore, n), [(ka, Qh[:, q0:n], kiv + Qh.iv())])
                    pt_ = pT[ki % 3]
                    if rel is None:
                        P.op("act", lambda e, pscore=pscore, pt_=pt_, ba=ba, nk=nk: e.activation(out=pt_[0:nk, 0:n], in_=pscore[0:nk, 0:n], func=AF.Exp, bias=ba, scale=0.125), reads=piv(pscore, n) + biv, writes=pt_.iv())
                    else:
                        P.op("dve", lambda e, pscore=pscore, nk=nk, rel=rel, q0=q0: e.tensor_tensor(out=s2[0:nk, q0:n], in0=pscore[0:nk, q0:n], in1=maskw[0:nk, 384 - rel + q0:384 - rel + n], op=ALU.add), reads=piv(pscore, n) + maskw.iv(), writes=s2.iv())
                        P.op("act", lambda e, pt_=pt_, ba=ba, nk=nk, q0=q0: e.activation(out=pt_[0:nk, q0:n], in_=s2[0:nk, q0:n], func=AF.Exp, bias=ba, scale=0.125), reads=s2.iv() + biv, writes=pt_.iv())
                    pend.append((ki, va, viv, pt_, nk, q0))
                    if len(pend) > 2:
                        emit_pv(pend.pop(0))
                for x in pend:
                    emit_pv(x)
                P.op("act", lambda e, po=po: e.copy(out=oT[0:64, 0:n], in_=po[0:64, 0:n]), reads=piv(po, n), writes=oT.iv())
                P.op("dve", lambda e, po=po: e.reciprocal(out=rden[64:65, 0:n], in_=po[64:65, 0:n]), reads=piv(po, n), writes=rden.iv())
                P.op("dve", lambda e: e.tensor_copy(out=rdh[64:65, 0:n], in_=rden[64:65, 0:n]), reads=rden.iv(), writes=rdh.iv())
                P.op("dve", lambda e: e.tensor_tensor(out=rden[64:65, 0:n], in0=rden[64:65, 0:n], in1=rdh[64:65, 0:n], op=ALU.subtract), reads=rden.iv() + rdh.iv(), writes=rden.iv())
                P.op("dve", lambda e: e.tensor_copy(out=rdl[64:65, 0:n], in_=rden[64:65, 0:n]), reads=rden.iv(), writes=rdl.iv())
                pb = nps()
                mm(pb[0:64, 0:n], piv(pb, n), [(bsel[64:96, :], rdh[64:96, 0:n], bsel.iv() + rdh.iv()), (bsel[64:96, :], rdl[64:96, 0:n], bsel.iv() + rdl.iv())])
                P.op("dve", lambda e, pb=pb, hd=hd: e.tensor_tensor(out=att[:, hd, :], in0=oT[0:64, 0:n], in1=pb[0:64, 0:n], op=ALU.mult), reads=oT.iv() + piv(pb, n), writes=att.iv(hd))
            chain(0)
            chainB(0)
            for hd in range(8):
                if hd + 1 < 8:
                    chain(hd + 1)
                if STOP > 4:
                    attend(hd)
                if hd + 1 < 8:
                    chainB(hd + 1)
            if STOP <= 5:
                raise _Stop()
            for ti in range(nt_):
                t0 = ti * 128; nt = min(128, n - t0)
                dst = (o_lf_s[j] if smp else o_lf_p[j, seqi])[c0 + t0:c0 + t0 + nt, :]
                P.dma(lambda e, dst=dst, ti=ti, nt=nt: e.dma_start(out=dst, in_=lfT[0:nt, ti, :]), reads=lfT.iv(ti))
            P.release(sm2)
            ysb = P.alloc([128, 8, n], F32, "ysbO")
            rl = [((lambda t0, nn, hd=hd: att[:, hd, t0:t0 + nn]), (lambda t0, nn, hd=hd: att.iv(hd, (t0, t0 + nn))), 64) for hd in range(8)]
            rl += [((lambda t0, nn, kc=kc: ydb[:, kc, t0:t0 + nn]), (lambda t0, nn, kc=kc: ydb.iv(kc, (t0, t0 + nn))), 128) for kc in range(4)]
            out_proj(w_odd_out[j], rl, ysb, n)
            post_norm_add(1, layer, ysb, c0, n)
            P.release(sm)

        for c0_ in range(0, T_, 512):
            st_body(c0_)

    return odd_mixer


import math
import numpy as np
import concourse.bass as bass
import concourse.mybir as mybir
from concourse.bass_utils import run_bass_kernel_spmd

D = 1024
DEPTH = 4
SEQ = 2048
SL = 32
PAST = 1024
DFF = 2816
EPS = 1e-6
NEG = -1.0e30
PI = math.pi


def build(nc, passes=(("p", 0), ("p", 1), ("s", 0)), nlayers=4):
    P = Prog(nc)

    def din(name, shape):
        return nc.dram_tensor(name, list(shape), F32, kind="ExternalInput").ap()

    def dout(name, shape):
        return nc.dram_tensor(name, list(shape), F32, kind="ExternalOutput").ap()

    xp = din("xp", [2, SEQ, D]); xs = din("xs", [SL, D])
    pp = din("pp", [4, 2, SEQ, 256]); psm = din("psm", [4, SL, 256])
    cconv = din("cconv", [2, 2, 512]); sre = din("sre", [2, 32, 64]); sim = din("sim", [2, 32, 64])
    ck = din("ck", [2, PAST, 512]); cv = din("cv", [2, PAST, 512]); clf = din("clf", [2, PAST, 8])
    cffn = din("cffn", [4, 2, 2 * DFF])
    g_mix_pre = din("g_mix_pre", [4, D]); g_mix_post = din("g_mix_post", [4, D])
    g_ffn_pre = din("g_ffn_pre", [4, D]); g_ffn_post = din("g_ffn_post", [4, D])
    w_even_in = din("w_even_in", [2, D, 2048]); w_conv_a = din("w_conv_a", [2, 3, 512])
    s5_a_re = din("s5_a_re", [2, 32, 64]); s5_a_im = din("s5_a_im", [2, 32, 64]); s5_log_dt = din("s5_log_dt", [2, 32])
    s5_b_re = din("s5_b_re", [2, 32, 64, 16]); s5_b_im = din("s5_b_im", [2, 32, 64, 16])
    s5_c_re = din("s5_c_re", [2, 32, 16, 64]); s5_c_im = din("s5_c_im", [2, 32, 16, 64])
    s5_d = din("s5_d", [2, 32, 16]); w_glu = din("w_glu", [2, 512, 512]); w_even_out = din("w_even_out", [2, D, D])
    w_odd_in = din("w_odd_in", [2, D, 2568]); b_forget = din("b_forget", [2, 8])
    w_spatial = din("w_spatial", [2, 8, 128, 128]); b_spatial = din("b_spatial", [2, 8, 128])
    g_gmlp_v = din("g_gmlp_v", [2, 512]); w_odd_out = din("w_odd_out", [2, D, D])
    w_ffn_up = din("w_ffn_up", [4, D, 2 * DFF]); w_ffn_conv = din("w_ffn_conv", [4, 3, 2 * DFF])
    w_ffn_down = din("w_ffn_down", [4, DFF, D]); w_ple = din("w_ple", [4, 256, D]); w_ple_gate = din("w_ple_gate", [4, D, D])

    yp = dout("yp", [2, SEQ, D]); ys = dout("ys", [SL, D])
    o_conv_p = dout("o_conv_p", [2, 2, 2, 512]); o_sre_p = dout("o_sre_p", [2, 2, 32, 64]); o_sim_p = dout("o_sim_p", [2, 2, 32, 64])
    o_k_p = dout("o_k_p", [2, 2, SEQ, 512]); o_v_p = dout("o_v_p", [2, 2, SEQ, 512]); o_lf_p = dout("o_lf_p", [2, 2, SEQ, 8])
    o_ffn_p = dout("o_ffn_p", [4, 2, 2, 2 * DFF])
    o_conv_s = dout("o_conv_s", [2, 2, 512]); o_sre_s = dout("o_sre_s", [2, 32, 64]); o_sim_s = dout("o_sim_s", [2, 32, 64])
    o_k_s = dout("o_k_s", [2, SL, 512]); o_v_s = dout("o_v_s", [2, SL, 512]); o_lf_s = dout("o_lf_s", [2, SL, 8])
    o_gv_s = dout("o_gv_s", [2, SL, 512]); o_ffn_s = dout("o_ffn_s", [4, 2, 2 * DFF])

    import os
    DBG = os.environ.get("KDBG", "0") == "1"
    if DBG:
        dbg_y = dout("dbg_y", [128, 8, 512]); dbg_h = dout("dbg_h", [128, 8, 512]); dbg_h0 = dout("dbg_h0", [128, 8, 512])
    h = P.sb("h", [128, 8, SEQ], F32)
    hn = P.sb("hn", [128, 8, 1024], BF16)
    NW = 4
    wbf = [P.sb(f"wbf{i}", [128, 1024], BF16) for i in range(NW)]
    sqP = P.sb("sqP", [128, 8, 512], BF16)
    rstdP = P.sb("rstdP", [128, 512], F32)
    ident = P.sb("ident", [128, 128], F32)
    onesf = P.sb("onesf", [128, 128], F32)
    onesb = P.sb("onesb", [128, 128], BF16)
    maskw = P.sb("maskw", [128, 896], F32)
    negc = P.sb("negc", [128, 1], F32)
    gains = P.sb("gains", [128, 4, 4, 8], F32)
    wcva = P.sb("wcva", [128, 2, 3, 4], F32)
    wcvf = P.sb("wcvf", [128, 4, 3, 44], F32)
    ps = [P.ps(f"ps{i}", [128, 512]) for i in range(8)]
    P.make_arena(212863 - (SEQ * 8 * 4 + 8 * 1024 * 2 + (4 * 2048 + 8192 + 2048) + 512 * 2 + 256 + 896 * 4 + 64 + 512 + 96 + 2112) - 600)
    st_ = {"ps": 0, "w": 0}
    tabs = nc.dram_tensor("rot_tabs", [2, 16, 2, 128, 512], F32).ap()

    class _DT:
        def __init__(self):
            self.recs = []
    tabsT = _DT()
    tabs_done = set()

    def tab_iv(j, q):
        k = (j * 16 + q) * 2
        return [(tabsT, k, k + 2)]
    base_wbufs = [(wbf[i].h, (lambda n, i=i: [(wbf[i], 0, n)])) for i in range(NW)]
    st_["wbufs"] = list(base_wbufs)

    def nps():
        st_["ps"] = (st_["ps"] + 1) % st_.get("nrot", 8)
        return ps[st_["ps"]]

    def nps_held():
        st_["psh"] = 1 - st_.get("psh", 0)
        return ps[6 + st_["psh"]]

    def piv(p, n, lo=0):
        return [(p, lo, lo + n)]

    P.op("pool", lambda e: e.memset(onesf[:], 1.0), writes=onesf.iv())
    P.op("pool", lambda e: e.memset(onesb[:], 1.0), writes=onesb.iv())
    P.op("pool", lambda e: e.affine_select(out=ident[:], in_=onesf[:], pattern=[[-1, 128]], compare_op=ALU.is_equal, fill=0.0, base=0, channel_multiplier=1), reads=onesf.iv(), writes=ident.iv())
    P.op("pool", lambda e: e.memset(maskw[:], 0.0), writes=maskw.iv())
    P.op("pool", lambda e: e.affine_select(out=maskw[:], in_=maskw[:], pattern=[[1, 896]], compare_op=ALU.is_ge, fill=NEG, base=-384, channel_multiplier=-1), reads=maskw.iv(), writes=maskw.iv())
    P.op("dve", lambda e: e.tensor_scalar(out=negc[:], in0=ident[:, 65:66], scalar1=-1.0, scalar2=None, op0=ALU.mult), reads=ident.iv(), writes=negc.iv())
    for kind, g in enumerate([g_mix_pre, g_mix_post, g_ffn_pre, g_ffn_post]):
        for l in range(4):
            P.dma(lambda e, g=g, kind=kind, l=l: e.dma_start(out=gains[:, kind, l, :], in_=g[l].rearrange("(c p) -> p c", p=128)), writes=gains.iv(kind, l))
    for j in range(2):
        for tp in range(3):
            P.dma(lambda e, j=j, tp=tp: e.dma_start(out=wcva[:, j, tp, :], in_=w_conv_a[j, tp].rearrange("(c p) -> p c", p=128)), writes=wcva.iv(j, tp))
    for l in range(4):
        for tp in range(3):
            P.dma(lambda e, l=l, tp=tp: e.dma_start(out=wcvf[:, l, tp, :], in_=w_ffn_conv[l, tp].rearrange("(c p) -> p c", p=128)), writes=wcvf.iv(l, tp))

    def wblock(src, KC, ncol, dst=None, dst_iv=None, kp=128):
        n = KC * ncol
        if dst is None:
            bufs = st_["wbufs"]
            k = st_["w"] % len(bufs)
            st_["w"] += 1
            ap2, ivf = bufs[k]
            dst = ap2[0:kp, 0:n].rearrange("p (a b) -> p a b", a=KC)
            dst_iv = ivf(n)
        P.dma(lambda e: e.dma_start(out=dst, in_=src), writes=dst_iv, queue="pool")
        return dst, dst_iv

    def wcols(w2d, m0, ncol, KC=8):
        return w2d.rearrange("(kc p) m -> p kc m", p=128)[:, 0:KC, m0:m0 + ncol]

    def mm(out_ap, out_iv, terms):
        rd = []
        for t in terms:
            rd += t[2]

        def emit(e):
            inst = None
            for i, t in enumerate(terms):
                inst = e.matmul(out_ap, lhsT=t[0], rhs=t[1], start=(i == 0), stop=(i == len(terms) - 1))
            return inst
        P.op("pe", emit, reads=rd, writes=out_iv)

    def rms_rstd(sq_terms, n, scale, rstd, M=128):
        p = nps()
        mm(p[0:M, 0:n], piv(p, n), [(onesb[:, 0:M], a, onesb.iv() + iv) for (a, iv) in sq_terms])
        P.op("act", lambda e: e.activation(out=rstd[0:M, 0:n], in_=p[0:M, 0:n], func=AF.Sqrt, bias=EPS, scale=scale), reads=piv(p, n), writes=rstd.iv((0, n)))
        P.op("dve", lambda e: e.reciprocal(out=rstd[0:M, 0:n], in_=rstd[0:M, 0:n]), reads=rstd.iv((0, n)), writes=rstd.iv((0, n)))

    def pre_norm(kind, layer, c0, n, ho=0):
        sq = sqP; rstd = rstdP
        for t0 in range(0, n, 512):
            nn = min(512, n - t0)
            for c in range(8):
                P.op("act", lambda e, c=c, t0=t0, nn=nn: e.activation(out=sq[:, c, 0:nn], in_=h[:, c, c0 + t0:c0 + t0 + nn], func=AF.Square),
                     reads=h.iv(c, (c0 + t0, c0 + t0 + nn)), writes=sq.iv(c, (0, nn)))
            rms_rstd([(sq[:, c, 0:nn], sq.iv(c, (0, nn))) for c in range(8)], nn, 1.0 / D, rstd)
            for c in range(8):
                P.op("dve", lambda e, c=c, t0=t0, nn=nn: e.scalar_tensor_tensor(out=hn[:, c, ho + t0:ho + t0 + nn], in0=h[:, c, c0 + t0:c0 + t0 + nn], scalar=gains[:, kind, layer, c:c + 1], in1=rstd[:, 0:nn], op0=ALU.mult, op1=ALU.mult),
                     reads=h.iv(c, (c0 + t0, c0 + t0 + nn)) + gains.iv() + rstd.iv((0, nn)), writes=hn.iv(c, (ho + t0, ho + t0 + nn)))

    def post_norm_add(kind, layer, ysb, c0, n):
        sq = sqP; rstd = rstdP
        for t0 in range(0, n, 512):
            nn = min(512, n - t0)
            for c in range(8):
                P.op("act", lambda e, c=c, t0=t0, nn=nn: e.activation(out=sq[:, c, 0:nn], in_=ysb[:, c, t0:t0 + nn], func=AF.Square),
                     reads=ysb.iv(c, (t0, t0 + nn)), writes=sq.iv(c, (0, nn)))
            rms_rstd([(sq[:, c, 0:nn], sq.iv(c, (0, nn))) for c in range(8)], nn, 1.0 / D, rstd)
            for c in range(8):
                P.op("dve", lambda e, c=c, t0=t0, nn=nn: e.scalar_tensor_tensor(out=ysb[:, c, t0:t0 + nn], in0=ysb[:, c, t0:t0 + nn], scalar=gains[:, kind, layer, c:c + 1], in1=rstd[:, 0:nn], op0=ALU.mult, op1=ALU.mult),
                     reads=ysb.iv(c, (t0, t0 + nn)) + gains.iv() + rstd.iv((0, nn)), writes=ysb.iv(c, (t0, t0 + nn)))
                P.op("pool", lambda e, c=c, t0=t0, nn=nn: e.tensor_tensor(out=h[:, c, c0 + t0:c0 + t0 + nn], in0=h[:, c, c0 + t0:c0 + t0 + nn], in1=ysb[:, c, t0:t0 + nn], op=ALU.add),
                     reads=h.iv(c, (c0 + t0, c0 + t0 + nn)) + ysb.iv(c, (t0, t0 + nn)), writes=h.iv(c, (c0 + t0, c0 + t0 + nn)))

    def out_proj(w2d, rhs_list, ysb, n):
        for m_ in range(8):
            blocks = []
            kc0 = 0
            row = 0
            while kc0 < len(rhs_list):
                kp = rhs_list[kc0][2]
                kc1 = kc0
                while kc1 < len(rhs_list) and rhs_list[kc1][2] == kp and kc1 - kc0 < 8:
                    kc1 += 1
                KC = kc1 - kc0
                src = w2d[row:row + KC * kp, m_ * 128:(m_ + 1) * 128].rearrange("(kc p) m -> p kc m", p=kp)
                wv, wiv = wblock(src, KC, 128, kp=kp)
                blocks.append((kc0, kc1, wv, wiv, kp))
                row += KC * kp
                kc0 = kc1
            for t0 in range(0, n, 512):
                nn = min(512, n - t0)
                p = nps()
                terms = []
                for (a0, a1, wv, wiv, kp) in blocks:
                    for kc in range(a0, a1):
                        terms.append((wv[:, kc - a0, :], rhs_list[kc][0](t0, nn), wiv + rhs_list[kc][1](t0, nn)))
                mm(p[:, 0:nn], piv(p, nn), terms)
                P.op("act", lambda e, p=p, m_=m_, t0=t0, nn=nn: e.copy(out=ysb[:, m_, t0:t0 + nn], in_=p[:, 0:nn]), reads=piv(p, nn), writes=ysb.iv(m_, (t0, t0 + nn)))

    def even_prep(j):
        C = {}
        C["BTr"] = P.alloc([128, 16, 128], BF16, "BTr"); C["BTi"] = P.alloc([128, 16, 128], BF16, "BTi")
        C["CTr"] = P.alloc([128, 16, 128], BF16, "CTr"); C["CTi"] = P.alloc([128, 16, 128], BF16, "CTi")
        C["Dd"] = P.alloc([128, 4, 128], BF16, "Dd")
        C["pwr"] = P.alloc([128, 10, 16], F32, "pwr"); C["pwi"] = P.alloc([128, 10, 16], F32, "pwi"); C["npwi"] = P.alloc([128, 10, 16], F32, "npwi")
        C["hsr"] = P.alloc([128, 16], F32, "hsr"); C["hsi"] = P.alloc([128, 16], F32, "hsi")
        C["halo"] = P.alloc([128, 4, 2], F32, "haloA")
        C["upc"] = P.alloc([128, 9, 16], F32, "upc"); C["ups"] = P.alloc([128, 9, 16], F32, "ups"); C["nups"] = P.alloc([128, 9, 16], F32, "nups")
        C["rcol"] = P.alloc([128, 16], F32, "rcol")
        C["ones"] = P.alloc([128, 512], F32, "ones512")
        P.op("pool", lambda e: e.memset(C["ones"][:], 1.0), writes=C["ones"].iv())
        m = P.mark()
        are = P.alloc([128, 16], F32); aim = P.alloc([128, 16], F32); dt = P.alloc([128, 16], F32)
        t1 = P.alloc([128, 16], F32); t2 = P.alloc([128, 16], F32); mag = P.alloc([128, 16], F32)
        cs = P.alloc([128, 16], F32); sn = P.alloc([128, 16], F32); den = P.alloc([128, 16], F32)
        fr = P.alloc([128, 16], F32); fi = P.alloc([128, 16], F32); zr = P.alloc([128, 16], F32)
        bre = P.alloc([128, 16, 16], F32); bim = P.alloc([128, 16, 16], F32); bbr = P.alloc([128, 16, 16], F32); bbi = P.alloc([128, 16, 16], F32)
        tmp3 = P.alloc([128, 16, 16], F32)
        maskC = P.alloc([128, 4, 128], F32); natp = P.alloc([128, 4, 128], F32)
        cdup = P.alloc([128, 128], F32); cT = P.alloc([128, 128], F32); dcol = P.alloc([128, 4], F32)
        for g2 in range(2):
            sl = slice(g2 * 64, g2 * 64 + 64)
            P.dma(lambda e, sl=sl, g2=g2: e.dma_start(out=are[sl, :], in_=s5_a_re[j].rearrange("(q g) p -> g p q", g=2)[g2]), writes=are.iv())
            P.dma(lambda e, sl=sl, g2=g2: e.dma_start(out=aim[sl, :], in_=s5_a_im[j].rearrange("(q g) p -> g p q", g=2)[g2]), writes=aim.iv())
            P.dma(lambda e, sl=sl, g2=g2: e.dma_start(out=dt[sl, :], in_=s5_log_dt[j].rearrange("(q g) -> g q", g=2)[g2:g2 + 1, :].broadcast_to([64, 16])), writes=dt.iv())
            P.dma(lambda e, sl=sl, g2=g2: e.dma_start(out=bre[sl, :, :], in_=s5_b_re[j].rearrange("(q g) p c -> g p q c", g=2)[g2]), writes=bre.iv())
            P.dma(lambda e, sl=sl, g2=g2: e.dma_start(out=bim[sl, :, :], in_=s5_b_im[j].rearrange("(q g) p c -> g p q c", g=2)[g2]), writes=bim.iv())
        P.dma(lambda e: e.dma_start(out=dcol[:], in_=s5_d[j].rearrange("g c -> (g c)").rearrange("(f p) -> p f", p=128)), writes=dcol.iv())

        def A(eng, fn, rd, wr):
            P.op(eng, fn, reads=[x for v in rd for x in v.iv()], writes=[x for v in wr for x in v.iv()])
        A("act", lambda e: e.activation(out=dt[:], in_=dt[:], func=AF.Exp), [dt], [dt])
        A("dve", lambda e: e.tensor_tensor(out=t1[:], in0=are[:], in1=dt[:], op=ALU.mult), [are, dt], [t1])
        A("act", lambda e: e.activation(out=mag[:], in_=t1[:], func=AF.Exp), [t1], [mag])
        A("dve", lambda e: e.tensor_tensor(out=t1[:], in0=aim[:], in1=dt[:], op=ALU.mult), [aim, dt], [t1])
        ki = P.alloc([128, 16], I32)
        kf = P.alloc([128, 16], F32)

        def reduce_sin(dst, shift):
            A("dve", lambda e: e.tensor_scalar(out=t2[:], in0=t1[:], scalar1=shift, scalar2=1.0 / (2 * PI), op0=ALU.add, op1=ALU.mult), [t1], [t2])
            A("dve", lambda e: e.tensor_copy(out=ki[:], in_=t2[:]), [t2], [ki])
            A("dve", lambda e: e.tensor_copy(out=kf[:], in_=ki[:]), [ki], [kf])
            A("dve", lambda e: e.tensor_tensor(out=t2[:], in0=t2[:], in1=kf[:], op=ALU.subtract), [t2, kf], [t2])
            A("dve", lambda e: e.tensor_scalar(out=kf[:], in0=t2[:], scalar1=0.5, scalar2=None, op0=ALU.is_gt), [t2], [kf])
            A("dve", lambda e: e.tensor_tensor(out=t2[:], in0=t2[:], in1=kf[:], op=ALU.subtract), [t2, kf], [t2])
            A("dve", lambda e: e.tensor_scalar(out=kf[:], in0=t2[:], scalar1=-0.5, scalar2=None, op0=ALU.is_lt), [t2], [kf])
            A("dve", lambda e: e.tensor_tensor(out=t2[:], in0=t2[:], in1=kf[:], op=ALU.add), [t2, kf], [t2])
            A("act", lambda e: e.activation(out=dst[:], in_=t2[:], func=AF.Sin, scale=2 * PI), [t2], [dst])
        reduce_sin(sn, 0.0)
        reduce_sin(cs, 0.5 * PI)
        A("dve", lambda e: e.tensor_tensor(out=C["pwr"][:, 0, :], in0=cs[:], in1=mag[:], op=ALU.mult), [cs, mag], [C["pwr"]])
        A("dve", lambda e: e.tensor_tensor(out=C["pwi"][:, 0, :], in0=sn[:], in1=mag[:], op=ALU.mult), [sn, mag], [C["pwi"]])
        for l in range(1, 10):
            A("dve", lambda e, l=l: e.tensor_tensor(out=t1[:], in0=C["pwr"][:, l - 1, :], in1=C["pwr"][:, l - 1, :], op=ALU.mult), [C["pwr"]], [t1])
            A("dve", lambda e, l=l: e.tensor_tensor(out=t2[:], in0=C["pwi"][:, l - 1, :], in1=C["pwi"][:, l - 1, :], op=ALU.mult), [C["pwi"]], [t2])
            A("dve", lambda e, l=l: e.tensor_tensor(out=C["pwr"][:, l, :], in0=t1[:], in1=t2[:], op=ALU.subtract), [t1, t2], [C["pwr"]])
            A("dve", lambda e, l=l: e.scalar_tensor_tensor(out=C["pwi"][:, l, :], in0=C["pwr"][:, l - 1, :], scalar=2.0, in1=C["pwi"][:, l - 1, :], op0=ALU.mult, op1=ALU.mult), [C["pwr"], C["pwi"]], [C["pwi"]])
        A("dve", lambda e: e.tensor_scalar(out=C["npwi"][:], in0=C["pwi"][:], scalar1=-1.0, scalar2=None, op0=ALU.mult), [C["pwi"]], [C["npwi"]])
        A("dve", lambda e: e.tensor_copy(out=C["rcol"][:], in_=mag[:]), [mag], [C["rcol"]])
        A("dve", lambda e: e.tensor_copy(out=C["upc"][:, 0, :], in_=cs[:]), [cs], [C["upc"]])
        A("dve", lambda e: e.tensor_copy(out=C["ups"][:, 0, :], in_=sn[:]), [sn], [C["ups"]])
        for l in range(1, 9):
            A("dve", lambda e, l=l: e.tensor_tensor(out=t1[:], in0=C["upc"][:, l - 1, :], in1=C["upc"][:, l - 1, :], op=ALU.mult), [C["upc"]], [t1])
            A("dve", lambda e, l=l: e.tensor_tensor(out=t2[:], in0=C["ups"][:, l - 1, :], in1=C["ups"][:, l - 1, :], op=ALU.mult), [C["ups"]], [t2])
            A("dve", lambda e, l=l: e.tensor_tensor(out=C["upc"][:, l, :], in0=t1[:], in1=t2[:], op=ALU.subtract), [t1, t2], [C["upc"]])
            A("dve", lambda e, l=l: e.scalar_tensor_tensor(out=C["ups"][:, l, :], in0=C["upc"][:, l - 1, :], scalar=2.0, in1=C["ups"][:, l - 1, :], op0=ALU.mult, op1=ALU.mult), [C["upc"], C["ups"]], [C["ups"]])
        A("dve", lambda e: e.tensor_scalar(out=C["nups"][:], in0=C["ups"][:], scalar1=-1.0, scalar2=None, op0=ALU.mult), [C["ups"]], [C["nups"]])
        if j not in tabs_done:
            tabs_done.add(j)
            Tb = [[P.alloc([128, 512], F32, f"Tg{a}{b}") for b in range(2)] for a in range(2)]
            tmpg = P.alloc([128, 256], F32, "tmpg")
            for q in range(16):
                Tc, Ts = Tb[q % 2]
                P.op("dve", lambda e, Tc=Tc, q=q: e.tensor_copy(out=Tc[:, 0:1], in_=cs[:, q:q + 1]), reads=cs.iv(), writes=Tc.iv((0, 1)))
                P.op("dve", lambda e, Ts=Ts, q=q: e.tensor_copy(out=Ts[:, 0:1], in_=sn[:, q:q + 1]), reads=sn.iv(), writes=Ts.iv((0, 1)))
                for l in range(9):
                    w = 1 << l
                    cw = C["upc"][:, l, q:q + 1]; sw = C["ups"][:, l, q:q + 1]; nsw = C["nups"][:, l, q:q + 1]
                    rdp = C["upc"].iv(l) + C["ups"].iv(l) + C["nups"].iv(l)
                    P.op("dve", lambda e, Tc=Tc, w=w, cw=cw: e.tensor_scalar(out=tmpg[:, 0:w], in0=Tc[:, 0:w], scalar1=cw, scalar2=None, op0=ALU.mult), reads=Tc.iv((0, w)) + rdp, writes=tmpg.iv((0, w)))
                    P.op("dve", lambda e, Tc=Tc, Ts=Ts, w=w, nsw=nsw: e.scalar_tensor_tensor(out=Tc[:, w:2 * w], in0=Ts[:, 0:w], scalar=nsw, in1=tmpg[:, 0:w], op0=ALU.mult, op1=ALU.add), reads=Ts.iv((0, w)) + tmpg.iv((0, w)) + rdp, writes=Tc.iv((w, 2 * w)))
                    P.op("dve", lambda e, Ts=Ts, w=w, cw=cw: e.tensor_scalar(out=tmpg[:, 0:w], in0=Ts[:, 0:w], scalar1=cw, scalar2=None, op0=ALU.mult), reads=Ts.iv((0, w)) + rdp, writes=tmpg.iv((0, w)))
                    P.op("dve", lambda e, Tc=Tc, Ts=Ts, w=w, sw=sw: e.scalar_tensor_tensor(out=Ts[:, w:2 * w], in0=Tc[:, 0:w], scalar=sw, in1=tmpg[:, 0:w], op0=ALU.mult, op1=ALU.add), reads=Tc.iv((0, w)) + tmpg.iv((0, w)) + rdp, writes=Ts.iv((w, 2 * w)))
                P.dma(lambda e, Tc=Tc, q=q: e.dma_start(out=tabs[j, q, 0], in_=Tc[:]), reads=Tc.iv(), writes=tab_iv(j, q))
                P.dma(lambda e, Ts=Ts, q=q: e.dma_start(out=tabs[j, q, 1], in_=Ts[:]), reads=Ts.iv(), writes=tab_iv(j, q))
        A("dve", lambda e: e.tensor_scalar(out=zr[:], in0=C["pwr"][:, 0, :], scalar1=-1.0, scalar2=None, op0=ALU.add), [C["pwr"]], [zr])
        A("dve", lambda e: e.tensor_tensor(out=den[:], in0=are[:], in1=are[:], op=ALU.mult), [are], [den])
        A("dve", lambda e: e.tensor_tensor(out=t1[:], in0=aim[:], in1=aim[:], op=ALU.mult), [aim], [t1])
        A("dve", lambda e: e.tensor_tensor(out=den[:], in0=den[:], in1=t1[:], op=ALU.add), [den, t1], [den])
        A("dve", lambda e: e.reciprocal(out=den[:], in_=den[:]), [den], [den])
        A("dve", lambda e: e.tensor_tensor(out=t1[:], in0=zr[:], in1=are[:], op=ALU.mult), [zr, are], [t1])
        A("dve", lambda e: e.tensor_tensor(out=t2[:], in0=C["pwi"][:, 0, :], in1=aim[:], op=ALU.mult), [C["pwi"], aim], [t2])
        A("dve", lambda e: e.tensor_tensor(out=t1[:], in0=t1[:], in1=t2[:], op=ALU.add), [t1, t2], [t1])
        A("dve", lambda e: e.tensor_tensor(out=fr[:], in0=t1[:], in1=den[:], op=ALU.mult), [t1, den], [fr])
        A("dve", lambda e: e.tensor_tensor(out=t1[:], in0=C["pwi"][:, 0, :], in1=are[:], op=ALU.mult), [C["pwi"], are], [t1])
        A("dve", lambda e: e.tensor_tensor(out=t2[:], in0=zr[:], in1=aim[:], op=ALU.mult), [zr, aim], [t2])
        A("dve", lambda e: e.tensor_tensor(out=t1[:], in0=t1[:], in1=t2[:], op=ALU.subtract), [t1, t2], [t1])
        A("dve", lambda e: e.tensor_tensor(out=fi[:], in0=t1[:], in1=den[:], op=ALU.mult), [t1, den], [fi])
        frb = fr[:].unsqueeze(2).to_broadcast([128, 16, 16]); fib = fi[:].unsqueeze(2).to_broadcast([128, 16, 16])
        A("dve", lambda e: e.tensor_tensor(out=bbr[:], in0=bre[:], in1=frb, op=ALU.mult), [bre, fr], [bbr])
        A("dve", lambda e: e.tensor_tensor(out=tmp3[:], in0=bim[:], in1=fib, op=ALU.mult), [bim, fi], [tmp3])
        A("dve", lambda e: e.tensor_tensor(out=bbr[:], in0=bbr[:], in1=tmp3[:], op=ALU.subtract), [bbr, tmp3], [bbr])
        A("dve", lambda e: e.tensor_tensor(out=bbi[:], in0=bim[:], in1=frb, op=ALU.mult), [bim, fr], [bbi])
        A("dve", lambda e: e.tensor_tensor(out=tmp3[:], in0=bre[:], in1=fib, op=ALU.mult), [bre, fi], [tmp3])
        A("dve", lambda e: e.tensor_tensor(out=bbi[:], in0=bbi[:], in1=tmp3[:], op=ALU.add), [bbi, tmp3], [bbi])
        A("pool", lambda e: e.memset(maskC[:], 1.0), [], [maskC])
        for g2 in range(2):
            sl = slice(g2 * 64, g2 * 64 + 64)
            A("pool", lambda e, sl=sl, g2=g2: e.affine_select(out=maskC[sl, :, :], in_=maskC[sl, :, :], pattern=[[-32, 4], [1, 128]], compare_op=ALU.is_ge, fill=0.0, base=-16 * g2, channel_multiplier=0), [maskC], [maskC])
            A("pool", lambda e, sl=sl, g2=g2: e.affine_select(out=maskC[sl, :, :], in_=maskC[sl, :, :], pattern=[[32, 4], [-1, 128]], compare_op=ALU.is_ge, fill=0.0, base=16 * g2 + 15, channel_multiplier=0), [maskC], [maskC])
        for (bb, BT) in ((bbr, C["BTr"]), (bbi, C["BTi"])):
            for qg in range(4):
                p = nps()
                for ql in range(4):
                    q = qg * 4 + ql
                    A("dve", lambda e, bb=bb, q=q, ql=ql: e.tensor_tensor(out=natp[:, ql, :].rearrange("p (a b) -> p a b", a=8), in0=maskC[:, ql, :].rearrange("p (a b) -> p a b", a=8), in1=bb[:, q, :].unsqueeze(1).to_broadcast([128, 8, 16]), op=ALU.mult), [maskC, bb], [natp])
                    P.op("pe", lambda e, p=p, ql=ql: e.transpose(p[:, ql * 128:(ql + 1) * 128], natp[:, ql, :], ident[:]), reads=natp.iv(ql) + ident.iv(), writes=piv(p, 128, ql * 128))
                P.op("act", lambda e, p=p, BT=BT, qg=qg: e.copy(out=BT[:, qg * 4:(qg + 1) * 4, :], in_=p[:, :].rearrange("p (a b) -> p a b", a=4)), reads=piv(p, 512), writes=BT.iv((qg * 4, qg * 4 + 4)))
        for (csrc, CT, sgn) in ((s5_c_re, C["CTr"], 1.0), (s5_c_im, C["CTi"], -1.0)):
            for t in range(4):
                src = csrc[j].rearrange("g c p -> (g c) p")[t * 128:(t + 1) * 128, :]
                P.dma(lambda e, src=src: e.dma_start(out=cdup[:, 0:64], in_=src), writes=cdup.iv((0, 64)))
                P.dma(lambda e, src=src: e.dma_start(out=cdup[:, 64:128], in_=src), writes=cdup.iv((64, 128)))
                p = nps()
                P.op("pe", lambda e, p=p: e.transpose(p[:, 0:128], cdup[:], ident[:]), reads=cdup.iv() + ident.iv(), writes=piv(p, 128))
                P.op("act", lambda e, p=p, sgn=sgn: e.mul(out=cT[:], in_=p[:, 0:128], mul=sgn), reads=piv(p, 128), writes=cT.iv())
                for ql in range(4):
                    q = t * 4 + ql
                    A("dve", lambda e, CT=CT, q=q, ql=ql: e.tensor_tensor(out=CT[:, q, :], in0=cT[:], in1=maskC[:, ql, :], op=ALU.mult), [cT, maskC], [CT])
        for fc in range(4):
            A("dve", lambda e, fc=fc: e.tensor_scalar(out=C["Dd"][:, fc, :], in0=ident[:], scalar1=dcol[:, fc:fc + 1], scalar2=None, op0=ALU.mult), [ident, dcol], [C["Dd"]])
        P.release(m)
        return C

    def even_mixer(j, layer, smp, seqi, T_):
        lm = P.mark()
        C = even_prep(j)
        W = w_even_in[j]
        if smp:
            for g2 in range(2):
                sl = slice(g2 * 64, g2 * 64 + 64)
                P.dma(lambda e, sl=sl, g2=g2: e.dma_start(out=C["hsr"][sl, :], in_=sre[j].rearrange("(q g) p -> g p q", g=2)[g2]), writes=C["hsr"].iv())
                P.dma(lambda e, sl=sl, g2=g2: e.dma_start(out=C["hsi"][sl, :], in_=sim[j].rearrange("(q g) p -> g p q", g=2)[g2]), writes=C["hsi"].iv())
            for r_ in range(2):
                P.dma(lambda e, r_=r_: e.dma_start(out=C["halo"][:, :, r_], in_=cconv[j, r_].rearrange("(c p) -> p c", p=128)), writes=C["halo"].iv())
        else:
            P.op("pool", lambda e: e.memset(C["hsr"][:], 0.0), writes=C["hsr"].iv())
            P.op("pool", lambda e: e.memset(C["hsi"][:], 0.0), writes=C["hsi"].iv())
            P.op("pool", lambda e: e.memset(C["halo"][:], 0.0), writes=C["halo"].iv())
        nlev = int(math.log2(min(512, T_)))
        def st_body(c0):
            n = min(512, T_ - c0)
            sm = P.mark()
            ho = ((c0 // 512) % 2) * 512
            pre_norm(0, layer, c0, n, ho)
            yacat = P.alloc([128, 8, n], BF16, "yacat"); ubf = P.alloc([128, 4, n], BF16, "ubf")
            tgc = P.alloc([128, n], F32, "tgc"); prod = P.alloc([128, n + 2], F32, "prod"); cvb = P.alloc([128, n], F32, "cvb")
            wre = P.alloc([128, n], F32, "wre"); wim = P.alloc([128, n], F32, "wim"); gre = P.alloc([128, n], F32, "gre"); gim = P.alloc([128, n], F32, "gim")
            TB = [[P.alloc([128, n], F32, f"TB{a}{b}") for b in range(2)] for a in range(2)]
            rtab = P.alloc([128, n], F32, "rtab")
            hri = P.alloc([128, 4, 2, n], BF16, "hri"); gB = P.alloc([128, 4, n], BF16, "gB")
            sg = P.alloc([128, n], F32, "sg"); ysb = P.alloc([128, 8, n], F32, "ysb"); cc = P.alloc([128, 2, 16], F32, "cc")
            tt = P.alloc([128, 2, 16], F32, "tt")

            def proj_chunk(fch):
                wv, wiv = wblock(wcols(W, fch * 128, 128), 8, 128)
                p = nps()
                mm(p[:, 0:n], piv(p, n), [(wv[:, kc, :], hn[:, kc, ho:ho + n], wiv + hn.iv(kc, (ho, ho + n))) for kc in range(8)])
                return p
            for i in range(4):
                p = proj_chunk(4 + i)
                P.op("act", lambda e, p=p: e.copy(out=tgc[:], in_=p[:, 0:n]), reads=piv(p, n), writes=tgc.iv())
                p = proj_chunk(8 + i)
                P.op("dve", lambda e, i=i: e.tensor_copy(out=prod[:, 0:2], in_=C["halo"][:, i, :]), reads=C["halo"].iv(i), writes=prod.iv((0, 2)))
                P.op("dve", lambda e, p=p: e.tensor_tensor(out=prod[:, 2:n + 2], in0=tgc[:], in1=p[:, 0:n], op=ALU.mult), reads=tgc.iv() + piv(p, n), writes=prod.iv((2, n + 2)))
                P.op("dve", lambda e, i=i: e.tensor_copy(out=C["halo"][:, i, :], in_=prod[:, n:n + 2]), reads=prod.iv((n, n + 2)), writes=C["halo"].iv(i))
                P.op("dve", lambda e, i=i: e.tensor_scalar(out=cvb[:], in0=prod[:, 0:n], scalar1=wcva[:, j, 0, i:i + 1], scalar2=None, op0=ALU.mult), reads=prod.iv((0, n)) + wcva.iv(), writes=cvb.iv())
                P.op("dve", lambda e, i=i: e.scalar_tensor_tensor(out=cvb[:], in0=prod[:, 1:n + 1], scalar=wcva[:, j, 1, i:i + 1], in1=cvb[:], op0=ALU.mult, op1=ALU.add), reads=prod.iv((1, n + 1)) + wcva.iv() + cvb.iv(), writes=cvb.iv())
                P.op("dve", lambda e, i=i: e.scalar_tensor_tensor(out=cvb[:], in0=prod[:, 2:n + 2], scalar=wcva[:, j, 2, i:i + 1], in1=cvb[:], op0=ALU.mult, op1=ALU.add), reads=prod.iv((2, n + 2)) + wcva.iv() + cvb.iv(), writes=cvb.iv())
                p = proj_chunk(i)
                P.op("dve", lambda e, p=p, i=i: e.tensor_tensor(out=yacat[:, i, :], in0=cvb[:], in1=p[:, 0:n], op=ALU.mult), reads=cvb.iv() + piv(p, n), writes=yacat.iv(i))
            for i in range(4):
                p = proj_chunk(12 + i)
                P.op("act", lambda e, p=p, i=i: e.copy(out=ubf[:, i, :], in_=p[:, 0:n]), reads=piv(p, n), writes=ubf.iv(i))
            t1 = tgc; t2 = cvb
            for fc in range(4):
                py = nps()
                yterms = []
                for ql in range(4):
                    q = fc * 4 + ql
                    Tc, Ts = TB[q % 2]
                    P.dma(lambda e, Tc=Tc, q=q: e.dma_start(out=Tc[:], in_=tabs[j, q, 0][:, 0:n]), reads=tab_iv(j, q), writes=Tc.iv())
                    P.dma(lambda e, Ts=Ts, q=q: e.dma_start(out=Ts[:], in_=tabs[j, q, 1][:, 0:n]), reads=tab_iv(j, q), writes=Ts.iv())
                    P.op("act", lambda e, q=q: e.activation(out=rtab[:], in_=C["ones"][:, 0:n], func=AF.Copy, scale=C["rcol"][:, q:q + 1]), reads=C["ones"].iv() + C["rcol"].iv(), writes=rtab.iv())
                    pb = nps()
                    mm(pb[:, 0:n], piv(pb, n), [(C["BTr"][:, q, :], ubf[:, fc, :], C["BTr"].iv(q) + ubf.iv(fc))])
                    pb2 = nps()
                    mm(pb2[:, 0:n], piv(pb2, n), [(C["BTi"][:, q, :], ubf[:, fc, :], C["BTi"].iv(q) + ubf.iv(fc))])

                    def TT(out, a, b, op, rd, wr):
                        P.op("dve", lambda e: e.tensor_tensor(out=out, in0=a, in1=b, op=op), reads=rd, writes=wr)
                    t3 = sg; t4v = prod[:, 0:n]; t4iv = prod.iv((0, n))
                    TT(t1[:], Tc[:], pb[:, 0:n], ALU.mult, Tc.iv() + piv(pb, n), t1.iv())
                    TT(t2[:], Ts[:], pb2[:, 0:n], ALU.mult, Ts.iv() + piv(pb2, n), t2.iv())
                    TT(t3[:], Tc[:], pb2[:, 0:n], ALU.mult, Tc.iv() + piv(pb2, n), t3.iv())
                    TT(t4v, Ts[:], pb[:, 0:n], ALU.mult, Ts.iv() + piv(pb, n), t4iv)
                    TT(wre[:], t1[:], t2[:], ALU.add, t1.iv() + t2.iv(), wre.iv())
                    TT(wim[:], t3[:], t4v, ALU.subtract, t3.iv() + t4iv, wim.iv())
                    P.op("dve", lambda e, q=q: e.tensor_tensor_scan(out=gre[:], data0=rtab[:], data1=wre[:], initial=C["hsr"][:, q:q + 1], op0=ALU.mult, op1=ALU.add), reads=rtab.iv() + wre.iv() + C["hsr"].iv(), writes=gre.iv())
                    P.op("dve", lambda e, q=q: e.tensor_tensor_scan(out=gim[:], data0=rtab[:], data1=wim[:], initial=C["hsi"][:, q:q + 1], op0=ALU.mult, op1=ALU.add), reads=rtab.iv() + wim.iv() + C["hsi"].iv(), writes=gim.iv())
                    TT(t1[:], Tc[:], gre[:], ALU.mult, Tc.iv() + gre.iv(), t1.iv())
                    TT(t3[:], Ts[:], gre[:], ALU.mult, Ts.iv() + gre.iv(), t3.iv())
                    TT(t2[:], Ts[:], gim[:], ALU.mult, Ts.iv() + gim.iv(), t2.iv())
                    TT(t4v, Tc[:], gim[:], ALU.mult, Tc.iv() + gim.iv(), t4iv)
                    TT(wre[:], t1[:], t2[:], ALU.subtract, t1.iv() + t2.iv(), wre.iv())
                    TT(wim[:], t3[:], t4v, ALU.add, t3.iv() + t4iv, wim.iv())
                    P.op("act", lambda e, ql=ql: e.copy(out=hri[:, ql, 0, :], in_=wre[:]), reads=wre.iv(), writes=hri.iv(ql, 0))
                    P.op("act", lambda e, ql=ql: e.copy(out=hri[:, ql, 1, :], in_=wim[:]), reads=wim.iv(), writes=hri.iv(ql, 1))
                    P.op("dve", lambda e, q=q: e.tensor_copy(out=C["hsr"][:, q:q + 1], in_=wre[:, n - 1:n]), reads=wre.iv((n - 1, n)), writes=C["hsr"].iv())
                    P.op("dve", lambda e, q=q: e.tensor_copy(out=C["hsi"][:, q:q + 1], in_=wim[:, n - 1:n]), reads=wim.iv((n - 1, n)), writes=C["hsi"].iv())
                    yterms.append((C["CTr"][:, q, :], hri[:, ql, 0, :], C["CTr"].iv(q) + hri.iv(ql, 0)))
                    yterms.append((C["CTi"][:, q, :], hri[:, ql, 1, :], C["CTi"].iv(q) + hri.iv(ql, 1)))
                yterms.append((C["Dd"][:, fc, :], ubf[:, fc, :], C["Dd"].iv(fc) + ubf.iv(fc)))
                mm(py[:, 0:n], piv(py, n), yterms)
                P.op("act", lambda e, py=py, fc=fc: e.activation(out=gB[:, fc, :], in_=py[:, 0:n], func=AF.Gelu_apprx_tanh), reads=piv(py, n), writes=gB.iv(fc))
            for m_ in range(4):
                wv, wiv = wblock(wcols(w_glu[j], m_ * 128, 128, KC=4), 4, 128)
                p = nps()
                mm(p[:, 0:n], piv(p, n), [(wv[:, kc, :], gB[:, kc, :], wiv + gB.iv(kc)) for kc in range(4)])
                P.op("act", lambda e, p=p: e.activation(out=sg[:], in_=p[:, 0:n], func=AF.Sigmoid), reads=piv(p, n), writes=sg.iv())
                P.op("dve", lambda e, m_=m_: e.tensor_tensor(out=yacat[:, 4 + m_, :], in0=gB[:, m_, :], in1=sg[:], op=ALU.mult), reads=gB.iv(m_) + sg.iv(), writes=yacat.iv(4 + m_))
            if DBG and c0 == 0 and layer == 0 and not smp:
                P.dma(lambda e: e.dma_start(out=dbg_h0, in_=h[:, :, 0:512]), reads=h.iv())
                for kc in range(8):
                    P.op("dve", lambda e, kc=kc: e.tensor_copy(out=ysb[:, kc, :], in_=yacat[:, kc, :]), reads=yacat.iv(kc), writes=ysb.iv(kc))
                P.dma(lambda e: e.dma_start(out=dbg_y, in_=ysb[:]), reads=ysb.iv())
            out_proj(w_even_out[j], [((lambda t0, nn, kc=kc: yacat[:, kc, t0:t0 + nn]), (lambda t0, nn, kc=kc: yacat.iv(kc, (t0, t0 + nn))), 128) for kc in range(8)], ysb, n)
            post_norm_add(1, layer, ysb, c0, n)
            if DBG and c0 == 0 and layer == 0 and not smp:
                P.dma(lambda e: e.dma_start(out=dbg_h, in_=h[:, :, 0:512]), reads=h.iv())
            P.release(sm)
        for c0_ in range(0, T_, 512):
            st_body(c0_)
        oc = o_conv_s[j] if smp else o_conv_p[j, seqi]
        for r_ in range(2):
            P.dma(lambda e, r_=r_: e.dma_start(out=oc[r_].rearrange("(c p) -> p c", p=128), in_=C["halo"][:, :, r_]), reads=C["halo"].iv())
        for (hs, od) in ((C["hsr"], o_sre_s[j] if smp else o_sre_p[j, seqi]), (C["hsi"], o_sim_s[j] if smp else o_sim_p[j, seqi])):
            for g2 in range(2):
                sl = slice(g2 * 64, g2 * 64 + 64)
                P.dma(lambda e, hs=hs, od=od, sl=sl, g2=g2: e.dma_start(out=od.rearrange("(q g) p -> g p q", g=2)[g2], in_=hs[sl, :]), reads=hs.iv())
        P.release(lm)

    def ffn(layer, smp, seqi, T_):
        lm = P.mark()
        halo = P.alloc([128, 44, 2], F32, "haloF")
        if smp:
            for r_ in range(2):
                P.dma(lambda e, r_=r_: e.dma_start(out=halo[:, :, r_], in_=cffn[layer, r_].rearrange("(c p) -> p c", p=128)), writes=halo.iv())
        else:
            P.op("pool", lambda e: e.memset(halo[:], 0.0), writes=halo.iv())
        Wup = w_ffn_up[layer]
        def st_body(c0):
            n = min(1024, T_ - c0)
            cts = [(t0, min(512, n - t0)) for t0 in range(0, n, 512)]
            sm = P.mark()
            pre_norm(2, layer, c0, n)
            act = P.alloc([128, 22, n], BF16, "act")
            ysb = P.alloc([128, 8, n], F32, "ysbF")
            m2 = P.mark()
            U = [[P.alloc([128, 514], F32, f"U{a}{b}") for b in range(2)] for a in range(2)]
            cvt2 = [[P.alloc([128, 512], F32, f"cvt{a}{b}") for a in range(2)] for b in range(2)]
            gg2 = [P.alloc([128, 512], F32, f"gg{b}") for b in range(2)]
            xw = [P.alloc([128, 1024], BF16, f"xw{b}") for b in range(2)]
            st_["wbufs"] = list(base_wbufs) + [(xw[b].a, (lambda n, b=b: xw[b].iv((0, n)))) for b in range(2)]
            it = [0]
            for c in range(22):
                wg, wgiv = wblock(wcols(Wup, c * 128, 128), 8, 128)
                wvv, wviv = wblock(wcols(Wup, DFF + c * 128, 128), 8, 128)
                for (t0, nn) in cts:
                    cvt = cvt2[it[0] % 2]; gg = gg2[it[0] % 2]
                    it[0] += 1
                    for a, (wv, wiv, ch) in enumerate(((wg, wgiv, c), (wvv, wviv, 22 + c))):
                        p = nps()
                        mm(p[:, 0:nn], piv(p, nn), [(wv[:, kc, :], hn[:, kc, t0:t0 + nn], wiv + hn.iv(kc, (t0, t0 + nn))) for kc in range(8)])
                        Ub = U[a][(t0 // 512) % 2]
                        P.op("act", lambda e, Ub=Ub, ch=ch: e.copy(out=Ub[:, 0:2], in_=halo[:, ch, :]), reads=halo.iv(ch), writes=Ub.iv((0, 2)))
                        P.op("act", lambda e, Ub=Ub, p=p, nn=nn: e.copy(out=Ub[:, 2:nn + 2], in_=p[:, 0:nn]), reads=piv(p, nn), writes=Ub.iv((2, nn + 2)))
                        P.op("act", lambda e, Ub=Ub, ch=ch, nn=nn: e.copy(out=halo[:, ch, :], in_=Ub[:, nn:nn + 2]), reads=Ub.iv((nn, nn + 2)), writes=halo.iv(ch))
                        cb = cvt[a]
                        eng = "dve"
                        P.op("act", lambda e, Ub=Ub, cb=cb, ch=ch, nn=nn: e.activation(out=cb[:, 0:nn], in_=Ub[:, 0:nn], func=AF.Copy, scale=wcvf[:, layer, 0, ch:ch + 1]), reads=Ub.iv((0, nn)) + wcvf.iv(), writes=cb.iv((0, nn)))
                        P.op(eng, lambda e, Ub=Ub, cb=cb, ch=ch, nn=nn: e.scalar_tensor_tensor(out=cb[:, 0:nn], in0=Ub[:, 1:nn + 1], scalar=wcvf[:, layer, 1, ch:ch + 1], in1=cb[:, 0:nn], op0=ALU.mult, op1=ALU.add), reads=Ub.iv((1, nn + 1)) + wcvf.iv() + cb.iv((0, nn)), writes=cb.iv((0, nn)))
                        P.op(eng, lambda e, Ub=Ub, cb=cb, ch=ch, nn=nn: e.scalar_tensor_tensor(out=cb[:, 0:nn], in0=Ub[:, 2:nn + 2], scalar=wcvf[:, layer, 2, ch:ch + 1], in1=cb[:, 0:nn], op0=ALU.mult, op1=ALU.add), reads=Ub.iv((2, nn + 2)) + wcvf.iv() + cb.iv((0, nn)), writes=cb.iv((0, nn)))
                    P.op("act", lambda e, nn=nn, gg=gg, cvt=cvt: e.activation(out=gg[:, 0:nn], in_=cvt[0][:, 0:nn], func=AF.Gelu_apprx_tanh), reads=cvt[0].iv((0, nn)), writes=gg.iv((0, nn)))
                    P.op("dve", lambda e, c=c, t0=t0, nn=nn, gg=gg, cvt=cvt: e.tensor_tensor(out=act[:, c, t0:t0 + nn], in0=gg[:, 0:nn], in1=cvt[1][:, 0:nn], op=ALU.mult), reads=gg.iv((0, nn)) + cvt[1].iv((0, nn)), writes=act.iv(c, (t0, t0 + nn)))
            st_["wbufs"] = list(base_wbufs)
            P.release(m2)
            xw2 = [P.alloc([128, 1024], BF16, f"xw2{b}") for b in range(4)]
            st_["wbufs"] = list(base_wbufs) + [(xw2[b].a, (lambda n, b=b: xw2[b].iv((0, n)))) for b in range(4)]
            out_proj(w_ffn_down[layer], [((lambda t0, nn, kc=kc: act[:, kc, t0:t0 + nn]), (lambda t0, nn, kc=kc: act.iv(kc, (t0, t0 + nn))), 128) for kc in range(22)], ysb, n)
            post_norm_add(3, layer, ysb, c0, n)
            st_["wbufs"] = list(base_wbufs)
            P.release(sm)
            sm = P.mark()
            peT = P.alloc([128, 2, n], BF16, "peT")
            xin = [P.alloc([128, 256], F32, f"xin{a}") for a in range(2)]
            sig = P.alloc([128, 512], F32, "sig")
            wpl = P.alloc([128, 2, 1024], BF16, "wpl")
            xw3 = [P.alloc([128, 1024], BF16, f"xw3{b}") for b in range(4)]
            st_["wbufs"] = list(base_wbufs) + [(xw3[b].a, (lambda n, b=b: xw3[b].iv((0, n)))) for b in range(4)]
            pesrc = psm[layer] if smp else pp[layer, seqi]
            for ti, t0 in enumerate(range(0, n, 128)):
                nt = min(128, n - t0)
                xb = xin[ti % 2]
                P.dma(lambda e, xb=xb, t0=t0, nt=nt: e.dma_start(out=xb[0:nt, :], in_=pesrc[c0 + t0:c0 + t0 + nt, :]), writes=xb.iv())
                p = nps()
                for dch in range(2):
                    P.op("pe", lambda e, p=p, xb=xb, dch=dch, nt=nt: e.transpose(p[:, dch * 128:dch * 128 + nt], xb[0:nt, dch * 128:(dch + 1) * 128], ident[0:nt, 0:nt]), reads=xb.iv() + ident.iv(), writes=piv(p, nt, dch * 128))
                    P.op("act", lambda e, p=p, dch=dch, t0=t0, nt=nt: e.copy(out=peT[:, dch, t0:t0 + nt], in_=p[:, dch * 128:dch * 128 + nt]), reads=piv(p, nt, dch * 128), writes=peT.iv(dch, (t0, t0 + nt)))
            for c in range(8):
                for (t0, nn) in cts:
                    P.op("act", lambda e, c=c, t0=t0, nn=nn: e.copy(out=hn[:, c, t0:t0 + nn], in_=h[:, c, c0 + t0:c0 + t0 + nn]), reads=h.iv(c, (c0 + t0, c0 + t0 + nn)), writes=hn.iv(c, (t0, t0 + nn)))
            for hh in range(2):
                wblock(w_ple[layer].rearrange("(kc p) m -> p kc m", p=128)[:, :, hh * 512:(hh + 1) * 512], 2, 512, dst=wpl[:, :, hh * 512:(hh + 1) * 512], dst_iv=wpl.iv(None, (hh * 512, hh * 512 + 512)))
            for m_ in range(8):
                wv, wiv = wblock(wcols(w_ple_gate[layer], m_ * 128, 128), 8, 128)
                for (t0, nn) in cts:
                    p = nps()
                    mm(p[:, 0:nn], piv(p, nn), [(wv[:, kc, :], hn[:, kc, t0:t0 + nn], wiv + hn.iv(kc, (t0, t0 + nn))) for kc in range(8)])
                    p2 = nps()
                    mm(p2[:, 0:nn], piv(p2, nn), [(wpl[:, dch, m_ * 128:(m_ + 1) * 128], peT[:, dch, t0:t0 + nn], wpl.iv(dch, (m_ * 128, m_ * 128 + 128)) + peT.iv(dch, (t0, t0 + nn))) for dch in range(2)])
                    P.op("act", lambda e, p=p, nn=nn: e.activation(out=sig[:, 0:nn], in_=p[:, 0:nn], func=AF.Sigmoid), reads=piv(p, nn), writes=sig.iv((0, nn)))
                    P.op("dve", lambda e, p2=p2, nn=nn: e.tensor_tensor(out=sig[:, 0:nn], in0=sig[:, 0:nn], in1=p2[:, 0:nn], op=ALU.mult), reads=sig.iv((0, nn)) + piv(p2, nn), writes=sig.iv((0, nn)))
                    P.op("dve", lambda e, m_=m_, t0=t0, nn=nn: e.tensor_tensor(out=h[:, m_, c0 + t0:c0 + t0 + nn], in0=h[:, m_, c0 + t0:c0 + t0 + nn], in1=sig[:, 0:nn], op=ALU.add), reads=h.iv(m_, (c0 + t0, c0 + t0 + nn)) + sig.iv((0, nn)), writes=h.iv(m_, (c0 + t0, c0 + t0 + nn)))
            st_["wbufs"] = list(base_wbufs)
            P.release(sm)
        for c0_ in range(0, T_, 1024):
            st_body(c0_)
        of = o_ffn_s[layer] if smp else o_ffn_p[layer, seqi]
        for r_ in range(2):
            P.dma(lambda e, r_=r_: e.dma_start(out=of[r_].rearrange("(c p) -> p c", p=128), in_=halo[:, :, r_]), reads=halo.iv())
        P.release(lm)

    def load_x(smp, seqi, T_):
        m = P.mark()
        xin = [P.alloc([128, D], F32, f"xl{a}") for a in range(2)]
        src = xs if smp else xp[seqi]
        for ti, t0 in enumerate(range(0, T_, 128)):
            nt = min(128, T_ - t0)
            xb = xin[ti % 2]
            P.dma(lambda e, xb=xb, t0=t0, nt=nt: e.dma_start(out=xb[0:nt, :], in_=src[t0:t0 + nt, :]), writes=xb.iv())
            for c in range(8):
                p = nps()
                P.op("pe", lambda e, p=p, xb=xb, c=c, nt=nt: e.transpose(p[:, 0:nt], xb[0:nt, c * 128:(c + 1) * 128], ident[0:nt, 0:nt]), reads=xb.iv((c * 128, c * 128 + 128)) + ident.iv(), writes=piv(p, nt))
                P.op("act" if c % 2 else "dve", (lambda e, p=p, c=c, t0=t0, nt=nt: e.copy(out=h[:, c, t0:t0 + nt], in_=p[:, 0:nt])) if c % 2 else (lambda e, p=p, c=c, t0=t0, nt=nt: e.tensor_copy(out=h[:, c, t0:t0 + nt], in_=p[:, 0:nt])), reads=piv(p, nt), writes=h.iv(c, (t0, t0 + nt)))
        P.release(m)

    def store_y(smp, seqi, T_):
        m = P.mark()
        yo = [P.alloc([128, D], F32, f"yo{a}") for a in range(2)]
        dst = ys if smp else yp[seqi]
        for ti, t0 in enumerate(range(0, T_, 128)):
            nt = min(128, T_ - t0)
            yb = yo[ti % 2]
            for c in range(8):
                p = nps()
                P.op("pe", lambda e, p=p, c=c, t0=t0, nt=nt: e.transpose(p[0:nt, 0:128], h[:, c, t0:t0 + nt], ident[:]), reads=h.iv(c, (t0, t0 + nt)) + ident.iv(), writes=piv(p, 128))
                P.op("act" if c % 2 else "dve", (lambda e, p=p, c=c, yb=yb, nt=nt: e.copy(out=yb[0:nt, c * 128:(c + 1) * 128], in_=p[0:nt, 0:128])) if c % 2 else (lambda e, p=p, c=c, yb=yb, nt=nt: e.tensor_copy(out=yb[0:nt, c * 128:(c + 1) * 128], in_=p[0:nt, 0:128])), reads=piv(p, 128), writes=yb.iv((c * 128, c * 128 + 128)))
            P.dma(lambda e, yb=yb, t0=t0, nt=nt: e.dma_start(out=dst[t0:t0 + nt, :], in_=yb[0:nt, :]), reads=yb.iv())
        P.release(m)

    odd_mixer = make_odd(locals())

    for (kind, seqi) in passes:
        smp = kind == "s"
        T_ = SL if smp else SEQ
        if smp:
            pm_ = P.mark()
            xws = [P.alloc([128, 1024], BF16, f"xws{b}") for b in range(16)]
            base_wbufs.extend([(xws[b].a, (lambda n, b=b: xws[b].iv((0, n)))) for b in range(16)])
            st_["wbufs"] = list(base_wbufs)
        load_x(smp, seqi, T_)
        for layer in range(nlayers):
            if layer % 2 == 0:
                even_mixer(layer // 2, layer, smp, seqi, T_)
            else:
                st_["nrot"] = 6
                odd_mixer(layer // 2, layer, smp, seqi, T_)
                st_["nrot"] = 8
            ffn(layer, smp, seqi, T_)
        store_y(smp, seqi, T_)
    with nc.allow_non_contiguous_dma(reason="small param/state layouts"):
        P.run()
    return P


WNAMES = ["g_mix_pre", "g_mix_post", "g_ffn_pre", "g_ffn_post", "w_even_in", "w_conv_a", "s5_a_re", "s5_a_im", "s5_log_dt",
          "s5_b_re", "s5_b_im", "s5_c_re", "s5_c_im", "s5_d", "w_glu", "w_even_out", "w_odd_in", "b_forget", "w_spatial",
          "b_spatial", "g_gmlp_v", "w_odd_out", "w_ffn_up", "w_ffn_conv", "w_ffn_down", "w_ple", "w_ple_gate"]


def make_in_maps(inp):
    f = lambda a: np.ascontiguousarray(np.asarray(a, dtype=np.float32))
    W = {k: f(inp[k]) for k in WNAMES}
    maps = []
    for c in range(8):
        m = dict(W)
        m["xp"] = f(inp["x_prompt"][2 * c:2 * c + 2]); m["xs"] = f(inp["x_sample"][c])
        m["pp"] = f(inp["p_prompt"][:, 2 * c:2 * c + 2]); m["psm"] = f(inp["p_sample"][:, c])
        m["cconv"] = f(inp["cache_conv_a"][:, c]); m["sre"] = f(inp["state_ssm_re"][:, c]); m["sim"] = f(inp["state_ssm_im"][:, c])
        m["ck"] = f(np.asarray(inp["cache_k"])[:, c].reshape(2, 1024, 512)); m["cv"] = f(np.asarray(inp["cache_v"])[:, c].reshape(2, 1024, 512))
        m["clf"] = f(inp["cache_logf"][:, c]); m["cffn"] = f(inp["cache_ffn_conv"][:, c])
        maps.append(m)
    return maps


def gather(res):
    R = res
    cat = lambda name, ax: np.concatenate([r[name] for r in R], axis=ax)
    stk = lambda name, ax: np.stack([r[name] for r in R], axis=ax)
    y_prompt = cat("yp", 0)
    y_sample = stk("ys", 0)
    conv_p = cat("o_conv_p", 1); sre_p = cat("o_sre_p", 1); sim_p = cat("o_sim_p", 1)
    k_p = cat("o_k_p", 1).reshape(2, 16, 2048, 8, 64); v_p = cat("o_v_p", 1).reshape(2, 16, 2048, 8, 64)
    lf_p = cat("o_lf_p", 1); ffn_p = cat("o_ffn_p", 1)
    conv_s = stk("o_conv_s", 1); sre_s = stk("o_sre_s", 1); sim_s = stk("o_sim_s", 1)
    k_s = stk("o_k_s", 1).reshape(2, 8, 32, 8, 64); v_s = stk("o_v_s", 1).reshape(2, 8, 32, 8, 64)
    lf_s = stk("o_lf_s", 1); gv_s = stk("o_gv_s", 1); ffn_s = stk("o_ffn_s", 1)
    outs = (y_prompt, y_sample, conv_p, sre_p, sim_p, k_p, v_p, lf_p, ffn_p, conv_s, sre_s, sim_s, k_s, v_s, lf_s, gv_s, ffn_s)
    return tuple(np.ascontiguousarray(o.astype(np.float32)) for o in outs)


def kernel(**inputs):
    nc = bass.Bass("TRN2", target_bir_lowering=False)
    build(nc)
    maps = make_in_maps(inputs)
    res = run_bass_kernel_spmd(nc, maps, core_ids=list(range(8)))
    return gather(res.results)
```

```python
import numpy as np
import concourse.bass as bass
import concourse.mybir as mybir

F32 = mybir.dt.float32
BF16 = mybir.dt.bfloat16
I32 = mybir.dt.int32
ALU = mybir.AluOpType
AF = mybir.ActivationFunctionType
AX = mybir.AxisListType

ENGS = ("pe", "act", "dve", "pool", "sp")
EPOCH = 30000


class T:
    _n = 0

    def __init__(self, handle, shape, name):
        self.h = handle
        self.shape = list(shape)
        self.name = name
        self.id = T._n
        T._n += 1
        st = [1] * len(shape)
        for i in range(len(shape) - 2, 0, -1):
            st[i] = st[i + 1] * shape[i + 1]
        self.st = st
        self.recs = []

    def __getitem__(self, idx):
        return self.h[idx]

    def iv(self, *idx):
        nd = len(self.shape) - 1
        idx = list(idx) + [None] * (nd - len(idx))
        rng = []
        for d, ix in enumerate(idx):
            n = self.shape[d + 1]
            if ix is None:
                rng.append((0, n))
            elif isinstance(ix, tuple):
                rng.append(ix)
            else:
                rng.append((ix, ix + 1))
        out = [(0, 0)]
        out = []

        def rec(d, base):
            if d == nd - 1:
                out.append((self, base + rng[d][0] * self.st[d + 1], base + (rng[d][1] - 1) * self.st[d + 1] + 1))
                return
            full = all(rng[k] == (0, self.shape[k + 1]) for k in range(d + 1, nd))
            if full:
                out.append((self, base + rng[d][0] * self.st[d + 1], base + rng[d][1] * self.st[d + 1]))
                return
            for i in range(rng[d][0], rng[d][1]):
                rec(d + 1, base + i * self.st[d + 1])

        rec(0, 0)
        return out


class V:
    def __init__(self, arena, off, shape, dt, name="v"):
        self.t = arena
        self.off = off
        self.shape = list(shape)
        self.dt = dt
        self.u = 1 if dt == BF16 else 2
        n = 1
        for x in shape[1:]:
            n *= x
        self.n = n
        a = arena.h[0:shape[0], off:off + n * self.u]
        if dt != BF16:
            a = a.bitcast(dt)
        if len(shape) == 3:
            a = a.rearrange("p (a b) -> p a b", a=shape[1])
        elif len(shape) == 4:
            a = a.rearrange("p (a b c) -> p a b c", a=shape[1], b=shape[2])
        self.a = a
        st = [1] * len(shape)
        for i in range(len(shape) - 2, 0, -1):
            st[i] = st[i + 1] * shape[i + 1]
        self.st = st

    def __getitem__(self, idx):
        return self.a[idx]

    def iv(self, *idx):
        nd = len(self.shape) - 1
        idx = list(idx) + [None] * (nd - len(idx))
        rng = []
        for d, ix in enumerate(idx):
            n = self.shape[d + 1]
            if ix is None:
                rng.append((0, n))
            elif isinstance(ix, tuple):
                rng.append(ix)
            else:
                rng.append((ix, ix + 1))
        out = []
        u = self.u
        off = self.off

        def rec(d, base):
            if d == nd - 1:
                out.append((self.t, off + u * (base + rng[d][0]), off + u * (base + rng[d][1])))
                return
            full = all(rng[k] == (0, self.shape[k + 1]) for k in range(d + 1, nd))
            if full:
                out.append((self.t, off + u * (base + rng[d][0] * self.st[d + 1]), off + u * (base + rng[d][1] * self.st[d + 1])))
                return
            for i in range(rng[d][0], rng[d][1]):
                rec(d + 1, base + i * self.st[d + 1])

        rec(0, 0)
        return out


class Op:
    __slots__ = ("eng", "emit", "deps", "signal", "tok", "waits", "dma", "slot")

    def __init__(self, eng, emit):
        self.eng = eng
        self.emit = emit
        self.deps = set()
        self.signal = False
        self.tok = None
        self.waits = None
        self.dma = False
        self.slot = None


class Prog:
    def __init__(self, nc, n_dma_slots=48):
        self.nc = nc
        self.ops = []
        self.n_dma_slots = n_dma_slots
        self.dma_rr = 0
        self.dma_rr_sw = 0
        self.ctx = []
        self.bar = None

    def sb(self, name, shape, dt):
        g = self.nc.sbuf_tensor(name, list(shape), dt)
        h = g.__enter__()
        self.ctx.append(g)
        return T(h, shape, name)

    def ps(self, name, shape, dt=F32):
        g = self.nc.psum_tensor(name, list(shape), dt)
        h = g.__enter__()
        self.ctx.append(g)
        t = T(h, shape, name)
        t.psum = True
        return t

    def make_arena(self, nbytes):
        nbytes = nbytes // 256 * 256
        self.arena = self.sb("arena", [128, nbytes // 2], BF16)
        self.atop = 0
        self.bar = None

    def mark(self):
        return self.atop

    def release(self, m):
        self.atop = m

    def alloc(self, shape, dt, name="v"):
        es = 2 if dt == BF16 else 4
        n = 1
        for x in shape[1:]:
            n *= x
        nb = (n * es + 63) // 64 * 64
        off = self.atop
        self.atop += nb
        assert self.atop <= self.arena.shape[1] * 2, f"arena overflow {name} {self.atop}"
        return V(self.arena, off // 2, shape, dt, name)

    def barrier(self):
        last = {}
        for i in range(len(self.ops) - 1, -1, -1):
            o = self.ops[i]
            if o.dma:
                k = ("d", o.slot)
            else:
                k = o.eng
            if k not in last:
                last[k] = i
            if len(last) >= len(ENGS) + self.n_dma_slots:
                break
        self.bar = (set(last.values()), set())

    def _track(self, op_idx, reads, writes):
        op = self.ops[op_idx]
        isdma = op.dma
        pw = [(t, 0, t.shape[1]) for (t, lo, hi) in list(reads) + list(writes) if getattr(t, "psum", False)]
        if pw:
            reads = [x for x in reads if not getattr(x[0], "psum", False)]
            seen = set()
            writes = [x for x in writes if not getattr(x[0], "psum", False)]
            for x in pw:
                if id(x[0]) not in seen:
                    seen.add(id(x[0]))
                    writes.append(x)
        for (t, lo, hi) in reads:
            recs = t.recs
            keep = []
            for r in recs:
                (l2, h2, j, w) = r
                if w:
                    if l2 < hi and lo < h2:
                        op.deps.add(j)
                elif (not isdma) and lo <= l2 and h2 <= hi and self.ops[j].eng == op.eng and not self.ops[j].dma:
                    continue
                keep.append(r)
            keep.append((lo, hi, op_idx, False))
            t.recs = keep
        for (t, lo, hi) in writes:
            recs = t.recs
            keep = []
            for r in recs:
                (l2, h2, j, w) = r
                if l2 < hi and lo < h2:
                    if j != op_idx:
                        op.deps.add(j)
                    if lo <= l2 and h2 <= hi:
                        continue
                keep.append(r)
            keep.append((lo, hi, op_idx, True))
            t.recs = keep

    def op(self, eng, emit, reads=(), writes=(), dma=False):
        o = Op(eng, emit)
        o.dma = dma
        if self.bar is not None and eng not in self.bar[1]:
            o.deps |= self.bar[0]
            self.bar[1].add(eng)
        self.ops.append(o)
        self._track(len(self.ops) - 1, reads, writes)
        return o

    def dma(self, emit, reads=(), writes=(), queue="sp"):
        o = self.op(queue, emit, reads, writes, dma=True)
        half = self.n_dma_slots // 2
        if queue == "pool":
            o.slot = half + (self.dma_rr_sw % half)
            self.dma_rr_sw += 1
        else:
            o.slot = self.dma_rr % half
            self.dma_rr += 1
        return o

    def run(self):
        nc = self.nc
        ops = self.ops
        for o in ops:
            for j in o.deps:
                ops[j].signal = True
        cnt = {e: 0 for e in ENGS}
        slot_cnt = [0] * self.n_dma_slots
        slot_last = [None] * self.n_dma_slots
        for i, o in enumerate(ops):
            if o.dma:
                o.signal = True
                prev = slot_last[o.slot]
                if prev is not None:
                    o.deps.add(prev)
                slot_last[o.slot] = i
                slot_cnt[o.slot] += 1
                o.tok = ("d", o.slot, 16 * slot_cnt[o.slot])
            elif o.signal:
                cnt[o.eng] += 1
                n = cnt[o.eng]
                o.tok = ("e", o.eng, (n - 1) // EPOCH, (n - 1) % EPOCH + 1)
        n_ep = {e: (cnt[e] + EPOCH - 1) // EPOCH for e in ENGS}
        sems = {}
        guards = []
        for e in ENGS:
            for k in range(max(1, n_ep[e])):
                g = nc.semaphore(f"s_{e}_{k}")
                sems[("e", e, k)] = g.__enter__()
                guards.append(g)
        for s in range(self.n_dma_slots):
            g = nc.semaphore(f"s_dma_{s}")
            sems[("d", s)] = g.__enter__()
            guards.append(g)
        per_eng = {e: [] for e in ENGS}
        for i, o in enumerate(ops):
            per_eng[o.eng].append(i)
        self.n_waits = 0

        def tok_key(tok):
            if tok[0] == "d":
                return ("d", tok[1]), tok[2]
            return ("e", tok[1], tok[2]), tok[3]

        def emit_engine(engname, eng):
            waited = {}
            for i in per_eng[engname]:
                o = ops[i]
                need = {}
                for j in o.deps:
                    k, v = tok_key(ops[j].tok)
                    if need.get(k, 0) < v:
                        need[k] = v
                for k, v in need.items():
                    if waited.get(k, 0) >= v:
                        continue
                    waited[k] = v
                    eng.wait_ge(sems[k], v)
                    self.n_waits += 1
                inst = o.emit(eng)
                if o.signal:
                    k, v = tok_key(o.tok)
                    inst.then_inc(sems[k], 16 if o.dma else 1)
            if engname == "sp":
                for s in range(self.n_dma_slots):
                    if slot_cnt[s]:
                        eng.wait_ge(sems[("d", s)], 16 * slot_cnt[s])

        with nc.Block() as block:
            @block.tensor
            def _(e):
                emit_engine("pe", e)

            @block.scalar
            def _(e):
                emit_engine("act", e)

            @block.vector
            def _(e):
                emit_engine("dve", e)

            @block.gpsimd
            def _(e):
                emit_engine("pool", e)

            @block.sync
            def _(e):
                emit_engine("sp", e)
        for g in reversed(guards):
            g.__exit__(None, None, None)
        for g in reversed(self.ctx):
            g.__exit__(None, None, None)
        self.stats = dict(n_ops=len(ops), cnt=cnt, n_waits=self.n_waits)


import os
class _Stop(Exception):
    pass

def make_odd(L):
    STOP = float(os.environ.get('ODD_STOP', '99'))
    P = L["P"]; nps = L["nps"]; nps_held = L["nps_held"]; piv = L["piv"]; mm = L["mm"]; wblock = L["wblock"]; wcols = L["wcols"]
    hn = L["hn"]; h = L["h"]; ident = L["ident"]; onesf = L["onesf"]; onesb = L["onesb"]; maskw = L["maskw"]; negc = L["negc"]
    pre_norm = L["pre_norm"]; post_norm_add = L["post_norm_add"]; out_proj = L["out_proj"]
    w_odd_in = L["w_odd_in"]; b_forget = L["b_forget"]; w_spatial = L["w_spatial"]; b_spatial = L["b_spatial"]
    g_gmlp_v = L["g_gmlp_v"]; w_odd_out = L["w_odd_out"]
    ck = L["ck"]; cv = L["cv"]; clf = L["clf"]
    o_k_p = L["o_k_p"]; o_v_p = L["o_v_p"]; o_lf_p = L["o_lf_p"]; o_k_s = L["o_k_s"]; o_v_s = L["o_v_s"]; o_lf_s = L["o_lf_s"]; o_gv_s = L["o_gv_s"]
    EPS = 1e-6
    PAST = 1024

    def odd_mixer(j, layer, smp, seqi, T_):
        lm = P.mark()
        try:
            odd_body(j, layer, smp, seqi, T_)
        except _Stop:
            pass
        P.release(lm)

    def odd_body(j, layer, smp, seqi, T_):
        W = w_odd_in[j]
        NT = (T_ + 127) // 128
        Kaug = P.alloc([96, 8, T_], BF16, "Kaug")
        Vt = P.alloc([128, NT, 8, 65], BF16, "Vt")
        negF = P.alloc([128, NT, 8], F32, "negF")
        Fcar = P.alloc([128, 8], F32, "Fcar")
        WsT = P.alloc([128, 8, 128], BF16, "WsT")
        hselb = P.alloc([8, 4, 128], BF16, "hselb")
        bsph = P.alloc([8, 128], BF16, "bsph")
        bspl = P.alloc([8, 128], BF16, "bspl")
        gvb = P.alloc([128, 512], F32, "gvb")
        bfn = P.alloc([128, 8], F32, "bfn")
        wfl = P.alloc([128, 8, 8], BF16, "wfl")
        ones1 = P.alloc([128, 512], F32, "ones1")
        P.op("pool", lambda e: e.memset(Kaug[64:96, :, :], 0.0), writes=Kaug.iv())
        P.op("pool", lambda e: e.memset(Kaug[64:66, :, :], 1.0), writes=Kaug.iv())
        bsel = P.alloc([128, 64], BF16, "bsel")
        P.op("dve", lambda e: e.tensor_copy(out=bsel[:], in_=ident[:, 64:65].to_broadcast([128, 64])), reads=ident.iv(), writes=bsel.iv())
        P.op("pool", lambda e: e.memset(Vt[:], 1.0), writes=Vt.iv())
        P.op("pool", lambda e: e.memset(Fcar[:], 0.0), writes=Fcar.iv())
        P.op("pool", lambda e: e.memset(ones1[:], 1.0), writes=ones1.iv())
        P.dma(lambda e: e.dma_start(out=gvb[:], in_=g_gmlp_v[j:j + 1, :].broadcast_to([128, 512])), writes=gvb.iv())
        P.dma(lambda e: e.dma_start(out=bfn[64:66, :], in_=b_forget[j:j + 1, :].broadcast_to([2, 8])), writes=bfn.iv())
        P.op("dve", lambda e: e.tensor_scalar(out=bfn[64:66, :], in0=bfn[64:66, :], scalar1=-1.0, scalar2=None, op0=ALU.mult), reads=bfn.iv(), writes=bfn.iv())
        m0 = P.mark()
        hsel = P.alloc([8, 4, 128], F32, "hsel"); bsp = P.alloc([8, 128], F32, "bsp"); bspt = P.alloc([8, 128], F32, "bspt")
        P.dma(lambda e: e.dma_start(out=bsp[:], in_=b_spatial[j]), writes=bsp.iv())
        P.op("pool", lambda e: e.memset(hsel[:], 1.0), writes=hsel.iv())
        P.op("pool", lambda e: e.affine_select(out=hsel[:], in_=hsel[:], pattern=[[128, 4], [1, 128]], compare_op=ALU.is_ge, fill=0.0, base=0, channel_multiplier=-64), reads=hsel.iv(), writes=hsel.iv())
        P.op("pool", lambda e: e.affine_select(out=hsel[:], in_=hsel[:], pattern=[[-128, 4], [-1, 128]], compare_op=ALU.is_ge, fill=0.0, base=63, channel_multiplier=64), reads=hsel.iv(), writes=hsel.iv())
        P.op("dve", lambda e: e.tensor_copy(out=hselb[:], in_=hsel[:]), reads=hsel.iv(), writes=hselb.iv())
        P.op("dve", lambda e: e.tensor_copy(out=bsph[:], in_=bsp[:]), reads=bsp.iv(), writes=bsph.iv())
        P.op("dve", lambda e: e.tensor_copy(out=bspt[:], in_=bsph[:]), reads=bsph.iv(), writes=bspt.iv())
        P.op("dve", lambda e: e.tensor_tensor(out=bspt[:], in0=bsp[:], in1=bspt[:], op=ALU.subtract), reads=bsp.iv() + bspt.iv(), writes=bspt.iv())
        P.op("dve", lambda e: e.tensor_copy(out=bspl[:], in_=bspt[:]), reads=bspt.iv(), writes=bspl.iv())
        wsn = P.alloc([128, 8, 128], F32, "wsn")
        P.dma(lambda e: e.dma_start(out=wsn[:], in_=w_spatial[j].rearrange("h t s -> t h s")), writes=wsn.iv())
        P.op("pool", lambda e: e.affine_select(out=wsn[:], in_=wsn[:], pattern=[[0, 8], [-1, 128]], compare_op=ALU.is_ge, fill=0.0, base=0, channel_multiplier=1), reads=wsn.iv(), writes=wsn.iv())
        for hg in range(2):
            p = nps()
            for hl in range(4):
                hd = hg * 4 + hl
                P.op("pe", lambda e, p=p, hd=hd, hl=hl: e.transpose(p[:, hl * 128:(hl + 1) * 128], wsn[:, hd, :], ident[:]), reads=wsn.iv(hd) + ident.iv(), writes=piv(p, 128, hl * 128))
            P.op("act", lambda e, p=p, hg=hg: e.copy(out=WsT[:, hg * 4:(hg + 1) * 4, :], in_=p[:, :].rearrange("p (a b) -> p a b", a=4)), reads=piv(p, 512), writes=WsT.iv((hg * 4, hg * 4 + 4)))
        P.release(m0)
        if smp:
            Kc = P.alloc([96, 8, PAST], BF16, "Kc")
            Vc = P.alloc([128, 8, 8, 65], BF16, "Vc")
            Dc = P.alloc([128, 8, 8], F32, "Dc")
            m0 = P.mark()
            ctile = [P.alloc([128, 512], F32, f"ctile{a}") for a in range(2)]
            lfc = P.alloc([128, 8, 8], F32, "lfc")
            ustr = P.alloc([128, 128], F32, "ustr")
            P.op("pool", lambda e: e.memset(Kc[64:96, :, :], 0.0), writes=Kc.iv())
            P.op("pool", lambda e: e.memset(Kc[64:66, :, :], 1.0), writes=Kc.iv())
            P.op("pool", lambda e: e.memset(Vc[:], 1.0), writes=Vc.iv())
            P.op("pool", lambda e: e.affine_select(out=ustr[:], in_=onesf[:], pattern=[[-1, 128]], compare_op=ALU.is_gt, fill=0.0, base=0, channel_multiplier=1), reads=onesf.iv(), writes=ustr.iv())
            P.dma(lambda e: e.dma_start(out=lfc[:], in_=clf[j].rearrange("(t p) hd -> p t hd", p=128)), writes=lfc.iv())
            ustrb = P.alloc([128, 128], BF16, "ustrb"); lfh = P.alloc([128, 8, 8], BF16, "lfh"); lfl = P.alloc([128, 8, 8], BF16, "lfl"); lft = P.alloc([128, 8, 8], F32, "lft")
            P.op("dve", lambda e: e.tensor_copy(out=ustrb[:], in_=ustr[:]), reads=ustr.iv(), writes=ustrb.iv())
            P.op("dve", lambda e: e.tensor_copy(out=lfh[:], in_=lfc[:]), reads=lfc.iv(), writes=lfh.iv())
            P.op("dve", lambda e: e.tensor_copy(out=lft[:], in_=lfh[:]), reads=lfh.iv(), writes=lft.iv())
            P.op("dve", lambda e: e.tensor_tensor(out=lft[:], in0=lfc[:], in1=lft[:], op=ALU.subtract), reads=lfc.iv() + lft.iv(), writes=lft.iv())
            P.op("dve", lambda e: e.tensor_copy(out=lfl[:], in_=lft[:]), reads=lft.iv(), writes=lfl.iv())
            for t in range(8):
                cb = ctile[t % 2]
                P.dma(lambda e, cb=cb, t=t: e.dma_start(out=cb[:], in_=ck[j, t * 128:(t + 1) * 128, :]), writes=cb.iv())
                for hg in range(2):
                    p = nps()
                    for hl in range(4):
                        hd = hg * 4 + hl
                        P.op("pe", lambda e, p=p, cb=cb, hd=hd, hl=hl: e.transpose(p[0:64, hl * 128:(hl + 1) * 128], cb[:, hd * 64:(hd + 1) * 64], ident[:]), reads=cb.iv() + ident.iv(), writes=piv(p, 128, hl * 128))
                    P.op("act", lambda e, p=p, hg=hg, t=t: e.copy(out=Kc[0:64, hg * 4:(hg + 1) * 4, t * 128:(t + 1) * 128], in_=p[0:64, :].rearrange("p (a b) -> p a b", a=4)), reads=piv(p, 512), writes=Kc.iv())
                cb2 = ctile[(t + 1) % 2]
                P.dma(lambda e, cb2=cb2, t=t: e.dma_start(out=cb2[:], in_=cv[j, t * 128:(t + 1) * 128, :]), writes=cb2.iv())
                P.op("dve", lambda e, cb2=cb2, t=t: e.tensor_copy(out=Vc[:, t, :, 0:64], in_=cb2[:, :].rearrange("p (a b) -> p a b", a=8)), reads=cb2.iv(), writes=Vc.iv(t))
                p = nps()
                terms = [(ustrb[:], lfh[:, t, :], ustrb.iv() + lfh.iv(t)), (ustrb[:], lfl[:, t, :], ustrb.iv() + lfl.iv(t))]
                for t2 in range(t + 1, 8):
                    terms.append((onesb[:], lfh[:, t2, :], onesb.iv() + lfh.iv(t2)))
                    terms.append((onesb[:], lfl[:, t2, :], onesb.iv() + lfl.iv(t2)))
                mm(p[:, 0:8], piv(p, 8), terms)
                P.op("dve", lambda e, p=p, t=t: e.tensor_copy(out=Dc[:, t, :], in_=p[:, 0:8]), reads=piv(p, 8), writes=Dc.iv(t))
            P.release(m0)
        if STOP <= 1:
            raise _Stop()
        wblock(wcols(W, 1536, 8), 8, 8, dst=wfl[:], dst_iv=wfl.iv())

        def st_body(c0):
            n = min(512, T_ - c0)
            nt_ = (n + 127) // 128
            sm = P.mark()
            ho = ((c0 // 512) % 2) * 512
            pre_norm(0, layer, c0, n, ho)
            att = P.alloc([64, 8, n], BF16, "att"); ydb = P.alloc([128, 4, n], BF16, "ydb")
            sm2 = P.mark()
            Q = [P.alloc([96, n], BF16, f"Q{a}") for a in range(2)]
            wq = [P.alloc([128, 8, 66], BF16, f"wq{a}") for a in range(2)]
            wtok = P.alloc([128, 8, 512], BF16, "wtok")
            lfrow = P.alloc([128, 512], F32, "lfrow"); Frow = P.alloc([128, 512], F32, "Frow"); x8 = P.alloc([128, 512], F32, "x8")
            hib = P.alloc([128, 512], BF16, "hib")
            kvout = [lfrow, Frow]
            pT = [P.alloc([128, 512], BF16, f"pT{a}") for a in range(3)]
            s2 = P.alloc([128, 512], F32, "s2"); oT = P.alloc([128, 512], F32, "oT"); rden = P.alloc([128, 512], F32, "rden")
            rdh = P.alloc([128, 512], BF16, "rdh"); rdl = P.alloc([128, 512], BF16, "rdl")
            vnf = oT; vnpad = P.alloc([128, 8, 128], BF16, "vnpad"); junk = s2
            ssq = P.alloc([128, 2], F32, "ssq"); lfT = P.alloc([128, 4, 8], F32, "lfT")
            P.op("pool", lambda e: e.memset(vnpad[:], 0.0), writes=vnpad.iv())
            for qq in Q:
                P.op("pool", lambda e, qq=qq: e.memset(qq[64:96, :], 0.0), writes=qq.iv())
            for bb_ in (lfrow, Frow, rden):
                P.op("pool", lambda e, bb_=bb_: e.memset(bb_[64:96, :], 0.0), writes=bb_.iv())
            for bb_ in (rdh, rdl):
                P.op("pool", lambda e, bb_=bb_: e.memset(bb_[64:96, :], 0.0), writes=bb_.iv())

            def load_wtok(col0):
                for b in range(4):
                    wblock(wcols(W, col0 + b * 128, 128), 8, 128, dst=wtok[:, :, b * 128:(b + 1) * 128], dst_iv=wtok.iv(None, (b * 128, b * 128 + 128)))

            def tok_proj(ti):
                t0 = ti * 128
                nt = min(128, n - t0)
                p = nps()
                mm(p[0:nt, :], piv(p, 512), [(hn[:, kc, ho + t0:ho + t0 + nt], wtok[:, kc, :], hn.iv(kc, (ho + t0, ho + t0 + nt)) + wtok.iv(kc)) for kc in range(8)])
                return p, t0, nt
            if STOP <= 1.2:
                raise _Stop()
            load_wtok(1024)
            if STOP <= 1.4:
                raise _Stop()
            for ti in range(nt_):
                p, t0, nt = tok_proj(ti)
                if STOP <= 1.6:
                    raise _Stop()
                gt = (c0 + t0) // 128
                ko = kvout[ti % 2]
                P.op("act", lambda e, p=p, gt=gt, nt=nt: e.copy(out=Vt[0:nt, gt, :, 0:64], in_=p[0:nt, :].rearrange("p (a b) -> p a b", a=8)), reads=piv(p, 512), writes=Vt.iv(gt))
                if STOP <= 1.7:
                    raise _Stop()
                P.op("dve", lambda e, p=p, ko=ko, nt=nt: e.tensor_copy(out=ko[0:nt, :], in_=p[0:nt, :]), reads=piv(p, 512), writes=ko.iv())
                if STOP <= 1.8:
                    raise _Stop()
                dst = (o_v_s[j] if smp else o_v_p[j, seqi])[c0 + t0:c0 + t0 + nt, :]
                P.dma(lambda e, ko=ko, dst=dst, nt=nt: e.dma_start(out=dst, in_=ko[0:nt, :]), reads=ko.iv())
            load_wtok(512)
            for ti in range(nt_):
                p, t0, nt = tok_proj(ti)
                ko = kvout[ti % 2]
                P.op("dve", lambda e, p=p, ko=ko, nt=nt: e.tensor_copy(out=ko[0:nt, :], in_=p[0:nt, :]), reads=piv(p, 512), writes=ko.iv())
                dst = (o_k_s[j] if smp else o_k_p[j, seqi])[c0 + t0:c0 + t0 + nt, :]
                P.dma(lambda e, ko=ko, dst=dst, nt=nt: e.dma_start(out=dst, in_=ko[0:nt, :]), reads=ko.iv())
            if STOP <= 2:
                raise _Stop()
            for i in range(4):
                wv, wiv = wblock(wcols(W, 1544 + i * 128, 128), 8, 128)
                p = nps()
                mm(p[:, 0:n], piv(p, n), [(wv[:, kc, :], hn[:, kc, ho:ho + n], wiv + hn.iv(kc, (ho, ho + n))) for kc in range(8)])
                P.op("act", lambda e, p=p, i=i: e.copy(out=ydb[:, i, :], in_=p[:, 0:n]), reads=piv(p, n), writes=ydb.iv(i))
            load_wtok(2056)
            for ti in range(nt_):
                p, t0, nt = tok_proj(ti)
                P.op("act", lambda e, p=p, nt=nt: e.activation(out=junk[0:nt, :], in_=p[0:nt, :], func=AF.Square, accum_out=ssq[0:nt, 0:1]), reads=piv(p, 512), writes=junk.iv() + ssq.iv())
                P.op("act", lambda e, nt=nt: e.activation(out=ssq[0:nt, 1:2], in_=ssq[0:nt, 0:1], func=AF.Sqrt, bias=EPS, scale=1.0 / 512), reads=ssq.iv(), writes=ssq.iv())
                P.op("dve", lambda e, nt=nt: e.reciprocal(out=ssq[0:nt, 1:2], in_=ssq[0:nt, 1:2]), reads=ssq.iv(), writes=ssq.iv())
                P.op("dve", lambda e, p=p, nt=nt: e.scalar_tensor_tensor(out=vnf[0:nt, :], in0=p[0:nt, :], scalar=ssq[0:nt, 1:2], in1=gvb[0:nt, :], op0=ALU.mult, op1=ALU.mult), reads=piv(p, 512) + ssq.iv() + gvb.iv(), writes=vnf.iv())
                if smp:
                    P.dma(lambda e, t0=t0, nt=nt: e.dma_start(out=o_gv_s[j, c0 + t0:c0 + t0 + nt, :], in_=vnf[0:nt, :]), reads=vnf.iv())
                v4 = vnf[0:nt, :].rearrange("p (a b c) -> p a b c", a=4, b=2)
                vp4 = vnpad[0:nt, :, :].rearrange("p (a b) c -> p a b c", b=2)
                P.op("act", lambda e, v4=v4, vp4=vp4: e.copy(out=vp4[:, :, 0, 0:64], in_=v4[:, :, 0, :]), reads=vnf.iv(), writes=vnpad.iv())
                P.op("act", lambda e, v4=v4, vp4=vp4: e.copy(out=vp4[:, :, 1, 64:128], in_=v4[:, :, 1, :]), reads=vnf.iv(), writes=vnpad.iv())
                pm = nps()
                for jj in range(4):
                    terms = []
                    for hd in (2 * jj, 2 * jj + 1):
                        terms.append((vnpad[0:nt, hd, :], WsT[0:nt, hd, 0:nt], vnpad.iv(hd) + WsT.iv(hd)))
                    terms.append((hselb[0:8, jj, :], bsph[0:8, 0:nt], hselb.iv(jj) + bsph.iv()))
                    terms.append((hselb[0:8, jj, :], bspl[0:8, 0:nt], hselb.iv(jj) + bspl.iv()))
                    mm(pm[:, jj * 128:jj * 128 + nt], piv(pm, nt, jj * 128), terms)
                    P.op("dve", lambda e, pm=pm, jj=jj, t0=t0, nt=nt: e.tensor_tensor(out=ydb[:, jj, t0:t0 + nt], in0=ydb[:, jj, t0:t0 + nt], in1=pm[:, jj * 128:jj * 128 + nt], op=ALU.mult), reads=ydb.iv(jj, (t0, t0 + nt)) + piv(pm, nt, jj * 128), writes=ydb.iv(jj, (t0, t0 + nt)))
            if STOP <= 3:
                raise _Stop()
            def chain(hd):
                Qh = Q[hd % 2]; wqh = wq[hd % 2]
                wblock(wcols(W, hd * 64, 64), 8, 64, dst=wqh[:, :, 0:64], dst_iv=wqh.iv(None, (0, 64)))
                P.op("pool", lambda e, wqh=wqh, hd=hd: e.tensor_copy(out=wqh[:, :, 64:65], in_=wfl[:, :, hd:hd + 1]), reads=wfl.iv(), writes=wqh.iv(None, (64, 65)))
                P.op("pool", lambda e, wqh=wqh, hd=hd: e.tensor_copy(out=wqh[:, :, 65:66], in_=wfl[:, :, hd:hd + 1]), reads=wfl.iv(), writes=wqh.iv(None, (65, 66)))
                pq = nps()
                mm(pq[0:66, 0:n], piv(pq, n), [(wqh[:, kc, :], hn[:, kc, ho:ho + n], wqh.iv(kc) + hn.iv(kc, (ho, ho + n))) for kc in range(8)])
                P.op("act", lambda e, pq=pq, Qh=Qh: e.copy(out=Qh[0:64, :], in_=pq[0:64, 0:n]), reads=piv(pq, n), writes=Qh.iv())
                r = slice(64, 66)
                P.op("act", lambda e, pq=pq, hd=hd: e.activation(out=lfrow[r, 0:n], in_=pq[r, 0:n], func=AF.Exp, bias=bfn[r, hd:hd + 1], scale=-1.0), reads=piv(pq, n) + bfn.iv(), writes=lfrow.iv())
                P.op("act", lambda e: e.activation(out=lfrow[r, 0:n], in_=lfrow[r, 0:n], func=AF.Ln, bias=1.0, scale=1.0), reads=lfrow.iv(), writes=lfrow.iv())
                P.op("dve", lambda e: e.tensor_scalar(out=lfrow[r, 0:n], in0=lfrow[r, 0:n], scalar1=-1.0, scalar2=None, op0=ALU.mult), reads=lfrow.iv(), writes=lfrow.iv())
                P.op("dve", lambda e, hd=hd: e.tensor_tensor_scan(out=Frow[r, 0:n], data0=ones1[r, 0:n], data1=lfrow[r, 0:n], initial=Fcar[r, hd:hd + 1], op0=ALU.mult, op1=ALU.add), reads=ones1.iv() + lfrow.iv() + Fcar.iv(), writes=Frow.iv())
                P.op("dve", lambda e, hd=hd: e.tensor_copy(out=Fcar[r, hd:hd + 1], in_=Frow[r, n - 1:n]), reads=Frow.iv(), writes=Fcar.iv())
                P.op("dve", lambda e: e.tensor_scalar(out=x8[r, 0:n], in0=Frow[r, 0:n], scalar1=8.0, scalar2=None, op0=ALU.mult), reads=Frow.iv(), writes=x8.iv())
                P.op("dve", lambda e: e.tensor_copy(out=hib[r, 0:n], in_=x8[r, 0:n]), reads=x8.iv(), writes=hib.iv())
                P.op("dve", lambda e, Qh=Qh: e.scalar_tensor_tensor(out=Qh[r, :], in0=hib[r, 0:n], scalar=negc[r, 0:1], in1=x8[r, 0:n], op0=ALU.mult, op1=ALU.add), reads=hib.iv() + negc.iv() + x8.iv(), writes=Qh.iv())
                wv, wiv = wblock(wcols(W, 512 + hd * 64, 64), 8, 64)
                pk = nps()
                mm(pk[0:64, 0:n], piv(pk, n), [(wv[:, kc, :], hn[:, kc, ho:ho + n], wiv + hn.iv(kc, (ho, ho + n))) for kc in range(8)])
                P.op("act", lambda e, pk=pk, hd=hd: e.copy(out=Kaug[0:64, hd, c0:c0 + n], in_=pk[0:64, 0:n]), reads=piv(pk, n), writes=Kaug.iv(hd, (c0, c0 + n)))

            def chainB(hd):
                pf = nps()
                for ti in range(nt_):
                    t0 = ti * 128; nt = min(128, n - t0)
                    P.op("pe", lambda e, pf=pf, ti=ti, t0=t0, nt=nt: e.transpose(pf[0:nt, 64 * ti:64 * ti + 32], Frow[64:96, t0:t0 + nt], ident[64:96, 64:96]), reads=Frow.iv() + ident.iv(), writes=piv(pf, 32, 64 * ti))
                    P.op("pe", lambda e, pf=pf, ti=ti, t0=t0, nt=nt: e.transpose(pf[0:nt, 64 * ti + 32:64 * ti + 64], lfrow[64:96, t0:t0 + nt], ident[64:96, 64:96]), reads=lfrow.iv() + ident.iv(), writes=piv(pf, 32, 64 * ti + 32))
                    gt = (c0 + t0) // 128
                    P.op("dve", lambda e, pf=pf, ti=ti, gt=gt, nt=nt, hd=hd: e.tensor_scalar(out=negF[0:nt, gt, hd:hd + 1], in0=pf[0:nt, 64 * ti:64 * ti + 1], scalar1=-1.0, scalar2=None, op0=ALU.mult), reads=piv(pf, 32, 64 * ti), writes=negF.iv(gt))
                    P.op("dve", lambda e, pf=pf, ti=ti, nt=nt, hd=hd: e.tensor_copy(out=lfT[0:nt, ti, hd:hd + 1], in_=pf[0:nt, 64 * ti + 32:64 * ti + 33]), reads=piv(pf, 32, 64 * ti + 32), writes=lfT.iv(ti))

            def attend(hd):
                Qh = Q[hd % 2]
                keys = []
                if smp:
                    for t in range(8):
                        keys.append((Kc[:, hd, t * 128:(t + 1) * 128], Kc.iv(hd), Vc[:, t, hd, :], Vc.iv(t), Dc[:, t, hd:hd + 1], Dc.iv(t), 128, None))
                nkt = (c0 + n + 127) // 128
                for kt in range(nkt):
                    nk = min(128, T_ - kt * 128)
                    rel = kt * 128 - c0
                    keys.append((Kaug[:, hd, kt * 128:kt * 128 + nk], Kaug.iv(hd, (kt * 128, kt * 128 + nk)), Vt[0:nk, kt, hd, :], Vt.iv(kt), negF[0:nk, kt, hd:hd + 1], negF.iv(kt), nk, rel if rel >= 0 else None))
                po = nps_held()
                pend = []

                def emit_pv(x):
                    (ki, va, viv, pt_, nk, q0) = x
                    P.op("pe", lambda e, po=po, va=va, pt_=pt_, nk=nk, ki=ki, q0=q0, last=(ki == len(keys) - 1): e.matmul(po[0:65, q0:n], lhsT=va, rhs=pt_[0:nk, q0:n], start=(ki == 0), stop=last), reads=viv + pt_.iv() + (piv(po, n) if ki else []), writes=piv(po, n))
                for ki, (ka, kiv, va, viv, ba, biv, nk, rel) in enumerate(keys):
                    pscore = nps()
                    q0 = rel if (rel is not None and ki > 0) else 0
                    mm(pscore[0:nk, q0:n], piv(pscore, n), [(ka, Qh[:, q0:n], kiv + Qh.iv())])
                    pt_ = pT[ki % 3]
                    if rel is None:
                        P.op("act", lambda e, pscore=pscore, pt_=pt_, ba=ba, nk=nk: e.activation(out=pt_[0:nk, 0:n], in_=pscore[0:nk, 0:n], func=AF.Exp, bias=ba, scale=0.125), reads=piv(pscore, n) + biv, writes=pt_.iv())
                    else:
                        P.op("dve", lambda e, pscore=pscore, nk=nk, rel=rel, q0=q0: e.tensor_tensor(out=s2[0:nk, q0:n], in0=pscore[0:nk, q0:n], in1=maskw[0:nk, 384 - rel + q0:384 - rel + n], op=ALU.add), reads=piv(pscore, n) + maskw.iv(), writes=s2.iv())
                        P.op("act", lambda e, pt_=pt_, ba=ba, nk=nk, q0=q0: e.activation(out=pt_[0:nk, q0:n], in_=s2[0:nk, q0:n], func=AF.Exp, bias=ba, scale=0.125), reads=s2.iv() + biv, writes=pt_.iv())
                    pend.append((ki, va, viv, pt_, nk, q0))
                    if len(pend) > 2:
                        emit_pv(pend.pop(0))
                for x in pend:
                    emit_pv(x)
                P.op("act", lambda e, po=po: e.copy(out=oT[0:64, 0:n], in_=po[0:64, 0:n]), reads=piv(po, n), writes=oT.iv())
                P.op("dve", lambda e, po=po: e.reciprocal(out=rden[64:65, 0:n], in_=po[64:65, 0:n]), reads=piv(po, n), writes=rden.iv())
                P.op("dve", lambda e: e.tensor_copy(out=rdh[64:65, 0:n], in_=rden[64:65, 0:n]), reads=rden.iv(), writes=rdh.iv())
                P.op("dve", lambda e: e.tensor_tensor(out=rden[64:65, 0:n], in0=rden[64:65, 0:n], in1=rdh[64:65, 0:n], op=ALU.subtract), reads=rden.iv() + rdh.iv(), writes=rden.iv())
                P.op("dve", lambda e: e.tensor_copy(out=rdl[64:65, 0:n], in_=rden[64:65, 0:n]), reads=rden.iv(), writes=rdl.iv())
                pb = nps()
                mm(pb[0:64, 0:n], piv(pb, n), [(bsel[64:96, :], rdh[64:96, 0:n], bsel.iv() + rdh.iv()), (bsel[64:96, :], rdl[64:96, 0:n], bsel.iv() + rdl.iv())])
                P.op("dve", lambda e, pb=pb, hd=hd: e.tensor_tensor(out=att[:, hd, :], in0=oT[0:64, 0:n], in1=pb[0:64, 0:n], op=ALU.mult), reads=oT.iv() + piv(pb, n), writes=att.iv(hd))
            chain(0)
            chainB(0)
            for hd in range(8):
                if hd + 1 < 8:
                    chain(hd + 1)
                if STOP > 4:
                    attend(hd)
                if hd + 1 < 8:
                    chainB(hd + 1)
            if STOP <= 5:
                raise _Stop()
            for ti in range(nt_):
                t0 = ti * 128; nt = min(128, n - t0)
                dst = (o_lf_s[j] if smp else o_lf_p[j, seqi])[c0 + t0:c0 + t0 + nt, :]
                P.dma(lambda e, dst=dst, ti=ti, nt=nt: e.dma_start(out=dst, in_=lfT[0:nt, ti, :]), reads=lfT.iv(ti))
            P.release(sm2)
            ysb = P.alloc([128, 8, n], F32, "ysbO")
            rl = [((lambda t0, nn, hd=hd: att[:, hd, t0:t0 + nn]), (lambda t0, nn, hd=hd: att.iv(hd, (t0, t0 + nn))), 64) for hd in range(8)]
            rl += [((lambda t0, nn, kc=kc: ydb[:, kc, t0:t0 + nn]), (lambda t0, nn, kc=kc: ydb.iv(kc, (t0, t0 + nn))), 128) for kc in range(4)]
            out_proj(w_odd_out[j], rl, ysb, n)
            post_norm_add(1, layer, ysb, c0, n)
            P.release(sm)

        for c0_ in range(0, T_, 512):
            st_body(c0_)

    return odd_mixer


import math
import numpy as np
import concourse.bass as bass
import concourse.mybir as mybir
from concourse.bass_utils import run_bass_kernel_spmd

D = 1024
DEPTH = 4
SEQ = 2048
SL = 32
PAST = 1024
DFF = 2816
EPS = 1e-6
NEG = -1.0e30
PI = math.pi


def build(nc, passes=(("p", 0), ("p", 1), ("s", 0)), nlayers=4):
    P = Prog(nc)

    def din(name, shape):
        return nc.dram_tensor(name, list(shape), F32, kind="ExternalInput").ap()

    def dout(name, shape):
        return nc.dram_tensor(name, list(shape), F32, kind="ExternalOutput").ap()

    xp = din("xp", [2, SEQ, D]); xs = din("xs", [SL, D])
    pp = din("pp", [4, 2, SEQ, 256]); psm = din("psm", [4, SL, 256])
    cconv = din("cconv", [2, 2, 512]); sre = din("sre", [2, 32, 64]); sim = din("sim", [2, 32, 64])
    ck = din("ck", [2, PAST, 512]); cv = din("cv", [2, PAST, 512]); clf = din("clf", [2, PAST, 8])
    cffn = din("cffn", [4, 2, 2 * DFF])
    g_mix_pre = din("g_mix_pre", [4, D]); g_mix_post = din("g_mix_post", [4, D])
    g_ffn_pre = din("g_ffn_pre", [4, D]); g_ffn_post = din("g_ffn_post", [4, D])
    w_even_in = din("w_even_in", [2, D, 2048]); w_conv_a = din("w_conv_a", [2, 3, 512])
    s5_a_re = din("s5_a_re", [2, 32, 64]); s5_a_im = din("s5_a_im", [2, 32, 64]); s5_log_dt = din("s5_log_dt", [2, 32])
    s5_b_re = din("s5_b_re", [2, 32, 64, 16]); s5_b_im = din("s5_b_im", [2, 32, 64, 16])
    s5_c_re = din("s5_c_re", [2, 32, 16, 64]); s5_c_im = din("s5_c_im", [2, 32, 16, 64])
    s5_d = din("s5_d", [2, 32, 16]); w_glu = din("w_glu", [2, 512, 512]); w_even_out = din("w_even_out", [2, D, D])
    w_odd_in = din("w_odd_in", [2, D, 2568]); b_forget = din("b_forget", [2, 8])
    w_spatial = din("w_spatial", [2, 8, 128, 128]); b_spatial = din("b_spatial", [2, 8, 128])
    g_gmlp_v = din("g_gmlp_v", [2, 512]); w_odd_out = din("w_odd_out", [2, D, D])
    w_ffn_up = din("w_ffn_up", [4, D, 2 * DFF]); w_ffn_conv = din("w_ffn_conv", [4, 3, 2 * DFF])
    w_ffn_down = din("w_ffn_down", [4, DFF, D]); w_ple = din("w_ple", [4, 256, D]); w_ple_gate = din("w_ple_gate", [4, D, D])

    yp = dout("yp", [2, SEQ, D]); ys = dout("ys", [SL, D])
    o_conv_p = dout("o_conv_p", [2, 2, 2, 512]); o_sre_p = dout("o_sre_p", [2, 2, 32, 64]); o_sim_p = dout("o_sim_p", [2, 2, 32, 64])
    o_k_p = dout("o_k_p", [2, 2, SEQ, 512]); o_v_p = dout("o_v_p", [2, 2, SEQ, 512]); o_lf_p = dout("o_lf_p", [2, 2, SEQ, 8])
    o_ffn_p = dout("o_ffn_p", [4, 2, 2, 2 * DFF])
    o_conv_s = dout("o_conv_s", [2, 2, 512]); o_sre_s = dout("o_sre_s", [2, 32, 64]); o_sim_s = dout("o_sim_s", [2, 32, 64])
    o_k_s = dout("o_k_s", [2, SL, 512]); o_v_s = dout("o_v_s", [2, SL, 512]); o_lf_s = dout("o_lf_s", [2, SL, 8])
    o_gv_s = dout("o_gv_s", [2, SL, 512]); o_ffn_s = dout("o_ffn_s", [4, 2, 2 * DFF])

    import os
    DBG = os.environ.get("KDBG", "0") == "1"
    if DBG:
        dbg_y = dout("dbg_y", [128, 8, 512]); dbg_h = dout("dbg_h", [128, 8, 512]); dbg_h0 = dout("dbg_h0", [128, 8, 512])
    h = P.sb("h", [128, 8, SEQ], F32)
    hn = P.sb("hn", [128, 8, 1024], BF16)
    NW = 4
    wbf = [P.sb(f"wbf{i}", [128, 1024], BF16) for i in range(NW)]
    sqP = P.sb("sqP", [128, 8, 512], BF16)
    rstdP = P.sb("rstdP", [128, 512], F32)
    ident = P.sb("ident", [128, 128], F32)
    onesf = P.sb("onesf", [128, 128], F32)
    onesb = P.sb("onesb", [128, 128], BF16)
    maskw = P.sb("maskw", [128, 896], F32)
    negc = P.sb("negc", [128, 1], F32)
    gains = P.sb("gains", [128, 4, 4, 8], F32)
    wcva = P.sb("wcva", [128, 2, 3, 4], F32)
    wcvf = P.sb("wcvf", [128, 4, 3, 44], F32)
    ps = [P.ps(f"ps{i}", [128, 512]) for i in range(8)]
    P.make_arena(212863 - (SEQ * 8 * 4 + 8 * 1024 * 2 + (4 * 2048 + 8192 + 2048) + 512 * 2 + 256 + 896 * 4 + 64 + 512 + 96 + 2112) - 600)
    st_ = {"ps": 0, "w": 0}
    tabs = nc.dram_tensor("rot_tabs", [2, 16, 2, 128, 512], F32).ap()

    class _DT:
        def __init__(self):
            self.recs = []
    tabsT = _DT()
    tabs_done = set()

    def tab_iv(j, q):
        k = (j * 16 + q) * 2
        return [(tabsT, k, k + 2)]
    base_wbufs = [(wbf[i].h, (lambda n, i=i: [(wbf[i], 0, n)])) for i in range(NW)]
    st_["wbufs"] = list(base_wbufs)

    def nps():
        st_["ps"] = (st_["ps"] + 1) % st_.get("nrot", 8)
        return ps[st_["ps"]]

    def nps_held():
        st_["psh"] = 1 - st_.get("psh", 0)
        return ps[6 + st_["psh"]]

    def piv(p, n, lo=0):
        return [(p, lo, lo + n)]

    P.op("pool", lambda e: e.memset(onesf[:], 1.0), writes=onesf.iv())
    P.op("pool", lambda e: e.memset(onesb[:], 1.0), writes=onesb.iv())
    P.op("pool", lambda e: e.affine_select(out=ident[:], in_=onesf[:], pattern=[[-1, 128]], compare_op=ALU.is_equal, fill=0.0, base=0, channel_multiplier=1), reads=onesf.iv(), writes=ident.iv())
    P.op("pool", lambda e: e.memset(maskw[:], 0.0), writes=maskw.iv())
    P.op("pool", lambda e: e.affine_select(out=maskw[:], in_=maskw[:], pattern=[[1, 896]], compare_op=ALU.is_ge, fill=NEG, base=-384, channel_multiplier=-1), reads=maskw.iv(), writes=maskw.iv())
    P.op("dve", lambda e: e.tensor_scalar(out=negc[:], in0=ident[:, 65:66], scalar1=-1.0, scalar2=None, op0=ALU.mult), reads=ident.iv(), writes=negc.iv())
    for kind, g in enumerate([g_mix_pre, g_mix_post, g_ffn_pre, g_ffn_post]):
        for l in range(4):
            P.dma(lambda e, g=g, kind=kind, l=l: e.dma_start(out=gains[:, kind, l, :], in_=g[l].rearrange("(c p) -> p c", p=128)), writes=gains.iv(kind, l))
    for j in range(2):
        for tp in range(3):
            P.dma(lambda e, j=j, tp=tp: e.dma_start(out=wcva[:, j, tp, :], in_=w_conv_a[j, tp].rearrange("(c p) -> p c", p=128)), writes=wcva.iv(j, tp))
    for l in range(4):
        for tp in range(3):
            P.dma(lambda e, l=l, tp=tp: e.dma_start(out=wcvf[:, l, tp, :], in_=w_ffn_conv[l, tp].rearrange("(c p) -> p c", p=128)), writes=wcvf.iv(l, tp))

    def wblock(src, KC, ncol, dst=None, dst_iv=None, kp=128):
        n = KC * ncol
        if dst is None:
            bufs = st_["wbufs"]
            k = st_["w"] % len(bufs)
            st_["w"] += 1
            ap2, ivf = bufs[k]
            dst = ap2[0:kp, 0:n].rearrange("p (a b) -> p a b", a=KC)
            dst_iv = ivf(n)
        P.dma(lambda e: e.dma_start(out=dst, in_=src), writes=dst_iv, queue="pool")
        return dst, dst_iv

    def wcols(w2d, m0, ncol, KC=8):
        return w2d.rearrange("(kc p) m -> p kc m", p=128)[:, 0:KC, m0:m0 + ncol]

    def mm(out_ap, out_iv, terms):
        rd = []
        for t in terms:
            rd += t[2]

        def emit(e):
            inst = None
            for i, t in enumerate(terms):
                inst = e.matmul(out_ap, lhsT=t[0], rhs=t[1], start=(i == 0), stop=(i == len(terms) - 1))
            return inst
        P.op("pe", emit, reads=rd, writes=out_iv)

    def rms_rstd(sq_terms, n, scale, rstd, M=128):
        p = nps()
        mm(p[0:M, 0:n], piv(p, n), [(onesb[:, 0:M], a, onesb.iv() + iv) for (a, iv) in sq_terms])
        P.op("act", lambda e: e.activation(out=rstd[0:M, 0:n], in_=p[0:M, 0:n], func=AF.Sqrt, bias=EPS, scale=scale), reads=piv(p, n), writes=rstd.iv((0, n)))
        P.op("dve", lambda e: e.reciprocal(out=rstd[0:M, 0:n], in_=rstd[0:M, 0:n]), reads=rstd.iv((0, n)), writes=rstd.iv((0, n)))

    def pre_norm(kind, layer, c0, n, ho=0):
        sq = sqP; rstd = rstdP
        for t0 in range(0, n, 512):
            nn = min(512, n - t0)
            for c in range(8):
                P.op("act", lambda e, c=c, t0=t0, nn=nn: e.activation(out=sq[:, c, 0:nn], in_=h[:, c, c0 + t0:c0 + t0 + nn], func=AF.Square),
                     reads=h.iv(c, (c0 + t0, c0 + t0 + nn)), writes=sq.iv(c, (0, nn)))
            rms_rstd([(sq[:, c, 0:nn], sq.iv(c, (0, nn))) for c in range(8)], nn, 1.0 / D, rstd)
            for c in range(8):
                P.op("dve", lambda e, c=c, t0=t0, nn=nn: e.scalar_tensor_tensor(out=hn[:, c, ho + t0:ho + t0 + nn], in0=h[:, c, c0 + t0:c0 + t0 + nn], scalar=gains[:, kind, layer, c:c + 1], in1=rstd[:, 0:nn], op0=ALU.mult, op1=ALU.mult),
                     reads=h.iv(c, (c0 + t0, c0 + t0 + nn)) + gains.iv() + rstd.iv((0, nn)), writes=hn.iv(c, (ho + t0, ho + t0 + nn)))

    def post_norm_add(kind, layer, ysb, c0, n):
        sq = sqP; rstd = rstdP
        for t0 in range(0, n, 512):
            nn = min(512, n - t0)
            for c in range(8):
                P.op("act", lambda e, c=c, t0=t0, nn=nn: e.activation(out=sq[:, c, 0:nn], in_=ysb[:, c, t0:t0 + nn], func=AF.Square),
                     reads=ysb.iv(c, (t0, t0 + nn)), writes=sq.iv(c, (0, nn)))
            rms_rstd([(sq[:, c, 0:nn], sq.iv(c, (0, nn))) for c in range(8)], nn, 1.0 / D, rstd)
            for c in range(8):
                P.op("dve", lambda e, c=c, t0=t0, nn=nn: e.scalar_tensor_tensor(out=ysb[:, c, t0:t0 + nn], in0=ysb[:, c, t0:t0 + nn], scalar=gains[:, kind, layer, c:c + 1], in1=rstd[:, 0:nn], op0=ALU.mult, op1=ALU.mult),
                     reads=ysb.iv(c, (t0, t0 + nn)) + gains.iv() + rstd.iv((0, nn)), writes=ysb.iv(c, (t0, t0 + nn)))
                P.op("pool", lambda e, c=c, t0=t0, nn=nn: e.tensor_tensor(out=h[:, c, c0 + t0:c0 + t0 + nn], in0=h[:, c, c0 + t0:c0 + t0 + nn], in1=ysb[:, c, t0:t0 + nn], op=ALU.add),
                     reads=h.iv(c, (c0 + t0, c0 + t0 + nn)) + ysb.iv(c, (t0, t0 + nn)), writes=h.iv(c, (c0 + t0, c0 + t0 + nn)))

    def out_proj(w2d, rhs_list, ysb, n):
        for m_ in range(8):
            blocks = []
            kc0 = 0
            row = 0
            while kc0 < len(rhs_list):
                kp = rhs_list[kc0][2]
                kc1 = kc0
                while kc1 < len(rhs_list) and rhs_list[kc1][2] == kp and kc1 - kc0 < 8:
                    kc1 += 1
                KC = kc1 - kc0
                src = w2d[row:row + KC * kp, m_ * 128:(m_ + 1) * 128].rearrange("(kc p) m -> p kc m", p=kp)
                wv, wiv = wblock(src, KC, 128, kp=kp)
                blocks.append((kc0, kc1, wv, wiv, kp))
                row += KC * kp
                kc0 = kc1
            for t0 in range(0, n, 512):
                nn = min(512, n - t0)
                p = nps()
                terms = []
                for (a0, a1, wv, wiv, kp) in blocks:
                    for kc in range(a0, a1):
                        terms.append((wv[:, kc - a0, :], rhs_list[kc][0](t0, nn), wiv + rhs_list[kc][1](t0, nn)))
                mm(p[:, 0:nn], piv(p, nn), terms)
                P.op("act", lambda e, p=p, m_=m_, t0=t0, nn=nn: e.copy(out=ysb[:, m_, t0:t0 + nn], in_=p[:, 0:nn]), reads=piv(p, nn), writes=ysb.iv(m_, (t0, t0 + nn)))

    def even_prep(j):
        C = {}
        C["BTr"] = P.alloc([128, 16, 128], BF16, "BTr"); C["BTi"] = P.alloc([128, 16, 128], BF16, "BTi")
        C["CTr"] = P.alloc([128, 16, 128], BF16, "CTr"); C["CTi"] = P.alloc([128, 16, 128], BF16, "CTi")
        C["Dd"] = P.alloc([128, 4, 128], BF16, "Dd")
        C["pwr"] = P.alloc([128, 10, 16], F32, "pwr"); C["pwi"] = P.alloc([128, 10, 16], F32, "pwi"); C["npwi"] = P.alloc([128, 10, 16], F32, "npwi")
        C["hsr"] = P.alloc([128, 16], F32, "hsr"); C["hsi"] = P.alloc([128, 16], F32, "hsi")
        C["halo"] = P.alloc([128, 4, 2], F32, "haloA")
        C["upc"] = P.alloc([128, 9, 16], F32, "upc"); C["ups"] = P.alloc([128, 9, 16], F32, "ups"); C["nups"] = P.alloc([128, 9, 16], F32, "nups")
        C["rcol"] = P.alloc([128, 16], F32, "rcol")
        C["ones"] = P.alloc([128, 512], F32, "ones512")
        P.op("pool", lambda e: e.memset(C["ones"][:], 1.0), writes=C["ones"].iv())
        m = P.mark()
        are = P.alloc([128, 16], F32); aim = P.alloc([128, 16], F32); dt = P.alloc([128, 16], F32)
        t1 = P.alloc([128, 16], F32); t2 = P.alloc([128, 16], F32); mag = P.alloc([128, 16], F32)
        cs = P.alloc([128, 16], F32); sn = P.alloc([128, 16], F32); den = P.alloc([128, 16], F32)
        fr = P.alloc([128, 16], F32); fi = P.alloc([128, 16], F32); zr = P.alloc([128, 16], F32)
        bre = P.alloc([128, 16, 16], F32); bim = P.alloc([128, 16, 16], F32); bbr = P.alloc([128, 16, 16], F32); bbi = P.alloc([128, 16, 16], F32)
        tmp3 = P.alloc([128, 16, 16], F32)
        maskC = P.alloc([128, 4, 128], F32); natp = P.alloc([128, 4, 128], F32)
        cdup = P.alloc([128, 128], F32); cT = P.alloc([128, 128], F32); dcol = P.alloc([128, 4], F32)
        for g2 in range(2):
            sl = slice(g2 * 64, g2 * 64 + 64)
            P.dma(lambda e, sl=sl, g2=g2: e.dma_start(out=are[sl, :], in_=s5_a_re[j].rearrange("(q g) p -> g p q", g=2)[g2]), writes=are.iv())
            P.dma(lambda e, sl=sl, g2=g2: e.dma_start(out=aim[sl, :], in_=s5_a_im[j].rearrange("(q g) p -> g p q", g=2)[g2]), writes=aim.iv())
            P.dma(lambda e, sl=sl, g2=g2: e.dma_start(out=dt[sl, :], in_=s5_log_dt[j].rearrange("(q g) -> g q", g=2)[g2:g2 + 1, :].broadcast_to([64, 16])), writes=dt.iv())
            P.dma(lambda e, sl=sl, g2=g2: e.dma_start(out=bre[sl, :, :], in_=s5_b_re[j].rearrange("(q g) p c -> g p q c", g=2)[g2]), writes=bre.iv())
            P.dma(lambda e, sl=sl, g2=g2: e.dma_start(out=bim[sl, :, :], in_=s5_b_im[j].rearrange("(q g) p c -> g p q c", g=2)[g2]), writes=bim.iv())
        P.dma(lambda e: e.dma_start(out=dcol[:], in_=s5_d[j].rearrange("g c -> (g c)").rearrange("(f p) -> p f", p=128)), writes=dcol.iv())

        def A(eng, fn, rd, wr):
            P.op(eng, fn, reads=[x for v in rd for x in v.iv()], writes=[x for v in wr for x in v.iv()])
        A("act", lambda e: e.activation(out=dt[:], in_=dt[:], func=AF.Exp), [dt], [dt])
        A("dve", lambda e: e.tensor_tensor(out=t1[:], in0=are[:], in1=dt[:], op=ALU.mult), [are, dt], [t1])
        A("act", lambda e: e.activation(out=mag[:], in_=t1[:], func=AF.Exp), [t1], [mag])
        A("dve", lambda e: e.tensor_tensor(out=t1[:], in0=aim[:], in1=dt[:], op=ALU.mult), [aim, dt], [t1])
        ki = P.alloc([128, 16], I32)
        kf = P.alloc([128, 16], F32)

        def reduce_sin(dst, shift):
            A("dve", lambda e: e.tensor_scalar(out=t2[:], in0=t1[:], scalar1=shift, scalar2=1.0 / (2 * PI), op0=ALU.add, op1=ALU.mult), [t1], [t2])
            A("dve", lambda e: e.tensor_copy(out=ki[:], in_=t2[:]), [t2], [ki])
            A("dve", lambda e: e.tensor_copy(out=kf[:], in_=ki[:]), [ki], [kf])
            A("dve", lambda e: e.tensor_tensor(out=t2[:], in0=t2[:], in1=kf[:], op=ALU.subtract), [t2, kf], [t2])
            A("dve", lambda e: e.tensor_scalar(out=kf[:], in0=t2[:], scalar1=0.5, scalar2=None, op0=ALU.is_gt), [t2], [kf])
            A("dve", lambda e: e.tensor_tensor(out=t2[:], in0=t2[:], in1=kf[:], op=ALU.subtract), [t2, kf], [t2])
            A("dve", lambda e: e.tensor_scalar(out=kf[:], in0=t2[:], scalar1=-0.5, scalar2=None, op0=ALU.is_lt), [t2], [kf])
            A("dve", lambda e: e.tensor_tensor(out=t2[:], in0=t2[:], in1=kf[:], op=ALU.add), [t2, kf], [t2])
            A("act", lambda e: e.activation(out=dst[:], in_=t2[:], func=AF.Sin, scale=2 * PI), [t2], [dst])
        reduce_sin(sn, 0.0)
        reduce_sin(cs, 0.5 * PI)
        A("dve", lambda e: e.tensor_tensor(out=C["pwr"][:, 0, :], in0=cs[:], in1=mag[:], op=ALU.mult), [cs, mag], [C["pwr"]])
        A("dve", lambda e: e.tensor_tensor(out=C["pwi"][:, 0, :], in0=sn[:], in1=mag[:], op=ALU.mult), [sn, mag], [C["pwi"]])
        for l in range(1, 10):
            A("dve", lambda e, l=l: e.tensor_tensor(out=t1[:], in0=C["pwr"][:, l - 1, :], in1=C["pwr"][:, l - 1, :], op=ALU.mult), [C["pwr"]], [t1])
            A("dve", lambda e, l=l: e.tensor_tensor(out=t2[:], in0=C["pwi"][:, l - 1, :], in1=C["pwi"][:, l - 1, :], op=ALU.mult), [C["pwi"]], [t2])
            A("dve", lambda e, l=l: e.tensor_tensor(out=C["pwr"][:, l, :], in0=t1[:], in1=t2[:], op=ALU.subtract), [t1, t2], [C["pwr"]])
            A("dve", lambda e, l=l: e.scalar_tensor_tensor(out=C["pwi"][:, l, :], in0=C["pwr"][:, l - 1, :], scalar=2.0, in1=C["pwi"][:, l - 1, :], op0=ALU.mult, op1=ALU.mult), [C["pwr"], C["pwi"]], [C["pwi"]])
        A("dve", lambda e: e.tensor_scalar(out=C["npwi"][:], in0=C["pwi"][:], scalar1=-1.0, scalar2=None, op0=ALU.mult), [C["pwi"]], [C["npwi"]])
        A("dve", lambda e: e.tensor_copy(out=C["rcol"][:], in_=mag[:]), [mag], [C["rcol"]])
        A("dve", lambda e: e.tensor_copy(out=C["upc"][:, 0, :], in_=cs[:]), [cs], [C["upc"]])
        A("dve", lambda e: e.tensor_copy(out=C["ups"][:, 0, :], in_=sn[:]), [sn], [C["ups"]])
        for l in range(1, 9):
            A("dve", lambda e, l=l: e.tensor_tensor(out=t1[:], in0=C["upc"][:, l - 1, :], in1=C["upc"][:, l - 1, :], op=ALU.mult), [C["upc"]], [t1])
            A("dve", lambda e, l=l: e.tensor_tensor(out=t2[:], in0=C["ups"][:, l - 1, :], in1=C["ups"][:, l - 1, :], op=ALU.mult), [C["ups"]], [t2])
            A("dve", lambda e, l=l: e.tensor_tensor(out=C["upc"][:, l, :], in0=t1[:], in1=t2[:], op=ALU.subtract), [t1, t2], [C["upc"]])
            A("dve", lambda e, l=l: e.scalar_tensor_tensor(out=C["ups"][:, l, :], in0=C["upc"][:, l - 1, :], scalar=2.0, in1=C["ups"][:, l - 1, :], op0=ALU.mult, op1=ALU.mult), [C["upc"], C["ups"]], [C["ups"]])
        A("dve", lambda e: e.tensor_scalar(out=C["nups"][:], in0=C["ups"][:], scalar1=-1.0, scalar2=None, op0=ALU.mult), [C["ups"]], [C["nups"]])
        if j not in tabs_done:
            tabs_done.add(j)
            Tb = [[P.alloc([128, 512], F32, f"Tg{a}{b}") for b in range(2)] for a in range(2)]
            tmpg = P.alloc([128, 256], F32, "tmpg")
            for q in range(16):
                Tc, Ts = Tb[q % 2]
                P.op("dve", lambda e, Tc=Tc, q=q: e.tensor_copy(out=Tc[:, 0:1], in_=cs[:, q:q + 1]), reads=cs.iv(), writes=Tc.iv((0, 1)))
                P.op("dve", lambda e, Ts=Ts, q=q: e.tensor_copy(out=Ts[:, 0:1], in_=sn[:, q:q + 1]), reads=sn.iv(), writes=Ts.iv((0, 1)))
                for l in range(9):
                    w = 1 << l
                    cw = C["upc"][:, l, q:q + 1]; sw = C["ups"][:, l, q:q + 1]; nsw = C["nups"][:, l, q:q + 1]
                    rdp = C["upc"].iv(l) + C["ups"].iv(l) + C["nups"].iv(l)
                    P.op("dve", lambda e, Tc=Tc, w=w, cw=cw: e.tensor_scalar(out=tmpg[:, 0:w], in0=Tc[:, 0:w], scalar1=cw, scalar2=None, op0=ALU.mult), reads=Tc.iv((0, w)) + rdp, writes=tmpg.iv((0, w)))
                    P.op("dve", lambda e, Tc=Tc, Ts=Ts, w=w, nsw=nsw: e.scalar_tensor_tensor(out=Tc[:, w:2 * w], in0=Ts[:, 0:w], scalar=nsw, in1=tmpg[:, 0:w], op0=ALU.mult, op1=ALU.add), reads=Ts.iv((0, w)) + tmpg.iv((0, w)) + rdp, writes=Tc.iv((w, 2 * w)))
                    P.op("dve", lambda e, Ts=Ts, w=w, cw=cw: e.tensor_scalar(out=tmpg[:, 0:w], in0=Ts[:, 0:w], scalar1=cw, scalar2=None, op0=ALU.mult), reads=Ts.iv((0, w)) + rdp, writes=tmpg.iv((0, w)))
                    P.op("dve", lambda e, Tc=Tc, Ts=Ts, w=w, sw=sw: e.scalar_tensor_tensor(out=Ts[:, w:2 * w], in0=Tc[:, 0:w], scalar=sw, in1=tmpg[:, 0:w], op0=ALU.mult, op1=ALU.add), reads=Tc.iv((0, w)) + tmpg.iv((0, w)) + rdp, writes=Ts.iv((w, 2 * w)))
                P.dma(lambda e, Tc=Tc, q=q: e.dma_start(out=tabs[j, q, 0], in_=Tc[:]), reads=Tc.iv(), writes=tab_iv(j, q))
                P.dma(lambda e, Ts=Ts, q=q: e.dma_start(out=tabs[j, q, 1], in_=Ts[:]), reads=Ts.iv(), writes=tab_iv(j, q))
        A("dve", lambda e: e.tensor_scalar(out=zr[:], in0=C["pwr"][:, 0, :], scalar1=-1.0, scalar2=None, op0=ALU.add), [C["pwr"]], [zr])
        A("dve", lambda e: e.tensor_tensor(out=den[:], in0=are[:], in1=are[:], op=ALU.mult), [are], [den])
        A("dve", lambda e: e.tensor_tensor(out=t1[:], in0=aim[:], in1=aim[:], op=ALU.mult), [aim], [t1])
        A("dve", lambda e: e.tensor_tensor(out=den[:], in0=den[:], in1=t1[:], op=ALU.add), [den, t1], [den])
        A("dve", lambda e: e.reciprocal(out=den[:], in_=den[:]), [den], [den])
        A("dve", lambda e: e.tensor_tensor(out=t1[:], in0=zr[:], in1=are[:], op=ALU.mult), [zr, are], [t1])
        A("dve", lambda e: e.tensor_tensor(out=t2[:], in0=C["pwi"][:, 0, :], in1=aim[:], op=ALU.mult), [C["pwi"], aim], [t2])
        A("dve", lambda e: e.tensor_tensor(out=t1[:], in0=t1[:], in1=t2[:], op=ALU.add), [t1, t2], [t1])
        A("dve", lambda e: e.tensor_tensor(out=fr[:], in0=t1[:], in1=den[:], op=ALU.mult), [t1, den], [fr])
        A("dve", lambda e: e.tensor_tensor(out=t1[:], in0=C["pwi"][:, 0, :], in1=are[:], op=ALU.mult), [C["pwi"], are], [t1])
        A("dve", lambda e: e.tensor_tensor(out=t2[:], in0=zr[:], in1=aim[:], op=ALU.mult), [zr, aim], [t2])
        A("dve", lambda e: e.tensor_tensor(out=t1[:], in0=t1[:], in1=t2[:], op=ALU.subtract), [t1, t2], [t1])
        A("dve", lambda e: e.tensor_tensor(out=fi[:], in0=t1[:], in1=den[:], op=ALU.mult), [t1, den], [fi])
        frb = fr[:].unsqueeze(2).to_broadcast([128, 16, 16]); fib = fi[:].unsqueeze(2).to_broadcast([128, 16, 16])
        A("dve", lambda e: e.tensor_tensor(out=bbr[:], in0=bre[:], in1=frb, op=ALU.mult), [bre, fr], [bbr])
        A("dve", lambda e: e.tensor_tensor(out=tmp3[:], in0=bim[:], in1=fib, op=ALU.mult), [bim, fi], [tmp3])
        A("dve", lambda e: e.tensor_tensor(out=bbr[:], in0=bbr[:], in1=tmp3[:], op=ALU.subtract), [bbr, tmp3], [bbr])
        A("dve", lambda e: e.tensor_tensor(out=bbi[:], in0=bim[:], in1=frb, op=ALU.mult), [bim, fr], [bbi])
        A("dve", lambda e: e.tensor_tensor(out=tmp3[:], in0=bre[:], in1=fib, op=ALU.mult), [bre, fi], [tmp3])
        A("dve", lambda e: e.tensor_tensor(out=bbi[:], in0=bbi[:], in1=tmp3[:], op=ALU.add), [bbi, tmp3], [bbi])
        A("pool", lambda e: e.memset(maskC[:], 1.0), [], [maskC])
        for g2 in range(2):
            sl = slice(g2 * 64, g2 * 64 + 64)
            A("pool", lambda e, sl=sl, g2=g2: e.affine_select(out=maskC[sl, :, :], in_=maskC[sl, :, :], pattern=[[-32, 4], [1, 128]], compare_op=ALU.is_ge, fill=0.0, base=-16 * g2, channel_multiplier=0), [maskC], [maskC])
            A("pool", lambda e, sl=sl, g2=g2: e.affine_select(out=maskC[sl, :, :], in_=maskC[sl, :, :], pattern=[[32, 4], [-1, 128]], compare_op=ALU.is_ge, fill=0.0, base=16 * g2 + 15, channel_multiplier=0), [maskC], [maskC])
        for (bb, BT) in ((bbr, C["BTr"]), (bbi, C["BTi"])):
            for qg in range(4):
                p = nps()
                for ql in range(4):
                    q = qg * 4 + ql
                    A("dve", lambda e, bb=bb, q=q, ql=ql: e.tensor_tensor(out=natp[:, ql, :].rearrange("p (a b) -> p a b", a=8), in0=maskC[:, ql, :].rearrange("p (a b) -> p a b", a=8), in1=bb[:, q, :].unsqueeze(1).to_broadcast([128, 8, 16]), op=ALU.mult), [maskC, bb], [natp])
                    P.op("pe", lambda e, p=p, ql=ql: e.transpose(p[:, ql * 128:(ql + 1) * 128], natp[:, ql, :], ident[:]), reads=natp.iv(ql) + ident.iv(), writes=piv(p, 128, ql * 128))
                P.op("act", lambda e, p=p, BT=BT, qg=qg: e.copy(out=BT[:, qg * 4:(qg + 1) * 4, :], in_=p[:, :].rearrange("p (a b) -> p a b", a=4)), reads=piv(p, 512), writes=BT.iv((qg * 4, qg * 4 + 4)))
        for (csrc, CT, sgn) in ((s5_c_re, C["CTr"], 1.0), (s5_c_im, C["CTi"], -1.0)):
            for t in range(4):
                src = csrc[j].rearrange("g c p -> (g c) p")[t * 128:(t + 1) * 128, :]
                P.dma(lambda e, src=src: e.dma_start(out=cdup[:, 0:64], in_=src), writes=cdup.iv((0, 64)))
                P.dma(lambda e, src=src: e.dma_start(out=cdup[:, 64:128], in_=src), writes=cdup.iv((64, 128)))
                p = nps()
                P.op("pe", lambda e, p=p: e.transpose(p[:, 0:128], cdup[:], ident[:]), reads=cdup.iv() + ident.iv(), writes=piv(p, 128))
                P.op("act", lambda e, p=p, sgn=sgn: e.mul(out=cT[:], in_=p[:, 0:128], mul=sgn), reads=piv(p, 128), writes=cT.iv())
                for ql in range(4):
                    q = t * 4 + ql
                    A("dve", lambda e, CT=CT, q=q, ql=ql: e.tensor_tensor(out=CT[:, q, :], in0=cT[:], in1=maskC[:, ql, :], op=ALU.mult), [cT, maskC], [CT])
        for fc in range(4):
            A("dve", lambda e, fc=fc: e.tensor_scalar(out=C["Dd"][:, fc, :], in0=ident[:], scalar1=dcol[:, fc:fc + 1], scalar2=None, op0=ALU.mult), [ident, dcol], [C["Dd"]])
        P.release(m)
        return C

    def even_mixer(j, layer, smp, seqi, T_):
        lm = P.mark()
        C = even_prep(j)
        W = w_even_in[j]
        if smp:
            for g2 in range(2):
                sl = slice(g2 * 64, g2 * 64 + 64)
                P.dma(lambda e, sl=sl, g2=g2: e.dma_start(out=C["hsr"][sl, :], in_=sre[j].rearrange("(q g) p -> g p q", g=2)[g2]), writes=C["hsr"].iv())
                P.dma(lambda e, sl=sl, g2=g2: e.dma_start(out=C["hsi"][sl, :], in_=sim[j].rearrange("(q g) p -> g p q", g=2)[g2]), writes=C["hsi"].iv())
            for r_ in range(2):
                P.dma(lambda e, r_=r_: e.dma_start(out=C["halo"][:, :, r_], in_=cconv[j, r_].rearrange("(c p) -> p c", p=128)), writes=C["halo"].iv())
        else:
            P.op("pool", lambda e: e.memset(C["hsr"][:], 0.0), writes=C["hsr"].iv())
            P.op("pool", lambda e: e.memset(C["hsi"][:], 0.0), writes=C["hsi"].iv())
            P.op("pool", lambda e: e.memset(C["halo"][:], 0.0), writes=C["halo"].iv())
        nlev = int(math.log2(min(512, T_)))
        def st_body(c0):
            n = min(512, T_ - c0)
            sm = P.mark()
            ho = ((c0 // 512) % 2) * 512
            pre_norm(0, layer, c0, n, ho)
            yacat = P.alloc([128, 8, n], BF16, "yacat"); ubf = P.alloc([128, 4, n], BF16, "ubf")
            tgc = P.alloc([128, n], F32, "tgc"); prod = P.alloc([128, n + 2], F32, "prod"); cvb = P.alloc([128, n], F32, "cvb")
            wre = P.alloc([128, n], F32, "wre"); wim = P.alloc([128, n], F32, "wim"); gre = P.alloc([128, n], F32, "gre"); gim = P.alloc([128, n], F32, "gim")
            TB = [[P.alloc([128, n], F32, f"TB{a}{b}") for b in range(2)] for a in range(2)]
            rtab = P.alloc([128, n], F32, "rtab")
            hri = P.alloc([128, 4, 2, n], BF16, "hri"); gB = P.alloc([128, 4, n], BF16, "gB")
            sg = P.alloc([128, n], F32, "sg"); ysb = P.alloc([128, 8, n], F32, "ysb"); cc = P.alloc([128, 2, 16], F32, "cc")
            tt = P.alloc([128, 2, 16], F32, "tt")

            def proj_chunk(fch):
                wv, wiv = wblock(wcols(W, fch * 128, 128), 8, 128)
                p = nps()
                mm(p[:, 0:n], piv(p, n), [(wv[:, kc, :], hn[:, kc, ho:ho + n], wiv + hn.iv(kc, (ho, ho + n))) for kc in range(8)])
                return p
            for i in range(4):
                p = proj_chunk(12 + i)
                P.op("act", lambda e, p=p, i=i: e.copy(out=ubf[:, i, :], in_=p[:, 0:n]), reads=piv(p, n), writes=ubf.iv(i))

            def do_gc(i):
                p = proj_chunk(4 + i)
                P.op("act", lambda e, p=p: e.copy(out=tgc[:], in_=p[:, 0:n]), reads=piv(p, n), writes=tgc.iv())

            def do_xa(i):
                p = proj_chunk(8 + i)
                P.op("dve", lambda e, i=i: e.tensor_copy(out=prod[:, 0:2], in_=C["halo"][:, i, :]), reads=C["halo"].iv(i), writes=prod.iv((0, 2)))
                P.op("dve", lambda e, p=p: e.tensor_tensor(out=prod[:, 2:n + 2], in0=tgc[:], in1=p[:, 0:n], op=ALU.mult), reads=tgc.iv() + piv(p, n), writes=prod.iv((2, n + 2)))
                P.op("dve", lambda e, i=i: e.tensor_copy(out=C["halo"][:, i, :], in_=prod[:, n:n + 2]), reads=prod.iv((n, n + 2)), writes=C["halo"].iv(i))
                P.op("dve", lambda e, i=i: e.tensor_scalar(out=cvb[:], in0=prod[:, 0:n], scalar1=wcva[:, j, 0, i:i + 1], scalar2=None, op0=ALU.mult), reads=prod.iv((0, n)) + wcva.iv(), writes=cvb.iv())
                P.op("dve", lambda e, i=i: e.scalar_tensor_tensor(out=cvb[:], in0=prod[:, 1:n + 1], scalar=wcva[:, j, 1, i:i + 1], in1=cvb[:], op0=ALU.mult, op1=ALU.add), reads=prod.iv((1, n + 1)) + wcva.iv() + cvb.iv(), writes=cvb.iv())
                P.op("dve", lambda e, i=i: e.scalar_tensor_tensor(out=cvb[:], in0=prod[:, 2:n + 2], scalar=wcva[:, j, 2, i:i + 1], in1=cvb[:], op0=ALU.mult, op1=ALU.add), reads=prod.iv((2, n + 2)) + wcva.iv() + cvb.iv(), writes=cvb.iv())

            def do_gb(i):
                p = proj_chunk(i)
                P.op("dve", lambda e, p=p, i=i: e.tensor_tensor(out=yacat[:, i, :], in0=cvb[:], in1=p[:, 0:n], op=ALU.mult), reads=cvb.iv() + piv(p, n), writes=yacat.iv(i))
            side = []
            for i in range(4):
                side += [(lambda i=i: do_gc(i)), (lambda i=i: do_xa(i)), (lambda i=i: do_gb(i))]
            t1a = ysb[:, 0, :]; t1i = ysb.iv(0); t2a = ysb[:, 1, :]; t2i = ysb.iv(1); t3a = ysb[:, 2, :]; t3i = ysb.iv(2); t4v = ysb[:, 3, :]; t4iv = ysb.iv(3)
            for fc in range(4):
                py = nps()
                yterms = []
                for ql in range(4):
                    q = fc * 4 + ql
                    Tc, Ts = TB[q % 2]
                    P.dma(lambda e, Tc=Tc, q=q: e.dma_start(out=Tc[:], in_=tabs[j, q, 0][:, 0:n]), reads=tab_iv(j, q), writes=Tc.iv())
                    P.dma(lambda e, Ts=Ts, q=q: e.dma_start(out=Ts[:], in_=tabs[j, q, 1][:, 0:n]), reads=tab_iv(j, q), writes=Ts.iv())
                    P.op("act", lambda e, q=q: e.activation(out=rtab[:], in_=C["ones"][:, 0:n], func=AF.Copy, scale=C["rcol"][:, q:q + 1]), reads=C["ones"].iv() + C["rcol"].iv(), writes=rtab.iv())
                    pb = nps()
                    mm(pb[:, 0:n], piv(pb, n), [(C["BTr"][:, q, :], ubf[:, fc, :], C["BTr"].iv(q) + ubf.iv(fc))])
                    pb2 = nps()
                    mm(pb2[:, 0:n], piv(pb2, n), [(C["BTi"][:, q, :], ubf[:, fc, :], C["BTi"].iv(q) + ubf.iv(fc))])

                    def TT(out, a, b, op, rd, wr):
                        P.op("dve", lambda e: e.tensor_tensor(out=out, in0=a, in1=b, op=op), reads=rd, writes=wr)
                    TT(t1a, Tc[:], pb[:, 0:n], ALU.mult, Tc.iv() + piv(pb, n), t1i)
                    TT(t2a, Ts[:], pb2[:, 0:n], ALU.mult, Ts.iv() + piv(pb2, n), t2i)
                    TT(t3a, Tc[:], pb2[:, 0:n], ALU.mult, Tc.iv() + piv(pb2, n), t3i)
                    TT(t4v, Ts[:], pb[:, 0:n], ALU.mult, Ts.iv() + piv(pb, n), t4iv)
                    TT(wre[:], t1a, t2a, ALU.add, t1i + t2i, wre.iv())
                    TT(wim[:], t3a, t4v, ALU.subtract, t3i + t4iv, wim.iv())
                    P.op("dve", lambda e, q=q: e.tensor_tensor_scan(out=gre[:], data0=rtab[:], data1=wre[:], initial=C["hsr"][:, q:q + 1], op0=ALU.mult, op1=ALU.add), reads=rtab.iv() + wre.iv() + C["hsr"].iv(), writes=gre.iv())
                    P.op("dve", lambda e, q=q: e.tensor_tensor_scan(out=gim[:], data0=rtab[:], data1=wim[:], initial=C["hsi"][:, q:q + 1], op0=ALU.mult, op1=ALU.add), reads=rtab.iv() + wim.iv() + C["hsi"].iv(), writes=gim.iv())
                    TT(t1a, Tc[:], gre[:], ALU.mult, Tc.iv() + gre.iv(), t1i)
                    TT(t3a, Ts[:], gre[:], ALU.mult, Ts.iv() + gre.iv(), t3i)
                    TT(t2a, Ts[:], gim[:], ALU.mult, Ts.iv() + gim.iv(), t2i)
                    TT(t4v, Tc[:], gim[:], ALU.mult, Tc.iv() + gim.iv(), t4iv)
                    TT(wre[:], t1a, t2a, ALU.subtract, t1i + t2i, wre.iv())
                    TT(wim[:], t3a, t4v, ALU.add, t3i + t4iv, wim.iv())
                    P.op("act", lambda e, ql=ql: e.copy(out=hri[:, ql, 0, :], in_=wre[:]), reads=wre.iv(), writes=hri.iv(ql, 0))
                    P.op("act", lambda e, ql=ql: e.copy(out=hri[:, ql, 1, :], in_=wim[:]), reads=wim.iv(), writes=hri.iv(ql, 1))
                    P.op("dve", lambda e, q=q: e.tensor_copy(out=C["hsr"][:, q:q + 1], in_=wre[:, n - 1:n]), reads=wre.iv((n - 1, n)), writes=C["hsr"].iv())
                    P.op("dve", lambda e, q=q: e.tensor_copy(out=C["hsi"][:, q:q + 1], in_=wim[:, n - 1:n]), reads=wim.iv((n - 1, n)), writes=C["hsi"].iv())
                    yterms.append((C["CTr"][:, q, :], hri[:, ql, 0, :], C["CTr"].iv(q) + hri.iv(ql, 0)))
                    yterms.append((C["CTi"][:, q, :], hri[:, ql, 1, :], C["CTi"].iv(q) + hri.iv(ql, 1)))
                    if side:
                        side.pop(0)()
                yterms.append((C["Dd"][:, fc, :], ubf[:, fc, :], C["Dd"].iv(fc) + ubf.iv(fc)))
                mm(py[:, 0:n], piv(py, n), yterms)
                P.op("act", lambda e, py=py, fc=fc: e.activation(out=gB[:, fc, :], in_=py[:, 0:n], func=AF.Gelu_apprx_tanh), reads=piv(py, n), writes=gB.iv(fc))
            while side:
                side.pop(0)()
            for m_ in range(4):
                wv, wiv = wblock(wcols(w_glu[j], m_ * 128, 128, KC=4), 4, 128)
                p = nps()
                mm(p[:, 0:n], piv(p, n), [(wv[:, kc, :], gB[:, kc, :], wiv + gB.iv(kc)) for kc in range(4)])
                P.op("act", lambda e, p=p: e.activation(out=sg[:], in_=p[:, 0:n], func=AF.Sigmoid), reads=piv(p, n), writes=sg.iv())
                P.op("dve", lambda e, m_=m_: e.tensor_tensor(out=yacat[:, 4 + m_, :], in0=gB[:, m_, :], in1=sg[:], op=ALU.mult), reads=gB.iv(m_) + sg.iv(), writes=yacat.iv(4 + m_))
            if DBG and c0 == 0 and layer == 0 and not smp:
                P.dma(lambda e: e.dma_start(out=dbg_h0, in_=h[:, :, 0:512]), reads=h.iv())
                for kc in range(8):
                    P.op("dve", lambda e, kc=kc: e.tensor_copy(out=ysb[:, kc, :], in_=yacat[:, kc, :]), reads=yacat.iv(kc), writes=ysb.iv(kc))
                P.dma(lambda e: e.dma_start(out=dbg_y, in_=ysb[:]), reads=ysb.iv())
            out_proj(w_even_out[j], [((lambda t0, nn, kc=kc: yacat[:, kc, t0:t0 + nn]), (lambda t0, nn, kc=kc: yacat.iv(kc, (t0, t0 + nn))), 128) for kc in range(8)], ysb, n)
            post_norm_add(1, layer, ysb, c0, n)
            if DBG and c0 == 0 and layer == 0 and not smp:
                P.dma(lambda e: e.dma_start(out=dbg_h, in_=h[:, :, 0:512]), reads=h.iv())
            P.release(sm)
        for c0_ in range(0, T_, 512):
            st_body(c0_)
        oc = o_conv_s[j] if smp else o_conv_p[j, seqi]
        for r_ in range(2):
            P.dma(lambda e, r_=r_: e.dma_start(out=oc[r_].rearrange("(c p) -> p c", p=128), in_=C["halo"][:, :, r_]), reads=C["halo"].iv())
        for (hs, od) in ((C["hsr"], o_sre_s[j] if smp else o_sre_p[j, seqi]), (C["hsi"], o_sim_s[j] if smp else o_sim_p[j, seqi])):
            for g2 in range(2):
                sl = slice(g2 * 64, g2 * 64 + 64)
                P.dma(lambda e, hs=hs, od=od, sl=sl, g2=g2: e.dma_start(out=od.rearrange("(q g) p -> g p q", g=2)[g2], in_=hs[sl, :]), reads=hs.iv())
        P.release(lm)

    def ffn(layer, smp, seqi, T_):
        lm = P.mark()
        halo = P.alloc([128, 44, 2], F32, "haloF")
        if smp:
            for r_ in range(2):
                P.dma(lambda e, r_=r_: e.dma_start(out=halo[:, :, r_], in_=cffn[layer, r_].rearrange("(c p) -> p c", p=128)), writes=halo.iv())
        else:
            P.op("pool", lambda e: e.memset(halo[:], 0.0), writes=halo.iv())
        Wup = w_ffn_up[layer]
        def st_body(c0):
            n = min(1024, T_ - c0)
            cts = [(t0, min(512, n - t0)) for t0 in range(0, n, 512)]
            sm = P.mark()
            pre_norm(2, layer, c0, n)
            act = P.alloc([128, 22, n], BF16, "act")
            ysb = P.alloc([128, 8, n], F32, "ysbF")
            m2 = P.mark()
            U = [[P.alloc([128, 514], F32, f"U{a}{b}") for b in range(2)] for a in range(2)]
            cvt2 = [[P.alloc([128, 512], F32, f"cvt{a}{b}") for a in range(2)] for b in range(2)]
            gg2 = [P.alloc([128, 512], F32, f"gg{b}") for b in range(2)]
            xw = [P.alloc([128, 1024], BF16, f"xw{b}") for b in range(2)]
            st_["wbufs"] = list(base_wbufs) + [(xw[b].a, (lambda n, b=b: xw[b].iv((0, n)))) for b in range(2)]
            it = [0]
            for c in range(22):
                wg, wgiv = wblock(wcols(Wup, c * 128, 128), 8, 128)
                wvv, wviv = wblock(wcols(Wup, DFF + c * 128, 128), 8, 128)
                for (t0, nn) in cts:
                    cvt = cvt2[it[0] % 2]; gg = gg2[it[0] % 2]
                    it[0] += 1
                    for a, (wv, wiv, ch) in enumerate(((wg, wgiv, c), (wvv, wviv, 22 + c))):
                        p = nps()
                        mm(p[:, 0:nn], piv(p, nn), [(wv[:, kc, :], hn[:, kc, t0:t0 + nn], wiv + hn.iv(kc, (t0, t0 + nn))) for kc in range(8)])
                        Ub = U[a][(t0 // 512) % 2]
                        P.op("act", lambda e, Ub=Ub, ch=ch: e.copy(out=Ub[:, 0:2], in_=halo[:, ch, :]), reads=halo.iv(ch), writes=Ub.iv((0, 2)))
                        P.op("act", lambda e, Ub=Ub, p=p, nn=nn: e.copy(out=Ub[:, 2:nn + 2], in_=p[:, 0:nn]), reads=piv(p, nn), writes=Ub.iv((2, nn + 2)))
                        P.op("act", lambda e, Ub=Ub, ch=ch, nn=nn: e.copy(out=halo[:, ch, :], in_=Ub[:, nn:nn + 2]), reads=Ub.iv((nn, nn + 2)), writes=halo.iv(ch))
                        cb = cvt[a]
                        eng = "dve"
                        P.op("act", lambda e, Ub=Ub, cb=cb, ch=ch, nn=nn: e.activation(out=cb[:, 0:nn], in_=Ub[:, 0:nn], func=AF.Copy, scale=wcvf[:, layer, 0, ch:ch + 1]), reads=Ub.iv((0, nn)) + wcvf.iv(), writes=cb.iv((0, nn)))
                        P.op(eng, lambda e, Ub=Ub, cb=cb, ch=ch, nn=nn: e.scalar_tensor_tensor(out=cb[:, 0:nn], in0=Ub[:, 1:nn + 1], scalar=wcvf[:, layer, 1, ch:ch + 1], in1=cb[:, 0:nn], op0=ALU.mult, op1=ALU.add), reads=Ub.iv((1, nn + 1)) + wcvf.iv() + cb.iv((0, nn)), writes=cb.iv((0, nn)))
                        P.op(eng, lambda e, Ub=Ub, cb=cb, ch=ch, nn=nn: e.scalar_tensor_tensor(out=cb[:, 0:nn], in0=Ub[:, 2:nn + 2], scalar=wcvf[:, layer, 2, ch:ch + 1], in1=cb[:, 0:nn], op0=ALU.mult, op1=ALU.add), reads=Ub.iv((2, nn + 2)) + wcvf.iv() + cb.iv((0, nn)), writes=cb.iv((0, nn)))
                    P.op("act", lambda e, nn=nn, gg=gg, cvt=cvt: e.activation(out=gg[:, 0:nn], in_=cvt[0][:, 0:nn], func=AF.Gelu_apprx_tanh), reads=cvt[0].iv((0, nn)), writes=gg.iv((0, nn)))
                    P.op("dve", lambda e, c=c, t0=t0, nn=nn, gg=gg, cvt=cvt: e.tensor_tensor(out=act[:, c, t0:t0 + nn], in0=gg[:, 0:nn], in1=cvt[1][:, 0:nn], op=ALU.mult), reads=gg.iv((0, nn)) + cvt[1].iv((0, nn)), writes=act.iv(c, (t0, t0 + nn)))
            st_["wbufs"] = list(base_wbufs)
            P.release(m2)
            xw2 = [P.alloc([128, 1024], BF16, f"xw2{b}") for b in range(4)]
            st_["wbufs"] = list(base_wbufs) + [(xw2[b].a, (lambda n, b=b: xw2[b].iv((0, n)))) for b in range(4)]
            out_proj(w_ffn_down[layer], [((lambda t0, nn, kc=kc: act[:, kc, t0:t0 + nn]), (lambda t0, nn, kc=kc: act.iv(kc, (t0, t0 + nn))), 128) for kc in range(22)], ysb, n)
            post_norm_add(3, layer, ysb, c0, n)
            st_["wbufs"] = list(base_wbufs)
            P.release(sm)
            sm = P.mark()
            peT = P.alloc([128, 2, n], BF16, "peT")
            xin = [P.alloc([128, 256], F32, f"xin{a}") for a in range(2)]
            sig = P.alloc([128, 512], F32, "sig")
            wpl = P.alloc([128, 2, 1024], BF16, "wpl")
            xw3 = [P.alloc([128, 1024], BF16, f"xw3{b}") for b in range(4)]
            st_["wbufs"] = list(base_wbufs) + [(xw3[b].a, (lambda n, b=b: xw3[b].iv((0, n)))) for b in range(4)]
            pesrc = psm[layer] if smp else pp[layer, seqi]
            for ti, t0 in enumerate(range(0, n, 128)):
                nt = min(128, n - t0)
                xb = xin[ti % 2]
                P.dma(lambda e, xb=xb, t0=t0, nt=nt: e.dma_start(out=xb[0:nt, :], in_=pesrc[c0 + t0:c0 + t0 + nt, :]), writes=xb.iv())
                p = nps()
                for dch in range(2):
                    P.op("pe", lambda e, p=p, xb=xb, dch=dch, nt=nt: e.transpose(p[:, dch * 128:dch * 128 + nt], xb[0:nt, dch * 128:(dch + 1) * 128], ident[0:nt, 0:nt]), reads=xb.iv() + ident.iv(), writes=piv(p, nt, dch * 128))
                    P.op("act", lambda e, p=p, dch=dch, t0=t0, nt=nt: e.copy(out=peT[:, dch, t0:t0 + nt], in_=p[:, dch * 128:dch * 128 + nt]), reads=piv(p, nt, dch * 128), writes=peT.iv(dch, (t0, t0 + nt)))
            for c in range(8):
                for (t0, nn) in cts:
                    P.op("act", lambda e, c=c, t0=t0, nn=nn: e.copy(out=hn[:, c, t0:t0 + nn], in_=h[:, c, c0 + t0:c0 + t0 + nn]), reads=h.iv(c, (c0 + t0, c0 + t0 + nn)), writes=hn.iv(c, (t0, t0 + nn)))
            for hh in range(2):
                wblock(w_ple[layer].rearrange("(kc p) m -> p kc m", p=128)[:, :, hh * 512:(hh + 1) * 512], 2, 512, dst=wpl[:, :, hh * 512:(hh + 1) * 512], dst_iv=wpl.iv(None, (hh * 512, hh * 512 + 512)))
            for m_ in range(8):
                wv, wiv = wblock(wcols(w_ple_gate[layer], m_ * 128, 128), 8, 128)
                for (t0, nn) in cts:
                    p = nps()
                    mm(p[:, 0:nn], piv(p, nn), [(wv[:, kc, :], hn[:, kc, t0:t0 + nn], wiv + hn.iv(kc, (t0, t0 + nn))) for kc in range(8)])
                    p2 = nps()
                    mm(p2[:, 0:nn], piv(p2, nn), [(wpl[:, dch, m_ * 128:(m_ + 1) * 128], peT[:, dch, t0:t0 + nn], wpl.iv(dch, (m_ * 128, m_ * 128 + 128)) + peT.iv(dch, (t0, t0 + nn))) for dch in range(2)])
                    P.op("act", lambda e, p=p, nn=nn: e.activation(out=sig[:, 0:nn], in_=p[:, 0:nn], func=AF.Sigmoid), reads=piv(p, nn), writes=sig.iv((0, nn)))
                    P.op("dve", lambda e, p2=p2, nn=nn: e.tensor_tensor(out=sig[:, 0:nn], in0=sig[:, 0:nn], in1=p2[:, 0:nn], op=ALU.mult), reads=sig.iv((0, nn)) + piv(p2, nn), writes=sig.iv((0, nn)))
                    P.op("dve", lambda e, m_=m_, t0=t0, nn=nn: e.tensor_tensor(out=h[:, m_, c0 + t0:c0 + t0 + nn], in0=h[:, m_, c0 + t0:c0 + t0 + nn], in1=sig[:, 0:nn], op=ALU.add), reads=h.iv(m_, (c0 + t0, c0 + t0 + nn)) + sig.iv((0, nn)), writes=h.iv(m_, (c0 + t0, c0 + t0 + nn)))
            st_["wbufs"] = list(base_wbufs)
            P.release(sm)
        for c0_ in range(0, T_, 1024):
            st_body(c0_)
        of = o_ffn_s[layer] if smp else o_ffn_p[layer, seqi]
        for r_ in range(2):
            P.dma(lambda e, r_=r_: e.dma_start(out=of[r_].rearrange("(c p) -> p c", p=128), in_=halo[:, :, r_]), reads=halo.iv())
        P.release(lm)

    def load_x(smp, seqi, T_):
        m = P.mark()
        xin = [P.alloc([128, D], F32, f"xl{a}") for a in range(2)]
        src = xs if smp else xp[seqi]
        for ti, t0 in enumerate(range(0, T_, 128)):
            nt = min(128, T_ - t0)
            xb = xin[ti % 2]
            P.dma(lambda e, xb=xb, t0=t0, nt=nt: e.dma_start(out=xb[0:nt, :], in_=src[t0:t0 + nt, :]), writes=xb.iv())
            for c in range(8):
                p = nps()
                P.op("pe", lambda e, p=p, xb=xb, c=c, nt=nt: e.transpose(p[:, 0:nt], xb[0:nt, c * 128:(c + 1) * 128], ident[0:nt, 0:nt]), reads=xb.iv((c * 128, c * 128 + 128)) + ident.iv(), writes=piv(p, nt))
                P.op("act" if c % 2 else "dve", (lambda e, p=p, c=c, t0=t0, nt=nt: e.copy(out=h[:, c, t0:t0 + nt], in_=p[:, 0:nt])) if c % 2 else (lambda e, p=p, c=c, t0=t0, nt=nt: e.tensor_copy(out=h[:, c, t0:t0 + nt], in_=p[:, 0:nt])), reads=piv(p, nt), writes=h.iv(c, (t0, t0 + nt)))
        P.release(m)

    def store_y(smp, seqi, T_):
        m = P.mark()
        yo = [P.alloc([128, D], F32, f"yo{a}") for a in range(2)]
        dst = ys if smp else yp[seqi]
        for ti, t0 in enumerate(range(0, T_, 128)):
            nt = min(128, T_ - t0)
            yb = yo[ti % 2]
            for c in range(8):
                p = nps()
                P.op("pe", lambda e, p=p, c=c, t0=t0, nt=nt: e.transpose(p[0:nt, 0:128], h[:, c, t0:t0 + nt], ident[:]), reads=h.iv(c, (t0, t0 + nt)) + ident.iv(), writes=piv(p, 128))
                P.op("act" if c % 2 else "dve", (lambda e, p=p, c=c, yb=yb, nt=nt: e.copy(out=yb[0:nt, c * 128:(c + 1) * 128], in_=p[0:nt, 0:128])) if c % 2 else (lambda e, p=p, c=c, yb=yb, nt=nt: e.tensor_copy(out=yb[0:nt, c * 128:(c + 1) * 128], in_=p[0:nt, 0:128])), reads=piv(p, 128), writes=yb.iv((c * 128, c * 128 + 128)))
            P.dma(lambda e, yb=yb, t0=t0, nt=nt: e.dma_start(out=dst[t0:t0 + nt, :], in_=yb[0:nt, :]), reads=yb.iv())
        P.release(m)

    odd_mixer = make_odd(locals())

    for (kind, seqi) in passes:
        smp = kind == "s"
        T_ = SL if smp else SEQ
        if smp:
            pm_ = P.mark()
            xws = [P.alloc([128, 1024], BF16, f"xws{b}") for b in range(16)]
            base_wbufs.extend([(xws[b].a, (lambda n, b=b: xws[b].iv((0, n)))) for b in range(16)])
            st_["wbufs"] = list(base_wbufs)
        load_x(smp, seqi, T_)
        for layer in range(nlayers):
            if layer % 2 == 0:
                even_mixer(layer // 2, layer, smp, seqi, T_)
            else:
                st_["nrot"] = 6
                odd_mixer(layer // 2, layer, smp, seqi, T_)
                st_["nrot"] = 8
            ffn(layer, smp, seqi, T_)
        store_y(smp, seqi, T_)
    with nc.allow_non_contiguous_dma(reason="small param/state layouts"):
        P.run()
    return P


WNAMES = ["g_mix_pre", "g_mix_post", "g_ffn_pre", "g_ffn_post", "w_even_in", "w_conv_a", "s5_a_re", "s5_a_im", "s5_log_dt",
          "s5_b_re", "s5_b_im", "s5_c_re", "s5_c_im", "s5_d", "w_glu", "w_even_out", "w_odd_in", "b_forget", "w_spatial",
          "b_spatial", "g_gmlp_v", "w_odd_out", "w_ffn_up", "w_ffn_conv", "w_ffn_down", "w_ple", "w_ple_gate"]


def make_in_maps(inp):
    f = lambda a: np.ascontiguousarray(np.asarray(a, dtype=np.float32))
    W = {k: f(inp[k]) for k in WNAMES}
    maps = []
    for c in range(8):
        m = dict(W)
        m["xp"] = f(inp["x_prompt"][2 * c:2 * c + 2]); m["xs"] = f(inp["x_sample"][c])
        m["pp"] = f(inp["p_prompt"][:, 2 * c:2 * c + 2]); m["psm"] = f(inp["p_sample"][:, c])
        m["cconv"] = f(inp["cache_conv_a"][:, c]); m["sre"] = f(inp["state_ssm_re"][:, c]); m["sim"] = f(inp["state_ssm_im"][:, c])
        m["ck"] = f(np.asarray(inp["cache_k"])[:, c].reshape(2, 1024, 512)); m["cv"] = f(np.asarray(inp["cache_v"])[:, c].reshape(2, 1024, 512))
        m["clf"] = f(inp["cache_logf"][:, c]); m["cffn"] = f(inp["cache_ffn_conv"][:, c])
        maps.append(m)
    return maps


def gather(res):
    R = res
    cat = lambda name, ax: np.concatenate([r[name] for r in R], axis=ax)
    stk = lambda name, ax: np.stack([r[name] for r in R], axis=ax)
    y_prompt = cat("yp", 0)
    y_sample = stk("ys", 0)
    conv_p = cat("o_conv_p", 1); sre_p = cat("o_sre_p", 1); sim_p = cat("o_sim_p", 1)
    k_p = cat("o_k_p", 1).reshape(2, 16, 2048, 8, 64); v_p = cat("o_v_p", 1).reshape(2, 16, 2048, 8, 64)
    lf_p = cat("o_lf_p", 1); ffn_p = cat("o_ffn_p", 1)
    conv_s = stk("o_conv_s", 1); sre_s = stk("o_sre_s", 1); sim_s = stk("o_sim_s", 1)
    k_s = stk("o_k_s", 1).reshape(2, 8, 32, 8, 64); v_s = stk("o_v_s", 1).reshape(2, 8, 32, 8, 64)
    lf_s = stk("o_lf_s", 1); gv_s = stk("o_gv_s", 1); ffn_s = stk("o_ffn_s", 1)
    outs = (y_prompt, y_sample, conv_p, sre_p, sim_p, k_p, v_p, lf_p, ffn_p, conv_s, sre_s, sim_s, k_s, v_s, lf_s, gv_s, ffn_s)
    return tuple(np.ascontiguousarray(o.astype(np.float32)) for o in outs)


def kernel(**inputs):
    nc = bass.Bass("TRN2", target_bir_lowering=False)
    build(nc)
    maps = make_in_maps(inputs)
    res = run_bass_kernel_spmd(nc, maps, core_ids=list(range(8)))
    return gather(res.results)
```
